# Optimizing a Trainium2 kernel written in Bass

```python
import math
import jax, jax.numpy as jnp
from jax import lax
import numpy as np

D_MODEL = 1024
BATCH = 2
SEQ = 8192
DEPTH = 2

GRID_W = 64
CTX_LEN = 256
SSM_WIDTH = 3 * D_MODEL // 8
SSM_GROUP = 16
SSM_GROUPS = SSM_WIDTH // SSM_GROUP
SSM_STATE = 64
FNET_WIDTH = D_MODEL // 4
FNET_GROUPS = 4
FNET_GROUP = FNET_WIDTH // FNET_GROUPS
HEAD_DIM = 64
NA_WIDTH = 3 * D_MODEL // 8
NA_HEADS = NA_WIDTH // HEAD_DIM
IN_WIDTH = SSM_WIDTH + FNET_WIDTH + 3 * NA_WIDTH
NA_ROWS_MAX = 8
NA_COLS = 16
RPB_ROWS = 2 * NA_ROWS_MAX - 1
RPB_COLS = 2 * NA_COLS - 1
ROPE_BASE = 10000.0
D_FF = -(-8 * D_MODEL // (3 * 256)) * 256
EPS = 1e-6

kernel_name = "hybrid_s5_fnet_natten_dit_block"


def rmsnorm(x, g):
    x32 = x.astype(jnp.float32)
    y = x32 * lax.rsqrt(jnp.mean(x32 * x32, axis=-1, keepdims=True) + EPS)
    return (y * g.astype(jnp.float32)).astype(x.dtype)


def modulate(x, shift, scale):
    return x * (1 + scale) + shift


def axial_rope(x, row, col):
    half = HEAD_DIM // 2
    quarter = half // 2
    freqs = ROPE_BASE ** (-jnp.arange(quarter, dtype=jnp.float32) / quarter)

    def rot(xs, pos):
        ang = pos.astype(jnp.float32)[:, None] * freqs
        cos = jnp.cos(ang)[None, :, None, :]
        sin = jnp.sin(ang)[None, :, None, :]
        x1, x2 = xs[..., :quarter], xs[..., quarter:]
        return jnp.concatenate([x1 * cos - x2 * sin, x2 * cos + x1 * sin], axis=-1)

    x32 = x.astype(jnp.float32)
    return jnp.concatenate([rot(x32[..., :half], row), rot(x32[..., half:], col)], axis=-1).astype(x.dtype)


def ssm_discretise(a_re, a_im, log_dt, b_re, b_im):
    lam = lax.complex(a_re.astype(jnp.float32), a_im.astype(jnp.float32))
    dt = jnp.exp(log_dt.astype(jnp.float32))[:, None]
    lam_bar = jnp.exp(lam * dt)
    bmat = lax.complex(b_re.astype(jnp.float32), b_im.astype(jnp.float32))
    b_bar = ((lam_bar - 1) / lam)[..., None] * bmat
    return lam_bar, b_bar


def ssm_scan(lam_bar, bu, h0, reverse):
    if h0 is not None:
        edge = -1 if reverse else 0
        bu = bu.at[:, edge].add(lam_bar * h0)
    a = jnp.broadcast_to(lam_bar, bu.shape)

    def combine(e1, e2):
        a1, b1 = e1
        a2, b2 = e2
        return a1 * a2, a2 * b1 + b2

    _, h = lax.associative_scan(combine, (a, bu), axis=1, reverse=reverse)
    return h


def s5_mixer(u, uc, a_re, a_im, log_dt, b_re, b_im, c_re, c_im, d_skip, w_glu, need_ctx_out):
    B, L, _ = u.shape
    Lc = uc.shape[1]
    u32 = u.astype(jnp.float32)
    uc32 = uc.astype(jnp.float32)
    ug = u32.reshape(B, L, SSM_GROUPS, SSM_GROUP).astype(jnp.complex64)
    ucg = uc32.reshape(B, Lc, SSM_GROUPS, SSM_GROUP).astype(jnp.complex64)
    dsk = d_skip.astype(jnp.float32)
    y = u32 * dsk
    yc = uc32 * dsk if need_ctx_out else None
    for direction in range(2):
        rev = direction == 1
        lam_bar, b_bar = ssm_discretise(a_re[direction], a_im[direction], log_dt[direction],
                                        b_re[direction], b_im[direction])
        cmat = lax.complex(c_re[direction].astype(jnp.float32), c_im[direction].astype(jnp.float32))
        hc = ssm_scan(lam_bar, jnp.einsum('gph,blgh->blgp', b_bar, ucg), None, rev)
        h0 = hc[:, 0] if rev else hc[:, -1]
        h = ssm_scan(lam_bar, jnp.einsum('gph,blgh->blgp', b_bar, ug), h0, rev)
        y = y + jnp.einsum('ghp,blgp->blgh', cmat, h).real.reshape(B, L, SSM_WIDTH)
        if need_ctx_out:
            yc = yc + jnp.einsum('ghp,blgp->blgh', cmat, hc).real.reshape(B, Lc, SSM_WIDTH)

    def glu(z):
        z = jax.nn.gelu(z).astype(u.dtype)
        return z * jax.nn.sigmoid(z @ w_glu)

    return glu(y), (glu(yc) if need_ctx_out else None)


def fnet_mixer(f, w_fourier):
    B, L, _ = f.shape
    fg = f.astype(jnp.float32).reshape(B, L, FNET_GROUPS, FNET_GROUP)
    mixed = jnp.fft.fft2(fg, axes=(1, 3), norm="ortho").real.reshape(B, L, FNET_WIDTH)
    return mixed.astype(f.dtype) @ w_fourier


def na_latent(q, k, v, kc, vc, rpb):
    B, L, H, d = q.shape
    R = L // GRID_W
    wr = min(NA_ROWS_MAX, R)
    nwin = wr * NA_COLS
    qg = q.reshape(B, R, GRID_W, H, d)
    kg = k.reshape(B, R, GRID_W, H, d)
    vg = v.reshape(B, R, GRID_W, H, d)
    cols = jnp.arange(GRID_W)
    col_start = jnp.clip(cols - NA_COLS // 2, 0, GRID_W - NA_COLS)
    col_idx = col_start[:, None] + jnp.arange(NA_COLS)
    dc = col_idx - cols[:, None] + (NA_COLS - 1)
    scale = d ** -0.5

    def row_block(args):
        r, q_row = args
        rs = jnp.clip(r - wr // 2, 0, R - wr)
        k_rows = lax.dynamic_slice_in_dim(kg, rs, wr, axis=1)
        v_rows = lax.dynamic_slice_in_dim(vg, rs, wr, axis=1)
        k_win = k_rows[:, :, col_idx]
        v_win = v_rows[:, :, col_idx]
        dr = rs + jnp.arange(wr) - r + (NA_ROWS_MAX - 1)
        bias = rpb[:, dr[:, None, None], dc[None]]
        bias = jnp.transpose(bias, (0, 2, 1, 3)).astype(jnp.float32)
        s_win = jnp.einsum('bqhd,bwqkhd->bhqwk', q_row, k_win).astype(jnp.float32) * scale + bias[None]
        s_win = s_win.reshape(B, H, GRID_W, nwin)
        s_ctx = jnp.einsum('bqhd,bchd->bhqc', q_row, kc).astype(jnp.float32) * scale
        p = jax.nn.softmax(jnp.concatenate([s_win, s_ctx], axis=-1), axis=-1).astype(q.dtype)
        p_win = p[..., :nwin].reshape(B, H, GRID_W, wr, NA_COLS)
        p_ctx = p[..., nwin:]
        return (jnp.einsum('bhqwk,bwqkhd->bqhd', p_win, v_win)
                + jnp.einsum('bhqc,bchd->bqhd', p_ctx, vc))

    o = lax.map(row_block, (jnp.arange(R), jnp.moveaxis(qg, 1, 0)))
    return jnp.moveaxis(o, 0, 1).reshape(B, L, H * d)


def na_context(qc, kc, vc):
    B, Lc, H, d = qc.shape
    s = jnp.einsum('bqhd,bkhd->bhqk', qc, kc).astype(jnp.float32) * d ** -0.5
    p = jax.nn.softmax(s, axis=-1).astype(qc.dtype)
    return jnp.einsum('bhqk,bkhd->bqhd', p, vc).reshape(B, Lc, H * d)


def swiglu(h, w_gate, w_up, w_down):
    return (jax.nn.silu(h @ w_gate) * (h @ w_up)) @ w_down


def trunk_layer(x, xc, c, c_ctx, row, col, w_mod, b_mod, g_pre_mix, g_post_mix, w_in,
                ssm_a_re, ssm_a_im, ssm_log_dt, ssm_b_re, ssm_b_im, ssm_c_re, ssm_c_im, ssm_d, w_glu,
                w_fourier, na_rpb, w_out, g_pre_ffn, g_post_ffn, w_ffn_gate, w_ffn_up, w_ffn_down, last):
    B, L, _ = x.shape
    Lc = xc.shape[1]
    mod = (jax.nn.silu(c) @ w_mod + b_mod)[:, None, :]
    mod_c = jax.nn.silu(c_ctx) @ w_mod + b_mod
    sh1, sc1, g1, sh2, sc2, g2 = jnp.split(mod, 6, axis=-1)
    sh1c, sc1c, g1c, sh2c, sc2c, g2c = jnp.split(mod_c, 6, axis=-1)

    h = modulate(rmsnorm(x, g_pre_mix), sh1, sc1) @ w_in
    hc = modulate(rmsnorm(xc, g_pre_mix), sh1c, sc1c) @ w_in
    splits = list(np.cumsum([SSM_WIDTH, FNET_WIDTH, NA_WIDTH, NA_WIDTH]))
    u, f, q, k, v = jnp.split(h, splits, axis=-1)
    uc, fc, qc, kc, vc = jnp.split(hc, splits, axis=-1)

    y_ssm, yc_ssm = s5_mixer(u, uc, ssm_a_re, ssm_a_im, ssm_log_dt, ssm_b_re, ssm_b_im,
                             ssm_c_re, ssm_c_im, ssm_d, w_glu, not last)
    y_fft = fnet_mixer(f, w_fourier)
    q = axial_rope(q.reshape(B, L, NA_HEADS, HEAD_DIM), row, col)
    k = axial_rope(k.reshape(B, L, NA_HEADS, HEAD_DIM), row, col)
    v = v.reshape(B, L, NA_HEADS, HEAD_DIM)
    kc = kc.reshape(B, Lc, NA_HEADS, HEAD_DIM)
    vc = vc.reshape(B, Lc, NA_HEADS, HEAD_DIM)
    y_na = na_latent(q, k, v, kc, vc, na_rpb)
    o = jnp.concatenate([y_ssm, y_fft, y_na], axis=-1) @ w_out
    x = x + g1 * rmsnorm(o, g_post_mix)
    if not last:
        y_fft_c = fnet_mixer(fc, w_fourier)
        y_na_c = na_context(qc.reshape(B, Lc, NA_HEADS, HEAD_DIM), kc, vc)
        oc = jnp.concatenate([yc_ssm, y_fft_c, y_na_c], axis=-1) @ w_out
        xc = xc + g1c * rmsnorm(oc, g_post_mix)

    x = x + g2 * rmsnorm(swiglu(modulate(rmsnorm(x, g_pre_ffn), sh2, sc2), w_ffn_gate, w_ffn_up, w_ffn_down), g_post_ffn)
    if not last:
        xc = xc + g2c * rmsnorm(swiglu(modulate(rmsnorm(xc, g_pre_ffn), sh2c, sc2c), w_ffn_gate, w_ffn_up, w_ffn_down), g_post_ffn)
    else:
        xc = None
    return x, xc


def setup_inputs(seed: int = 0) -> dict:
    key = jax.random.key(seed)
    ks = jax.random.split(key, 26)
    f32 = jnp.float32

    def nrm(k, shape, scale):
        return jax.random.normal(k, shape, f32) * scale

    G, P, Hc = SSM_GROUPS, SSM_STATE, SSM_GROUP
    n_idx = jnp.arange(P, dtype=f32)
    return {
        "x": nrm(ks[0], (BATCH, SEQ, D_MODEL), 1.0),
        "c": nrm(ks[1], (BATCH, D_MODEL), 1.0),
        "ctx": nrm(ks[2], (BATCH, CTX_LEN, D_MODEL), 1.0),
        "c_ctx": nrm(ks[3], (D_MODEL,), 1.0),
        "w_mod": nrm(ks[4], (DEPTH, D_MODEL, 6 * D_MODEL), D_MODEL ** -0.5),
        "b_mod": nrm(ks[5], (DEPTH, 6 * D_MODEL), 0.01),
        "g_pre_mix": 1.0 + nrm(ks[6], (DEPTH, D_MODEL), 0.1),
        "g_post_mix": 1.0 + nrm(ks[7], (DEPTH, D_MODEL), 0.1),
        "w_in": nrm(ks[8], (DEPTH, D_MODEL, IN_WIDTH), D_MODEL ** -0.5),
        "ssm_a_re": -0.5 + nrm(ks[9], (DEPTH, 2, G, P), 0.01),
        "ssm_a_im": math.pi * n_idx + nrm(ks[10], (DEPTH, 2, G, P), 0.01),
        "ssm_log_dt": jax.random.uniform(ks[11], (DEPTH, 2, G), f32, math.log(1e-3), math.log(1e-1)),
        "ssm_b_re": nrm(ks[12], (DEPTH, 2, G, P, Hc), (2 * Hc) ** -0.5),
        "ssm_b_im": nrm(ks[13], (DEPTH, 2, G, P, Hc), (2 * Hc) ** -0.5),
        "ssm_c_re": nrm(ks[14], (DEPTH, 2, G, Hc, P), (2 * P) ** -0.5),
        "ssm_c_im": nrm(ks[15], (DEPTH, 2, G, Hc, P), (2 * P) ** -0.5),
        "ssm_d": nrm(ks[16], (DEPTH, SSM_WIDTH), 1.0),
        "w_glu": nrm(ks[17], (DEPTH, SSM_WIDTH, SSM_WIDTH), SSM_WIDTH ** -0.5),
        "w_fourier": nrm(ks[18], (DEPTH, FNET_WIDTH, FNET_WIDTH), FNET_WIDTH ** -0.5),
        "na_rpb": nrm(ks[19], (DEPTH, NA_HEADS, RPB_ROWS, RPB_COLS), 0.02),
        "w_out": nrm(ks[20], (DEPTH, D_MODEL, D_MODEL), D_MODEL ** -0.5),
        "g_pre_ffn": 1.0 + nrm(ks[21], (DEPTH, D_MODEL), 0.1),
        "g_post_ffn": 1.0 + nrm(ks[22], (DEPTH, D_MODEL), 0.1),
        "w_ffn_gate": nrm(ks[23], (DEPTH, D_MODEL, D_FF), D_MODEL ** -0.5),
        "w_ffn_up": nrm(ks[24], (DEPTH, D_MODEL, D_FF), D_MODEL ** -0.5),
        "w_ffn_down": nrm(ks[25], (DEPTH, D_FF, D_MODEL), D_FF ** -0.5),
    }


def reference(x, c, ctx, c_ctx, w_mod, b_mod, g_pre_mix, g_post_mix, w_in,
              ssm_a_re, ssm_a_im, ssm_log_dt, ssm_b_re, ssm_b_im, ssm_c_re, ssm_c_im, ssm_d, w_glu,
              w_fourier, na_rpb, w_out, g_pre_ffn, g_post_ffn, w_ffn_gate, w_ffn_up, w_ffn_down):
    L = x.shape[1]
    t = jnp.arange(L)
    row = t // GRID_W
    col = t % GRID_W
    xc = ctx
    for layer in range(DEPTH):
        x, xc = trunk_layer(
            x, xc, c, c_ctx, row, col, w_mod[layer], b_mod[layer], g_pre_mix[layer], g_post_mix[layer], w_in[layer],
            ssm_a_re[layer], ssm_a_im[layer], ssm_log_dt[layer], ssm_b_re[layer], ssm_b_im[layer],
            ssm_c_re[layer], ssm_c_im[layer], ssm_d[layer], w_glu[layer], w_fourier[layer], na_rpb[layer],
            w_out[layer], g_pre_ffn[layer], g_post_ffn[layer], w_ffn_gate[layer], w_ffn_up[layer], w_ffn_down[layer],
            last=(layer == DEPTH - 1))
    return x
```

```python
import math
import numpy as np
from contextlib import ExitStack
import concourse.bass as bass
import concourse.mybir as mybir
from concourse.bass_utils import run_bass_kernel_spmd


F32 = mybir.dt.float32
BF16 = mybir.dt.bfloat16
ALU = mybir.AluOpType
AF = mybir.ActivationFunctionType
AX = mybir.AxisListType

COMPUTE = ("tensor", "vector", "scalar", "gpsimd")


class Prog:
    def __init__(self):
        self.nc = bass.Bass("TRN2", target_bir_lowering=False)
        self.ops = []
        self.stack = ExitStack()
        self.ndram = 0

    def dram_in(self, name, shape, dtype=F32):
        return self.nc.dram_tensor(name, list(shape), dtype, kind="ExternalInput").ap()

    def dram_out(self, name, shape, dtype=F32):
        return self.nc.dram_tensor(name, list(shape), dtype, kind="ExternalOutput").ap()

    def sbuf(self, name, shape, dtype=F32):
        return self.stack.enter_context(self.nc.sbuf_tensor(name, list(shape), dtype))

    def psum(self, name, shape=(128, 512), dtype=F32):
        return self.stack.enter_context(self.nc.psum_tensor(name, list(shape), dtype))

    def op(self, eng, fn, reads=(), writes=(), chan=None, nosync_same=False):
        self.ops.append(dict(eng=eng, fn=fn, reads=tuple(reads), writes=tuple(writes),
                             chan=chan, nosync_same=nosync_same))

    def dma(self, eng, out, in_, reads=(), writes=(), chan="ld", **kw):
        self.op(eng, lambda e: e.dma_start(out=out, in_=in_, **kw), reads, writes, chan=chan)

    def mm(self, out, lhsT, rhs, start, stop, reads=(), writes=()):
        self.op("tensor", lambda e: e.matmul(out, lhsT, rhs, start=start, stop=stop),
                reads, writes, nosync_same=True)

    def act(self, out, in_, func, reads=(), writes=(), **kw):
        self.op("scalar", lambda e: e.activation(out=out, in_=in_, func=func, **kw), reads, writes)

    def tt(self, out, in0, in1, op, reads=(), writes=(), eng="vector"):
        self.op(eng, lambda e: e.tensor_tensor(out=out, in0=in0, in1=in1, op=op), reads, writes)

    def ts(self, out, in0, s1, s2, op0, op1=None, reads=(), writes=(), eng="vector"):
        if op1 is None:
            self.op(eng, lambda e: e.tensor_scalar(out=out, in0=in0, scalar1=s1, scalar2=None, op0=op0),
                    reads, writes)
        else:
            self.op(eng, lambda e: e.tensor_scalar(out=out, in0=in0, scalar1=s1, scalar2=s2, op0=op0, op1=op1),
                    reads, writes)

    def stt(self, out, in0, scalar, in1, op0, op1, reads=(), writes=()):
        self.op("vector", lambda e: e.scalar_tensor_tensor(out=out, in0=in0, scalar=scalar, in1=in1,
                                                            op0=op0, op1=op1), reads, writes)

    def copy(self, out, in_, reads=(), writes=(), eng="vector"):
        if eng == "scalar":
            self.op(eng, lambda e: e.copy(out=out, in_=in_), reads, writes)
        else:
            self.op(eng, lambda e: e.tensor_copy(out=out, in_=in_), reads, writes)

    def memset(self, ap, val, writes=(), eng="vector"):
        self.op(eng, lambda e: e.memset(ap, val), (), writes)

    def finish(self):
        nc = self.nc
        ops = self.ops
        engines = []
        for o in ops:
            if o["eng"] not in engines:
                engines.append(o["eng"])
        chans = []
        for o in ops:
            if o["chan"] is not None and o["chan"] not in chans:
                chans.append(o["chan"])
        sems = {}
        for e in engines:
            sems[("e", e)] = self.stack.enter_context(nc.semaphore("s_" + e))
        for c in chans:
            sems[("c", c)] = self.stack.enter_context(nc.semaphore("c_" + c))
        eng_count = {e: 0 for e in engines}
        chan_count = {c: 0 for c in chans}
        last_writer = {}
        readers = {}
        known = {e: {} for e in engines}
        plan = {e: [] for e in engines}
        done = []
        for i, o in enumerate(ops):
            e = o["eng"]
            deps = set()
            for r in o["reads"]:
                if r in last_writer:
                    deps.add(last_writer[r])
            for w in o["writes"]:
                if w in last_writer:
                    deps.add(last_writer[w])
                for rd in readers.get(w, ()):
                    deps.add(rd)
            need = {}
            for d in deps:
                od = ops[d]
                if od["chan"] is not None:
                    key = ("c", od["chan"])
                    val = 16 * chan_count[od["chan"]]
                else:
                    if od["eng"] == e and (o["nosync_same"] and od["nosync_same"]):
                        continue
                    key = ("e", od["eng"])
                    val = done[d][1]
                if val > need.get(key, 0):
                    need[key] = val
            waits = []
            for key, val in need.items():
                if known[e].get(key, 0) >= val:
                    continue
                known[e][key] = val
                waits.append((key, val))
            if o["chan"] is not None:
                chan_count[o["chan"]] += 1
                done.append((("c", o["chan"]), 16 * chan_count[o["chan"]]))
                inc = (("c", o["chan"]), 16)
            else:
                eng_count[e] += 1
                done.append((("e", e), eng_count[e]))
                inc = (("e", e), 1)
            plan[e].append((waits, o["fn"], inc))
            for r in o["reads"]:
                readers.setdefault(r, []).append(i)
            for w in o["writes"]:
                last_writer[w] = i
                readers[w] = []
        final_waits = {e: [] for e in engines}
        chan_eng = {}
        for o in ops:
            if o["chan"] is not None:
                chan_eng[o["chan"]] = o["eng"]
        for c, e in chan_eng.items():
            final_waits[e].append((("c", c), 16 * chan_count[c]))

        with nc.Block() as block:
            def make(e):
                def body(eng):
                    for waits, fn, inc in plan[e]:
                        for key, val in waits:
                            eng.wait_ge(sems[key], val)
                        fn(eng).then_inc(sems[inc[0]], inc[1])
                    for key, val in final_waits[e]:
                        eng.wait_ge(sems[key], val)
                return body
            for e in engines:
                getattr(block, e)(make(e))
        self.stack.close()
        return nc

GRID_W = 64
def rope_tables(tok0, n):
    t = np.arange(tok0, tok0 + n); row = (t // GRID_W).astype(np.float32); col = (t % GRID_W).astype(np.float32)
    quarter = 16
    freqs = (10000.0 ** (-np.arange(quarter, dtype=np.float32) / quarter)).astype(np.float32)
    cos = np.zeros((64, n), np.float32); sin = np.zeros((64, n), np.float32)
    for d in range(64):
        pos = row if d < 32 else col
        dd = d % 32
        f = freqs[dd % 16]
        ang = (pos * f).astype(np.float32)
        cos[d] = np.cos(ang); s = np.sin(ang)
        sin[d] = -s if dd < 16 else s
    return np.concatenate([cos, cos], 0), np.concatenate([sin, sin], 0)
def perm_matrix():
    Pm = np.zeros((128, 128), np.float32)
    for m in range(128):
        dd = m % 32
        k = m + 16 if dd < 16 else m - 16
        Pm[k, m] = 1.0
    return Pm
def colT(v, n):
    return np.ascontiguousarray(v.reshape(n, 128).T)

EPS = 1e-6
NT = 2304
SLABS = [(0, 512), (512, 512), (1024, 512), (1536, 512), (2048, 256)]

def build_l1():
    P = Prog()
    xT = P.dram_in("xT", [1024, NT])
    w_in = P.dram_in("w_in", [1024, 1792])
    w_mod = P.dram_in("w_mod", [1024, 2048])
    b_modT = P.dram_in("b_modT", [128, 16])
    g_preT = P.dram_in("g_preT", [128, 8])
    cT = P.dram_in("cT", [128, 16])
    cos = P.dram_in("cos", [128, 2048]); sin = P.dram_in("sin", [128, 2048])
    perm = P.dram_in("perm", [128, 128])
    hT = P.dram_out("hT", [1792, NT])

    xs = [P.sbuf(f"xs{i}", [128, 8, 512]) for i in range(2)]
    sqb = P.sbuf("sqb", [128, 8, 512], BF16)
    tt_ = P.sbuf("tt", [128, 8, 512])
    xn = P.sbuf("xn", [128, 8, 512], BF16)
    w_bf = P.sbuf("w_bf", [128, 8, 1792], BF16)
    wst = [P.sbuf(f"wst{i}", [128, 1792]) for i in range(2)]
    wm = [P.sbuf(f"wm{i}", [128, 8, 128]) for i in range(2)]
    ones = P.sbuf("ones", [128, 128], BF16)
    sd = P.sbuf("sd", [128, 512]); rstd = P.sbuf("rstd", [128, 512])
    ho = [P.sbuf(f"ho{i}", [128, 512]) for i in range(3)]
    r1 = [P.sbuf(f"r1{i}", [128, 512]) for i in range(2)]
    r2 = [P.sbuf(f"r2{i}", [128, 512]) for i in range(2)]
    ho2 = [P.sbuf(f"ho2{i}", [128, 512]) for i in range(2)]
    cos_s = P.sbuf("cos_s", [128, 2048]); sin_s = P.sbuf("sin_s", [128, 2048])
    perm_s = P.sbuf("perm_s", [128, 128])
    c_s = P.sbuf("c_s", [128, 16]); sc_s = P.sbuf("sc_s", [128, 16])
    bm_s = P.sbuf("bm_s", [128, 16]); gp_s = P.sbuf("gp_s", [128, 8])
    modv = P.sbuf("modv", [128, 32])
    Av = P.sbuf("Av", [128, 16])
    ps_mod = P.psum("ps_mod"); ps_ss = P.psum("ps_ss")
    psm = [P.psum(f"psm{i}") for i in range(4)]
    psr = [P.psum(f"psr{i}") for i in range(2)]

    P.dma("sync", c_s[:], cT, writes=["c_s"], chan="ld0")
    P.dma("sync", bm_s[:], b_modT, writes=["bm_s"], chan="ld0")
    P.dma("sync", gp_s[:], g_preT, writes=["gp_s"], chan="ld0")
    P.dma("sync", cos_s[:], cos, writes=["cos_s"], chan="ld0")
    P.dma("sync", sin_s[:], sin, writes=["sin_s"], chan="ld0")
    P.dma("sync", perm_s[:], perm, writes=["perm_s"], chan="ld0")
    P.memset(ones[:], 1.0, writes=["ones"])
    P.act(sc_s[:], c_s[:], AF.Silu, reads=["c_s"], writes=["sc_s"])
    wmv = w_mod.rearrange("(kc p) n -> p kc n", p=128)
    for ct in range(16):
        b = ct % 2
        P.dma("sync", wm[b][:], wmv[:, :, ct * 128:(ct + 1) * 128], writes=[("wm", b)], chan=f"wm{b}")
        for j in range(2):
            for kc in range(8):
                P.mm(ps_mod[:, j * 16 + ct: j * 16 + ct + 1], wm[b][:, kc, :], sc_s[:, j * 8 + kc: j * 8 + kc + 1],
                     kc == 0, kc == 7, reads=[("wm", b), "sc_s"], writes=["ps_mod"])
    for j in range(2):
        P.tt(modv[:, j * 16:(j + 1) * 16], ps_mod[:, j * 16:(j + 1) * 16], bm_s[:], ALU.add,
             reads=["ps_mod", "bm_s"], writes=["modv"])
    for j in range(2):
        P.stt(Av[:, j * 8:(j + 1) * 8], modv[:, j * 16 + 8: j * 16 + 16], 1.0, gp_s[:], ALU.add, ALU.mult,
              reads=["modv", "gp_s"], writes=["Av"])
    wv = w_in.rearrange("(kc p) n -> p kc n", p=128)
    for kc in range(8):
        b = kc % 2
        P.dma("sync", wst[b][:], wv[:, kc, :], writes=[("wst", b)], chan=f"wst{b}")
        P.copy(w_bf[:, kc, :], wst[b][:], reads=[("wst", b)], writes=[("w_bf", kc)], eng="gpsimd")
    xv = xT.rearrange("(kc p) t -> p kc t", p=128)
    nho = 0; nr = 0; nps = 0
    for si, (t0, n) in enumerate(SLABS):
        b = si % 2
        j = 0 if si < 4 else 1
        P.dma("sync", xs[b][:, :, 0:n], xv[:, :, t0:t0 + n], writes=[("xs", b)], chan=f"xs{b}")
        P.act(sqb[:, :, 0:n], xs[b][:, :, 0:n], AF.Square, reads=[("xs", b)], writes=["sqb"])
        for kc in range(8):
            P.mm(ps_ss[:, 0:n], ones[:], sqb[:, kc, 0:n], kc == 0, kc == 7, reads=["sqb", "ones"], writes=["ps_ss"])
        P.act(sd[:, 0:n], ps_ss[:, 0:n], AF.Sqrt, reads=["ps_ss"], writes=["sd"], scale=1.0 / 1024, bias=EPS)
        P.op("vector", lambda e, n=n: e.reciprocal(out=rstd[:, 0:n], in_=sd[:, 0:n]), reads=["sd"], writes=["rstd"])
        for kc in range(8):
            P.tt(tt_[:, kc, 0:n], xs[b][:, kc, 0:n], rstd[:, 0:n], ALU.mult, reads=[("xs", b), "rstd"], writes=[("tt", kc)])
            P.act(xn[:, kc, 0:n], tt_[:, kc, 0:n], AF.Identity, reads=[("tt", kc), "Av", "modv"], writes=[("xn", kc)],
                  scale=Av[:, j * 8 + kc: j * 8 + kc + 1], bias=modv[:, j * 16 + kc: j * 16 + kc + 1])
        for m in range(14):
            pb = nps % 4; nps += 1
            for kc in range(8):
                P.mm(psm[pb][:, 0:n], w_bf[:, kc, m * 128:(m + 1) * 128], xn[:, kc, 0:n], kc == 0, kc == 7,
                     reads=[("w_bf", kc), ("xn", kc)], writes=[("psm", pb)])
            hb = nho % 3; nho += 1
            P.copy(ho[hb][:, 0:n], psm[pb][:, 0:n], reads=[("psm", pb)], writes=[("ho", hb)], eng="scalar")
            if 5 <= m <= 10 and si < 4:
                rb = nr % 2; nr += 1
                P.mm(psr[rb][:, 0:n], perm_s[:], ho[hb][:, 0:n], True, True, reads=["perm_s", ("ho", hb)], writes=[("psr", rb)])
                P.tt(r1[rb][:, 0:n], ho[hb][:, 0:n], cos_s[:, t0:t0 + n], ALU.mult, reads=[("ho", hb), "cos_s"], writes=[("r1", rb)])
                P.tt(r2[rb][:, 0:n], psr[rb][:, 0:n], sin_s[:, t0:t0 + n], ALU.mult, reads=[("psr", rb), "sin_s"], writes=[("r2", rb)])
                P.tt(ho2[rb][:, 0:n], r1[rb][:, 0:n], r2[rb][:, 0:n], ALU.add, reads=[("r1", rb), ("r2", rb)], writes=[("ho2", rb)], eng="gpsimd")
                P.dma("gpsimd", hT[m * 128:(m + 1) * 128, t0:t0 + n], ho2[rb][:, 0:n], reads=[("ho2", rb)], chan="st")
            else:
                P.dma("gpsimd", hT[m * 128:(m + 1) * 128, t0:t0 + n], ho[hb][:, 0:n], reads=[("ho", hb)], chan="st")
    return P.finish()

EPS = 1e-6

def get_stage(P):
    if not hasattr(P, "_stage"):
        P._stage = [P.sbuf(f"stage{i}", [128, 1024]) for i in range(3)]
        P._stage_i = 0
    return P._stage

def emit_mod(P, w_mod, b_modT, cT, nct, ps_mod, tag="m"):
    c_s = P.sbuf(tag + "c_s", [128, 16]); sc_s = P.sbuf(tag + "sc_s", [128, 16])
    bm_s = P.sbuf(tag + "bm_s", [128, nct]); modv = P.sbuf(tag + "modv", [128, 2 * nct])
    st = get_stage(P)
    P.dma("sync", c_s[:], cT, writes=[tag + "c_s"], chan="ld0")
    P.dma("sync", bm_s[:], b_modT, writes=[tag + "bm_s"], chan="ld0")
    P.act(sc_s[:], c_s[:], AF.Silu, reads=[tag + "c_s"], writes=[tag + "sc_s"])
    wmv = w_mod.rearrange("(kc p) n -> p kc n", p=128)
    for ct in range(nct):
        b = P._stage_i % 3; P._stage_i += 1
        wmb = st[b][:].rearrange("p (k n) -> p k n", k=8)
        P.dma("sync", wmb, wmv[:, :, ct * 128:(ct + 1) * 128], writes=[("stage", b)], chan=f"stage{b}")
        for j in range(2):
            for kc in range(8):
                P.mm(ps_mod[:, j * nct + ct: j * nct + ct + 1], st[b][:, kc * 128:(kc + 1) * 128], sc_s[:, j * 8 + kc: j * 8 + kc + 1],
                     kc == 0, kc == 7, reads=[("stage", b), tag + "sc_s"], writes=["ps_mod"])
    for j in range(2):
        P.tt(modv[:, j * nct:(j + 1) * nct], ps_mod[:, j * nct:(j + 1) * nct], bm_s[:], ALU.add,
             reads=["ps_mod", tag + "bm_s"], writes=[tag + "modv"])
    return modv

def load_cast(P, w_dram, w_bf, nk, ncols, tag, piece=1024):
    wv = w_dram.rearrange("(kc p) n -> p kc n", p=128)
    st = get_stage(P)
    engs = ["gpsimd", "vector", "scalar"]
    for kc in range(nk):
        for c0 in range(0, ncols, piece):
            cn = min(piece, ncols - c0)
            b = P._stage_i % 3; P._stage_i += 1
            P.dma("sync", st[b][:, 0:cn], wv[:, kc, c0:c0 + cn], writes=[("stage", b)], chan=f"stage{b}")
            P.copy(w_bf[:, kc, c0:c0 + cn], st[b][:, 0:cn], reads=[("stage", b)], writes=[(tag, kc)], eng=engs[b])

def emit_rstd(P, src, nk, n, sqb, ones, ps_ss, sd, rstd, src_res):
    P.act(sqb[:, 0:nk, 0:n], src[:, 0:nk, 0:n], AF.Square, reads=src_res, writes=["sqb"])
    for kc in range(nk):
        P.mm(ps_ss[:, 0:n], ones[:], sqb[:, kc, 0:n], kc == 0, kc == nk - 1, reads=["sqb", "ones"], writes=["ps_ss"])
    P.act(sd[:, 0:n], ps_ss[:, 0:n], AF.Sqrt, reads=["ps_ss"], writes=["sd"], scale=1.0 / (128 * nk), bias=EPS)
    P.op("vector", lambda e: e.reciprocal(out=rstd[:, 0:n], in_=sd[:, 0:n]), reads=["sd"], writes=["rstd"])

def build_l3b(with_ctx, N=256):
    NT = 2304 if with_ctx else 2048
    P = Prog()
    xT = P.dram_in("xT", [1024, NT])
    w_mod = P.dram_in("w_mod", [1024, 3072]); b_modT = P.dram_in("b_modT", [128, 24]); cT = P.dram_in("cT", [128, 16])
    g_preT = P.dram_in("g_preT", [128, 8]); g_postT = P.dram_in("g_postT", [128, 8])
    w_gate = P.dram_in("w_gate", [1024, 2816]); w_up = P.dram_in("w_up", [1024, 2816]); w_down = P.dram_in("w_down", [2816, 1024])
    xoT = P.dram_out("xoT", [1024, NT])
    wg = P.sbuf("wg", [128, 8, 2816], BF16); wu = P.sbuf("wu", [128, 8, 2816], BF16); wd = P.sbuf("wd", [128, 22, 1024], BF16)
    xs = [P.sbuf(f"xs{i}", [128, 8, N]) for i in range(2)]
    sqb = P.sbuf("sqb", [128, 8, N], BF16)
    tt_ = [P.sbuf(f"tt{i}", [128, N]) for i in range(2)]; xn = P.sbuf("xn", [128, 8, N], BF16)
    hmid = P.sbuf("hmid", [128, 22, N], BF16)
    sg = [P.sbuf(f"sg{i}", [128, N]) for i in range(2)]
    o2 = P.sbuf("o2", [128, 8, N]); tmp = [P.sbuf(f"tmp{i}", [128, N]) for i in range(2)]
    xo = [P.sbuf(f"xo{i}", [128, N]) for i in range(2)]
    ones = P.sbuf("ones", [128, 128], BF16)
    sd = P.sbuf("sd", [128, N]); rstd = P.sbuf("rstd", [128, N])
    gp_s = P.sbuf("gp_s", [128, 8]); gq_s = P.sbuf("gq_s", [128, 8])
    Av = P.sbuf("Av", [128, 16]); Gv = P.sbuf("Gv", [128, 16])
    ps_mod = P.psum("ps_mod"); ps_ss = P.psum("ps_ss")
    psg = [P.psum(f"psg{i}") for i in range(2)]; psu = [P.psum(f"psu{i}") for i in range(2)]; pso = [P.psum(f"pso{i}") for i in range(2)]
    P.memset(ones[:], 1.0, writes=["ones"])
    P.dma("sync", gp_s[:], g_preT, writes=["gp_s"], chan="ld0")
    P.dma("sync", gq_s[:], g_postT, writes=["gq_s"], chan="ld0")
    modv = emit_mod(P, w_mod, b_modT, cT, 24, ps_mod)
    for j in range(2):
        P.stt(Av[:, j * 8:(j + 1) * 8], modv[:, j * 24 + 8: j * 24 + 16], 1.0, gp_s[:], ALU.add, ALU.mult,
              reads=["mmodv", "gp_s"], writes=["Av"])
        P.tt(Gv[:, j * 8:(j + 1) * 8], modv[:, j * 24 + 16: j * 24 + 24], gq_s[:], ALU.mult,
             reads=["mmodv", "gq_s"], writes=["Gv"])
    load_cast(P, w_gate, wg, 8, 2816, "wg")
    load_cast(P, w_up, wu, 8, 2816, "wu")
    load_cast(P, w_down, wd, 22, 1024, "wd")
    xv = xT.rearrange("(kc p) t -> p kc t", p=128); xov = xoT.rearrange("(kc p) t -> p kc t", p=128)
    ng = 0; no = 0; nt = 0
    for si, t0 in enumerate(range(0, NT, N)):
        n = N; b = si % 2; j = 0 if t0 < 2048 else 1
        P.dma("sync", xs[b][:], xv[:, :, t0:t0 + n], writes=[("xs", b)], chan=f"xs{b}")
        emit_rstd(P, xs[b], 8, n, sqb, ones, ps_ss, sd, rstd, [("xs", b)])
        for kc in range(8):
            P.tt(tt_[kc % 2][:], xs[b][:, kc, :], rstd[:], ALU.mult, reads=[("xs", b), "rstd"], writes=[("tt", kc % 2)])
            P.act(xn[:, kc, :], tt_[kc % 2][:], AF.Identity, reads=[("tt", kc % 2), "Av", "mmodv"], writes=[("xn", kc)],
                  scale=Av[:, j * 8 + kc: j * 8 + kc + 1], bias=modv[:, j * 24 + kc: j * 24 + kc + 1])
        for jj in range(22):
            pb = ng % 2; ng += 1
            for kc in range(8):
                P.mm(psg[pb][:, 0:n], wg[:, kc, jj * 128:(jj + 1) * 128], xn[:, kc, :], kc == 0, kc == 7,
                     reads=[("wg", kc), ("xn", kc)], writes=[("psg", pb)])
            for kc in range(8):
                P.mm(psu[pb][:, 0:n], wu[:, kc, jj * 128:(jj + 1) * 128], xn[:, kc, :], kc == 0, kc == 7,
                     reads=[("wu", kc), ("xn", kc)], writes=[("psu", pb)])
            P.act(sg[pb][:], psg[pb][:, 0:n], AF.Silu, reads=[("psg", pb)], writes=[("sg", pb)])
            P.tt(hmid[:, jj, :], sg[pb][:], psu[pb][:, 0:n], ALU.mult, reads=[("sg", pb), ("psu", pb)], writes=[("hmid", jj)])
        for m in range(8):
            pb = no % 2; no += 1
            for jj in range(22):
                P.mm(pso[pb][:, 0:n], wd[:, jj, m * 128:(m + 1) * 128], hmid[:, jj, :], jj == 0, jj == 21,
                     reads=[("wd", jj), ("hmid", jj)], writes=[("pso", pb)])
            P.copy(o2[:, m, :], pso[pb][:, 0:n], reads=[("pso", pb)], writes=[("o2", m)], eng="scalar")
        emit_rstd(P, o2, 8, n, sqb, ones, ps_ss, sd, rstd, [("o2", m) for m in range(8)])
        for m in range(8):
            tb = nt % 2; nt += 1
            P.stt(tmp[tb][:], o2[:, m, :], Gv[:, j * 8 + m: j * 8 + m + 1], rstd[:], ALU.mult, ALU.mult,
                  reads=[("o2", m), "Gv", "rstd"], writes=[("tmp", tb)])
            P.tt(xo[tb][:], xs[b][:, m, :], tmp[tb][:], ALU.add, reads=[("xs", b), ("tmp", tb)], writes=[("xo", tb)], eng="gpsimd")
            P.dma("gpsimd", xov[:, m, t0:t0 + n], xo[tb][:], reads=[("xo", tb)], chan="st")
    return P.finish()

def build_l3a(with_ctx, N=512):
    NT = 2304 if with_ctx else 2048
    P = Prog()
    xT = P.dram_in("xT", [1024, NT]); ysT = P.dram_in("ysT", [384, NT]); mxT = P.dram_in("mxT", [256, NT]); naT = P.dram_in("naT", [384, NT])
    w_mod = P.dram_in("w_mod", [1024, 1024]); b_modT = P.dram_in("b_modT", [128, 8]); cT = P.dram_in("cT", [128, 16])
    g_postT = P.dram_in("g_postT", [128, 8])
    w_glu = P.dram_in("w_glu", [384, 384]); w_fourier = P.dram_in("w_fourier", [256, 256]); w_out = P.dram_in("w_out", [1024, 1024])
    xoT = P.dram_out("xoT", [1024, NT])
    wglu = P.sbuf("wglu", [128, 3, 384], BF16); wf = P.sbuf("wf", [128, 2, 256], BF16); wo = P.sbuf("wo", [128, 8, 1024], BF16)
    xs = [P.sbuf(f"xs{i}", [128, 8, N]) for i in range(2)]
    ys = [P.sbuf(f"ys{i}", [128, 3, N]) for i in range(2)]
    mx = [P.sbuf(f"mx{i}", [128, 2, N]) for i in range(2)]
    na = [P.sbuf(f"na{i}", [128, 3, N]) for i in range(2)]
    sq = P.sbuf("sq", [128, 3, N]); t1 = P.sbuf("t1", [128, 3, N]); sgm = P.sbuf("sgm", [128, 3, N])
    zf = P.sbuf("zf", [128, 3, N]); zb = P.sbuf("zb", [128, 3, N], BF16); mxb = P.sbuf("mxb", [128, 2, N], BF16)
    sg2 = [P.sbuf(f"sg2{i}", [128, N]) for i in range(2)]
    cat = P.sbuf("cat", [128, 8, N], BF16)
    sqb = P.sbuf("sqb", [128, 8, N], BF16)
    o2 = P.sbuf("o2", [128, 8, N]); tmp = [P.sbuf(f"tmp{i}", [128, N]) for i in range(2)]
    xo = [P.sbuf(f"xo{i}", [128, N]) for i in range(2)]
    ones = P.sbuf("ones", [128, 128], BF16)
    sd = P.sbuf("sd", [128, N]); rstd = P.sbuf("rstd", [128, N])
    gq_s = P.sbuf("gq_s", [128, 8]); Gv = P.sbuf("Gv", [128, 16])
    ps_mod = P.psum("ps_mod"); ps_ss = P.psum("ps_ss")
    psa = [P.psum(f"psa{i}") for i in range(3)]; pso = [P.psum(f"pso{i}") for i in range(3)]
    P.memset(ones[:], 1.0, writes=["ones"])
    P.dma("sync", gq_s[:], g_postT, writes=["gq_s"], chan="ld0")
    modv = emit_mod(P, w_mod, b_modT, cT, 8, ps_mod)
    for j in range(2):
        P.tt(Gv[:, j * 8:(j + 1) * 8], modv[:, j * 8:(j + 1) * 8], gq_s[:], ALU.mult, reads=["mmodv", "gq_s"], writes=["Gv"])
    load_cast(P, w_glu, wglu, 3, 384, "wglu")
    load_cast(P, w_fourier, wf, 2, 256, "wf")
    load_cast(P, w_out, wo, 8, 1024, "wo")
    xv = xT.rearrange("(kc p) t -> p kc t", p=128); xov = xoT.rearrange("(kc p) t -> p kc t", p=128)
    ysv = ysT.rearrange("(kc p) t -> p kc t", p=128); mxv = mxT.rearrange("(kc p) t -> p kc t", p=128); nav = naT.rearrange("(kc p) t -> p kc t", p=128)
    na_ = 0; no = 0; nt = 0
    for si, t0 in enumerate(range(0, NT, N)):
        n = min(N, NT - t0); b = si % 2; j = 0 if t0 < 2048 else 1
        P.dma("sync", xs[b][:, :, 0:n], xv[:, :, t0:t0 + n], writes=[("xs", b)], chan=f"xs{b}")
        P.dma("sync", ys[b][:, :, 0:n], ysv[:, :, t0:t0 + n], writes=[("ys", b)], chan=f"ys{b}")
        P.dma("sync", mx[b][:, :, 0:n], mxv[:, :, t0:t0 + n], writes=[("mx", b)], chan=f"mx{b}")
        P.dma("sync", na[b][:, :, 0:n], nav[:, :, t0:t0 + n], writes=[("na", b)], chan=f"na{b}")
        Y = ys[b][:, :, 0:n]
        P.tt(sq[:, :, 0:n], Y, Y, ALU.mult, reads=[("ys", b)], writes=["sq"], eng="gpsimd")
        P.ts(t1[:, :, 0:n], sq[:, :, 0:n], 0.044715, 1.0, ALU.mult, ALU.add, reads=["sq"], writes=["t1"])
        P.tt(sq[:, :, 0:n], t1[:, :, 0:n], Y, ALU.mult, reads=["t1", ("ys", b)], writes=["sq"])
        P.act(sgm[:, :, 0:n], sq[:, :, 0:n], AF.Sigmoid, reads=["sq"], writes=["sgm"], scale=1.5957691216057308)
        P.tt(zf[:, :, 0:n], Y, sgm[:, :, 0:n], ALU.mult, reads=[("ys", b), "sgm"], writes=["zf"])
        P.copy(zb[:, :, 0:n], zf[:, :, 0:n], reads=["zf"], writes=["zb"], eng="gpsimd")
        for m in range(3):
            pb = na_ % 3; na_ += 1
            for kc in range(3):
                P.mm(psa[pb][:, 0:n], wglu[:, kc, m * 128:(m + 1) * 128], zb[:, kc, 0:n], kc == 0, kc == 2,
                     reads=[("wglu", kc), "zb"], writes=[("psa", pb)])
            P.act(sg2[m % 2][:, 0:n], psa[pb][:, 0:n], AF.Sigmoid, reads=[("psa", pb)], writes=[("sg2", m % 2)])
            P.tt(cat[:, m, 0:n], zf[:, m, 0:n], sg2[m % 2][:, 0:n], ALU.mult, reads=["zf", ("sg2", m % 2)], writes=[("cat", m)])
        P.copy(mxb[:, :, 0:n], mx[b][:, :, 0:n], reads=[("mx", b)], writes=["mxb"], eng="gpsimd")
        for m in range(2):
            pb = na_ % 3; na_ += 1
            for kc in range(2):
                P.mm(psa[pb][:, 0:n], wf[:, kc, m * 128:(m + 1) * 128], mxb[:, kc, 0:n], kc == 0, kc == 1,
                     reads=[("wf", kc), "mxb"], writes=[("psa", pb)])
            P.copy(cat[:, 3 + m, 0:n], psa[pb][:, 0:n], reads=[("psa", pb)], writes=[("cat", 3 + m)], eng="scalar")
        P.copy(cat[:, 5:8, 0:n], na[b][:, :, 0:n], reads=[("na", b)], writes=[("cat", 5), ("cat", 6), ("cat", 7)], eng="gpsimd")
        for m in range(8):
            pb = no % 3; no += 1
            for kc in range(8):
                P.mm(pso[pb][:, 0:n], wo[:, kc, m * 128:(m + 1) * 128], cat[:, kc, 0:n], kc == 0, kc == 7,
                     reads=[("wo", kc), ("cat", kc)], writes=[("pso", pb)])
            P.copy(o2[:, m, 0:n], pso[pb][:, 0:n], reads=[("pso", pb)], writes=[("o2", m)], eng="scalar")
        emit_rstd(P, o2, 8, n, sqb, ones, ps_ss, sd, rstd, [("o2", m) for m in range(8)])
        for m in range(8):
            tb = nt % 2; nt += 1
            P.stt(tmp[tb][:, 0:n], o2[:, m, 0:n], Gv[:, j * 8 + m: j * 8 + m + 1], rstd[:, 0:n], ALU.mult, ALU.mult,
                  reads=[("o2", m), "Gv", "rstd"], writes=[("tmp", tb)])
            P.tt(xo[tb][:, 0:n], xs[b][:, m, 0:n], tmp[tb][:, 0:n], ALU.add, reads=[("xs", b), ("tmp", tb)], writes=[("xo", tb)], eng="gpsimd")
            P.dma("gpsimd", xov[:, m, t0:t0 + n], xo[tb][:, 0:n], reads=[("xo", tb)], chan="st")
    return P.finish()


def fnet_consts():
    n1 = np.arange(128); a = 2 * np.pi * np.outer(n1, n1) / 128
    cs128 = np.concatenate([np.cos(a), -np.sin(a)], 1).astype(np.float32)
    n2 = np.arange(64); a = 2 * np.pi * np.outer(n2, n2) / 64
    C64, S64 = np.cos(a), np.sin(a)
    fb1 = np.concatenate([C64, -S64], 1).astype(np.float32); fb2 = np.concatenate([S64, C64], 1).astype(np.float32)
    a = 2 * np.pi * np.outer(n2, n1) / 8192
    tw = np.concatenate([np.cos(a), np.sin(a)], 1).astype(np.float32)
    fc = (np.concatenate([C64, S64], 1) / np.sqrt(8192 * 64)).astype(np.float32)
    n = np.arange(256); a = 2 * np.pi * np.outer(n, n) / 256
    cs = np.concatenate([np.cos(a), -np.sin(a)], 1)
    cs256 = np.ascontiguousarray(cs.reshape(2, 128, 512).transpose(1, 0, 2)).astype(np.float32)
    fcc = (np.concatenate([C64, S64], 1) / np.sqrt(256 * 64)).astype(np.float32)
    return dict(cs128=cs128, fb1=fb1, fb2=fb2, tw=tw, fc=fc, cs256=cs256, fcc=fcc)

def build_fnet(with_ctx):
    P = Prog()
    z = P.dram_in("z", [128, 4096]); cs128 = P.dram_in("cs128", [128, 256])
    fb1 = P.dram_in("fb1", [64, 128]); fb2 = P.dram_in("fb2", [64, 128]); tw = P.dram_in("tw", [64, 256]); fc = P.dram_in("fc", [64, 128])
    R = P.dram_out("R", [64, 8192])
    zs = P.sbuf("zs", [128, 64, 64]); cs_s = P.sbuf("cs_s", [128, 256])
    fb1_s = P.sbuf("fb1_s", [64, 128]); fb2_s = P.sbuf("fb2_s", [64, 128]); tw_s = P.sbuf("tw_s", [64, 256]); fc_s = P.sbuf("fc_s", [64, 128])
    Ych = [P.sbuf(f"Ych{i}", [64, 8, 256]) for i in range(2)]
    ta = P.sbuf("ta", [64, 8, 128]); tb = P.sbuf("tb", [64, 8, 128]); tc = P.sbuf("tc", [64, 8, 128]); td = P.sbuf("td", [64, 8, 128])
    Yp = P.sbuf("Yp", [64, 64, 256]); X1 = P.sbuf("X1", [64, 2, 8192])
    Rs = [P.sbuf(f"Rs{i}", [64, 512]) for i in range(2)]
    psA = [P.psum(f"psA{i}") for i in range(3)]; psB = [P.psum(f"psB{i}") for i in range(3)]; psC = [P.psum(f"psC{i}") for i in range(2)]
    P.dma("sync", zs[:].rearrange("p a b -> p (a b)"), z, writes=["zs"], chan="ld0")
    for (s, d, nm) in ((cs_s, cs128, "cs_s"), (fb1_s, fb1, "fb1_s"), (fb2_s, fb2, "fb2_s"), (tw_s, tw, "tw_s"), (fc_s, fc, "fc_s")):
        P.dma("sync", s[:], d, writes=[nm], chan="ld1")
    twr = tw_s[:, 0:128].rearrange("p (o k) -> p o k", o=1).to_broadcast([64, 8, 128])
    tws = tw_s[:, 128:256].rearrange("p (o k) -> p o k", o=1).to_broadcast([64, 8, 128])
    na_ = 0
    for ch in range(8):
        cb = ch % 2
        for pair in range(4):
            pb = na_ % 3; na_ += 1
            for h in range(2):
                c = ch * 8 + pair * 2 + h
                P.mm(psA[pb][0:64, h * 256:(h + 1) * 256], zs[:, :, c], cs_s[:], True, True, reads=["zs", "cs_s"], writes=[("psA", pb)])
            P.copy(Ych[cb][:, pair * 2:pair * 2 + 2, :].rearrange("p a b -> p (a b)"), psA[pb][0:64, :], reads=[("psA", pb)],
                   writes=[("Ych", cb)], eng="scalar")
        Yr = Ych[cb][:, :, 0:128]; Yi = Ych[cb][:, :, 128:256]; c0 = ch * 8
        P.tt(ta[:], Yr, twr, ALU.mult, reads=[("Ych", cb), "tw_s"], writes=["ta"])
        P.tt(tb[:], Yi, tws, ALU.mult, reads=[("Ych", cb), "tw_s"], writes=["tb"], eng="gpsimd")
        P.tt(Yp[:, c0:c0 + 8, 0:128], ta[:], tb[:], ALU.add, reads=["ta", "tb"], writes=[("Yp", ch)])
        P.tt(tc[:], Yi, twr, ALU.mult, reads=[("Ych", cb), "tw_s"], writes=["tc"], eng="gpsimd")
        P.tt(td[:], Yr, tws, ALU.mult, reads=[("Ych", cb), "tw_s"], writes=["td"])
        P.tt(Yp[:, c0:c0 + 8, 128:256], tc[:], td[:], ALU.subtract, reads=["tc", "td"], writes=[("Yp", ch)], eng="gpsimd")
    allYp = [("Yp", ch) for ch in range(8)]
    X1v = X1[:].rearrange("p c (k2 k1) -> p c k2 k1", k1=128)
    for g in range(32):
        pb = g % 3
        for q in range(4):
            k1 = 4 * g + q
            P.mm(psB[pb][0:64, q * 128:(q + 1) * 128], Yp[:, :, k1], fb1_s[:], True, False, reads=allYp + ["fb1_s"], writes=[("psB", pb)])
            P.mm(psB[pb][0:64, q * 128:(q + 1) * 128], Yp[:, :, 128 + k1], fb2_s[:], False, True, reads=allYp + ["fb2_s"], writes=[("psB", pb)])
        pv = psB[pb][0:64, :].rearrange("p (q c k) -> p c k q", q=4, c=2)
        for comp in range(2):
            P.copy(X1v[:, comp, :, 4 * g:4 * g + 4], pv[:, comp, :, :], reads=[("psB", pb)], writes=[("X1", g)],
                   eng=("scalar" if comp == 0 else "vector"))
    allX1 = [("X1", g) for g in range(32)]
    for blk in range(16):
        pb = blk % 2
        P.mm(psC[pb][0:64, :], fc_s[:, 0:64], X1[:, 0, blk * 512:(blk + 1) * 512], True, False, reads=allX1 + ["fc_s"], writes=[("psC", pb)])
        P.mm(psC[pb][0:64, :], fc_s[:, 64:128], X1[:, 1, blk * 512:(blk + 1) * 512], False, True, reads=allX1 + ["fc_s"], writes=[("psC", pb)])
        P.copy(Rs[pb][:], psC[pb][0:64, :], reads=[("psC", pb)], writes=[("Rs", pb)], eng="scalar")
        P.dma("gpsimd", R[:, blk * 512:(blk + 1) * 512], Rs[pb][:], reads=[("Rs", pb)], chan="st")
    if with_ctx:
        zc = P.dram_in("zc", [128, 2, 64]); cs256 = P.dram_in("cs256", [128, 2, 512]); fcc = P.dram_in("fcc", [64, 128])
        Rc = P.dram_out("Rc", [64, 256])
        zc_s = P.sbuf("zc_s", [128, 2, 64]); c2_s = P.sbuf("c2_s", [128, 2, 512]); fcc_s = P.sbuf("fcc_s", [64, 128])
        Pc = P.sbuf("Pc", [64, 512]); Rc_s = P.sbuf("Rc_s", [64, 256])
        P.dma("sync", zc_s[:], zc, writes=["zc_s"], chan="ld2"); P.dma("sync", c2_s[:], cs256, writes=["c2_s"], chan="ld2")
        P.dma("sync", fcc_s[:], fcc, writes=["fcc_s"], chan="ld2")
        for t in range(2):
            P.mm(psA[0][0:64, :], zc_s[:, t, :], c2_s[:, t, :], t == 0, t == 1, reads=["zc_s", "c2_s"], writes=[("psA", 0)])
        P.copy(Pc[:], psA[0][0:64, :], reads=[("psA", 0)], writes=["Pc"])
        P.mm(psA[1][0:64, 0:256], fcc_s[:, 0:64], Pc[:, 0:256], True, False, reads=["Pc", "fcc_s"], writes=[("psA", 1)])
        P.mm(psA[1][0:64, 0:256], fcc_s[:, 64:128], Pc[:, 256:512], False, True, reads=["Pc", "fcc_s"], writes=[("psA", 1)])
        P.copy(Rc_s[:], psA[1][0:64, 0:256], reads=[("psA", 1)], writes=["Rc_s"])
        P.dma("gpsimd", Rc, Rc_s[:], reads=["Rc_s"], chan="st")
    return P.finish()

NEG = -30000.0

def build_na(with_ctx):
    P = Prog()
    qT = P.dram_in("qT", [384, 2048]); kwT = P.dram_in("kwT", [16, 384, 576]); vw = P.dram_in("vw", [16, 128, 5, 384])
    kcT = P.dram_in("kcT", [384, 256]); vc = P.dram_in("vc", [128, 2, 384])
    tbraw = P.dram_in("tbraw", [5, 128, 6, 576]); mask = P.dram_in("mask", [5, 128, 576]); ident = P.dram_in("ident", [128, 128])
    Y = P.dram_out("Y", [2048, 384])
    qb = P.sbuf("qb", [128, 3, 2048], BF16); kcb = P.sbuf("kcb", [128, 3, 256], BF16); vcb = P.sbuf("vcb", [128, 2, 384], BF16)
    TB = P.sbuf("TB", [128, 5, 6, 576]); mk = P.sbuf("mk", [128, 5, 576])
    idb = P.sbuf("idb", [128, 128], BF16)
    stg = [P.sbuf(f"stg{i}", [128, 2048]) for i in range(2)]
    kst = [P.sbuf(f"kst{i}", [128, 3, 576]) for i in range(2)]; vst = [P.sbuf(f"vst{i}", [128, 5, 384]) for i in range(2)]
    kb = [P.sbuf(f"kb{i}", [128, 3, 576], BF16) for i in range(2)]; vb = [P.sbuf(f"vb{i}", [128, 5, 384], BF16) for i in range(2)]
    S = [P.sbuf(f"S{i}", [128, 832]) for i in range(2)]; Pb = [P.sbuf(f"Pb{i}", [128, 832], BF16) for i in range(2)]
    PT = [P.sbuf(f"PT{i}", [128, 896], BF16) for i in range(2)]
    Osb = [P.sbuf(f"Osb{i}", [128, 384]) for i in range(2)]
    mx = [P.sbuf(f"mx{i}", [128, 1]) for i in range(2)]; ssum = [P.sbuf(f"ssum{i}", [128, 1]) for i in range(2)]
    rinv = [P.sbuf(f"rinv{i}", [128, 1]) for i in range(2)]
    psA = [P.psum(f"psA{i}") for i in range(2)]; psB = [P.psum(f"psB{i}") for i in range(2)]
    psT = [P.psum(f"psT{i}", [128, 1024], BF16) for i in range(2)]; psO = [P.psum(f"psO{i}") for i in range(2)]
    si = [0]
    def load_cast(dst, src_ap, n, dres):
        b = si[0] % 2; si[0] += 1
        P.dma("sync", stg[b][:, 0:n], src_ap, writes=[("stg", b)], chan=f"stg{b}")
        P.copy(dst, stg[b][:, 0:n], reads=[("stg", b)], writes=[dres], eng="gpsimd")
    qv = qT.rearrange("(c p) n -> p c n", p=128)
    for c in range(3):
        load_cast(qb[:, c, :], qv[:, c, :], 2048, "qb")
    kcv = kcT.rearrange("(c p) n -> p c n", p=128)
    for c in range(3):
        load_cast(kcb[:, c, :], kcv[:, c, :], 256, "kcb")
    load_cast(vcb[:].rearrange("p a b -> p (a b)"), vc.rearrange("p a b -> p (a b)"), 768, "vcb")
    load_cast(idb[:], ident, 128, "idb")
    for ty in range(5):
        P.dma("sync", TB[:, ty, :, :], tbraw[ty], writes=[("TB", ty)], chan="ld0")
    P.dma("sync", mk[:], mask.rearrange("t p n -> p t n"), writes=["mk"], chan="ld0")
    for ty in range(5):
        mb = mk[:, ty, :].rearrange("p (o n) -> p o n", o=1).to_broadcast([128, 6, 576])
        P.tt(TB[:, ty, :, :], TB[:, ty, :, :], mb, ALU.add, reads=[("TB", ty), "mk"], writes=[("TB", ty)], eng="gpsimd")
    tiles = [("main", t) for t in range(16)]
    if with_ctx:
        qcT = P.dram_in("qcT", [384, 256]); Yc = P.dram_out("Yc", [256, 384])
        qcb = P.sbuf("qcb", [128, 3, 256], BF16)
        qcv = qcT.rearrange("(c p) n -> p c n", p=128)
        for c in range(3):
            load_cast(qcb[:, c, :], qcv[:, c, :], 256, "qcb")
        tiles += [("ctx", 0), ("ctx", 1)]
    it = 0
    for ti, (kind, t) in enumerate(tiles):
        wb = ti % 2
        if kind == "main":
            ty = {0: 0, 1: 1, 14: 3, 15: 4}.get(t, 2)
            P.dma("sync", kst[wb][:], kwT[t].rearrange("(c p) n -> p c n", p=128), writes=[("kst", wb)], chan=f"kst{wb}")
            P.dma("sync", vst[wb][:], vw[t], writes=[("vst", wb)], chan=f"vst{wb}")
            P.copy(kb[wb][:], kst[wb][:], reads=[("kst", wb)], writes=[("kb", wb)], eng="gpsimd")
            P.copy(vb[wb][:], vst[wb][:], reads=[("vst", wb)], writes=[("vb", wb)], eng="gpsimd")
            W = 832
        else:
            W = 256
        ob = ti % 2
        for h in range(6):
            b = it % 2; it += 1
            c, p0 = h // 2, (h % 2) * 64
            if kind == "main":
                qs = qb[p0:p0 + 64, c, t * 128:(t + 1) * 128]
            else:
                qs = qcb[p0:p0 + 64, c, t * 128:(t + 1) * 128]
            qres = "qb" if kind == "main" else "qcb"
            P.mm(psB[b][:, 64:320], qs, kcb[p0:p0 + 64, c, :], True, True, reads=[qres, "kcb"], writes=[("psB", b)])
            if kind == "main":
                P.mm(psA[b][:, 0:512], qs, kb[wb][p0:p0 + 64, c, 0:512], True, True, reads=[qres, ("kb", wb)], writes=[("psA", b)])
                P.mm(psB[b][:, 0:64], qs, kb[wb][p0:p0 + 64, c, 512:576], True, True, reads=[qres, ("kb", wb)], writes=[("psB", b)])
            P.act(S[b][:, 0:256], psB[b][:, 64:320], AF.Copy, reads=[("psB", b)], writes=[("S", b)], scale=0.125)
            if kind == "main":
                P.stt(S[b][:, 256:768], psA[b][:, 0:512], 0.125, TB[:, ty, h, 0:512], ALU.mult, ALU.add,
                      reads=[("psA", b), ("TB", ty)], writes=[("S", b)])
                P.stt(S[b][:, 768:832], psB[b][:, 0:64], 0.125, TB[:, ty, h, 512:576], ALU.mult, ALU.add,
                      reads=[("psB", b), ("TB", ty)], writes=[("S", b)])
            P.op("vector", lambda e, b=b, W=W: e.tensor_reduce(out=mx[b][:], in_=S[b][:, 0:W], axis=AX.X, op=ALU.max, negate=True),
                 reads=[("S", b)], writes=[("mx", b)])
            P.act(Pb[b][:, 0:W], S[b][:, 0:W], AF.Exp, reads=[("S", b), ("mx", b)], writes=[("Pb", b), ("ssum", b)],
                  bias=mx[b][:], scale=1.0, accum_out=ssum[b][:])
            nblk = 7 if kind == "main" else 2
            for kbk in range(nblk):
                kw = 64 if kbk == 6 else 128
                P.op("tensor", lambda e, b=b, kbk=kbk, kw=kw: e.transpose(psT[b][0:kw, kbk * 128:(kbk + 1) * 128], Pb[b][:, kbk * 128:kbk * 128 + kw], idb[:]),
                     reads=[("Pb", b), "idb"], writes=[("psT", b)], nosync_same=True)
            P.copy(PT[b][:, 0:nblk * 128], psT[b][:, 0:nblk * 128], reads=[("psT", b)], writes=[("PT", b)], eng="scalar")
            for kbk in range(nblk):
                kw = 64 if kbk == 6 else 128
                if kbk < 2:
                    rhs = vcb[:, kbk, h * 64:(h + 1) * 64]; rres = "vcb"
                else:
                    rhs = vb[wb][0:kw, kbk - 2, h * 64:(h + 1) * 64]; rres = ("vb", wb)
                P.mm(psO[ob][:, h * 64:(h + 1) * 64], PT[b][0:kw, kbk * 128:(kbk + 1) * 128], rhs, kbk == 0, kbk == nblk - 1,
                     reads=[("PT", b), rres], writes=[("psO", ob)])
            P.op("vector", lambda e, b=b: e.reciprocal(out=rinv[b][:], in_=ssum[b][:]), reads=[("ssum", b)], writes=[("rinv", b)])
            P.ts(Osb[ob][:, h * 64:(h + 1) * 64], psO[ob][:, h * 64:(h + 1) * 64], rinv[b][:], None, ALU.mult,
                 reads=[("psO", ob), ("rinv", b)], writes=[("Osb", ob)])
        dst = Y[t * 128:(t + 1) * 128, :] if kind == "main" else Yc[t * 128:(t + 1) * 128, :]
        P.dma("gpsimd", dst, Osb[ob][:], reads=[("Osb", ob)], chan="st")
    return P.finish()

def na_tile_geometry(r0):
    R = 128
    rs0 = int(np.clip(r0 - 4, 0, R - 8)); rs1 = int(np.clip(r0 + 1 - 4, 0, R - 8))
    return rs0, rs1

def na_tables(rpb, q):
    cols = np.arange(64); cs = np.clip(cols - 8, 0, 48)
    kc = np.arange(64)
    inwin = (kc[None, :] >= cs[:, None]) & (kc[None, :] < cs[:, None] + 16)
    dc = np.clip(kc[None, :] - cols[:, None] + 15, 0, 30)
    types = [32 * q, 32 * q + 2, 32 * q + 16, 32 * q + 28, 32 * q + 30]
    tbraw = np.zeros((5, 128, 6, 9, 64), np.float32); mask = np.zeros((5, 128, 9, 64), np.float32)
    for ti, r0 in enumerate(types):
        rs0, rs1 = na_tile_geometry(r0)
        for half, (r, rs) in enumerate(((r0, rs0), (r0 + 1, rs1))):
            for slot in range(9):
                krow = rs0 + slot
                valid = (krow >= rs) and (krow < rs + 8)
                dr = int(np.clip(krow - r + 7, 0, 14))
                g = rpb[:, dr][:, dc]
                tbraw[ti, half * 64:(half + 1) * 64, :, slot, :] = g.transpose(1, 0, 2)
                m = np.where(inwin & valid, 0.0, NEG).astype(np.float32)
                mask[ti, half * 64:(half + 1) * 64, slot, :] = m
    return tbraw.reshape(5, 128, 6, 576), mask.reshape(5, 128, 576)

def na_windows(k_b, v_b, q):
    kp = np.concatenate([k_b, np.zeros((64 * 16, 384), np.float32)], 0); vp = np.concatenate([v_b, np.zeros((64 * 16, 384), np.float32)], 0)
    kwT = np.zeros((16, 384, 576), np.float32); vw = np.zeros((16, 128, 5, 384), np.float32)
    for t in range(16):
        r0 = 32 * q + 2 * t
        rs0, _ = na_tile_geometry(r0)
        kwT[t] = kp[rs0 * 64:(rs0 + 9) * 64].T
        for j in range(4):
            vw[t, :, j, :] = vp[(rs0 + 2 * j) * 64:(rs0 + 2 * j + 2) * 64]
        vw[t, 0:64, 4, :] = vp[(rs0 + 8) * 64:(rs0 + 9) * 64]
    return kwT, vw

NCH = 1056
PI = math.pi

class A:
    def __init__(self, P): self.P = P
    @staticmethod
    def nm(*aps): return [a.tensor.name for a in aps if hasattr(a, "tensor")]
    def tt(self, o, a, b, op, eng="vector"): self.P.tt(o, a, b, op, reads=self.nm(a, b), writes=self.nm(o), eng=eng)
    def ts(self, o, a, s1, op0, s2=None, op1=None, eng="vector"):
        self.P.ts(o, a, s1, s2, op0, op1, reads=self.nm(a, s1, s2), writes=self.nm(o), eng=eng)
    def stt(self, o, a, s, b, op0, op1): self.P.stt(o, a, s, b, op0, op1, reads=self.nm(a, s, b), writes=self.nm(o))
    def act(self, o, a, f, **kw): self.P.act(o, a, f, reads=self.nm(a, *[v for v in kw.values()]), writes=self.nm(o), **kw)
    def copy(self, o, a, eng="vector"): self.P.copy(o, a, reads=self.nm(a), writes=self.nm(o), eng=eng)
    def memset(self, o, v, eng="vector"): self.P.memset(o, v, writes=self.nm(o), eng=eng)
    def mm(self, o, l, r, st, sp): self.P.mm(o, l, r, st, sp, reads=self.nm(l, r), writes=self.nm(o))
    def dma_in(self, o, src, chan): self.P.dma("sync", o, src, writes=self.nm(o), chan=chan)
    def dma_out(self, dst, a, chan="st"): self.P.dma("gpsimd", dst, a, reads=self.nm(a), chan=chan)
    def scan(self, o, d0, d1, init):
        self.P.op("vector", lambda e: e.tensor_tensor_scan(out=o, data0=d0, data1=d1, initial=init, op0=ALU.mult, op1=ALU.add),
                  reads=self.nm(d0, d1, init), writes=self.nm(o))
    def recip(self, o, a): self.P.op("vector", lambda e: e.reciprocal(out=o, in_=a), reads=self.nm(a), writes=self.nm(o))
    def transpose(self, o, a, ident):
        self.P.op("tensor", lambda e: e.transpose(o, a, ident), reads=self.nm(a, ident), writes=self.nm(o), nosync_same=True)
    def cmul_s(self, o_re, o_im, a_re, a_im, s_re, s_im, s_imn):
        self.ts(o_re, a_re, s_re, ALU.mult)
        self.stt(o_re, a_im, s_imn, o_re, ALU.mult, ALU.add)
        self.ts(o_im, a_re, s_im, ALU.mult)
        self.stt(o_im, a_im, s_re, o_im, ALU.mult, ALU.add)

def build_ssm():
    P = Prog(); a = A(P)
    d_in = {}
    for nm_, shp in (("are", [128, 6]), ("aim", [128, 6]), ("ldt", [128, 6]), ("Bre", [128, 96]), ("Bim", [128, 96]),
                     ("Cre", [128, 96]), ("Cim", [128, 96]), ("maskF", [128, 128]), ("maskB", [128, 128]), ("sgn", [128, 1]),
                     ("ident", [128, 128])):
        d_in[nm_] = P.dram_in(nm_, shp)
    Ddiag = P.dram_in("Ddiag", [6, 128, 128]); U = P.dram_in("U", [6, 128, NCH]); Yg = P.dram_out("Yg", [6, 128, NCH])
    s = {}
    for nm_, ap in d_in.items():
        shp = list(ap.shape)
        s[nm_] = P.sbuf("s_" + nm_, shp)
        a.dma_in(s[nm_][:], ap, "ld0")
    def T(name, shape): return P.sbuf(name, shape)
    dt = T("dt", [128, 6]); x = T("x", [128, 6]); th = T("th", [128, 6]); er = T("er", [128, 6]); m = T("m", [128, 6])
    y2 = T("y2", [128, 6]); sn = T("sn", [128, 6]); cs = T("cs", [128, 6]); lbr = T("lbr", [128, 6]); lbi = T("lbi", [128, 6])
    n2 = T("n2", [128, 6]); t1 = T("t1", [128, 6]); t2 = T("t2", [128, 6]); am1 = T("am1", [128, 6])
    qr = T("qr", [128, 6]); qi = T("qi", [128, 6]); qin = T("qin", [128, 6])
    a.act(dt[:], s["ldt"][:], AF.Exp)
    a.tt(x[:], s["are"][:], dt[:], ALU.mult); a.tt(th[:], s["aim"][:], dt[:], ALU.mult)
    a.act(er[:], x[:], AF.Exp)
    for _ in range(4):
        a.ts(m[:], th[:], PI, ALU.is_gt)
        a.stt(th[:], m[:], -2 * PI, th[:], ALU.mult, ALU.add)
    a.ts(y2[:], th[:], PI / 2, ALU.add)
    a.ts(m[:], y2[:], PI, ALU.is_gt)
    a.stt(y2[:], m[:], -2 * PI, y2[:], ALU.mult, ALU.add)
    a.act(sn[:], th[:], AF.Sin); a.act(cs[:], y2[:], AF.Sin)
    a.tt(lbr[:], er[:], cs[:], ALU.mult); a.tt(lbi[:], er[:], sn[:], ALU.mult)
    a.tt(n2[:], s["are"][:], s["are"][:], ALU.mult); a.tt(t1[:], s["aim"][:], s["aim"][:], ALU.mult); a.tt(n2[:], n2[:], t1[:], ALU.add)
    a.recip(n2[:], n2[:])
    a.ts(am1[:], lbr[:], -1.0, ALU.add)
    a.tt(t1[:], am1[:], s["are"][:], ALU.mult); a.tt(t2[:], lbi[:], s["aim"][:], ALU.mult); a.tt(t1[:], t1[:], t2[:], ALU.add)
    a.tt(qr[:], t1[:], n2[:], ALU.mult)
    a.tt(t1[:], lbi[:], s["are"][:], ALU.mult); a.tt(t2[:], am1[:], s["aim"][:], ALU.mult); a.tt(t1[:], t1[:], t2[:], ALU.subtract)
    a.tt(qi[:], t1[:], n2[:], ALU.mult)
    a.ts(qin[:], qi[:], -1.0, ALU.mult)
    Lr = T("Lr", [128, 6, 9]); Li = T("Li", [128, 6, 9]); Vr = T("Vr", [128, 6, 8]); Vi = T("Vi", [128, 6, 8])
    Rr = T("Rr", [128, 6, 9]); Ri = T("Ri", [128, 6, 9])
    e2 = T("e2", [128, 6]); ivr = T("ivr", [128, 6]); ivi = T("ivi", [128, 6])
    a.memset(Lr[:, :, 0], 1.0); a.memset(Li[:, :, 0], 0.0); a.memset(Vr[:, :, 0], 1.0); a.memset(Vi[:, :, 0], 0.0)
    a.act(e2[:], x[:], AF.Exp, scale=-2.0)
    a.tt(ivr[:], lbr[:], e2[:], ALU.mult); a.tt(ivi[:], lbi[:], e2[:], ALU.mult); a.ts(ivi[:], ivi[:], -1.0, ALU.mult)
    def cmul_t(o_r, o_i, p_r, p_i, q_r, q_i):
        a.tt(t1[:], p_r, q_r, ALU.mult); a.tt(t2[:], p_i, q_i, ALU.mult); a.tt(o_r, t1[:], t2[:], ALU.subtract)
        a.tt(t1[:], p_r, q_i, ALU.mult); a.tt(t2[:], p_i, q_r, ALU.mult); a.tt(o_i, t1[:], t2[:], ALU.add)
    for k in range(8):
        cmul_t(Lr[:, :, k + 1], Li[:, :, k + 1], Lr[:, :, k], Li[:, :, k], lbr[:], lbi[:])
    for k in range(7):
        cmul_t(Vr[:, :, k + 1], Vi[:, :, k + 1], Vr[:, :, k], Vi[:, :, k], ivr[:], ivi[:])
    for k in range(9):
        a.copy(Rr[:, :, k], Lr[:, :, 8 - k], eng="gpsimd"); a.copy(Ri[:, :, k], Li[:, :, 8 - k], eng="gpsimd")
    tabs = {}
    for nm_, (lo_r, lo_i, hi_r, hi_i) in dict(
            XL=(Vr[0:64, :, 0:8], Vi[0:64, :, 0:8], Lr[64:128, :, 0:8], Li[64:128, :, 0:8]),
            YL=(Lr[0:64, :, 0:8], Li[0:64, :, 0:8], Vr[64:128, :, 0:8], Vi[64:128, :, 0:8]),
            SL=(Rr[0:64, :, 1:9], Ri[0:64, :, 1:9], Lr[64:128, :, 0:8], Li[64:128, :, 0:8]),
            OL=(Lr[0:64, :, 1:9], Li[0:64, :, 1:9], Rr[64:128, :, 0:8], Ri[64:128, :, 0:8])).items():
        tr = T(nm_ + "r", [128, 6, 8]); ti = T(nm_ + "i", [128, 6, 8]); tn = T(nm_ + "n", [128, 6, 8])
        a.copy(tr[0:64], lo_r); a.copy(ti[0:64], lo_i); a.copy(tr[64:128], hi_r); a.copy(ti[64:128], hi_i)
        a.ts(tn[:], ti[:], -1.0, ALU.mult)
        tabs[nm_] = (tr, ti, tn)
    rho8 = T("rho8", [128, 6]); c8 = T("c8", [128, 6]); s8 = T("s8", [128, 6]); e8 = T("e8", [128, 6])
    a.act(rho8[:], x[:], AF.Exp, scale=8.0); a.act(e8[:], x[:], AF.Exp, scale=-8.0)
    a.tt(c8[:], Lr[:, :, 8], e8[:], ALU.mult); a.tt(s8[:], Li[:, :, 8], e8[:], ALU.mult)
    a.ts(s8[:], s8[:], s["sgn"][:, 0:1], ALU.mult)
    onesT = T("onesT", [128, NCH]); a.memset(onesT[:], 1.0, eng="gpsimd")
    Bbr = T("Bbr", [128, 16]); Bbi = T("Bbi", [128, 16])
    Xr = T("Xr", [128, 8, 16]); Xi = T("Xi", [128, 8, 16]); Yr = T("Yr", [128, 8, 16]); Yin = T("Yin", [128, 8, 16])
    Wtr = T("Wtr", [128, 8, 16]); Wti = T("Wti", [128, 8, 16]); Wor = T("Wor", [128, 8, 16]); Woin = T("Woin", [128, 8, 16])
    Wsr = T("Wsr", [128, 128]); Wsi = T("Wsi", [128, 128]); Msb = T("Msb", [128, 128]); Mtmp = T("Mtmp", [128, 128]); Dd = T("Dd", [128, 128])
    Us = [T(f"Us{i}", [128, NCH]) for i in range(2)]
    Sre = T("Sre", [128, NCH]); Sim = T("Sim", [128, NCH]); Spr = T("Spr", [128, NCH]); Spi = T("Spi", [128, NCH])
    Gre = T("Gre", [128, NCH]); Gim = T("Gim", [128, NCH]); Hor = T("Hor", [128, NCH]); Hoi = T("Hoi", [128, NCH])
    Hir = T("Hir", [128, NCH]); Hii = T("Hii", [128, NCH])
    Tr = T("Tr", [128, NCH + 1]); Ti = T("Ti", [128, NCH + 1]); rhoT = T("rhoT", [128, NCH])
    w1 = T("w1", [128, NCH]); w2 = T("w2", [128, NCH])
    mult = T("mult", [128, 11, 3]); ini = T("ini", [128, 4]); Ysb = T("Ysb", [128, NCH])
    ps = [P.psum(f"ps{i}") for i in range(8)]
    BLK = [(0, 512), (512, 512), (1024, NCH - 1024)]
    for gi in range(6):
        ub = gi % 2
        a.dma_in(Us[ub][:], U[gi], f"u{ub}")
        a.dma_in(Dd[:], Ddiag[gi], "dd")
        g16 = slice(gi * 16, gi * 16 + 16)
        a.cmul_s(Bbr[:], Bbi[:], s["Bre"][:, g16], s["Bim"][:, g16], qr[:, gi:gi + 1], qi[:, gi:gi + 1], qin[:, gi:gi + 1])
        XL, YL, SL, OL = tabs["XL"], tabs["YL"], tabs["SL"], tabs["OL"]
        for k in range(8):
            sc = lambda tb: (tb[0][:, gi, k:k + 1], tb[1][:, gi, k:k + 1], tb[2][:, gi, k:k + 1])
            a.cmul_s(Xr[:, k, :], Xi[:, k, :], Bbr[:], Bbi[:], *sc(XL))
            a.cmul_s(Wtr[:, k, :], Wti[:, k, :], Bbr[:], Bbi[:], *sc(SL))
            a.cmul_s(Yr[:, k, :], Yin[:, k, :], s["Cre"][:, g16], s["Cim"][:, g16], *sc(YL))
            a.cmul_s(Wor[:, k, :], Woin[:, k, :], s["Cre"][:, g16], s["Cim"][:, g16], *sc(OL))
        a.ts(Yin[:], Yin[:], -1.0, ALU.mult); a.ts(Woin[:], Woin[:], -1.0, ALU.mult)
        f2 = lambda t_: t_[:].rearrange("p a b -> p (a b)")
        for half, pb in ((0, 6), (1, 7)):
            rows = slice(half * 64, half * 64 + 64)
            a.mm(ps[pb][:, 0:128], f2(Xr)[rows], f2(Yr)[rows], True, False)
            a.mm(ps[pb][:, 0:128], f2(Xi)[rows], f2(Yin)[rows], False, True)
        a.tt(Msb[:], ps[6][:, 0:128], s["maskF"][:], ALU.mult)
        a.tt(Mtmp[:], ps[7][:, 0:128], s["maskB"][:], ALU.mult)
        a.tt(Msb[:], Msb[:], Mtmp[:], ALU.add, eng="gpsimd"); a.tt(Msb[:], Msb[:], Dd[:], ALU.add, eng="gpsimd")
        a.transpose(ps[6][:, 128:256], f2(Wtr), s["ident"][:]); a.transpose(ps[7][:, 128:256], f2(Wti), s["ident"][:])
        a.copy(Wsr[:], ps[6][:, 128:256], eng="scalar"); a.copy(Wsi[:], ps[7][:, 128:256], eng="scalar")
        for bi, (c0, cn) in enumerate(BLK):
            a.mm(ps[bi][:, 0:cn], Wsr[:], Us[ub][:, c0:c0 + cn], True, True)
            a.mm(ps[3 + bi][:, 0:cn], Wsi[:], Us[ub][:, c0:c0 + cn], True, True)
            a.copy(Sre[:, c0:c0 + cn], ps[bi][:, 0:cn], eng="scalar"); a.copy(Sim[:, c0:c0 + cn], ps[3 + bi][:, 0:cn], eng="scalar")
        a.memset(Tr[:, 0:1], 1.0); a.memset(Ti[:, 0:1], 0.0)
        a.copy(mult[:, 0, 0:1], c8[:, gi:gi + 1]); a.copy(mult[:, 0, 1:2], s8[:, gi:gi + 1])
        a.ts(mult[:, 0, 2:3], mult[:, 0, 1:2], -1.0, ALU.mult)
        for k in range(1, 11):
            a.tt(ini[:, 0:1], mult[:, k - 1, 0:1], mult[:, k - 1, 0:1], ALU.mult); a.tt(ini[:, 1:2], mult[:, k - 1, 1:2], mult[:, k - 1, 1:2], ALU.mult)
            a.tt(mult[:, k, 0:1], ini[:, 0:1], ini[:, 1:2], ALU.subtract)
            a.tt(ini[:, 0:1], mult[:, k - 1, 0:1], mult[:, k - 1, 1:2], ALU.mult)
            a.ts(mult[:, k, 1:2], ini[:, 0:1], 2.0, ALU.mult); a.ts(mult[:, k, 2:3], ini[:, 0:1], -2.0, ALU.mult)
        for k in range(11):
            n = 1 << k
            cnt = min(n, NCH + 1 - n)
            a.cmul_s(Tr[:, n:n + cnt], Ti[:, n:n + cnt], Tr[:, 0:cnt], Ti[:, 0:cnt], mult[:, k, 0:1], mult[:, k, 1:2], mult[:, k, 2:3])
        a.ts(rhoT[:], onesT[:], rho8[:, gi:gi + 1], ALU.mult, eng="gpsimd")
        a.tt(w1[:], Sre[:], Tr[:, 0:NCH], ALU.mult); a.tt(w2[:], Sim[:], Ti[:, 0:NCH], ALU.mult, eng="gpsimd")
        a.tt(Spr[:], w1[:], w2[:], ALU.subtract)
        a.tt(w1[:], Sre[:], Ti[:, 0:NCH], ALU.mult); a.tt(w2[:], Sim[:], Tr[:, 0:NCH], ALU.mult, eng="gpsimd")
        a.tt(Spi[:], w1[:], w2[:], ALU.add)
        for (Gx, Sx) in ((Gre, Spr), (Gim, Spi)):
            a.scan(Gx[0:64, :], rhoT[0:64, :], Sx[0:64, :], 0.0)
            a.scan(Gx[64:128, 0:32][:, ::-1], rhoT[64:128, 0:32], Sx[64:128, 0:32][:, ::-1], 0.0)
        lo = slice(64, 128)
        a.tt(ini[lo, 0:1], Gre[lo, 0:1], Tr[lo, NCH:NCH + 1], ALU.mult); a.tt(ini[lo, 1:2], Gim[lo, 0:1], Ti[lo, NCH:NCH + 1], ALU.mult)
        a.tt(ini[lo, 2:3], ini[lo, 0:1], ini[lo, 1:2], ALU.subtract)
        a.tt(ini[lo, 0:1], Gre[lo, 0:1], Ti[lo, NCH:NCH + 1], ALU.mult); a.tt(ini[lo, 1:2], Gim[lo, 0:1], Tr[lo, NCH:NCH + 1], ALU.mult)
        a.tt(ini[lo, 3:4], ini[lo, 0:1], ini[lo, 1:2], ALU.add)
        a.scan(Gre[lo, 32:NCH][:, ::-1], rhoT[lo, 32:NCH], Spr[lo, 32:NCH][:, ::-1], ini[lo, 2:3])
        a.scan(Gim[lo, 32:NCH][:, ::-1], rhoT[lo, 32:NCH], Spi[lo, 32:NCH][:, ::-1], ini[lo, 3:4])
        a.tt(w1[:], Gre[:], Tr[:, 0:NCH], ALU.mult); a.tt(w2[:], Gim[:], Ti[:, 0:NCH], ALU.mult, eng="gpsimd")
        a.tt(Hor[:], w1[:], w2[:], ALU.add)
        a.tt(w1[:], Gim[:], Tr[:, 0:NCH], ALU.mult); a.tt(w2[:], Gre[:], Ti[:, 0:NCH], ALU.mult, eng="gpsimd")
        a.tt(Hoi[:], w1[:], w2[:], ALU.subtract)
        for (Hi_, Ho_, Gx) in ((Hir, Hor, Gre), (Hii, Hoi, Gim)):
            a.copy(Hi_[0:64, 1:NCH], Ho_[0:64, 0:NCH - 1], eng="scalar"); a.memset(Hi_[0:64, 0:1], 0.0)
            a.copy(Hi_[lo, 0:NCH - 1], Ho_[lo, 1:NCH], eng="scalar"); a.memset(Hi_[lo, 31:32], 0.0)
            a.copy(Hi_[lo, NCH - 1:NCH], Gx[lo, 0:1])
        for bi, (c0, cn) in enumerate(BLK):
            a.mm(ps[bi][:, 0:cn], Msb[:], Us[ub][:, c0:c0 + cn], True, False)
            a.mm(ps[bi][:, 0:cn], f2(Wor), Hir[:, c0:c0 + cn], False, False)
            a.mm(ps[bi][:, 0:cn], f2(Woin), Hii[:, c0:c0 + cn], False, True)
            a.copy(Ysb[:, c0:c0 + cn], ps[bi][:, 0:cn], eng="scalar")
        a.dma_out(Yg[gi], Ysb[:])
    return P.finish()

def ssm_inputs(inp, l, j4, u_b, uc_b):
    gs = np.arange(6 * j4, 6 * j4 + 6)
    def rows(arr):
        return np.ascontiguousarray(arr[:, gs, :].transpose(0, 2, 1).reshape(128, 6))
    are = rows(inp["ssm_a_re"][l]); aim = rows(inp["ssm_a_im"][l])
    ldt = np.ascontiguousarray(np.repeat(inp["ssm_log_dt"][l][:, gs][:, None, :], 64, axis=1).reshape(128, 6))
    def rowsB(arr):
        return np.ascontiguousarray(arr[:, gs].transpose(0, 2, 1, 3).reshape(128, 96))
    def rowsC(arr):
        return np.ascontiguousarray(arr[:, gs].transpose(0, 3, 1, 2).reshape(128, 96))
    s_ = np.arange(8)
    mF = (s_[None, :] >= s_[:, None]).astype(np.float32)
    maskF = np.kron(mF, np.ones((16, 16), np.float32)); maskB = np.kron(mF.T, np.ones((16, 16), np.float32))
    sgn = np.concatenate([-np.ones((64, 1), np.float32), np.ones((64, 1), np.float32)], 0)
    dsk = inp["ssm_d"][l]
    Dd = np.zeros((6, 128, 128), np.float32)
    for gi, g in enumerate(gs):
        dd = np.zeros((8, 16, 8, 16), np.float32)
        for t in range(8):
            dd[t, np.arange(16), t, np.arange(16)] = dsk[16 * g:16 * g + 16]
        Dd[gi] = dd.reshape(128, 128)
    seq = np.concatenate([uc_b, u_b], 0)
    U = np.zeros((6, 128, NCH), np.float32)
    for gi, g in enumerate(gs):
        U[gi] = seq[:, 16 * g:16 * g + 16].reshape(NCH, 128).T
    return dict(are=are, aim=aim, ldt=ldt, Bre=rowsB(inp["ssm_b_re"][l]), Bim=rowsB(inp["ssm_b_im"][l]),
                Cre=rowsC(inp["ssm_c_re"][l]), Cim=rowsC(inp["ssm_c_im"][l]), maskF=maskF, maskB=maskB, sgn=sgn,
                ident=np.eye(128, dtype=np.float32), Ddiag=Dd, U=U)

def ssm_unpack(Yg):
    return np.ascontiguousarray(Yg.transpose(2, 1, 0).reshape(NCH, 8, 16, 6).transpose(0, 1, 3, 2).reshape(NCH * 8, 96))


_PROGS = {}
def _prog(name, fn):
    if name not in _PROGS:
        _PROGS[name] = fn()
    return _PROGS[name]

def _run(nc, maps):
    res = run_bass_kernel_spmd(nc, maps, core_ids=list(range(8)))
    return res.results

def kernel(x, c, ctx, c_ctx, w_mod, b_mod, g_pre_mix, g_post_mix, w_in, ssm_a_re, ssm_a_im, ssm_log_dt, ssm_b_re, ssm_b_im,
           ssm_c_re, ssm_c_im, ssm_d, w_glu, w_fourier, na_rpb, w_out, g_pre_ffn, g_post_ffn, w_ffn_gate, w_ffn_up, w_ffn_down):
    f32 = lambda a: np.ascontiguousarray(np.asarray(a, dtype=np.float32))
    inp = dict(ssm_a_re=f32(ssm_a_re), ssm_a_im=f32(ssm_a_im), ssm_log_dt=f32(ssm_log_dt), ssm_b_re=f32(ssm_b_re), ssm_b_im=f32(ssm_b_im),
               ssm_c_re=f32(ssm_c_re), ssm_c_im=f32(ssm_c_im), ssm_d=f32(ssm_d))
    x = f32(x); c = f32(c); ctx = f32(ctx); c_ctx = f32(c_ctx); w_mod = f32(w_mod); b_mod = f32(b_mod)
    w_in = f32(w_in); w_glu = f32(w_glu); w_fourier = f32(w_fourier); na_rpb = f32(na_rpb); w_out = f32(w_out)
    g_pre_mix = f32(g_pre_mix); g_post_mix = f32(g_post_mix); g_pre_ffn = f32(g_pre_ffn); g_post_ffn = f32(g_post_ffn)
    w_ffn_gate = f32(w_ffn_gate); w_ffn_up = f32(w_ffn_up); w_ffn_down = f32(w_ffn_down)
    DEPTH = 2
    cores = [(k // 4, k % 4) for k in range(8)]
    cTs = [np.ascontiguousarray(np.concatenate([colT(c[b], 8), colT(c_ctx, 8)], axis=1)) for b in range(2)]
    xT = [np.ascontiguousarray(np.concatenate([x[b, q * 2048:(q + 1) * 2048].T, ctx[b].T], axis=1)) for (b, q) in cores]
    KF = fnet_consts(); permm = perm_matrix(); ident = np.eye(128, dtype=np.float32)
    ropes = [rope_tables(q * 2048, 2048) for q in range(4)]
    for l in range(DEPTH):
        maps = []
        for k, (b, q) in enumerate(cores):
            maps.append(dict(xT=xT[k], w_in=w_in[l], w_mod=np.ascontiguousarray(w_mod[l][:, 0:2048]), b_modT=colT(b_mod[l][0:2048], 16),
                             g_preT=colT(g_pre_mix[l], 8), cT=cTs[b], cos=ropes[q][0], sin=ropes[q][1], perm=permm))
        res = _run(_prog("l1", build_l1), maps)
        h_lat = [np.concatenate([res[4 * b + q]["hT"][:, 0:2048].T for q in range(4)], 0) for b in range(2)]
        h_ctx = [np.ascontiguousarray(res[4 * b]["hT"][:, 2048:2304].T) for b in range(2)]
        del res
        maps = [ssm_inputs(inp, l, j4, h_lat[b][:, 0:384], h_ctx[b][:, 0:384]) for (b, j4) in cores]
        res = _run(_prog("ssm", build_ssm), maps)
        ys = [[ssm_unpack(res[4 * b + j4]["Yg"]) for j4 in range(4)] for b in range(2)]
        ysT = [np.ascontiguousarray(np.concatenate(ys[b], 1).T) for b in range(2)]
        del res, ys
        maps = []
        for (b, g) in cores:
            m = dict(z=np.ascontiguousarray(h_lat[b][:, 384 + 64 * g:448 + 64 * g].reshape(128, 4096)),
                     zc=np.ascontiguousarray(h_ctx[b][:, 384 + 64 * g:448 + 64 * g].reshape(2, 128, 64).transpose(1, 0, 2)))
            m.update(KF); maps.append(m)
        res = _run(_prog("fnet", lambda: build_fnet(True)), maps)
        mxT = [np.concatenate([res[4 * b + g]["R"] for g in range(4)], 0) for b in range(2)]
        mxcT = [np.concatenate([res[4 * b + g]["Rc"] for g in range(4)], 0) for b in range(2)]
        del res
        maps = []
        for (b, q) in cores:
            kwT, vw = na_windows(h_lat[b][:, 1024:1408], h_lat[b][:, 1408:1792], q)
            tbraw, mask = na_tables(na_rpb[l], q)
            maps.append(dict(qT=np.ascontiguousarray(h_lat[b][q * 2048:(q + 1) * 2048, 640:1024].T), kwT=kwT, vw=vw,
                             kcT=np.ascontiguousarray(h_ctx[b][:, 1024:1408].T),
                             vc=np.ascontiguousarray(h_ctx[b][:, 1408:1792].reshape(2, 128, 384).transpose(1, 0, 2)),
                             tbraw=tbraw, mask=mask, ident=ident, qcT=np.ascontiguousarray(h_ctx[b][:, 640:1024].T)))
        res = _run(_prog("na", lambda: build_na(True)), maps)
        naT = [np.ascontiguousarray(np.concatenate([res[4 * b + q]["Y"] for q in range(4)], 0).T) for b in range(2)]
        nacT = [np.ascontiguousarray(res[4 * b]["Yc"].T) for b in range(2)]
        del res, h_lat
        maps = []
        for k, (b, q) in enumerate(cores):
            sl = slice(q * 2048, (q + 1) * 2048)
            maps.append(dict(xT=xT[k], ysT=np.ascontiguousarray(np.concatenate([ysT[b][:, 256 + q * 2048:256 + (q + 1) * 2048], ysT[b][:, 0:256]], 1)),
                             mxT=np.ascontiguousarray(np.concatenate([mxT[b][:, sl], mxcT[b]], 1)),
                             naT=np.ascontiguousarray(np.concatenate([naT[b][:, sl], nacT[b]], 1)),
                             w_mod=np.ascontiguousarray(w_mod[l][:, 2048:3072]), b_modT=colT(b_mod[l][2048:3072], 8), cT=cTs[b],
                             g_postT=colT(g_post_mix[l], 8), w_glu=w_glu[l], w_fourier=w_fourier[l], w_out=w_out[l]))
        res = _run(_prog("l3a", lambda: build_l3a(True)), maps)
        xT = [res[k]["xoT"] for k in range(8)]
        del res
        maps = []
        for k, (b, q) in enumerate(cores):
            maps.append(dict(xT=xT[k], w_mod=np.ascontiguousarray(w_mod[l][:, 3072:6144]), b_modT=colT(b_mod[l][3072:6144], 24), cT=cTs[b],
                             g_preT=colT(g_pre_ffn[l], 8), g_postT=colT(g_post_ffn[l], 8),
                             w_gate=w_ffn_gate[l], w_up=w_ffn_up[l], w_down=w_ffn_down[l]))
        res = _run(_prog("l3b", lambda: build_l3b(True)), maps)
        xT = [np.ascontiguousarray(res[k]["xoT"]) for k in range(8)]
        del res
    out = np.empty((2, 8192, 1024), np.float32)
    for k, (b, q) in enumerate(cores):
        out[b, q * 2048:(q + 1) * 2048] = xT[k][:, 0:2048].T
    return out
```

```python
import math
import numpy as np
from contextlib import ExitStack
import concourse.bass as bass
import concourse.mybir as mybir
from concourse.bass_utils import run_bass_kernel_spmd


F32 = mybir.dt.float32
BF16 = mybir.dt.bfloat16
ALU = mybir.AluOpType
AF = mybir.ActivationFunctionType
AX = mybir.AxisListType

COMPUTE = ("tensor", "vector", "scalar", "gpsimd")


class Prog:
    def __init__(self):
        self.nc = bass.Bass("TRN2", target_bir_lowering=False)
        self.ops = []
        self.stack = ExitStack()
        self.ndram = 0

    def dram_in(self, name, shape, dtype=F32):
        return self.nc.dram_tensor(name, list(shape), dtype, kind="ExternalInput").ap()

    def dram_out(self, name, shape, dtype=F32):
        return self.nc.dram_tensor(name, list(shape), dtype, kind="ExternalOutput").ap()

    def sbuf(self, name, shape, dtype=F32):
        return self.stack.enter_context(self.nc.sbuf_tensor(name, list(shape), dtype))

    def psum(self, name, shape=(128, 512), dtype=F32):
        return self.stack.enter_context(self.nc.psum_tensor(name, list(shape), dtype))

    def op(self, eng, fn, reads=(), writes=(), chan=None, nosync_same=False, inc=True):
        self.ops.append(dict(eng=eng, fn=fn, reads=tuple(reads), writes=tuple(writes),
                             chan=chan, nosync_same=nosync_same, inc=inc))

    def dma(self, eng, out, in_, reads=(), writes=(), chan="ld", **kw):
        self.op(eng, lambda e: e.dma_start(out=out, in_=in_, **kw), reads, writes, chan=chan)

    def mm(self, out, lhsT, rhs, start, stop, reads=(), writes=()):
        self.op("tensor", lambda e: e.matmul(out, lhsT, rhs, start=start, stop=stop),
                reads, writes, nosync_same=True, inc=True)

    def act(self, out, in_, func, reads=(), writes=(), **kw):
        self.op("scalar", lambda e: e.activation(out=out, in_=in_, func=func, **kw), reads, writes)

    def tt(self, out, in0, in1, op, reads=(), writes=(), eng="vector"):
        self.op(eng, lambda e: e.tensor_tensor(out=out, in0=in0, in1=in1, op=op), reads, writes)

    def ts(self, out, in0, s1, s2, op0, op1=None, reads=(), writes=(), eng="vector"):
        if op1 is None:
            self.op(eng, lambda e: e.tensor_scalar(out=out, in0=in0, scalar1=s1, scalar2=None, op0=op0),
                    reads, writes)
        else:
            self.op(eng, lambda e: e.tensor_scalar(out=out, in0=in0, scalar1=s1, scalar2=s2, op0=op0, op1=op1),
                    reads, writes)

    def stt(self, out, in0, scalar, in1, op0, op1, reads=(), writes=()):
        self.op("vector", lambda e: e.scalar_tensor_tensor(out=out, in0=in0, scalar=scalar, in1=in1,
                                                            op0=op0, op1=op1), reads, writes)

    def copy(self, out, in_, reads=(), writes=(), eng="vector"):
        if eng == "scalar":
            self.op(eng, lambda e: e.copy(out=out, in_=in_), reads, writes)
        else:
            self.op(eng, lambda e: e.tensor_copy(out=out, in_=in_), reads, writes)

    def memset(self, ap, val, writes=(), eng="vector"):
        self.op(eng, lambda e: e.memset(ap, val), (), writes)

    def finish(self):
        nc = self.nc
        ops = self.ops
        engines = []
        for o in ops:
            if o["eng"] not in engines:
                engines.append(o["eng"])
        chans = []
        for o in ops:
            if o["chan"] is not None and o["chan"] not in chans:
                chans.append(o["chan"])
        sems = {}
        for e in engines:
            sems[("e", e)] = self.stack.enter_context(nc.semaphore("s_" + e))
        for c in chans:
            sems[("c", c)] = self.stack.enter_context(nc.semaphore("c_" + c))
        def plan_pass():
            viol = set()
            eng_count = {e: 0 for e in engines}
            chan_count = {c: 0 for c in chans}
            last_writer = {}
            readers = {}
            known = {e: {} for e in engines}
            plan = {e: [] for e in engines}
            done = []
            for i, o in enumerate(ops):
                e = o["eng"]
                deps = set()
                for r in o["reads"]:
                    if r in last_writer:
                        deps.add(last_writer[r])
                for w in o["writes"]:
                    if w in last_writer:
                        deps.add(last_writer[w])
                    for rd in readers.get(w, ()):
                        deps.add(rd)
                need = {}
                for d in deps:
                    od = ops[d]
                    if od["chan"] is not None:
                        key = ("c", od["chan"])
                        val = 16 * chan_count[od["chan"]]
                    else:
                        if od["eng"] == e and (o["nosync_same"] and od["nosync_same"]):
                            continue
                        key = ("e", od["eng"])
                        val = done[d][1]
                        if val > eng_count[od["eng"]]:
                            viol.add(d)
                    if val > need.get(key, 0):
                        need[key] = val
                waits = []
                for key, val in need.items():
                    if known[e].get(key, 0) >= val:
                        continue
                    known[e][key] = val
                    waits.append((key, val))
                if o["chan"] is not None:
                    chan_count[o["chan"]] += 1
                    done.append((("c", o["chan"]), 16 * chan_count[o["chan"]]))
                    inc = (("c", o["chan"]), 16)
                elif not o["inc"]:
                    done.append((("e", e), eng_count[e] + 1))
                    inc = None
                else:
                    eng_count[e] += 1
                    done.append((("e", e), eng_count[e]))
                    inc = (("e", e), 1)
                plan[e].append((waits, o["fn"], inc))
                for r in o["reads"]:
                    readers.setdefault(r, []).append(i)
                for w in o["writes"]:
                    last_writer[w] = i
                    readers[w] = []
            return viol, plan, chan_count
        while True:
            viol, plan, chan_count = plan_pass()
            if not viol:
                break
            for d in viol:
                ops[d]["inc"] = True
        final_waits = {e: [] for e in engines}
        chan_eng = {}
        for o in ops:
            if o["chan"] is not None:
                chan_eng[o["chan"]] = o["eng"]
        for c, e in chan_eng.items():
            final_waits[e].append((("c", c), 16 * chan_count[c]))

        semv = {k: 0 for k in sems}
        ptr = {e: 0 for e in engines}
        progressed = True
        while progressed:
            progressed = False
            for e in engines:
                while ptr[e] < len(plan[e]):
                    waits, _fn, inc = plan[e][ptr[e]]
                    if any(semv[k] < v for k, v in waits):
                        break
                    if inc is not None:
                        semv[inc[0]] += inc[1]
                    ptr[e] += 1
                    progressed = True
        stuck = {e: (ptr[e], len(plan[e])) for e in engines if ptr[e] < len(plan[e])}
        if stuck:
            det = {e: [(k, v, semv[k]) for k, v in plan[e][ptr[e]][0] if semv[k] < v] for e in stuck}
            raise RuntimeError(f"sync plan deadlocks: {stuck} waiting on {det}")

        with nc.Block() as block:
            def make(e):
                def body(eng):
                    for waits, fn, inc in plan[e]:
                        for key, val in waits:
                            eng.wait_ge(sems[key], val)
                        ins = fn(eng)
                        if inc is not None:
                            ins.then_inc(sems[inc[0]], inc[1])
                    for key, val in final_waits[e]:
                        eng.wait_ge(sems[key], val)
                return body
            for e in engines:
                getattr(block, e)(make(e))
        self.stack.close()
        return nc

GRID_W = 64
def rope_tables(tok0, n):
    t = np.arange(tok0, tok0 + n); row = (t // GRID_W).astype(np.float32); col = (t % GRID_W).astype(np.float32)
    quarter = 16
    freqs = (10000.0 ** (-np.arange(quarter, dtype=np.float32) / quarter)).astype(np.float32)
    cos = np.zeros((64, n), np.float32); sin = np.zeros((64, n), np.float32)
    for d in range(64):
        pos = row if d < 32 else col
        dd = d % 32
        f = freqs[dd % 16]
        ang = (pos * f).astype(np.float32)
        cos[d] = np.cos(ang); s = np.sin(ang)
        sin[d] = -s if dd < 16 else s
    return np.concatenate([cos, cos], 0), np.concatenate([sin, sin], 0)
def perm_matrix():
    Pm = np.zeros((128, 128), np.float32)
    for m in range(128):
        dd = m % 32
        k = m + 16 if dd < 16 else m - 16
        Pm[k, m] = 1.0
    return Pm
def colT(v, n):
    return np.ascontiguousarray(v.reshape(n, 128).T)

EPS = 1e-6

def get_stage(P):
    if not hasattr(P, "_stage"):
        P._stage_n = getattr(P, "_stage_n", 3)
        P._stage = [P.sbuf(f"stage{i}", [128, 1024]) for i in range(P._stage_n)]
        P._stage_i = 0
    return P._stage

def emit_mod(P, w_mod, b_modT, cT, nct, ps_mod, tag="m"):
    ncols = nct * 128
    c_s = P.sbuf(tag + "c_s", [128, 16]); sc_s = P.sbuf(tag + "sc_s", [128, 16])
    bm_s = P.sbuf(tag + "bm_s", [128, nct]); modv = P.sbuf(tag + "modv", [128, 2 * nct]); modc = P.sbuf(tag + "modc", [128, 2 * nct])
    modrow = [P.sbuf(f"{tag}modrow{i}", [2, 512]) for i in range(2)]
    scr = P.nc.dram_tensor(tag + "_modscr", [2, ncols], F32, kind="Internal").ap()
    st = get_stage(P)
    P.dma("sync", c_s[:], cT, writes=[tag + "c_s"], chan="ld0")
    P.dma("sync", bm_s[:], b_modT, writes=[tag + "bm_s"], chan="ld0")
    P.act(sc_s[:], c_s[:], AF.Silu, reads=[tag + "c_s"], writes=[tag + "sc_s"])
    wmv = w_mod.rearrange("(kc p) n -> p kc n", p=128)
    for pc in range(ncols // 512):
        for i in range(4):
            b = P._stage_i % P._stage_n; P._stage_i += 1
            P.dma("sync", st[b][:].rearrange("p (k n) -> p k n", k=2), wmv[:, 2 * i:2 * i + 2, pc * 512:(pc + 1) * 512],
                  writes=[("stage", b)], chan=f"stage{b}")
            for k2 in range(2):
                kc = 2 * i + k2
                P.mm(ps_mod[0:2, 0:512], sc_s[:, kc:16:8], st[b][:, k2 * 512:(k2 + 1) * 512], kc == 0, kc == 7,
                     reads=[("stage", b), tag + "sc_s"], writes=["ps_mod"])
        P.copy(modrow[pc % 2][:], ps_mod[0:2, 0:512], reads=["ps_mod"], writes=[(tag + "modrow", pc % 2)], eng="scalar")
        P.dma("sync", scr[:, pc * 512:(pc + 1) * 512], modrow[pc % 2][:], reads=[(tag + "modrow", pc % 2)], writes=[tag + "scr"], chan="modw")
    P.dma("sync", modc[:].rearrange("p (j t) -> p j t", j=2), scr.rearrange("j (t p) -> p j t", p=128),
          reads=[tag + "scr"], writes=[tag + "modc"], chan="modr", allow_slow_non_contiguous=True)
    for j in range(2):
        P.tt(modv[:, j * nct:(j + 1) * nct], modc[:, j * nct:(j + 1) * nct], bm_s[:], ALU.add,
             reads=[tag + "modc", tag + "bm_s"], writes=[tag + "modv"])
    return modv

def load_cast(P, w_dram, w_bf, nk, ncols, tag, piece=1024):
    wv = w_dram.rearrange("(kc p) n -> p kc n", p=128)
    st = get_stage(P)
    engs = ["vector", "scalar", "vector"]
    for kc in range(nk):
        for c0 in range(0, ncols, piece):
            cn = min(piece, ncols - c0)
            b = P._stage_i % P._stage_n; P._stage_i += 1
            P.dma("sync", st[b][:, 0:cn], wv[:, kc, c0:c0 + cn], writes=[("stage", b)], chan=f"stage{b}")
            P.copy(w_bf[:, kc, c0:c0 + cn], st[b][:, 0:cn], reads=[("stage", b)], writes=[(tag, kc)], eng=engs[P._stage_i % 2])

def emit_rstd(P, src, nk, n, sqb, ones, ps_ss, sd, rstd, src_res, tag=""):
    P.act(sqb[:, 0:nk, 0:n], src[:, 0:nk, 0:n], AF.Square, reads=src_res, writes=["sqb"])
    for kc in range(nk):
        P.mm(ps_ss[:, 0:n], ones[:], sqb[:, kc, 0:n], kc == 0, kc == nk - 1, reads=["sqb", "ones"], writes=["ps_ss"])
    P.act(sd[:, 0:n], ps_ss[:, 0:n], AF.Sqrt, reads=["ps_ss"], writes=["sd" + tag], scale=1.0 / (128 * nk), bias=EPS)
    P.op("vector", lambda e: e.reciprocal(out=rstd[:, 0:n], in_=sd[:, 0:n]), reads=["sd" + tag], writes=["rstd" + tag])

def build_l3b(with_ctx, N=256):
    NT = 2304 if with_ctx else 2048
    P = Prog(); P._stage_n = 2
    xT = P.dram_in("xT", [1024, NT])
    w_mod = P.dram_in("w_mod", [1024, 3072]); b_modT = P.dram_in("b_modT", [128, 24]); cT = P.dram_in("cT", [128, 16])
    g_preT = P.dram_in("g_preT", [128, 8]); g_postT = P.dram_in("g_postT", [128, 8])
    w_gate = P.dram_in("w_gate", [1024, 2816]); w_up = P.dram_in("w_up", [1024, 2816]); w_down = P.dram_in("w_down", [2816, 1024])
    xoT = P.dram_out("xoT", [1024, NT])
    wg = P.sbuf("wg", [128, 8, 2816], BF16); wu = P.sbuf("wu", [128, 8, 2816], BF16); wd = P.sbuf("wd", [128, 22, 1024], BF16)
    xs = [P.sbuf(f"xs{i}", [128, 8, N]) for i in range(2)]
    sqb = P.sbuf("sqb", [128, 8, N], BF16)
    tt_ = [P.sbuf(f"tt{i}", [128, N]) for i in range(2)]; xn = [P.sbuf(f"xn{i}", [128, 8, N], BF16) for i in range(2)]
    sd2 = P.sbuf("sd2", [128, N]); rstd2 = P.sbuf("rstd2", [128, N])
    hmid = P.sbuf("hmid", [128, 22, N], BF16)
    sg = [P.sbuf(f"sg{i}", [128, N]) for i in range(2)]
    o2 = P.sbuf("o2", [128, 8, N]); tmp = [P.sbuf(f"tmp{i}", [128, N]) for i in range(2)]
    xo = [P.sbuf(f"xo{i}", [128, N]) for i in range(2)]
    ones = P.sbuf("ones", [128, 128], BF16)
    sd = P.sbuf("sd", [128, N]); rstd = P.sbuf("rstd", [128, N])
    gp_s = P.sbuf("gp_s", [128, 8]); gq_s = P.sbuf("gq_s", [128, 8])
    Av = P.sbuf("Av", [128, 16]); Gv = P.sbuf("Gv", [128, 16])
    ps_mod = P.psum("ps_mod"); ps_ss = P.psum("ps_ss")
    psg = [P.psum(f"psg{i}") for i in range(2)]; psu = [P.psum(f"psu{i}") for i in range(2)]; pso = [P.psum(f"pso{i}") for i in range(2)]
    P.memset(ones[:], 1.0, writes=["ones"])
    P.dma("sync", gp_s[:], g_preT, writes=["gp_s"], chan="ld0")
    P.dma("sync", gq_s[:], g_postT, writes=["gq_s"], chan="ld0")
    modv = emit_mod(P, w_mod, b_modT, cT, 24, ps_mod)
    for j in range(2):
        P.stt(Av[:, j * 8:(j + 1) * 8], modv[:, j * 24 + 8: j * 24 + 16], 1.0, gp_s[:], ALU.add, ALU.mult,
              reads=["mmodv", "gp_s"], writes=["Av"])
        P.tt(Gv[:, j * 8:(j + 1) * 8], modv[:, j * 24 + 16: j * 24 + 24], gq_s[:], ALU.mult,
             reads=["mmodv", "gq_s"], writes=["Gv"])
    load_cast(P, w_gate, wg, 8, 2816, "wg")
    load_cast(P, w_up, wu, 8, 2816, "wu")
    xv = xT.rearrange("(kc p) t -> p kc t", p=128); xov = xoT.rearrange("(kc p) t -> p kc t", p=128)
    slabs = list(range(0, NT, N)); n = N
    cnt = dict(ng=0, no=0, nt=0)
    def pre(si):
        t0 = slabs[si]; b = si % 2; j = 0 if t0 < 2048 else 1
        P.dma("sync", xs[b][:], xv[:, :, t0:t0 + n], writes=[("xs", b)], chan=f"xs{b}")
        emit_rstd(P, xs[b], 8, n, sqb, ones, ps_ss, sd, rstd, [("xs", b)])
        for kc in range(8):
            P.tt(tt_[kc % 2][:], xs[b][:, kc, :], rstd[:], ALU.mult, reads=[("xs", b), "rstd"], writes=[("tt", kc % 2)])
            P.act(xn[b][:, kc, :], tt_[kc % 2][:], AF.Identity, reads=[("tt", kc % 2), "Av", "mmodv"], writes=[("xn", b, kc)],
                  scale=Av[:, j * 8 + kc: j * 8 + kc + 1], bias=modv[:, j * 24 + kc: j * 24 + kc + 1])
    def gu(si):
        b = si % 2
        for jj in range(22):
            pb = cnt["ng"] % 2; cnt["ng"] += 1
            for kc in range(8):
                P.mm(psg[pb][:, 0:n], wg[:, kc, jj * 128:(jj + 1) * 128], xn[b][:, kc, :], kc == 0, kc == 7,
                     reads=[("wg", kc), ("xn", b, kc)], writes=[("psg", pb)])
            for kc in range(8):
                P.mm(psu[pb][:, 0:n], wu[:, kc, jj * 128:(jj + 1) * 128], xn[b][:, kc, :], kc == 0, kc == 7,
                     reads=[("wu", kc), ("xn", b, kc)], writes=[("psu", pb)])
            P.act(sg[pb][:], psg[pb][:, 0:n], AF.Silu, reads=[("psg", pb)], writes=[("sg", pb)])
            P.tt(hmid[:, jj, :], sg[pb][:], psu[pb][:, 0:n], ALU.mult, reads=[("sg", pb), ("psu", pb)], writes=[("hmid", jj)])
    def dn(si):
        for m in range(8):
            pb = cnt["no"] % 2; cnt["no"] += 1
            for jj in range(22):
                P.mm(pso[pb][:, 0:n], wd[:, jj, m * 128:(m + 1) * 128], hmid[:, jj, :], jj == 0, jj == 21,
                     reads=[("wd", jj), ("hmid", jj)], writes=[("pso", pb)])
            P.copy(o2[:, m, :], pso[pb][:, 0:n], reads=[("pso", pb)], writes=[("o2", m)], eng="scalar")
    def post(si):
        t0 = slabs[si]; b = si % 2; j = 0 if t0 < 2048 else 1
        emit_rstd(P, o2, 8, n, sqb, ones, ps_ss, sd2, rstd2, [("o2", m) for m in range(8)], tag="2")
        for m in range(8):
            tb = cnt["nt"] % 2; cnt["nt"] += 1
            P.stt(tmp[tb][:], o2[:, m, :], Gv[:, j * 8 + m: j * 8 + m + 1], rstd2[:], ALU.mult, ALU.mult,
                  reads=[("o2", m), "Gv", "rstd2"], writes=[("tmp", tb)])
            P.tt(xo[tb][:], xs[b][:, m, :], tmp[tb][:], ALU.add, reads=[("xs", b), ("tmp", tb)], writes=[("xo", tb)], eng="gpsimd")
            P.dma("gpsimd", xov[:, m, t0:t0 + n], xo[tb][:], reads=[("xo", tb)], chan="st")
    pre(0)
    for si in range(len(slabs)):
        gu(si)
        if si == 0:
            load_cast(P, w_down, wd, 22, 1024, "wd")
        if si + 1 < len(slabs):
            pre(si + 1)
        dn(si)
        post(si)
    return P.finish()

def build_l3a(with_ctx, N=512):
    NT = 2304 if with_ctx else 2048
    P = Prog()
    xT = P.dram_in("xT", [1024, NT]); ysT = P.dram_in("ysT", [384, NT]); mxT = P.dram_in("mxT", [256, NT]); naT = P.dram_in("naT", [384, NT])
    w_mod = P.dram_in("w_mod", [1024, 1024]); b_modT = P.dram_in("b_modT", [128, 8]); cT = P.dram_in("cT", [128, 16])
    g_postT = P.dram_in("g_postT", [128, 8])
    w_glu = P.dram_in("w_glu", [384, 384]); w_fourier = P.dram_in("w_fourier", [256, 256]); w_out = P.dram_in("w_out", [1024, 1024])
    xoT = P.dram_out("xoT", [1024, NT])
    wglu = P.sbuf("wglu", [128, 3, 384], BF16); wf = P.sbuf("wf", [128, 2, 256], BF16); wo = P.sbuf("wo", [128, 8, 1024], BF16)
    xs = [P.sbuf(f"xs{i}", [128, 8, N]) for i in range(2)]
    ys = [P.sbuf(f"ys{i}", [128, 3, N]) for i in range(2)]
    mx = [P.sbuf(f"mx{i}", [128, 2, N]) for i in range(2)]
    na = [P.sbuf(f"na{i}", [128, 3, N]) for i in range(2)]
    sq = P.sbuf("sq", [128, 3, N]); t1 = P.sbuf("t1", [128, 3, N]); sgm = P.sbuf("sgm", [128, 3, N])
    zf = P.sbuf("zf", [128, 3, N]); zb = P.sbuf("zb", [128, 3, N], BF16); mxb = P.sbuf("mxb", [128, 2, N], BF16)
    sg2 = [P.sbuf(f"sg2{i}", [128, N]) for i in range(2)]
    cat = P.sbuf("cat", [128, 8, N], BF16)
    sqb = P.sbuf("sqb", [128, 8, N], BF16)
    o2 = P.sbuf("o2", [128, 8, N]); tmp = [P.sbuf(f"tmp{i}", [128, N]) for i in range(2)]
    xo = [P.sbuf(f"xo{i}", [128, N]) for i in range(2)]
    ones = P.sbuf("ones", [128, 128], BF16)
    sd = P.sbuf("sd", [128, N]); rstd = P.sbuf("rstd", [128, N])
    gq_s = P.sbuf("gq_s", [128, 8]); Gv = P.sbuf("Gv", [128, 16])
    ps_mod = P.psum("ps_mod"); ps_ss = P.psum("ps_ss")
    psa = [P.psum(f"psa{i}") for i in range(3)]; pso = [P.psum(f"pso{i}") for i in range(3)]
    P.memset(ones[:], 1.0, writes=["ones"])
    P.dma("sync", gq_s[:], g_postT, writes=["gq_s"], chan="ld0")
    modv = emit_mod(P, w_mod, b_modT, cT, 8, ps_mod)
    for j in range(2):
        P.tt(Gv[:, j * 8:(j + 1) * 8], modv[:, j * 8:(j + 1) * 8], gq_s[:], ALU.mult, reads=["mmodv", "gq_s"], writes=["Gv"])
    load_cast(P, w_glu, wglu, 3, 384, "wglu")
    load_cast(P, w_fourier, wf, 2, 256, "wf")
    load_cast(P, w_out, wo, 8, 1024, "wo")
    xv = xT.rearrange("(kc p) t -> p kc t", p=128); xov = xoT.rearrange("(kc p) t -> p kc t", p=128)
    ysv = ysT.rearrange("(kc p) t -> p kc t", p=128); mxv = mxT.rearrange("(kc p) t -> p kc t", p=128); nav = naT.rearrange("(kc p) t -> p kc t", p=128)
    na_ = 0; no = 0; nt = 0
    for si, t0 in enumerate(range(0, NT, N)):
        n = min(N, NT - t0); b = si % 2; j = 0 if t0 < 2048 else 1
        P.dma("sync", xs[b][:, :, 0:n], xv[:, :, t0:t0 + n], writes=[("xs", b)], chan=f"xs{b}")
        P.dma("sync", ys[b][:, :, 0:n], ysv[:, :, t0:t0 + n], writes=[("ys", b)], chan=f"ys{b}")
        P.dma("sync", mx[b][:, :, 0:n], mxv[:, :, t0:t0 + n], writes=[("mx", b)], chan=f"mx{b}")
        P.dma("sync", na[b][:, :, 0:n], nav[:, :, t0:t0 + n], writes=[("na", b)], chan=f"na{b}")
        Y = ys[b][:, :, 0:n]
        P.tt(sq[:, :, 0:n], Y, Y, ALU.mult, reads=[("ys", b)], writes=["sq"], eng="gpsimd")
        P.ts(t1[:, :, 0:n], sq[:, :, 0:n], 0.044715, 1.0, ALU.mult, ALU.add, reads=["sq"], writes=["t1"])
        P.tt(sq[:, :, 0:n], t1[:, :, 0:n], Y, ALU.mult, reads=["t1", ("ys", b)], writes=["sq"])
        P.act(sgm[:, :, 0:n], sq[:, :, 0:n], AF.Sigmoid, reads=["sq"], writes=["sgm"], scale=1.5957691216057308)
        P.tt(zf[:, :, 0:n], Y, sgm[:, :, 0:n], ALU.mult, reads=[("ys", b), "sgm"], writes=["zf"])
        P.copy(zb[:, :, 0:n], zf[:, :, 0:n], reads=["zf"], writes=["zb"], eng="gpsimd")
        for m in range(3):
            pb = na_ % 3; na_ += 1
            for kc in range(3):
                P.mm(psa[pb][:, 0:n], wglu[:, kc, m * 128:(m + 1) * 128], zb[:, kc, 0:n], kc == 0, kc == 2,
                     reads=[("wglu", kc), "zb"], writes=[("psa", pb)])
            P.act(sg2[m % 2][:, 0:n], psa[pb][:, 0:n], AF.Sigmoid, reads=[("psa", pb)], writes=[("sg2", m % 2)])
            P.tt(cat[:, m, 0:n], zf[:, m, 0:n], sg2[m % 2][:, 0:n], ALU.mult, reads=["zf", ("sg2", m % 2)], writes=[("cat", m)])
        P.copy(mxb[:, :, 0:n], mx[b][:, :, 0:n], reads=[("mx", b)], writes=["mxb"], eng="gpsimd")
        for m in range(2):
            pb = na_ % 3; na_ += 1
            for kc in range(2):
                P.mm(psa[pb][:, 0:n], wf[:, kc, m * 128:(m + 1) * 128], mxb[:, kc, 0:n], kc == 0, kc == 1,
                     reads=[("wf", kc), "mxb"], writes=[("psa", pb)])
            P.copy(cat[:, 3 + m, 0:n], psa[pb][:, 0:n], reads=[("psa", pb)], writes=[("cat", 3 + m)], eng="scalar")
        P.copy(cat[:, 5:8, 0:n], na[b][:, :, 0:n], reads=[("na", b)], writes=[("cat", 5), ("cat", 6), ("cat", 7)], eng="gpsimd")
        for m in range(8):
            pb = no % 3; no += 1
            for kc in range(8):
                P.mm(pso[pb][:, 0:n], wo[:, kc, m * 128:(m + 1) * 128], cat[:, kc, 0:n], kc == 0, kc == 7,
                     reads=[("wo", kc), ("cat", kc)], writes=[("pso", pb)])
            P.copy(o2[:, m, 0:n], pso[pb][:, 0:n], reads=[("pso", pb)], writes=[("o2", m)], eng="scalar")
        emit_rstd(P, o2, 8, n, sqb, ones, ps_ss, sd, rstd, [("o2", m) for m in range(8)])
        for m in range(8):
            tb = nt % 2; nt += 1
            P.stt(tmp[tb][:, 0:n], o2[:, m, 0:n], Gv[:, j * 8 + m: j * 8 + m + 1], rstd[:, 0:n], ALU.mult, ALU.mult,
                  reads=[("o2", m), "Gv", "rstd"], writes=[("tmp", tb)])
            P.tt(xo[tb][:, 0:n], xs[b][:, m, 0:n], tmp[tb][:, 0:n], ALU.add, reads=[("xs", b), ("tmp", tb)], writes=[("xo", tb)], eng="gpsimd")
            P.dma("gpsimd", xov[:, m, t0:t0 + n], xo[tb][:, 0:n], reads=[("xo", tb)], chan="st")
    return P.finish()

EPS = 1e-6
NT = 2304
SLABS = [(0, 512), (512, 512), (1024, 512), (1536, 512), (2048, 256)]

def build_l1():
    P = Prog(); P._stage_n = 4
    xT = P.dram_in("xT", [1024, NT])
    w_in = P.dram_in("w_in", [1024, 1792])
    w_mod = P.dram_in("w_mod", [1024, 2048])
    b_modT = P.dram_in("b_modT", [128, 16])
    g_preT = P.dram_in("g_preT", [128, 8])
    cT = P.dram_in("cT", [128, 16])
    cos = P.dram_in("cos", [128, 2048]); sin = P.dram_in("sin", [128, 2048])
    perm = P.dram_in("perm", [128, 128])
    hT = P.dram_out("hT", [640, NT])
    hTb = P.dram_out("hTb", [1152, NT])

    xs = [P.sbuf(f"xs{i}", [128, 8, 512]) for i in range(2)]
    sqb = P.sbuf("sqb", [128, 8, 512], BF16)
    tt_ = [P.sbuf(f"tt{i}", [128, 512]) for i in range(2)]
    xn = [P.sbuf(f"xn{i}", [128, 8, 512], BF16) for i in range(2)]
    w_bf = P.sbuf("w_bf", [128, 8, 1792], BF16)
    ones = P.sbuf("ones", [128, 128], BF16)
    sd = P.sbuf("sd", [128, 512]); rstd = P.sbuf("rstd", [128, 512])
    ho = [P.sbuf(f"ho{i}", [128, 512]) for i in range(3)]
    hob = [P.sbuf(f"hob{i}", [128, 512]) for i in range(3)]
    r1 = [P.sbuf(f"r1{i}", [128, 512]) for i in range(2)]
    r2 = [P.sbuf(f"r2{i}", [128, 512]) for i in range(2)]
    cos_s = P.sbuf("cos_s", [128, 2048]); sin_s = P.sbuf("sin_s", [128, 2048])
    perm_s = P.sbuf("perm_s", [128, 128])
    gp_s = P.sbuf("gp_s", [128, 8]); Av = P.sbuf("Av", [128, 16])
    ps_mod = P.psum("ps_mod"); ps_ss = P.psum("ps_ss")
    psm = [P.psum(f"psm{i}") for i in range(4)]
    psr = [P.psum(f"psr{i}") for i in range(2)]

    P.dma("sync", gp_s[:], g_preT, writes=["gp_s"], chan="ld0")
    P.dma("sync", cos_s[:], cos, writes=["cos_s"], chan="ld0")
    P.dma("sync", sin_s[:], sin, writes=["sin_s"], chan="ld0")
    P.dma("sync", perm_s[:], perm, writes=["perm_s"], chan="ld0")
    P.memset(ones[:], 1.0, writes=["ones"])
    modv = emit_mod(P, w_mod, b_modT, cT, 16, ps_mod)
    for j in range(2):
        P.stt(Av[:, j * 8:(j + 1) * 8], modv[:, j * 16 + 8: j * 16 + 16], 1.0, gp_s[:], ALU.add, ALU.mult,
              reads=["mmodv", "gp_s"], writes=["Av"])
    load_cast(P, w_in, w_bf, 8, 1792, "w_bf", piece=1024)
    xv = xT.rearrange("(kc p) t -> p kc t", p=128)
    cnt = dict(nho=0, nr=0, nps=0)
    def pre(si):
        t0, n = SLABS[si]; b = si % 2; j = 0 if si < 4 else 1
        P.dma("sync", xs[b][:, :, 0:n], xv[:, :, t0:t0 + n], writes=[("xs", b)], chan=f"xs{b}")
        P.act(sqb[:, :, 0:n], xs[b][:, :, 0:n], AF.Square, reads=[("xs", b)], writes=["sqb"])
        for kc in range(8):
            P.mm(ps_ss[:, 0:n], ones[:], sqb[:, kc, 0:n], kc == 0, kc == 7, reads=["sqb", "ones"], writes=["ps_ss"])
        P.act(sd[:, 0:n], ps_ss[:, 0:n], AF.Sqrt, reads=["ps_ss"], writes=["sd"], scale=1.0 / 1024, bias=EPS)
        P.op("vector", lambda e: e.reciprocal(out=rstd[:, 0:n], in_=sd[:, 0:n]), reads=["sd"], writes=["rstd"])
        for kc in range(8):
            P.tt(tt_[kc % 2][:, 0:n], xs[b][:, kc, 0:n], rstd[:, 0:n], ALU.mult, reads=[("xs", b), "rstd"], writes=[("tt", kc % 2)])
            P.act(xn[b][:, kc, 0:n], tt_[kc % 2][:, 0:n], AF.Identity, reads=[("tt", kc % 2), "Av", "mmodv"], writes=[("xn", b, kc)],
                  scale=Av[:, j * 8 + kc: j * 8 + kc + 1], bias=modv[:, j * 16 + kc: j * 16 + kc + 1])
    def main(si):
        t0, n = SLABS[si]; b = si % 2
        for m in range(14):
            pb = cnt["nps"] % 4; cnt["nps"] += 1
            for kc in range(8):
                P.mm(psm[pb][:, 0:n], w_bf[:, kc, m * 128:(m + 1) * 128], xn[b][:, kc, 0:n], kc == 0, kc == 7,
                     reads=[("w_bf", kc), ("xn", b, kc)], writes=[("psm", pb)])
            hb = cnt["nho"] % 3; cnt["nho"] += 1
            if m < 5:
                P.copy(ho[hb][:, 0:n], psm[pb][:, 0:n], reads=[("psm", pb)], writes=[("ho", hb)], eng="scalar")
                P.dma("gpsimd", hT[m * 128:(m + 1) * 128, t0:t0 + n], ho[hb][:, 0:n], reads=[("ho", hb)], chan="st")
            elif m <= 10 and si < 4:
                rb = cnt["nr"] % 2; cnt["nr"] += 1
                P.copy(ho[hb][:, 0:n], psm[pb][:, 0:n], reads=[("psm", pb)], writes=[("ho", hb)], eng="scalar")
                P.mm(psr[rb][:, 0:n], perm_s[:], ho[hb][:, 0:n], True, True, reads=["perm_s", ("ho", hb)], writes=[("psr", rb)])
                P.tt(r1[rb][:, 0:n], ho[hb][:, 0:n], cos_s[:, t0:t0 + n], ALU.mult, reads=[("ho", hb), "cos_s"], writes=[("r1", rb)])
                P.tt(r2[rb][:, 0:n], psr[rb][:, 0:n], sin_s[:, t0:t0 + n], ALU.mult, reads=[("psr", rb), "sin_s"], writes=[("r2", rb)])
                P.tt(hob[hb][:, 0:n], r1[rb][:, 0:n], r2[rb][:, 0:n], ALU.add, reads=[("r1", rb), ("r2", rb)], writes=[("hob", hb)], eng="gpsimd")
                P.dma("gpsimd", hTb[(m - 5) * 128:(m - 4) * 128, t0:t0 + n], hob[hb][:, 0:n], reads=[("hob", hb)], chan="st")
            else:
                P.copy(hob[hb][:, 0:n], psm[pb][:, 0:n], reads=[("psm", pb)], writes=[("hob", hb)], eng="scalar")
                P.dma("gpsimd", hTb[(m - 5) * 128:(m - 4) * 128, t0:t0 + n], hob[hb][:, 0:n], reads=[("hob", hb)], chan="st")
    pre(0)
    for si in range(len(SLABS)):
        if si + 1 < len(SLABS):
            pre(si + 1)
        main(si)
    return P.finish()


def fnet_consts():
    n1 = np.arange(128); a = 2 * np.pi * np.outer(n1, n1) / 128
    cs128 = np.concatenate([np.cos(a), -np.sin(a)], 1).astype(np.float32)
    n2 = np.arange(64); a = 2 * np.pi * np.outer(n2, n2) / 64
    C64, S64 = np.cos(a), np.sin(a)
    fb1 = np.concatenate([C64, -S64], 1).astype(np.float32); fb2 = np.concatenate([S64, C64], 1).astype(np.float32)
    a = 2 * np.pi * np.outer(n2, n1) / 8192
    tw = np.concatenate([np.cos(a), np.sin(a)], 1).astype(np.float32)
    fc = (np.concatenate([C64, S64], 1) / np.sqrt(8192 * 64)).astype(np.float32)
    n = np.arange(256); a = 2 * np.pi * np.outer(n, n) / 256
    cs = np.concatenate([np.cos(a), -np.sin(a)], 1)
    cs256 = np.ascontiguousarray(cs.reshape(2, 128, 512).transpose(1, 0, 2)).astype(np.float32)
    fcc = (np.concatenate([C64, S64], 1) / np.sqrt(256 * 64)).astype(np.float32)
    return dict(cs128=cs128, fb1=fb1, fb2=fb2, tw=tw, fc=fc, cs256=cs256, fcc=fcc)

def build_fnet(with_ctx):
    P = Prog()
    z = P.dram_in("z", [128, 4096]); cs128 = P.dram_in("cs128", [128, 256])
    fb1 = P.dram_in("fb1", [64, 128]); fb2 = P.dram_in("fb2", [64, 128]); tw = P.dram_in("tw", [64, 256]); fc = P.dram_in("fc", [64, 128])
    R = P.dram_out("R", [64, 8192])
    zs = P.sbuf("zs", [128, 64, 64]); cs_s = P.sbuf("cs_s", [128, 256])
    fb1_s = P.sbuf("fb1_s", [64, 128]); fb2_s = P.sbuf("fb2_s", [64, 128]); tw_s = P.sbuf("tw_s", [64, 256]); fc_s = P.sbuf("fc_s", [64, 128])
    Ych = [P.sbuf(f"Ych{i}", [64, 8, 256]) for i in range(2)]
    ta = P.sbuf("ta", [64, 8, 128]); tb = P.sbuf("tb", [64, 8, 128]); tc = P.sbuf("tc", [64, 8, 128]); td = P.sbuf("td", [64, 8, 128])
    Yp = P.sbuf("Yp", [64, 64, 256]); X1 = P.sbuf("X1", [64, 2, 8192])
    Rs = [P.sbuf(f"Rs{i}", [64, 512]) for i in range(2)]
    psA = [P.psum(f"psA{i}") for i in range(3)]; psB = [P.psum(f"psB{i}") for i in range(3)]; psC = [P.psum(f"psC{i}") for i in range(2)]
    P.dma("sync", zs[:].rearrange("p a b -> p (a b)"), z, writes=["zs"], chan="ld0")
    for (s, d, nm) in ((cs_s, cs128, "cs_s"), (fb1_s, fb1, "fb1_s"), (fb2_s, fb2, "fb2_s"), (tw_s, tw, "tw_s"), (fc_s, fc, "fc_s")):
        P.dma("sync", s[:], d, writes=[nm], chan="ld1")
    twr = tw_s[:, 0:128].rearrange("p (o k) -> p o k", o=1).to_broadcast([64, 8, 128])
    tws = tw_s[:, 128:256].rearrange("p (o k) -> p o k", o=1).to_broadcast([64, 8, 128])
    na_ = 0
    for ch in range(8):
        cb = ch % 2
        for pair in range(4):
            pb = na_ % 3; na_ += 1
            for h in range(2):
                c = ch * 8 + pair * 2 + h
                P.mm(psA[pb][0:64, h * 256:(h + 1) * 256], zs[:, :, c], cs_s[:], True, True, reads=["zs", "cs_s"], writes=[("psA", pb)])
            P.copy(Ych[cb][:, pair * 2:pair * 2 + 2, :].rearrange("p a b -> p (a b)"), psA[pb][0:64, :], reads=[("psA", pb)],
                   writes=[("Ych", cb)], eng="scalar")
        Yr = Ych[cb][:, :, 0:128]; Yi = Ych[cb][:, :, 128:256]; c0 = ch * 8
        P.tt(ta[:], Yr, twr, ALU.mult, reads=[("Ych", cb), "tw_s"], writes=["ta"])
        P.tt(tb[:], Yi, tws, ALU.mult, reads=[("Ych", cb), "tw_s"], writes=["tb"], eng="gpsimd")
        P.tt(Yp[:, c0:c0 + 8, 0:128], ta[:], tb[:], ALU.add, reads=["ta", "tb"], writes=[("Yp", ch)])
        P.tt(tc[:], Yi, twr, ALU.mult, reads=[("Ych", cb), "tw_s"], writes=["tc"], eng="gpsimd")
        P.tt(td[:], Yr, tws, ALU.mult, reads=[("Ych", cb), "tw_s"], writes=["td"])
        P.tt(Yp[:, c0:c0 + 8, 128:256], tc[:], td[:], ALU.subtract, reads=["tc", "td"], writes=[("Yp", ch)], eng="gpsimd")
    allYp = [("Yp", ch) for ch in range(8)]
    X1v = X1[:].rearrange("p c (k2 k1) -> p c k2 k1", k1=128)
    for g in range(32):
        pb = g % 3
        for q in range(4):
            k1 = 4 * g + q
            P.mm(psB[pb][0:64, q * 128:(q + 1) * 128], Yp[:, :, k1], fb1_s[:], True, False, reads=allYp + ["fb1_s"], writes=[("psB", pb)])
            P.mm(psB[pb][0:64, q * 128:(q + 1) * 128], Yp[:, :, 128 + k1], fb2_s[:], False, True, reads=allYp + ["fb2_s"], writes=[("psB", pb)])
        pv = psB[pb][0:64, :].rearrange("p (q c k) -> p c k q", q=4, c=2)
        for comp in range(2):
            P.copy(X1v[:, comp, :, 4 * g:4 * g + 4], pv[:, comp, :, :], reads=[("psB", pb)], writes=[("X1", g)],
                   eng=("scalar" if comp == 0 else "vector"))
    allX1 = [("X1", g) for g in range(32)]
    for blk in range(16):
        pb = blk % 2
        P.mm(psC[pb][0:64, :], fc_s[:, 0:64], X1[:, 0, blk * 512:(blk + 1) * 512], True, False, reads=allX1 + ["fc_s"], writes=[("psC", pb)])
        P.mm(psC[pb][0:64, :], fc_s[:, 64:128], X1[:, 1, blk * 512:(blk + 1) * 512], False, True, reads=allX1 + ["fc_s"], writes=[("psC", pb)])
        P.copy(Rs[pb][:], psC[pb][0:64, :], reads=[("psC", pb)], writes=[("Rs", pb)], eng="scalar")
        P.dma("gpsimd", R[:, blk * 512:(blk + 1) * 512], Rs[pb][:], reads=[("Rs", pb)], chan="st")
    if with_ctx:
        zc = P.dram_in("zc", [128, 2, 64]); cs256 = P.dram_in("cs256", [128, 2, 512]); fcc = P.dram_in("fcc", [64, 128])
        Rc = P.dram_out("Rc", [64, 256])
        zc_s = P.sbuf("zc_s", [128, 2, 64]); c2_s = P.sbuf("c2_s", [128, 2, 512]); fcc_s = P.sbuf("fcc_s", [64, 128])
        Pc = P.sbuf("Pc", [64, 512]); Rc_s = P.sbuf("Rc_s", [64, 256])
        P.dma("sync", zc_s[:], zc, writes=["zc_s"], chan="ld2"); P.dma("sync", c2_s[:], cs256, writes=["c2_s"], chan="ld2")
        P.dma("sync", fcc_s[:], fcc, writes=["fcc_s"], chan="ld2")
        for t in range(2):
            P.mm(psA[0][0:64, :], zc_s[:, t, :], c2_s[:, t, :], t == 0, t == 1, reads=["zc_s", "c2_s"], writes=[("psA", 0)])
        P.copy(Pc[:], psA[0][0:64, :], reads=[("psA", 0)], writes=["Pc"])
        P.mm(psA[1][0:64, 0:256], fcc_s[:, 0:64], Pc[:, 0:256], True, False, reads=["Pc", "fcc_s"], writes=[("psA", 1)])
        P.mm(psA[1][0:64, 0:256], fcc_s[:, 64:128], Pc[:, 256:512], False, True, reads=["Pc", "fcc_s"], writes=[("psA", 1)])
        P.copy(Rc_s[:], psA[1][0:64, 0:256], reads=[("psA", 1)], writes=["Rc_s"])
        P.dma("gpsimd", Rc, Rc_s[:], reads=["Rc_s"], chan="st")
    return P.finish()

NEG = -30000.0

def build_na(with_ctx):
    P = Prog()
    qT = P.dram_in("qT", [384, 2048]); kwT = P.dram_in("kwT", [16, 384, 576]); vw = P.dram_in("vw", [16, 128, 5, 384])
    kcT = P.dram_in("kcT", [384, 256]); vc = P.dram_in("vc", [128, 2, 384])
    tbraw = P.dram_in("tbraw", [5, 128, 6, 576]); mask = P.dram_in("mask", [5, 128, 576]); ident = P.dram_in("ident", [128, 128])
    Y = P.dram_out("Y", [2048, 384])
    qb = P.sbuf("qb", [128, 3, 2048], BF16); kcb = P.sbuf("kcb", [128, 3, 256], BF16); vcb = P.sbuf("vcb", [128, 2, 384], BF16)
    TB = P.sbuf("TB", [128, 5, 6, 576]); mk = P.sbuf("mk", [128, 5, 576])
    idb = P.sbuf("idb", [128, 128], BF16)
    stg = [P.sbuf(f"stg{i}", [128, 2048]) for i in range(2)]
    kst = [P.sbuf(f"kst{i}", [128, 3, 576]) for i in range(2)]; vst = [P.sbuf(f"vst{i}", [128, 5, 384]) for i in range(2)]
    kb = [P.sbuf(f"kb{i}", [128, 3, 576], BF16) for i in range(2)]; vb = [P.sbuf(f"vb{i}", [128, 5, 384], BF16) for i in range(2)]
    S = [P.sbuf(f"S{i}", [128, 832]) for i in range(3)]; Pb = [P.sbuf(f"Pb{i}", [128, 832], BF16) for i in range(3)]
    PT = [P.sbuf(f"PT{i}", [128, 896], BF16) for i in range(3)]
    Osb = [P.sbuf(f"Osb{i}", [128, 384]) for i in range(2)]
    mx = [P.sbuf(f"mx{i}", [128, 1]) for i in range(3)]; ssum = [P.sbuf(f"ssum{i}", [128, 1]) for i in range(3)]
    rinv = [P.sbuf(f"rinv{i}", [128, 1]) for i in range(3)]
    psA = [P.psum(f"psA{i}") for i in range(2)]; psB = [P.psum(f"psB{i}") for i in range(2)]
    psT = [P.psum(f"psT{i}", [128, 1024], BF16) for i in range(2)]; psO = [P.psum(f"psO{i}") for i in range(2)]
    si = [0]
    def load_cast(dst, src_ap, n, dres):
        b = si[0] % 2; si[0] += 1
        P.dma("sync", stg[b][:, 0:n], src_ap, writes=[("stg", b)], chan=f"stg{b}")
        P.copy(dst, stg[b][:, 0:n], reads=[("stg", b)], writes=[dres], eng="gpsimd")
    qv = qT.rearrange("(c p) n -> p c n", p=128)
    for c in range(3):
        load_cast(qb[:, c, :], qv[:, c, :], 2048, "qb")
    kcv = kcT.rearrange("(c p) n -> p c n", p=128)
    for c in range(3):
        load_cast(kcb[:, c, :], kcv[:, c, :], 256, "kcb")
    load_cast(vcb[:].rearrange("p a b -> p (a b)"), vc.rearrange("p a b -> p (a b)"), 768, "vcb")
    load_cast(idb[:], ident, 128, "idb")
    for ty in range(5):
        P.dma("sync", TB[:, ty, :, :], tbraw[ty], writes=[("TB", ty)], chan="ld0")
    P.dma("sync", mk[:], mask.rearrange("t p n -> p t n"), writes=["mk"], chan="ld0")
    for ty in range(5):
        mb = mk[:, ty, :].rearrange("p (o n) -> p o n", o=1).to_broadcast([128, 6, 576])
        P.tt(TB[:, ty, :, :], TB[:, ty, :, :], mb, ALU.add, reads=[("TB", ty), "mk"], writes=[("TB", ty)], eng="gpsimd")
    tiles = [("main", t) for t in range(16)]
    if with_ctx:
        qcT = P.dram_in("qcT", [384, 256]); Yc = P.dram_out("Yc", [256, 384])
        qcb = P.sbuf("qcb", [128, 3, 256], BF16)
        qcv = qcT.rearrange("(c p) n -> p c n", p=128)
        for c in range(3):
            load_cast(qcb[:, c, :], qcv[:, c, :], 256, "qcb")
        tiles += [("ctx", 0), ("ctx", 1)]
    units = [(ti, kind, t, h) for ti, (kind, t) in enumerate(tiles) for h in range(6)]
    loaded = set()
    def tile_load(ti, kind, t):
        if ti in loaded or kind != "main":
            return
        loaded.add(ti)
        wb = ti % 2
        P.dma("sync", kst[wb][:], kwT[t].rearrange("(c p) n -> p c n", p=128), writes=[("kst", wb)], chan=f"kst{wb}")
        P.dma("sync", vst[wb][:], vw[t], writes=[("vst", wb)], chan=f"vst{wb}")
        P.copy(kb[wb][:], kst[wb][:], reads=[("kst", wb)], writes=[("kb", wb)], eng="gpsimd")
        P.copy(vb[wb][:], vst[wb][:], reads=[("vst", wb)], writes=[("vb", wb)], eng="gpsimd")
    def info(u):
        ti, kind, t, h = units[u]
        return ti, kind, t, h, u % 2, u % 3, ti % 2, h // 2, (h % 2) * 64
    def stA(u):
        ti, kind, t, h, b, b3, wb, c, p0 = info(u)
        tile_load(ti, kind, t)
        if kind == "main":
            qs = qb[p0:p0 + 64, c, t * 128:(t + 1) * 128]; qres = "qb"
        else:
            qs = qcb[p0:p0 + 64, c, t * 128:(t + 1) * 128]; qres = "qcb"
        P.mm(psB[b][:, 64:320], qs, kcb[p0:p0 + 64, c, :], True, True, reads=[qres, "kcb"], writes=[("psB", b)])
        if kind == "main":
            P.mm(psA[b][:, 0:512], qs, kb[wb][p0:p0 + 64, c, 0:512], True, True, reads=[qres, ("kb", wb)], writes=[("psA", b)])
            P.mm(psB[b][:, 0:64], qs, kb[wb][p0:p0 + 64, c, 512:576], True, True, reads=[qres, ("kb", wb)], writes=[("psB", b)])
    def stB(u):
        ti, kind, t, h, b, b3, wb, c, p0 = info(u)
        W = 832 if kind == "main" else 256
        P.act(S[b3][:, 0:256], psB[b][:, 64:320], AF.Copy, reads=[("psB", b)], writes=[("S", b3)], scale=0.125)
        if kind == "main":
            ty = {0: 0, 1: 1, 14: 3, 15: 4}.get(t, 2)
            P.stt(S[b3][:, 256:768], psA[b][:, 0:512], 0.125, TB[:, ty, h, 0:512], ALU.mult, ALU.add,
                  reads=[("psA", b), ("TB", ty)], writes=[("S", b3)])
            P.stt(S[b3][:, 768:832], psB[b][:, 0:64], 0.125, TB[:, ty, h, 512:576], ALU.mult, ALU.add,
                  reads=[("psB", b), ("TB", ty)], writes=[("S", b3)])
        P.op("vector", lambda e: e.tensor_reduce(out=mx[b3][:], in_=S[b3][:, 0:W], axis=AX.X, op=ALU.max, negate=True),
             reads=[("S", b3)], writes=[("mx", b3)])
        P.act(Pb[b3][:, 0:W], S[b3][:, 0:W], AF.Exp, reads=[("S", b3), ("mx", b3)], writes=[("Pb", b3), ("ssum", b3)],
              bias=mx[b3][:], scale=1.0, accum_out=ssum[b3][:])
        P.op("vector", lambda e: e.reciprocal(out=rinv[b3][:], in_=ssum[b3][:]), reads=[("ssum", b3)], writes=[("rinv", b3)])
    def stC(u):
        ti, kind, t, h, b, b3, wb, c, p0 = info(u)
        nblk = 7 if kind == "main" else 2
        for kbk in range(nblk):
            kw = 64 if kbk == 6 else 128
            P.op("tensor", lambda e, kbk=kbk, kw=kw: e.transpose(psT[b][0:kw, kbk * 128:(kbk + 1) * 128], Pb[b3][:, kbk * 128:kbk * 128 + kw], idb[:]),
                 reads=[("Pb", b3), "idb"], writes=[("psT", b)], nosync_same=True)
        P.copy(PT[b3][:, 0:nblk * 128], psT[b][:, 0:nblk * 128], reads=[("psT", b)], writes=[("PT", b3)], eng="scalar")
    def stD(u):
        ti, kind, t, h, b, b3, wb, c, p0 = info(u)
        ob = ti % 2
        nblk = 7 if kind == "main" else 2
        for kbk in range(nblk):
            kw = 64 if kbk == 6 else 128
            if kbk < 2:
                rhs = vcb[:, kbk, h * 64:(h + 1) * 64]; rres = "vcb"
            else:
                rhs = vb[wb][0:kw, kbk - 2, h * 64:(h + 1) * 64]; rres = ("vb", wb)
            P.mm(psO[ob][:, h * 64:(h + 1) * 64], PT[b3][0:kw, kbk * 128:(kbk + 1) * 128], rhs, kbk == 0, kbk == nblk - 1,
                 reads=[("PT", b3), rres], writes=[("psO", ob)])
        P.ts(Osb[ob][:, h * 64:(h + 1) * 64], psO[ob][:, h * 64:(h + 1) * 64], rinv[b3][:], None, ALU.mult,
             reads=[("psO", ob), ("rinv", b3)], writes=[("Osb", ob)])
        if h == 5:
            dst = Y[t * 128:(t + 1) * 128, :] if kind == "main" else Yc[t * 128:(t + 1) * 128, :]
            P.dma("gpsimd", dst, Osb[ob][:], reads=[("Osb", ob)], chan="st")
    NU = len(units)
    for step in range(NU + 3):
        if step < NU: stA(step)
        if 0 <= step - 1 < NU: stB(step - 1)
        if 0 <= step - 2 < NU: stC(step - 2)
        if 0 <= step - 3 < NU: stD(step - 3)
    return P.finish()

def na_tile_geometry(r0):
    R = 128
    rs0 = int(np.clip(r0 - 4, 0, R - 8)); rs1 = int(np.clip(r0 + 1 - 4, 0, R - 8))
    return rs0, rs1

def na_tables(rpb, q):
    cols = np.arange(64); cs = np.clip(cols - 8, 0, 48)
    kc = np.arange(64)
    inwin = (kc[None, :] >= cs[:, None]) & (kc[None, :] < cs[:, None] + 16)
    dc = np.clip(kc[None, :] - cols[:, None] + 15, 0, 30)
    types = [32 * q, 32 * q + 2, 32 * q + 16, 32 * q + 28, 32 * q + 30]
    tbraw = np.zeros((5, 128, 6, 9, 64), np.float32); mask = np.zeros((5, 128, 9, 64), np.float32)
    for ti, r0 in enumerate(types):
        rs0, rs1 = na_tile_geometry(r0)
        for half, (r, rs) in enumerate(((r0, rs0), (r0 + 1, rs1))):
            for slot in range(9):
                krow = rs0 + slot
                valid = (krow >= rs) and (krow < rs + 8)
                dr = int(np.clip(krow - r + 7, 0, 14))
                g = rpb[:, dr][:, dc]
                tbraw[ti, half * 64:(half + 1) * 64, :, slot, :] = g.transpose(1, 0, 2)
                m = np.where(inwin & valid, 0.0, NEG).astype(np.float32)
                mask[ti, half * 64:(half + 1) * 64, slot, :] = m
    return tbraw.reshape(5, 128, 6, 576), mask.reshape(5, 128, 576)

def na_windows(k_b, v_b, q):
    kp = np.concatenate([k_b, np.zeros((64 * 16, 384), k_b.dtype)], 0); vp = np.concatenate([v_b, np.zeros((64 * 16, 384), v_b.dtype)], 0)
    kwT = np.zeros((16, 384, 576), k_b.dtype); vw = np.zeros((16, 128, 5, 384), v_b.dtype)
    for t in range(16):
        r0 = 32 * q + 2 * t
        rs0, _ = na_tile_geometry(r0)
        kwT[t] = kp[rs0 * 64:(rs0 + 9) * 64].T
        for j in range(4):
            vw[t, :, j, :] = vp[(rs0 + 2 * j) * 64:(rs0 + 2 * j + 2) * 64]
        vw[t, 0:64, 4, :] = vp[(rs0 + 8) * 64:(rs0 + 9) * 64]
    return kwT, vw

NCH = 1056
PI = math.pi

class A:
    def __init__(self, P): self.P = P
    @staticmethod
    def nm(*aps): return [a.tensor.name for a in aps if hasattr(a, "tensor")]
    def tt(self, o, a, b, op, eng="vector"): self.P.tt(o, a, b, op, reads=self.nm(a, b), writes=self.nm(o), eng=eng)
    def ts(self, o, a, s1, op0, s2=None, op1=None, eng="vector"):
        self.P.ts(o, a, s1, s2, op0, op1, reads=self.nm(a, s1, s2), writes=self.nm(o), eng=eng)
    def stt(self, o, a, s, b, op0, op1): self.P.stt(o, a, s, b, op0, op1, reads=self.nm(a, s, b), writes=self.nm(o))
    def act(self, o, a, f, **kw): self.P.act(o, a, f, reads=self.nm(a, *[v for v in kw.values()]), writes=self.nm(o), **kw)
    def copy(self, o, a, eng="vector"): self.P.copy(o, a, reads=self.nm(a), writes=self.nm(o), eng=eng)
    def memset(self, o, v, eng="vector"): self.P.memset(o, v, writes=self.nm(o), eng=eng)
    def mm(self, o, l, r, st, sp): self.P.mm(o, l, r, st, sp, reads=self.nm(l, r), writes=self.nm(o))
    def dma_in(self, o, src, chan): self.P.dma("sync", o, src, writes=self.nm(o), chan=chan)
    def dma_out(self, dst, a, chan="st"): self.P.dma("gpsimd", dst, a, reads=self.nm(a), chan=chan)
    def scan(self, o, d0, d1, init):
        self.P.op("vector", lambda e: e.tensor_tensor_scan(out=o, data0=d0, data1=d1, initial=init, op0=ALU.mult, op1=ALU.add),
                  reads=self.nm(d0, d1, init), writes=self.nm(o))
    def recip(self, o, a): self.P.op("vector", lambda e: e.reciprocal(out=o, in_=a), reads=self.nm(a), writes=self.nm(o))
    def transpose(self, o, a, ident):
        self.P.op("tensor", lambda e: e.transpose(o, a, ident), reads=self.nm(a, ident), writes=self.nm(o), nosync_same=True)
    def cmul_s(self, o_re, o_im, a_re, a_im, s_re, s_im, s_imn):
        self.ts(o_re, a_re, s_re, ALU.mult)
        self.stt(o_re, a_im, s_imn, o_re, ALU.mult, ALU.add)
        self.ts(o_im, a_re, s_im, ALU.mult)
        self.stt(o_im, a_im, s_re, o_im, ALU.mult, ALU.add)

def build_ssm():
    P = Prog(); a = A(P)
    d_in = {}
    for nm_, shp in (("are", [128, 6]), ("aim", [128, 6]), ("ldt", [128, 6]), ("Bre", [128, 96]), ("Bim", [128, 96]),
                     ("Cre", [128, 96]), ("Cim", [128, 96]), ("maskF", [128, 128]), ("maskB", [128, 128]), ("sgn", [128, 1]),
                     ("ident", [128, 128])):
        d_in[nm_] = P.dram_in(nm_, shp)
    Ddiag = P.dram_in("Ddiag", [6, 128, 128]); U = P.dram_in("U", [6, 128, NCH]); Yg = P.dram_out("Yg", [6, 128, NCH])
    s = {}
    for nm_, ap in d_in.items():
        shp = list(ap.shape)
        s[nm_] = P.sbuf("s_" + nm_, shp)
        a.dma_in(s[nm_][:], ap, "ld0")
    def T(name, shape): return P.sbuf(name, shape)
    dt = T("dt", [128, 6]); x = T("x", [128, 6]); th = T("th", [128, 6]); er = T("er", [128, 6]); m = T("m", [128, 6])
    y2 = T("y2", [128, 6]); sn = T("sn", [128, 6]); cs = T("cs", [128, 6]); lbr = T("lbr", [128, 6]); lbi = T("lbi", [128, 6])
    n2 = T("n2", [128, 6]); t1 = T("t1", [128, 6]); t2 = T("t2", [128, 6]); am1 = T("am1", [128, 6])
    qr = T("qr", [128, 6]); qi = T("qi", [128, 6]); qin = T("qin", [128, 6])
    a.act(dt[:], s["ldt"][:], AF.Exp)
    a.tt(x[:], s["are"][:], dt[:], ALU.mult); a.tt(th[:], s["aim"][:], dt[:], ALU.mult)
    a.act(er[:], x[:], AF.Exp)
    for _ in range(4):
        a.ts(m[:], th[:], PI, ALU.is_gt)
        a.stt(th[:], m[:], -2 * PI, th[:], ALU.mult, ALU.add)
    a.ts(y2[:], th[:], PI / 2, ALU.add)
    a.ts(m[:], y2[:], PI, ALU.is_gt)
    a.stt(y2[:], m[:], -2 * PI, y2[:], ALU.mult, ALU.add)
    a.act(sn[:], th[:], AF.Sin); a.act(cs[:], y2[:], AF.Sin)
    a.tt(lbr[:], er[:], cs[:], ALU.mult); a.tt(lbi[:], er[:], sn[:], ALU.mult)
    a.tt(n2[:], s["are"][:], s["are"][:], ALU.mult); a.tt(t1[:], s["aim"][:], s["aim"][:], ALU.mult); a.tt(n2[:], n2[:], t1[:], ALU.add)
    a.recip(n2[:], n2[:])
    a.ts(am1[:], lbr[:], -1.0, ALU.add)
    a.tt(t1[:], am1[:], s["are"][:], ALU.mult); a.tt(t2[:], lbi[:], s["aim"][:], ALU.mult); a.tt(t1[:], t1[:], t2[:], ALU.add)
    a.tt(qr[:], t1[:], n2[:], ALU.mult)
    a.tt(t1[:], lbi[:], s["are"][:], ALU.mult); a.tt(t2[:], am1[:], s["aim"][:], ALU.mult); a.tt(t1[:], t1[:], t2[:], ALU.subtract)
    a.tt(qi[:], t1[:], n2[:], ALU.mult)
    a.ts(qin[:], qi[:], -1.0, ALU.mult)
    Lr = T("Lr", [128, 6, 9]); Li = T("Li", [128, 6, 9]); Vr = T("Vr", [128, 6, 8]); Vi = T("Vi", [128, 6, 8])
    Rr = T("Rr", [128, 6, 9]); Ri = T("Ri", [128, 6, 9])
    e2 = T("e2", [128, 6]); ivr = T("ivr", [128, 6]); ivi = T("ivi", [128, 6])
    a.memset(Lr[:, :, 0], 1.0); a.memset(Li[:, :, 0], 0.0); a.memset(Vr[:, :, 0], 1.0); a.memset(Vi[:, :, 0], 0.0)
    a.act(e2[:], x[:], AF.Exp, scale=-2.0)
    a.tt(ivr[:], lbr[:], e2[:], ALU.mult); a.tt(ivi[:], lbi[:], e2[:], ALU.mult); a.ts(ivi[:], ivi[:], -1.0, ALU.mult)
    def cmul_t(o_r, o_i, p_r, p_i, q_r, q_i):
        a.tt(t1[:], p_r, q_r, ALU.mult); a.tt(t2[:], p_i, q_i, ALU.mult); a.tt(o_r, t1[:], t2[:], ALU.subtract)
        a.tt(t1[:], p_r, q_i, ALU.mult); a.tt(t2[:], p_i, q_r, ALU.mult); a.tt(o_i, t1[:], t2[:], ALU.add)
    for k in range(8):
        cmul_t(Lr[:, :, k + 1], Li[:, :, k + 1], Lr[:, :, k], Li[:, :, k], lbr[:], lbi[:])
    for k in range(7):
        cmul_t(Vr[:, :, k + 1], Vi[:, :, k + 1], Vr[:, :, k], Vi[:, :, k], ivr[:], ivi[:])
    for k in range(9):
        a.copy(Rr[:, :, k], Lr[:, :, 8 - k], eng="gpsimd"); a.copy(Ri[:, :, k], Li[:, :, 8 - k], eng="gpsimd")
    tabs = {}
    for nm_, (lo_r, lo_i, hi_r, hi_i) in dict(
            XL=(Vr[0:64, :, 0:8], Vi[0:64, :, 0:8], Lr[64:128, :, 0:8], Li[64:128, :, 0:8]),
            YL=(Lr[0:64, :, 0:8], Li[0:64, :, 0:8], Vr[64:128, :, 0:8], Vi[64:128, :, 0:8]),
            SL=(Rr[0:64, :, 1:9], Ri[0:64, :, 1:9], Lr[64:128, :, 0:8], Li[64:128, :, 0:8]),
            OL=(Lr[0:64, :, 1:9], Li[0:64, :, 1:9], Rr[64:128, :, 0:8], Ri[64:128, :, 0:8])).items():
        tr = T(nm_ + "r", [128, 6, 8]); ti = T(nm_ + "i", [128, 6, 8]); tn = T(nm_ + "n", [128, 6, 8])
        a.copy(tr[0:64], lo_r); a.copy(ti[0:64], lo_i); a.copy(tr[64:128], hi_r); a.copy(ti[64:128], hi_i)
        a.ts(tn[:], ti[:], -1.0, ALU.mult)
        tabs[nm_] = (tr, ti, tn)
    rho8 = T("rho8", [128, 6]); c8 = T("c8", [128, 6]); s8 = T("s8", [128, 6]); e8 = T("e8", [128, 6])
    a.act(rho8[:], x[:], AF.Exp, scale=8.0); a.act(e8[:], x[:], AF.Exp, scale=-8.0)
    a.tt(c8[:], Lr[:, :, 8], e8[:], ALU.mult); a.tt(s8[:], Li[:, :, 8], e8[:], ALU.mult)
    a.ts(s8[:], s8[:], s["sgn"][:, 0:1], ALU.mult)
    onesT = T("onesT", [128, NCH]); a.memset(onesT[:], 1.0, eng="gpsimd")
    Bbr = T("Bbr", [128, 16]); Bbi = T("Bbi", [128, 16])
    Xr = T("Xr", [128, 8, 16]); Xi = T("Xi", [128, 8, 16]); Yr = T("Yr", [128, 8, 16]); Yin = T("Yin", [128, 8, 16])
    Wtr = T("Wtr", [128, 8, 16]); Wti = T("Wti", [128, 8, 16]); Wor = T("Wor", [128, 8, 16]); Woin = T("Woin", [128, 8, 16])
    Wsr = T("Wsr", [128, 128]); Wsi = T("Wsi", [128, 128]); Msb = T("Msb", [128, 128]); Mtmp = T("Mtmp", [128, 128]); Dd = T("Dd", [128, 128])
    Us = [T(f"Us{i}", [128, NCH]) for i in range(2)]
    Sre = T("Sre", [128, NCH]); Sim = T("Sim", [128, NCH]); Spr = T("Spr", [128, NCH]); Spi = T("Spi", [128, NCH])
    Gre = T("Gre", [128, NCH]); Gim = T("Gim", [128, NCH]); Hor = T("Hor", [128, NCH]); Hoi = T("Hoi", [128, NCH])
    Hir = T("Hir", [128, NCH]); Hii = T("Hii", [128, NCH])
    Tr = T("Tr", [128, NCH + 1]); Ti = T("Ti", [128, NCH + 1]); rhoT = T("rhoT", [128, NCH])
    w1 = T("w1", [128, NCH]); w2 = T("w2", [128, NCH])
    mult = T("mult", [128, 11, 3]); ini = T("ini", [128, 4]); Ysb = T("Ysb", [128, NCH])
    ps = [P.psum(f"ps{i}") for i in range(8)]
    BLK = [(0, 512), (512, 512), (1024, NCH - 1024)]
    for gi in range(6):
        ub = gi % 2
        a.dma_in(Us[ub][:], U[gi], f"u{ub}")
        a.dma_in(Dd[:], Ddiag[gi], "dd")
        g16 = slice(gi * 16, gi * 16 + 16)
        a.cmul_s(Bbr[:], Bbi[:], s["Bre"][:, g16], s["Bim"][:, g16], qr[:, gi:gi + 1], qi[:, gi:gi + 1], qin[:, gi:gi + 1])
        XL, YL, SL, OL = tabs["XL"], tabs["YL"], tabs["SL"], tabs["OL"]
        for k in range(8):
            sc = lambda tb: (tb[0][:, gi, k:k + 1], tb[1][:, gi, k:k + 1], tb[2][:, gi, k:k + 1])
            a.cmul_s(Xr[:, k, :], Xi[:, k, :], Bbr[:], Bbi[:], *sc(XL))
            a.cmul_s(Wtr[:, k, :], Wti[:, k, :], Bbr[:], Bbi[:], *sc(SL))
            a.cmul_s(Yr[:, k, :], Yin[:, k, :], s["Cre"][:, g16], s["Cim"][:, g16], *sc(YL))
            a.cmul_s(Wor[:, k, :], Woin[:, k, :], s["Cre"][:, g16], s["Cim"][:, g16], *sc(OL))
        a.ts(Yin[:], Yin[:], -1.0, ALU.mult); a.ts(Woin[:], Woin[:], -1.0, ALU.mult)
        f2 = lambda t_: t_[:].rearrange("p a b -> p (a b)")
        for half, pb in ((0, 6), (1, 7)):
            rows = slice(half * 64, half * 64 + 64)
            a.mm(ps[pb][:, 0:128], f2(Xr)[rows], f2(Yr)[rows], True, False)
            a.mm(ps[pb][:, 0:128], f2(Xi)[rows], f2(Yin)[rows], False, True)
        a.tt(Msb[:], ps[6][:, 0:128], s["maskF"][:], ALU.mult)
        a.tt(Mtmp[:], ps[7][:, 0:128], s["maskB"][:], ALU.mult)
        a.tt(Msb[:], Msb[:], Mtmp[:], ALU.add, eng="gpsimd"); a.tt(Msb[:], Msb[:], Dd[:], ALU.add, eng="gpsimd")
        a.transpose(ps[6][:, 128:256], f2(Wtr), s["ident"][:]); a.transpose(ps[7][:, 128:256], f2(Wti), s["ident"][:])
        a.copy(Wsr[:], ps[6][:, 128:256], eng="scalar"); a.copy(Wsi[:], ps[7][:, 128:256], eng="scalar")
        for bi, (c0, cn) in enumerate(BLK):
            a.mm(ps[bi][:, 0:cn], Wsr[:], Us[ub][:, c0:c0 + cn], True, True)
            a.mm(ps[3 + bi][:, 0:cn], Wsi[:], Us[ub][:, c0:c0 + cn], True, True)
            a.copy(Sre[:, c0:c0 + cn], ps[bi][:, 0:cn], eng="scalar"); a.copy(Sim[:, c0:c0 + cn], ps[3 + bi][:, 0:cn], eng="scalar")
        a.memset(Tr[:, 0:1], 1.0); a.memset(Ti[:, 0:1], 0.0)
        a.copy(mult[:, 0, 0:1], c8[:, gi:gi + 1]); a.copy(mult[:, 0, 1:2], s8[:, gi:gi + 1])
        a.ts(mult[:, 0, 2:3], mult[:, 0, 1:2], -1.0, ALU.mult)
        for k in range(1, 11):
            a.tt(ini[:, 0:1], mult[:, k - 1, 0:1], mult[:, k - 1, 0:1], ALU.mult); a.tt(ini[:, 1:2], mult[:, k - 1, 1:2], mult[:, k - 1, 1:2], ALU.mult)
            a.tt(mult[:, k, 0:1], ini[:, 0:1], ini[:, 1:2], ALU.subtract)
            a.tt(ini[:, 0:1], mult[:, k - 1, 0:1], mult[:, k - 1, 1:2], ALU.mult)
            a.ts(mult[:, k, 1:2], ini[:, 0:1], 2.0, ALU.mult); a.ts(mult[:, k, 2:3], ini[:, 0:1], -2.0, ALU.mult)
        for k in range(11):
            n = 1 << k
            cnt = min(n, NCH + 1 - n)
            a.cmul_s(Tr[:, n:n + cnt], Ti[:, n:n + cnt], Tr[:, 0:cnt], Ti[:, 0:cnt], mult[:, k, 0:1], mult[:, k, 1:2], mult[:, k, 2:3])
        a.ts(rhoT[:], onesT[:], rho8[:, gi:gi + 1], ALU.mult, eng="gpsimd")
        a.tt(w1[:], Sre[:], Tr[:, 0:NCH], ALU.mult); a.tt(w2[:], Sim[:], Ti[:, 0:NCH], ALU.mult, eng="gpsimd")
        a.tt(Spr[:], w1[:], w2[:], ALU.subtract)
        a.tt(w1[:], Sre[:], Ti[:, 0:NCH], ALU.mult); a.tt(w2[:], Sim[:], Tr[:, 0:NCH], ALU.mult, eng="gpsimd")
        a.tt(Spi[:], w1[:], w2[:], ALU.add)
        for (Gx, Sx) in ((Gre, Spr), (Gim, Spi)):
            a.scan(Gx[0:64, :], rhoT[0:64, :], Sx[0:64, :], 0.0)
            a.scan(Gx[64:128, 0:32][:, ::-1], rhoT[64:128, 0:32], Sx[64:128, 0:32][:, ::-1], 0.0)
        lo = slice(64, 128)
        a.tt(ini[lo, 0:1], Gre[lo, 0:1], Tr[lo, NCH:NCH + 1], ALU.mult); a.tt(ini[lo, 1:2], Gim[lo, 0:1], Ti[lo, NCH:NCH + 1], ALU.mult)
        a.tt(ini[lo, 2:3], ini[lo, 0:1], ini[lo, 1:2], ALU.subtract)
        a.tt(ini[lo, 0:1], Gre[lo, 0:1], Ti[lo, NCH:NCH + 1], ALU.mult); a.tt(ini[lo, 1:2], Gim[lo, 0:1], Tr[lo, NCH:NCH + 1], ALU.mult)
        a.tt(ini[lo, 3:4], ini[lo, 0:1], ini[lo, 1:2], ALU.add)
        a.scan(Gre[lo, 32:NCH][:, ::-1], rhoT[lo, 32:NCH], Spr[lo, 32:NCH][:, ::-1], ini[lo, 2:3])
        a.scan(Gim[lo, 32:NCH][:, ::-1], rhoT[lo, 32:NCH], Spi[lo, 32:NCH][:, ::-1], ini[lo, 3:4])
        a.tt(w1[:], Gre[:], Tr[:, 0:NCH], ALU.mult); a.tt(w2[:], Gim[:], Ti[:, 0:NCH], ALU.mult, eng="gpsimd")
        a.tt(Hor[:], w1[:], w2[:], ALU.add)
        a.tt(w1[:], Gim[:], Tr[:, 0:NCH], ALU.mult); a.tt(w2[:], Gre[:], Ti[:, 0:NCH], ALU.mult, eng="gpsimd")
        a.tt(Hoi[:], w1[:], w2[:], ALU.subtract)
        for (Hi_, Ho_, Gx) in ((Hir, Hor, Gre), (Hii, Hoi, Gim)):
            a.copy(Hi_[0:64, 1:NCH], Ho_[0:64, 0:NCH - 1], eng="scalar"); a.memset(Hi_[0:64, 0:1], 0.0)
            a.copy(Hi_[lo, 0:NCH - 1], Ho_[lo, 1:NCH], eng="scalar"); a.memset(Hi_[lo, 31:32], 0.0)
            a.copy(Hi_[lo, NCH - 1:NCH], Gx[lo, 0:1])
        for bi, (c0, cn) in enumerate(BLK):
            a.mm(ps[bi][:, 0:cn], Msb[:], Us[ub][:, c0:c0 + cn], True, False)
            a.mm(ps[bi][:, 0:cn], f2(Wor), Hir[:, c0:c0 + cn], False, False)
            a.mm(ps[bi][:, 0:cn], f2(Woin), Hii[:, c0:c0 + cn], False, True)
            a.copy(Ysb[:, c0:c0 + cn], ps[bi][:, 0:cn], eng="scalar")
        a.dma_out(Yg[gi], Ysb[:])
    return P.finish()

def ssm_inputs(inp, l, j4, u_b, uc_b):
    gs = np.arange(6 * j4, 6 * j4 + 6)
    def rows(arr):
        return np.ascontiguousarray(arr[:, gs, :].transpose(0, 2, 1).reshape(128, 6))
    are = rows(inp["ssm_a_re"][l]); aim = rows(inp["ssm_a_im"][l])
    ldt = np.ascontiguousarray(np.repeat(inp["ssm_log_dt"][l][:, gs][:, None, :], 64, axis=1).reshape(128, 6))
    def rowsB(arr):
        return np.ascontiguousarray(arr[:, gs].transpose(0, 2, 1, 3).reshape(128, 96))
    def rowsC(arr):
        return np.ascontiguousarray(arr[:, gs].transpose(0, 3, 1, 2).reshape(128, 96))
    s_ = np.arange(8)
    mF = (s_[None, :] >= s_[:, None]).astype(np.float32)
    maskF = np.kron(mF, np.ones((16, 16), np.float32)); maskB = np.kron(mF.T, np.ones((16, 16), np.float32))
    sgn = np.concatenate([-np.ones((64, 1), np.float32), np.ones((64, 1), np.float32)], 0)
    dsk = inp["ssm_d"][l]
    Dd = np.zeros((6, 128, 128), np.float32)
    for gi, g in enumerate(gs):
        dd = np.zeros((8, 16, 8, 16), np.float32)
        for t in range(8):
            dd[t, np.arange(16), t, np.arange(16)] = dsk[16 * g:16 * g + 16]
        Dd[gi] = dd.reshape(128, 128)
    seq = np.concatenate([uc_b, u_b], 0)
    U = np.zeros((6, 128, NCH), np.float32)
    for gi, g in enumerate(gs):
        U[gi] = seq[:, 16 * g:16 * g + 16].reshape(NCH, 128).T
    return dict(are=are, aim=aim, ldt=ldt, Bre=rowsB(inp["ssm_b_re"][l]), Bim=rowsB(inp["ssm_b_im"][l]),
                Cre=rowsC(inp["ssm_c_re"][l]), Cim=rowsC(inp["ssm_c_im"][l]), maskF=maskF, maskB=maskB, sgn=sgn,
                ident=np.eye(128, dtype=np.float32), Ddiag=Dd, U=U)

def ssm_unpack(Yg):
    return np.ascontiguousarray(Yg.transpose(2, 1, 0).reshape(NCH, 8, 16, 6).transpose(0, 1, 3, 2).reshape(NCH * 8, 96))


_PROGS = {}
def _prog(name, fn):
    if name not in _PROGS:
        _PROGS[name] = fn()
    return _PROGS[name]

def _run(nc, maps):
    res = run_bass_kernel_spmd(nc, maps, core_ids=list(range(8)))
    return res.results

def kernel(x, c, ctx, c_ctx, w_mod, b_mod, g_pre_mix, g_post_mix, w_in, ssm_a_re, ssm_a_im, ssm_log_dt, ssm_b_re, ssm_b_im,
           ssm_c_re, ssm_c_im, ssm_d, w_glu, w_fourier, na_rpb, w_out, g_pre_ffn, g_post_ffn, w_ffn_gate, w_ffn_up, w_ffn_down):
    f32 = lambda a: np.ascontiguousarray(np.asarray(a, dtype=np.float32))
    inp = dict(ssm_a_re=f32(ssm_a_re), ssm_a_im=f32(ssm_a_im), ssm_log_dt=f32(ssm_log_dt), ssm_b_re=f32(ssm_b_re), ssm_b_im=f32(ssm_b_im),
               ssm_c_re=f32(ssm_c_re), ssm_c_im=f32(ssm_c_im), ssm_d=f32(ssm_d))
    x = f32(x); c = f32(c); ctx = f32(ctx); c_ctx = f32(c_ctx); w_mod = f32(w_mod); b_mod = f32(b_mod)
    w_in = f32(w_in); w_glu = f32(w_glu); w_fourier = f32(w_fourier); na_rpb = f32(na_rpb); w_out = f32(w_out)
    g_pre_mix = f32(g_pre_mix); g_post_mix = f32(g_post_mix); g_pre_ffn = f32(g_pre_ffn); g_post_ffn = f32(g_post_ffn)
    w_ffn_gate = f32(w_ffn_gate); w_ffn_up = f32(w_ffn_up); w_ffn_down = f32(w_ffn_down)
    DEPTH = 2
    cores = [(k // 4, k % 4) for k in range(8)]
    cTs = [np.ascontiguousarray(np.concatenate([colT(c[b], 8), colT(c_ctx, 8)], axis=1)) for b in range(2)]
    xT = [np.ascontiguousarray(np.concatenate([x[b, q * 2048:(q + 1) * 2048].T, ctx[b].T], axis=1)) for (b, q) in cores]
    KF = fnet_consts(); permm = perm_matrix(); ident = np.eye(128, dtype=np.float32)
    ropes = [rope_tables(q * 2048, 2048) for q in range(4)]
    for l in range(DEPTH):
        maps = []
        for k, (b, q) in enumerate(cores):
            maps.append(dict(xT=xT[k], w_in=w_in[l], w_mod=np.ascontiguousarray(w_mod[l][:, 0:2048]), b_modT=colT(b_mod[l][0:2048], 16),
                             g_preT=colT(g_pre_mix[l], 8), cT=cTs[b], cos=ropes[q][0], sin=ropes[q][1], perm=permm))
        res = _run(_prog("l1", build_l1), maps)
        hfull = [np.concatenate([res[k]["hT"], res[k]["hTb"]], 0) for k in range(8)]
        h_lat = [np.concatenate([hfull[4 * b + q][:, 0:2048].T for q in range(4)], 0) for b in range(2)]
        h_ctx = [np.ascontiguousarray(hfull[4 * b][:, 2048:2304].T) for b in range(2)]
        del hfull
        del res
        maps = [ssm_inputs(inp, l, j4, h_lat[b][:, 0:384], h_ctx[b][:, 0:384]) for (b, j4) in cores]
        res = _run(_prog("ssm", build_ssm), maps)
        ys = [[ssm_unpack(res[4 * b + j4]["Yg"]) for j4 in range(4)] for b in range(2)]
        ysT = [np.ascontiguousarray(np.concatenate(ys[b], 1).T) for b in range(2)]
        del res, ys
        maps = []
        for (b, g) in cores:
            m = dict(z=np.ascontiguousarray(h_lat[b][:, 384 + 64 * g:448 + 64 * g].reshape(128, 4096)),
                     zc=np.ascontiguousarray(h_ctx[b][:, 384 + 64 * g:448 + 64 * g].reshape(2, 128, 64).transpose(1, 0, 2)))
            m.update(KF); maps.append(m)
        res = _run(_prog("fnet", lambda: build_fnet(True)), maps)
        mxT = [np.concatenate([res[4 * b + g]["R"] for g in range(4)], 0) for b in range(2)]
        mxcT = [np.concatenate([res[4 * b + g]["Rc"] for g in range(4)], 0) for b in range(2)]
        del res
        maps = []
        for (b, q) in cores:
            kwT, vw = na_windows(h_lat[b][:, 1024:1408], h_lat[b][:, 1408:1792], q)
            tbraw, mask = na_tables(na_rpb[l], q)
            maps.append(dict(qT=np.ascontiguousarray(h_lat[b][q * 2048:(q + 1) * 2048, 640:1024].T), kwT=kwT, vw=vw,
                             kcT=np.ascontiguousarray(h_ctx[b][:, 1024:1408].T),
                             vc=np.ascontiguousarray(h_ctx[b][:, 1408:1792].reshape(2, 128, 384).transpose(1, 0, 2)),
                             tbraw=tbraw, mask=mask, ident=ident, qcT=np.ascontiguousarray(h_ctx[b][:, 640:1024].T)))
        res = _run(_prog("na", lambda: build_na(True)), maps)
        naT = [np.ascontiguousarray(np.concatenate([res[4 * b + q]["Y"] for q in range(4)], 0).T) for b in range(2)]
        nacT = [np.ascontiguousarray(res[4 * b]["Yc"].T) for b in range(2)]
        del res, h_lat
        maps = []
        for k, (b, q) in enumerate(cores):
            sl = slice(q * 2048, (q + 1) * 2048)
            maps.append(dict(xT=xT[k], ysT=np.ascontiguousarray(np.concatenate([ysT[b][:, 256 + q * 2048:256 + (q + 1) * 2048], ysT[b][:, 0:256]], 1)),
                             mxT=np.ascontiguousarray(np.concatenate([mxT[b][:, sl], mxcT[b]], 1)),
                             naT=np.ascontiguousarray(np.concatenate([naT[b][:, sl], nacT[b]], 1)),
                             w_mod=np.ascontiguousarray(w_mod[l][:, 2048:3072]), b_modT=colT(b_mod[l][2048:3072], 8), cT=cTs[b],
                             g_postT=colT(g_post_mix[l], 8), w_glu=w_glu[l], w_fourier=w_fourier[l], w_out=w_out[l]))
        res = _run(_prog("l3a", lambda: build_l3a(True)), maps)
        xT = [res[k]["xoT"] for k in range(8)]
        del res
        maps = []
        for k, (b, q) in enumerate(cores):
            maps.append(dict(xT=xT[k], w_mod=np.ascontiguousarray(w_mod[l][:, 3072:6144]), b_modT=colT(b_mod[l][3072:6144], 24), cT=cTs[b],
                             g_preT=colT(g_pre_ffn[l], 8), g_postT=colT(g_post_ffn[l], 8),
                             w_gate=w_ffn_gate[l], w_up=w_ffn_up[l], w_down=w_ffn_down[l]))
        res = _run(_prog("l3b", lambda: build_l3b(True)), maps)
        xT = [np.ascontiguousarray(res[k]["xoT"]) for k in range(8)]
        del res
    out = np.empty((2, 8192, 1024), np.float32)
    for k, (b, q) in enumerate(cores):
        out[b, q * 2048:(q + 1) * 2048] = xT[k][:, 0:2048].T
    return out
```

```python
import math
import numpy as np
from contextlib import ExitStack
import concourse.bass as bass
import concourse.mybir as mybir
from concourse.bass_utils import run_bass_kernel_spmd


F32 = mybir.dt.float32
BF16 = mybir.dt.bfloat16
ALU = mybir.AluOpType
AF = mybir.ActivationFunctionType
AX = mybir.AxisListType

COMPUTE = ("tensor", "vector", "scalar", "gpsimd")


class Prog:
    def __init__(self):
        self.nc = bass.Bass("TRN2", target_bir_lowering=False)
        self.ops = []
        self.stack = ExitStack()
        self.ndram = 0

    def dram_in(self, name, shape, dtype=F32):
        return self.nc.dram_tensor(name, list(shape), dtype, kind="ExternalInput").ap()

    def dram_out(self, name, shape, dtype=F32):
        return self.nc.dram_tensor(name, list(shape), dtype, kind="ExternalOutput").ap()

    def sbuf(self, name, shape, dtype=F32):
        return self.stack.enter_context(self.nc.sbuf_tensor(name, list(shape), dtype))

    def psum(self, name, shape=(128, 512), dtype=F32):
        return self.stack.enter_context(self.nc.psum_tensor(name, list(shape), dtype))

    def op(self, eng, fn, reads=(), writes=(), chan=None, nosync_same=False, inc=True):
        self.ops.append(dict(eng=eng, fn=fn, reads=tuple(reads), writes=tuple(writes),
                             chan=chan, nosync_same=nosync_same, inc=inc))

    def dma(self, eng, out, in_, reads=(), writes=(), chan="ld", **kw):
        self.op(eng, lambda e: e.dma_start(out=out, in_=in_, **kw), reads, writes, chan=chan)

    def mm(self, out, lhsT, rhs, start, stop, reads=(), writes=()):
        self.op("tensor", lambda e: e.matmul(out, lhsT, rhs, start=start, stop=stop),
                reads, writes, nosync_same=True, inc=True)

    def act(self, out, in_, func, reads=(), writes=(), **kw):
        self.op("scalar", lambda e: e.activation(out=out, in_=in_, func=func, **kw), reads, writes)

    def tt(self, out, in0, in1, op, reads=(), writes=(), eng="vector"):
        self.op(eng, lambda e: e.tensor_tensor(out=out, in0=in0, in1=in1, op=op), reads, writes)

    def ts(self, out, in0, s1, s2, op0, op1=None, reads=(), writes=(), eng="vector"):
        if op1 is None:
            self.op(eng, lambda e: e.tensor_scalar(out=out, in0=in0, scalar1=s1, scalar2=None, op0=op0),
                    reads, writes)
        else:
            self.op(eng, lambda e: e.tensor_scalar(out=out, in0=in0, scalar1=s1, scalar2=s2, op0=op0, op1=op1),
                    reads, writes)

    def stt(self, out, in0, scalar, in1, op0, op1, reads=(), writes=()):
        self.op("vector", lambda e: e.scalar_tensor_tensor(out=out, in0=in0, scalar=scalar, in1=in1,
                                                            op0=op0, op1=op1), reads, writes)

    def copy(self, out, in_, reads=(), writes=(), eng="vector"):
        if eng == "scalar":
            self.op(eng, lambda e: e.copy(out=out, in_=in_), reads, writes)
        else:
            self.op(eng, lambda e: e.tensor_copy(out=out, in_=in_), reads, writes)

    def memset(self, ap, val, writes=(), eng="vector"):
        self.op(eng, lambda e: e.memset(ap, val), (), writes)

    def finish(self):
        nc = self.nc
        ops = self.ops
        engines = []
        for o in ops:
            if o["eng"] not in engines:
                engines.append(o["eng"])
        chans = []
        for o in ops:
            if o["chan"] is not None and o["chan"] not in chans:
                chans.append(o["chan"])
        sems = {}
        for e in engines:
            sems[("e", e)] = self.stack.enter_context(nc.semaphore("s_" + e))
        for c in chans:
            sems[("c", c)] = self.stack.enter_context(nc.semaphore("c_" + c))
        def plan_pass():
            viol = set()
            eng_count = {e: 0 for e in engines}
            chan_count = {c: 0 for c in chans}
            last_writer = {}
            readers = {}
            known = {e: {} for e in engines}
            plan = {e: [] for e in engines}
            done = []
            for i, o in enumerate(ops):
                e = o["eng"]
                deps = set()
                for r in o["reads"]:
                    if r in last_writer:
                        deps.add(last_writer[r])
                for w in o["writes"]:
                    if w in last_writer:
                        deps.add(last_writer[w])
                    for rd in readers.get(w, ()):
                        deps.add(rd)
                need = {}
                for d in deps:
                    od = ops[d]
                    if od["chan"] is not None:
                        key = ("c", od["chan"])
                        val = 16 * chan_count[od["chan"]]
                    else:
                        if od["eng"] == e and (o["nosync_same"] and od["nosync_same"]):
                            continue
                        key = ("e", od["eng"])
                        val = done[d][1]
                        if val > eng_count[od["eng"]]:
                            viol.add(d)
                    if val > need.get(key, 0):
                        need[key] = val
                waits = []
                for key, val in need.items():
                    if known[e].get(key, 0) >= val:
                        continue
                    known[e][key] = val
                    waits.append((key, val))
                if o["chan"] is not None:
                    chan_count[o["chan"]] += 1
                    done.append((("c", o["chan"]), 16 * chan_count[o["chan"]]))
                    inc = (("c", o["chan"]), 16)
                elif not o["inc"]:
                    done.append((("e", e), eng_count[e] + 1))
                    inc = None
                else:
                    eng_count[e] += 1
                    done.append((("e", e), eng_count[e]))
                    inc = (("e", e), 1)
                plan[e].append((waits, o["fn"], inc))
                for r in o["reads"]:
                    readers.setdefault(r, []).append(i)
                for w in o["writes"]:
                    last_writer[w] = i
                    readers[w] = []
            return viol, plan, chan_count
        while True:
            viol, plan, chan_count = plan_pass()
            if not viol:
                break
            for d in viol:
                ops[d]["inc"] = True
        final_waits = {e: [] for e in engines}
        chan_eng = {}
        for o in ops:
            if o["chan"] is not None:
                chan_eng[o["chan"]] = o["eng"]
        for c, e in chan_eng.items():
            final_waits[e].append((("c", c), 16 * chan_count[c]))

        semv = {k: 0 for k in sems}
        ptr = {e: 0 for e in engines}
        progressed = True
        while progressed:
            progressed = False
            for e in engines:
                while ptr[e] < len(plan[e]):
                    waits, _fn, inc = plan[e][ptr[e]]
                    if any(semv[k] < v for k, v in waits):
                        break
                    if inc is not None:
                        semv[inc[0]] += inc[1]
                    ptr[e] += 1
                    progressed = True
        stuck = {e: (ptr[e], len(plan[e])) for e in engines if ptr[e] < len(plan[e])}
        if stuck:
            det = {e: [(k, v, semv[k]) for k, v in plan[e][ptr[e]][0] if semv[k] < v] for e in stuck}
            raise RuntimeError(f"sync plan deadlocks: {stuck} waiting on {det}")

        with nc.Block() as block:
            def make(e):
                def body(eng):
                    for waits, fn, inc in plan[e]:
                        for key, val in waits:
                            eng.wait_ge(sems[key], val)
                        ins = fn(eng)
                        if inc is not None:
                            ins.then_inc(sems[inc[0]], inc[1])
                    for key, val in final_waits[e]:
                        eng.wait_ge(sems[key], val)
                return body
            for e in engines:
                getattr(block, e)(make(e))
        self.stack.close()
        return nc

GRID_W = 64
def rope_tables(tok0, n):
    t = np.arange(tok0, tok0 + n); row = (t // GRID_W).astype(np.float32); col = (t % GRID_W).astype(np.float32)
    quarter = 16
    freqs = (10000.0 ** (-np.arange(quarter, dtype=np.float32) / quarter)).astype(np.float32)
    cos = np.zeros((64, n), np.float32); sin = np.zeros((64, n), np.float32)
    for d in range(64):
        pos = row if d < 32 else col
        dd = d % 32
        f = freqs[dd % 16]
        ang = (pos * f).astype(np.float32)
        cos[d] = np.cos(ang); s = np.sin(ang)
        sin[d] = -s if dd < 16 else s
    return np.concatenate([cos, cos], 0), np.concatenate([sin, sin], 0)
def perm_matrix():
    Pm = np.zeros((128, 128), np.float32)
    for m in range(128):
        dd = m % 32
        k = m + 16 if dd < 16 else m - 16
        Pm[k, m] = 1.0
    return Pm
def colT(v, n):
    return np.ascontiguousarray(v.reshape(n, 128).T)

EPS = 1e-6

def get_stage(P):
    if not hasattr(P, "_stage"):
        P._stage_n = getattr(P, "_stage_n", 3)
        P._stage = [P.sbuf(f"stage{i}", [128, 1024]) for i in range(P._stage_n)]
        P._stage_i = 0
    return P._stage

def emit_mod(P, w_mod, b_modT, cT, nct, ps_mod, tag="m"):
    ncols = nct * 128
    c_s = P.sbuf(tag + "c_s", [128, 16]); sc_s = P.sbuf(tag + "sc_s", [128, 16])
    bm_s = P.sbuf(tag + "bm_s", [128, nct]); modv = P.sbuf(tag + "modv", [128, 2 * nct]); modc = P.sbuf(tag + "modc", [128, 2 * nct])
    modrow = [P.sbuf(f"{tag}modrow{i}", [2, 512]) for i in range(2)]
    scr = P.nc.dram_tensor(tag + "_modscr", [2, ncols], F32, kind="Internal").ap()
    st = get_stage(P)
    P.dma("sync", c_s[:], cT, writes=[tag + "c_s"], chan="ld0")
    P.dma("sync", bm_s[:], b_modT, writes=[tag + "bm_s"], chan="ld0")
    P.act(sc_s[:], c_s[:], AF.Silu, reads=[tag + "c_s"], writes=[tag + "sc_s"])
    wmv = w_mod.rearrange("(kc p) n -> p kc n", p=128)
    for pc in range(ncols // 512):
        for i in range(4):
            b = P._stage_i % P._stage_n; P._stage_i += 1
            P.dma("sync", st[b][:].rearrange("p (k n) -> p k n", k=2), wmv[:, 2 * i:2 * i + 2, pc * 512:(pc + 1) * 512],
                  writes=[("stage", b)], chan=f"stage{b}")
            for k2 in range(2):
                kc = 2 * i + k2
                P.mm(ps_mod[0:2, 0:512], sc_s[:, kc:16:8], st[b][:, k2 * 512:(k2 + 1) * 512], kc == 0, kc == 7,
                     reads=[("stage", b), tag + "sc_s"], writes=["ps_mod"])
        P.copy(modrow[pc % 2][:], ps_mod[0:2, 0:512], reads=["ps_mod"], writes=[(tag + "modrow", pc % 2)], eng="scalar")
        P.dma("sync", scr[:, pc * 512:(pc + 1) * 512], modrow[pc % 2][:], reads=[(tag + "modrow", pc % 2)], writes=[tag + "scr"], chan="modw")
    P.dma("sync", modc[:].rearrange("p (j t) -> p j t", j=2), scr.rearrange("j (t p) -> p j t", p=128),
          reads=[tag + "scr"], writes=[tag + "modc"], chan="modr", allow_slow_non_contiguous=True)
    for j in range(2):
        P.tt(modv[:, j * nct:(j + 1) * nct], modc[:, j * nct:(j + 1) * nct], bm_s[:], ALU.add,
             reads=[tag + "modc", tag + "bm_s"], writes=[tag + "modv"])
    return modv

def load_cast(P, w_dram, w_bf, nk, ncols, tag, piece=1024):
    wv = w_dram.rearrange("(kc p) n -> p kc n", p=128)
    for kc in range(nk):
        P.dma("gpsimd", w_bf[:, kc, :], wv[:, kc, :], writes=[(tag, kc)], chan="wld_" + tag, max_dma_last_dim=4096)

def emit_rstd(P, src, nk, n, sqb, ones, ps_ss, sd, rstd, src_res, tag=""):
    P.act(sqb[:, 0:nk, 0:n], src[:, 0:nk, 0:n], AF.Square, reads=src_res, writes=["sqb"])
    for kc in range(nk):
        P.mm(ps_ss[:, 0:n], ones[:], sqb[:, kc, 0:n], kc == 0, kc == nk - 1, reads=["sqb", "ones"], writes=["ps_ss"])
    P.act(sd[:, 0:n], ps_ss[:, 0:n], AF.Sqrt, reads=["ps_ss"], writes=["sd" + tag], scale=1.0 / (128 * nk), bias=EPS)
    P.op("vector", lambda e: e.reciprocal(out=rstd[:, 0:n], in_=sd[:, 0:n]), reads=["sd" + tag], writes=["rstd" + tag])

def build_l3b(with_ctx, N=256):
    NT = 2304 if with_ctx else 2048
    P = Prog(); P._stage_n = 2
    xT = P.dram_in("xT", [1024, NT])
    w_mod = P.dram_in("w_mod", [1024, 3072]); b_modT = P.dram_in("b_modT", [128, 24]); cT = P.dram_in("cT", [128, 16])
    g_preT = P.dram_in("g_preT", [128, 8]); g_postT = P.dram_in("g_postT", [128, 8])
    w_gate = P.dram_in("w_gate", [1024, 2816]); w_up = P.dram_in("w_up", [1024, 2816]); w_down = P.dram_in("w_down", [2816, 1024])
    xoT = P.dram_out("xoT", [1024, NT])
    wg = P.sbuf("wg", [128, 8, 2816], BF16); wu = P.sbuf("wu", [128, 8, 2816], BF16); wd = P.sbuf("wd", [128, 22, 1024], BF16)
    xs = [P.sbuf(f"xs{i}", [128, 8, N]) for i in range(2)]
    sqb = P.sbuf("sqb", [128, 8, N], BF16)
    tt_ = [P.sbuf(f"tt{i}", [128, N]) for i in range(2)]; xn = [P.sbuf(f"xn{i}", [128, 8, N], BF16) for i in range(2)]
    sd2 = P.sbuf("sd2", [128, N]); rstd2 = P.sbuf("rstd2", [128, N])
    hmid = P.sbuf("hmid", [128, 22, N], BF16)
    sg = [P.sbuf(f"sg{i}", [128, N]) for i in range(2)]
    o2 = P.sbuf("o2", [128, 8, N]); tmp = [P.sbuf(f"tmp{i}", [128, N]) for i in range(2)]
    xo = [P.sbuf(f"xo{i}", [128, N]) for i in range(2)]
    ones = P.sbuf("ones", [128, 128], BF16)
    sd = P.sbuf("sd", [128, N]); rstd = P.sbuf("rstd", [128, N])
    gp_s = P.sbuf("gp_s", [128, 8]); gq_s = P.sbuf("gq_s", [128, 8])
    Av = P.sbuf("Av", [128, 16]); Gv = P.sbuf("Gv", [128, 16])
    ps_mod = P.psum("ps_mod"); ps_ss = P.psum("ps_ss")
    psg = [P.psum(f"psg{i}") for i in range(2)]; psu = [P.psum(f"psu{i}") for i in range(2)]; pso = [P.psum(f"pso{i}") for i in range(2)]
    P.memset(ones[:], 1.0, writes=["ones"])
    P.dma("sync", gp_s[:], g_preT, writes=["gp_s"], chan="ld0")
    P.dma("sync", gq_s[:], g_postT, writes=["gq_s"], chan="ld0")
    modv = emit_mod(P, w_mod, b_modT, cT, 24, ps_mod)
    for j in range(2):
        P.stt(Av[:, j * 8:(j + 1) * 8], modv[:, j * 24 + 8: j * 24 + 16], 1.0, gp_s[:], ALU.add, ALU.mult,
              reads=["mmodv", "gp_s"], writes=["Av"])
        P.tt(Gv[:, j * 8:(j + 1) * 8], modv[:, j * 24 + 16: j * 24 + 24], gq_s[:], ALU.mult,
             reads=["mmodv", "gq_s"], writes=["Gv"])
    load_cast(P, w_gate, wg, 8, 2816, "wg")
    load_cast(P, w_up, wu, 8, 2816, "wu")
    xv = xT.rearrange("(kc p) t -> p kc t", p=128); xov = xoT.rearrange("(kc p) t -> p kc t", p=128)
    slabs = list(range(0, NT, N)); n = N
    cnt = dict(ng=0, no=0, nt=0)
    def pre(si):
        t0 = slabs[si]; b = si % 2; j = 0 if t0 < 2048 else 1
        P.dma("sync", xs[b][:], xv[:, :, t0:t0 + n], writes=[("xs", b)], chan=f"xs{b}")
        emit_rstd(P, xs[b], 8, n, sqb, ones, ps_ss, sd, rstd, [("xs", b)])
        for kc in range(8):
            P.tt(tt_[kc % 2][:], xs[b][:, kc, :], rstd[:], ALU.mult, reads=[("xs", b), "rstd"], writes=[("tt", kc % 2)])
            P.act(xn[b][:, kc, :], tt_[kc % 2][:], AF.Identity, reads=[("tt", kc % 2), "Av", "mmodv"], writes=[("xn", b, kc)],
                  scale=Av[:, j * 8 + kc: j * 8 + kc + 1], bias=modv[:, j * 24 + kc: j * 24 + kc + 1])
    def gu(si):
        b = si % 2
        for jj in range(22):
            pb = cnt["ng"] % 2; cnt["ng"] += 1
            for kc in range(8):
                P.mm(psg[pb][:, 0:n], wg[:, kc, jj * 128:(jj + 1) * 128], xn[b][:, kc, :], kc == 0, kc == 7,
                     reads=[("wg", kc), ("xn", b, kc)], writes=[("psg", pb)])
            for kc in range(8):
                P.mm(psu[pb][:, 0:n], wu[:, kc, jj * 128:(jj + 1) * 128], xn[b][:, kc, :], kc == 0, kc == 7,
                     reads=[("wu", kc), ("xn", b, kc)], writes=[("psu", pb)])
            P.act(sg[pb][:], psg[pb][:, 0:n], AF.Silu, reads=[("psg", pb)], writes=[("sg", pb)])
            P.tt(hmid[:, jj, :], sg[pb][:], psu[pb][:, 0:n], ALU.mult, reads=[("sg", pb), ("psu", pb)], writes=[("hmid", jj)])
    def dn(si):
        for m in range(8):
            pb = cnt["no"] % 2; cnt["no"] += 1
            for jj in range(22):
                P.mm(pso[pb][:, 0:n], wd[:, jj, m * 128:(m + 1) * 128], hmid[:, jj, :], jj == 0, jj == 21,
                     reads=[("wd", jj), ("hmid", jj)], writes=[("pso", pb)])
            P.copy(o2[:, m, :], pso[pb][:, 0:n], reads=[("pso", pb)], writes=[("o2", m)], eng="scalar")
    def post(si):
        t0 = slabs[si]; b = si % 2; j = 0 if t0 < 2048 else 1
        emit_rstd(P, o2, 8, n, sqb, ones, ps_ss, sd2, rstd2, [("o2", m) for m in range(8)], tag="2")
        for m in range(8):
            P.stt(o2[:, m, :], o2[:, m, :], Gv[:, j * 8 + m: j * 8 + m + 1], rstd2[:], ALU.mult, ALU.mult,
                  reads=[("o2", m), "Gv", "rstd2"], writes=[("o2", m)])
            P.tt(o2[:, m, :], xs[b][:, m, :], o2[:, m, :], ALU.add, reads=[("xs", b), ("o2", m)], writes=[("o2", m)], eng="gpsimd")
        P.dma("gpsimd", xov[:, :, t0:t0 + n], o2[:], reads=[("o2", m) for m in range(8)], chan="st")
    pre(0)
    for si in range(len(slabs)):
        gu(si)
        if si == 0:
            load_cast(P, w_down, wd, 22, 1024, "wd")
        if si + 1 < len(slabs):
            pre(si + 1)
        dn(si)
        post(si)
    return P.finish()

def build_l3a(with_ctx, N=512):
    NT = 2304 if with_ctx else 2048
    P = Prog()
    xT = P.dram_in("xT", [1024, NT]); ysT = P.dram_in("ysT", [384, NT]); mxT = P.dram_in("mxT", [256, NT]); naT = P.dram_in("naT", [384, NT])
    w_mod = P.dram_in("w_mod", [1024, 1024]); b_modT = P.dram_in("b_modT", [128, 8]); cT = P.dram_in("cT", [128, 16])
    g_postT = P.dram_in("g_postT", [128, 8])
    w_glu = P.dram_in("w_glu", [384, 384]); w_fourier = P.dram_in("w_fourier", [256, 256]); w_out = P.dram_in("w_out", [1024, 1024])
    xoT = P.dram_out("xoT", [1024, NT])
    wglu = P.sbuf("wglu", [128, 3, 384], BF16); wf = P.sbuf("wf", [128, 2, 256], BF16); wo = P.sbuf("wo", [128, 8, 1024], BF16)
    xs = [P.sbuf(f"xs{i}", [128, 8, N]) for i in range(2)]
    ys = [P.sbuf(f"ys{i}", [128, 3, N]) for i in range(2)]
    mx = [P.sbuf(f"mx{i}", [128, 2, N]) for i in range(2)]
    na = [P.sbuf(f"na{i}", [128, 3, N]) for i in range(2)]
    sq = P.sbuf("sq", [128, 3, N]); t1 = P.sbuf("t1", [128, 3, N]); sgm = P.sbuf("sgm", [128, 3, N])
    zf = P.sbuf("zf", [128, 3, N]); zb = P.sbuf("zb", [128, 3, N], BF16); mxb = P.sbuf("mxb", [128, 2, N], BF16)
    sg2 = [P.sbuf(f"sg2{i}", [128, N]) for i in range(2)]
    cat = P.sbuf("cat", [128, 8, N], BF16)
    sqb = P.sbuf("sqb", [128, 8, N], BF16)
    o2 = P.sbuf("o2", [128, 8, N]); tmp = [P.sbuf(f"tmp{i}", [128, N]) for i in range(2)]
    xo = [P.sbuf(f"xo{i}", [128, N]) for i in range(2)]
    ones = P.sbuf("ones", [128, 128], BF16)
    sd = P.sbuf("sd", [128, N]); rstd = P.sbuf("rstd", [128, N])
    gq_s = P.sbuf("gq_s", [128, 8]); Gv = P.sbuf("Gv", [128, 16])
    ps_mod = P.psum("ps_mod"); ps_ss = P.psum("ps_ss")
    psa = [P.psum(f"psa{i}") for i in range(3)]; pso = [P.psum(f"pso{i}") for i in range(3)]
    P.memset(ones[:], 1.0, writes=["ones"])
    P.dma("sync", gq_s[:], g_postT, writes=["gq_s"], chan="ld0")
    modv = emit_mod(P, w_mod, b_modT, cT, 8, ps_mod)
    for j in range(2):
        P.tt(Gv[:, j * 8:(j + 1) * 8], modv[:, j * 8:(j + 1) * 8], gq_s[:], ALU.mult, reads=["mmodv", "gq_s"], writes=["Gv"])
    load_cast(P, w_glu, wglu, 3, 384, "wglu")
    load_cast(P, w_fourier, wf, 2, 256, "wf")
    load_cast(P, w_out, wo, 8, 1024, "wo")
    xv = xT.rearrange("(kc p) t -> p kc t", p=128); xov = xoT.rearrange("(kc p) t -> p kc t", p=128)
    ysv = ysT.rearrange("(kc p) t -> p kc t", p=128); mxv = mxT.rearrange("(kc p) t -> p kc t", p=128); nav = naT.rearrange("(kc p) t -> p kc t", p=128)
    na_ = 0; no = 0; nt = 0
    for si, t0 in enumerate(range(0, NT, N)):
        n = min(N, NT - t0); b = si % 2; j = 0 if t0 < 2048 else 1
        P.dma("sync", xs[b][:, :, 0:n], xv[:, :, t0:t0 + n], writes=[("xs", b)], chan=f"xs{b}")
        P.dma("sync", ys[b][:, :, 0:n], ysv[:, :, t0:t0 + n], writes=[("ys", b)], chan=f"ys{b}")
        P.dma("sync", mx[b][:, :, 0:n], mxv[:, :, t0:t0 + n], writes=[("mx", b)], chan=f"mx{b}")
        P.dma("sync", na[b][:, :, 0:n], nav[:, :, t0:t0 + n], writes=[("na", b)], chan=f"na{b}")
        Y = ys[b][:, :, 0:n]
        P.tt(sq[:, :, 0:n], Y, Y, ALU.mult, reads=[("ys", b)], writes=["sq"], eng="gpsimd")
        P.ts(t1[:, :, 0:n], sq[:, :, 0:n], 0.044715, 1.0, ALU.mult, ALU.add, reads=["sq"], writes=["t1"])
        P.tt(sq[:, :, 0:n], t1[:, :, 0:n], Y, ALU.mult, reads=["t1", ("ys", b)], writes=["sq"])
        P.act(sgm[:, :, 0:n], sq[:, :, 0:n], AF.Sigmoid, reads=["sq"], writes=["sgm"], scale=1.5957691216057308)
        P.tt(zf[:, :, 0:n], Y, sgm[:, :, 0:n], ALU.mult, reads=[("ys", b), "sgm"], writes=["zf"])
        P.copy(zb[:, :, 0:n], zf[:, :, 0:n], reads=["zf"], writes=["zb"], eng="gpsimd")
        for m in range(3):
            pb = na_ % 3; na_ += 1
            for kc in range(3):
                P.mm(psa[pb][:, 0:n], wglu[:, kc, m * 128:(m + 1) * 128], zb[:, kc, 0:n], kc == 0, kc == 2,
                     reads=[("wglu", kc), "zb"], writes=[("psa", pb)])
            P.act(sg2[m % 2][:, 0:n], psa[pb][:, 0:n], AF.Sigmoid, reads=[("psa", pb)], writes=[("sg2", m % 2)])
            P.tt(cat[:, m, 0:n], zf[:, m, 0:n], sg2[m % 2][:, 0:n], ALU.mult, reads=["zf", ("sg2", m % 2)], writes=[("cat", m)])
        P.copy(mxb[:, :, 0:n], mx[b][:, :, 0:n], reads=[("mx", b)], writes=["mxb"], eng="gpsimd")
        for m in range(2):
            pb = na_ % 3; na_ += 1
            for kc in range(2):
                P.mm(psa[pb][:, 0:n], wf[:, kc, m * 128:(m + 1) * 128], mxb[:, kc, 0:n], kc == 0, kc == 1,
                     reads=[("wf", kc), "mxb"], writes=[("psa", pb)])
            P.copy(cat[:, 3 + m, 0:n], psa[pb][:, 0:n], reads=[("psa", pb)], writes=[("cat", 3 + m)], eng="scalar")
        P.copy(cat[:, 5:8, 0:n], na[b][:, :, 0:n], reads=[("na", b)], writes=[("cat", 5), ("cat", 6), ("cat", 7)], eng="gpsimd")
        for m in range(8):
            pb = no % 3; no += 1
            for kc in range(8):
                P.mm(pso[pb][:, 0:n], wo[:, kc, m * 128:(m + 1) * 128], cat[:, kc, 0:n], kc == 0, kc == 7,
                     reads=[("wo", kc), ("cat", kc)], writes=[("pso", pb)])
            P.copy(o2[:, m, 0:n], pso[pb][:, 0:n], reads=[("pso", pb)], writes=[("o2", m)], eng="scalar")
        emit_rstd(P, o2, 8, n, sqb, ones, ps_ss, sd, rstd, [("o2", m) for m in range(8)])
        for m in range(8):
            P.stt(o2[:, m, 0:n], o2[:, m, 0:n], Gv[:, j * 8 + m: j * 8 + m + 1], rstd[:, 0:n], ALU.mult, ALU.mult,
                  reads=[("o2", m), "Gv", "rstd"], writes=[("o2", m)])
            P.tt(o2[:, m, 0:n], xs[b][:, m, 0:n], o2[:, m, 0:n], ALU.add, reads=[("xs", b), ("o2", m)], writes=[("o2", m)], eng="gpsimd")
        P.dma("gpsimd", xov[:, :, t0:t0 + n], o2[:, :, 0:n], reads=[("o2", m) for m in range(8)], chan="st")
    return P.finish()

EPS = 1e-6
NT = 2304
SLABS = [(0, 512), (512, 512), (1024, 512), (1536, 512), (2048, 256)]

def build_l1():
    P = Prog(); P._stage_n = 4
    xT = P.dram_in("xT", [1024, NT])
    w_in = P.dram_in("w_in", [1024, 1792])
    w_mod = P.dram_in("w_mod", [1024, 2048])
    b_modT = P.dram_in("b_modT", [128, 16])
    g_preT = P.dram_in("g_preT", [128, 8])
    cT = P.dram_in("cT", [128, 16])
    cos = P.dram_in("cos", [128, 2048]); sin = P.dram_in("sin", [128, 2048])
    perm = P.dram_in("perm", [128, 128])
    hT = P.dram_out("hT", [640, NT])
    hTb = P.dram_out("hTb", [1152, NT])

    xs = [P.sbuf(f"xs{i}", [128, 8, 512]) for i in range(2)]
    sqb = P.sbuf("sqb", [128, 8, 512], BF16)
    tt_ = [P.sbuf(f"tt{i}", [128, 512]) for i in range(2)]
    xn = [P.sbuf(f"xn{i}", [128, 8, 512], BF16) for i in range(2)]
    w_bf = P.sbuf("w_bf", [128, 8, 1792], BF16)
    ones = P.sbuf("ones", [128, 128], BF16)
    sd = P.sbuf("sd", [128, 512]); rstd = P.sbuf("rstd", [128, 512])
    ho = [P.sbuf(f"ho{i}", [128, 512]) for i in range(3)]
    hob = [P.sbuf(f"hob{i}", [128, 512]) for i in range(3)]
    r1 = [P.sbuf(f"r1{i}", [128, 512]) for i in range(2)]
    r2 = [P.sbuf(f"r2{i}", [128, 512]) for i in range(2)]
    cos_s = P.sbuf("cos_s", [128, 2048]); sin_s = P.sbuf("sin_s", [128, 2048])
    perm_s = P.sbuf("perm_s", [128, 128])
    gp_s = P.sbuf("gp_s", [128, 8]); Av = P.sbuf("Av", [128, 16])
    ps_mod = P.psum("ps_mod"); ps_ss = P.psum("ps_ss")
    psm = [P.psum(f"psm{i}") for i in range(4)]
    psr = [P.psum(f"psr{i}") for i in range(2)]

    P.dma("sync", gp_s[:], g_preT, writes=["gp_s"], chan="ld0")
    P.dma("sync", cos_s[:], cos, writes=["cos_s"], chan="ld0")
    P.dma("sync", sin_s[:], sin, writes=["sin_s"], chan="ld0")
    P.dma("sync", perm_s[:], perm, writes=["perm_s"], chan="ld0")
    P.memset(ones[:], 1.0, writes=["ones"])
    modv = emit_mod(P, w_mod, b_modT, cT, 16, ps_mod)
    for j in range(2):
        P.stt(Av[:, j * 8:(j + 1) * 8], modv[:, j * 16 + 8: j * 16 + 16], 1.0, gp_s[:], ALU.add, ALU.mult,
              reads=["mmodv", "gp_s"], writes=["Av"])
    load_cast(P, w_in, w_bf, 8, 1792, "w_bf", piece=1024)
    xv = xT.rearrange("(kc p) t -> p kc t", p=128)
    cnt = dict(nho=0, nr=0, nps=0)
    def pre(si):
        t0, n = SLABS[si]; b = si % 2; j = 0 if si < 4 else 1
        P.dma("sync", xs[b][:, :, 0:n], xv[:, :, t0:t0 + n], writes=[("xs", b)], chan=f"xs{b}")
        P.act(sqb[:, :, 0:n], xs[b][:, :, 0:n], AF.Square, reads=[("xs", b)], writes=["sqb"])
        for kc in range(8):
            P.mm(ps_ss[:, 0:n], ones[:], sqb[:, kc, 0:n], kc == 0, kc == 7, reads=["sqb", "ones"], writes=["ps_ss"])
        P.act(sd[:, 0:n], ps_ss[:, 0:n], AF.Sqrt, reads=["ps_ss"], writes=["sd"], scale=1.0 / 1024, bias=EPS)
        P.op("vector", lambda e: e.reciprocal(out=rstd[:, 0:n], in_=sd[:, 0:n]), reads=["sd"], writes=["rstd"])
        for kc in range(8):
            P.tt(tt_[kc % 2][:, 0:n], xs[b][:, kc, 0:n], rstd[:, 0:n], ALU.mult, reads=[("xs", b), "rstd"], writes=[("tt", kc % 2)])
            P.act(xn[b][:, kc, 0:n], tt_[kc % 2][:, 0:n], AF.Identity, reads=[("tt", kc % 2), "Av", "mmodv"], writes=[("xn", b, kc)],
                  scale=Av[:, j * 8 + kc: j * 8 + kc + 1], bias=modv[:, j * 16 + kc: j * 16 + kc + 1])
    def main(si):
        t0, n = SLABS[si]; b = si % 2
        for m in range(14):
            pb = cnt["nps"] % 4; cnt["nps"] += 1
            for kc in range(8):
                P.mm(psm[pb][:, 0:n], w_bf[:, kc, m * 128:(m + 1) * 128], xn[b][:, kc, 0:n], kc == 0, kc == 7,
                     reads=[("w_bf", kc), ("xn", b, kc)], writes=[("psm", pb)])
            hb = cnt["nho"] % 3; cnt["nho"] += 1
            if m < 5:
                P.copy(ho[hb][:, 0:n], psm[pb][:, 0:n], reads=[("psm", pb)], writes=[("ho", hb)], eng="scalar")
                P.dma("gpsimd", hT[m * 128:(m + 1) * 128, t0:t0 + n], ho[hb][:, 0:n], reads=[("ho", hb)], chan="st")
            elif m <= 10 and si < 4:
                rb = cnt["nr"] % 2; cnt["nr"] += 1
                P.copy(ho[hb][:, 0:n], psm[pb][:, 0:n], reads=[("psm", pb)], writes=[("ho", hb)], eng="scalar")
                P.mm(psr[rb][:, 0:n], perm_s[:], ho[hb][:, 0:n], True, True, reads=["perm_s", ("ho", hb)], writes=[("psr", rb)])
                P.tt(r1[rb][:, 0:n], ho[hb][:, 0:n], cos_s[:, t0:t0 + n], ALU.mult, reads=[("ho", hb), "cos_s"], writes=[("r1", rb)])
                P.tt(r2[rb][:, 0:n], psr[rb][:, 0:n], sin_s[:, t0:t0 + n], ALU.mult, reads=[("psr", rb), "sin_s"], writes=[("r2", rb)])
                P.tt(hob[hb][:, 0:n], r1[rb][:, 0:n], r2[rb][:, 0:n], ALU.add, reads=[("r1", rb), ("r2", rb)], writes=[("hob", hb)], eng="gpsimd")
                P.dma("gpsimd", hTb[(m - 5) * 128:(m - 4) * 128, t0:t0 + n], hob[hb][:, 0:n], reads=[("hob", hb)], chan="st")
            else:
                P.copy(hob[hb][:, 0:n], psm[pb][:, 0:n], reads=[("psm", pb)], writes=[("hob", hb)], eng="scalar")
                P.dma("gpsimd", hTb[(m - 5) * 128:(m - 4) * 128, t0:t0 + n], hob[hb][:, 0:n], reads=[("hob", hb)], chan="st")
    pre(0)
    for si in range(len(SLABS)):
        if si + 1 < len(SLABS):
            pre(si + 1)
        main(si)
    return P.finish()


def fnet_consts():
    n1 = np.arange(128); a = 2 * np.pi * np.outer(n1, n1) / 128
    cs128 = np.concatenate([np.cos(a), -np.sin(a)], 1).astype(np.float32)
    n2 = np.arange(64); a = 2 * np.pi * np.outer(n2, n2) / 64
    C64, S64 = np.cos(a), np.sin(a)
    fb1 = np.concatenate([C64, -S64], 1).astype(np.float32); fb2 = np.concatenate([S64, C64], 1).astype(np.float32)
    a = 2 * np.pi * np.outer(n2, n1) / 8192
    tw = np.concatenate([np.cos(a), np.sin(a)], 1).astype(np.float32)
    fc = (np.concatenate([C64, S64], 1) / np.sqrt(8192 * 64)).astype(np.float32)
    n = np.arange(256); a = 2 * np.pi * np.outer(n, n) / 256
    cs = np.concatenate([np.cos(a), -np.sin(a)], 1)
    cs256 = np.ascontiguousarray(cs.reshape(2, 128, 512).transpose(1, 0, 2)).astype(np.float32)
    fcc = (np.concatenate([C64, S64], 1) / np.sqrt(256 * 64)).astype(np.float32)
    return dict(cs128=cs128, fb1=fb1, fb2=fb2, tw=tw, fc=fc, cs256=cs256, fcc=fcc)

def build_fnet(with_ctx):
    P = Prog()
    z = P.dram_in("z", [128, 4096]); cs128 = P.dram_in("cs128", [128, 256])
    fb1 = P.dram_in("fb1", [64, 128]); fb2 = P.dram_in("fb2", [64, 128]); tw = P.dram_in("tw", [64, 256]); fc = P.dram_in("fc", [64, 128])
    R = P.dram_out("R", [64, 8192])
    zs = P.sbuf("zs", [128, 64, 64]); cs_s = P.sbuf("cs_s", [128, 256])
    fb1_s = P.sbuf("fb1_s", [64, 128]); fb2_s = P.sbuf("fb2_s", [64, 128]); tw_s = P.sbuf("tw_s", [64, 256]); fc_s = P.sbuf("fc_s", [64, 128])
    Ych = [P.sbuf(f"Ych{i}", [64, 8, 256]) for i in range(2)]
    ta = P.sbuf("ta", [64, 8, 128]); tb = P.sbuf("tb", [64, 8, 128]); tc = P.sbuf("tc", [64, 8, 128]); td = P.sbuf("td", [64, 8, 128])
    Yp = P.sbuf("Yp", [64, 64, 256]); X1 = P.sbuf("X1", [64, 2, 8192])
    Rs = [P.sbuf(f"Rs{i}", [64, 512]) for i in range(2)]
    psA = [P.psum(f"psA{i}") for i in range(3)]; psB = [P.psum(f"psB{i}") for i in range(3)]; psC = [P.psum(f"psC{i}") for i in range(2)]
    P.dma("sync", zs[:].rearrange("p a b -> p (a b)"), z, writes=["zs"], chan="ld0")
    for (s, d, nm) in ((cs_s, cs128, "cs_s"), (fb1_s, fb1, "fb1_s"), (fb2_s, fb2, "fb2_s"), (tw_s, tw, "tw_s"), (fc_s, fc, "fc_s")):
        P.dma("sync", s[:], d, writes=[nm], chan="ld1")
    twr = tw_s[:, 0:128].rearrange("p (o k) -> p o k", o=1).to_broadcast([64, 8, 128])
    tws = tw_s[:, 128:256].rearrange("p (o k) -> p o k", o=1).to_broadcast([64, 8, 128])
    na_ = 0
    for ch in range(8):
        cb = ch % 2
        for pair in range(4):
            pb = na_ % 3; na_ += 1
            for h in range(2):
                c = ch * 8 + pair * 2 + h
                P.mm(psA[pb][0:64, h * 256:(h + 1) * 256], zs[:, :, c], cs_s[:], True, True, reads=["zs", "cs_s"], writes=[("psA", pb)])
            P.copy(Ych[cb][:, pair * 2:pair * 2 + 2, :].rearrange("p a b -> p (a b)"), psA[pb][0:64, :], reads=[("psA", pb)],
                   writes=[("Ych", cb)], eng="scalar")
        Yr = Ych[cb][:, :, 0:128]; Yi = Ych[cb][:, :, 128:256]; c0 = ch * 8
        P.tt(ta[:], Yr, twr, ALU.mult, reads=[("Ych", cb), "tw_s"], writes=["ta"])
        P.tt(tb[:], Yi, tws, ALU.mult, reads=[("Ych", cb), "tw_s"], writes=["tb"], eng="gpsimd")
        P.tt(Yp[:, c0:c0 + 8, 0:128], ta[:], tb[:], ALU.add, reads=["ta", "tb"], writes=[("Yp", ch)])
        P.tt(tc[:], Yi, twr, ALU.mult, reads=[("Ych", cb), "tw_s"], writes=["tc"], eng="gpsimd")
        P.tt(td[:], Yr, tws, ALU.mult, reads=[("Ych", cb), "tw_s"], writes=["td"])
        P.tt(Yp[:, c0:c0 + 8, 128:256], tc[:], td[:], ALU.subtract, reads=["tc", "td"], writes=[("Yp", ch)], eng="gpsimd")
    allYp = [("Yp", ch) for ch in range(8)]
    X1v = X1[:].rearrange("p c (k2 k1) -> p c k2 k1", k1=128)
    for g in range(32):
        pb = g % 3
        for q in range(4):
            k1 = 4 * g + q
            P.mm(psB[pb][0:64, q * 128:(q + 1) * 128], Yp[:, :, k1], fb1_s[:], True, False, reads=allYp + ["fb1_s"], writes=[("psB", pb)])
            P.mm(psB[pb][0:64, q * 128:(q + 1) * 128], Yp[:, :, 128 + k1], fb2_s[:], False, True, reads=allYp + ["fb2_s"], writes=[("psB", pb)])
        pv = psB[pb][0:64, :].rearrange("p (q c k) -> p c k q", q=4, c=2)
        for comp in range(2):
            P.copy(X1v[:, comp, :, 4 * g:4 * g + 4], pv[:, comp, :, :], reads=[("psB", pb)], writes=[("X1", g)],
                   eng=("scalar" if comp == 0 else "vector"))
    allX1 = [("X1", g) for g in range(32)]
    for blk in range(16):
        pb = blk % 2
        P.mm(psC[pb][0:64, :], fc_s[:, 0:64], X1[:, 0, blk * 512:(blk + 1) * 512], True, False, reads=allX1 + ["fc_s"], writes=[("psC", pb)])
        P.mm(psC[pb][0:64, :], fc_s[:, 64:128], X1[:, 1, blk * 512:(blk + 1) * 512], False, True, reads=allX1 + ["fc_s"], writes=[("psC", pb)])
        P.copy(Rs[pb][:], psC[pb][0:64, :], reads=[("psC", pb)], writes=[("Rs", pb)], eng="scalar")
        P.dma("gpsimd", R[:, blk * 512:(blk + 1) * 512], Rs[pb][:], reads=[("Rs", pb)], chan="st")
    if with_ctx:
        zc = P.dram_in("zc", [128, 2, 64]); cs256 = P.dram_in("cs256", [128, 2, 512]); fcc = P.dram_in("fcc", [64, 128])
        Rc = P.dram_out("Rc", [64, 256])
        zc_s = P.sbuf("zc_s", [128, 2, 64]); c2_s = P.sbuf("c2_s", [128, 2, 512]); fcc_s = P.sbuf("fcc_s", [64, 128])
        Pc = P.sbuf("Pc", [64, 512]); Rc_s = P.sbuf("Rc_s", [64, 256])
        P.dma("sync", zc_s[:], zc, writes=["zc_s"], chan="ld2"); P.dma("sync", c2_s[:], cs256, writes=["c2_s"], chan="ld2")
        P.dma("sync", fcc_s[:], fcc, writes=["fcc_s"], chan="ld2")
        for t in range(2):
            P.mm(psA[0][0:64, :], zc_s[:, t, :], c2_s[:, t, :], t == 0, t == 1, reads=["zc_s", "c2_s"], writes=[("psA", 0)])
        P.copy(Pc[:], psA[0][0:64, :], reads=[("psA", 0)], writes=["Pc"])
        P.mm(psA[1][0:64, 0:256], fcc_s[:, 0:64], Pc[:, 0:256], True, False, reads=["Pc", "fcc_s"], writes=[("psA", 1)])
        P.mm(psA[1][0:64, 0:256], fcc_s[:, 64:128], Pc[:, 256:512], False, True, reads=["Pc", "fcc_s"], writes=[("psA", 1)])
        P.copy(Rc_s[:], psA[1][0:64, 0:256], reads=[("psA", 1)], writes=["Rc_s"])
        P.dma("gpsimd", Rc, Rc_s[:], reads=["Rc_s"], chan="st")
    return P.finish()

NEG = -30000.0

def build_na(with_ctx):
    P = Prog()
    qT = P.dram_in("qT", [384, 2048]); kwT = P.dram_in("kwT", [16, 384, 576]); vw = P.dram_in("vw", [16, 128, 5, 384])
    kcT = P.dram_in("kcT", [384, 256]); vc = P.dram_in("vc", [128, 2, 384])
    tbraw = P.dram_in("tbraw", [5, 128, 6, 576]); mask = P.dram_in("mask", [5, 128, 576]); ident = P.dram_in("ident", [128, 128])
    Y = P.dram_out("Y", [2048, 384])
    qb = P.sbuf("qb", [128, 3, 2048], BF16); kcb = P.sbuf("kcb", [128, 3, 256], BF16); vcb = P.sbuf("vcb", [128, 2, 384], BF16)
    TB = P.sbuf("TB", [128, 5, 6, 576]); mk = P.sbuf("mk", [128, 5, 576])
    idb = P.sbuf("idb", [128, 128], BF16)
    kb = [P.sbuf(f"kb{i}", [128, 3, 576], BF16) for i in range(2)]; vb = [P.sbuf(f"vb{i}", [128, 5, 384], BF16) for i in range(2)]
    S = [P.sbuf(f"S{i}", [128, 832]) for i in range(4)]; Pb = [P.sbuf(f"Pb{i}", [128, 832], BF16) for i in range(4)]
    PT = [P.sbuf(f"PT{i}", [128, 896], BF16) for i in range(4)]
    Osb = [P.sbuf(f"Osb{i}", [128, 384]) for i in range(2)]
    mx = [P.sbuf(f"mx{i}", [128, 1]) for i in range(8)]; ssum = [P.sbuf(f"ssum{i}", [128, 1]) for i in range(8)]
    rinv = [P.sbuf(f"rinv{i}", [128, 1]) for i in range(8)]
    psA = [P.psum(f"psA{i}") for i in range(2)]; psB = [P.psum(f"psB{i}") for i in range(2)]
    psT = [P.psum(f"psT{i}", [128, 1024], BF16) for i in range(2)]; psO = [P.psum(f"psO{i}") for i in range(2)]
    si = [0]
    def load_cast(dst, src_ap, n, dres):
        P.dma("gpsimd", dst, src_ap, writes=[dres], chan="ldc", max_dma_last_dim=4096)
    qv = qT.rearrange("(c p) n -> p c n", p=128)
    for c in range(3):
        load_cast(qb[:, c, :], qv[:, c, :], 2048, "qb")
    kcv = kcT.rearrange("(c p) n -> p c n", p=128)
    for c in range(3):
        load_cast(kcb[:, c, :], kcv[:, c, :], 256, "kcb")
    load_cast(vcb[:].rearrange("p a b -> p (a b)"), vc.rearrange("p a b -> p (a b)"), 768, "vcb")
    load_cast(idb[:], ident, 128, "idb")
    for ty in range(5):
        P.dma("sync", TB[:, ty, :, :], tbraw[ty], writes=[("TB", ty)], chan="ld0")
    P.dma("sync", mk[:], mask.rearrange("t p n -> p t n"), writes=["mk"], chan="ld0")
    for ty in range(5):
        mb = mk[:, ty, :].rearrange("p (o n) -> p o n", o=1).to_broadcast([128, 6, 576])
        P.tt(TB[:, ty, :, :], TB[:, ty, :, :], mb, ALU.add, reads=[("TB", ty), "mk"], writes=[("TB", ty)])
    tiles = [("main", t) for t in range(16)]
    if with_ctx:
        qcT = P.dram_in("qcT", [384, 256]); Yc = P.dram_out("Yc", [256, 384])
        qcb = P.sbuf("qcb", [128, 3, 256], BF16)
        qcv = qcT.rearrange("(c p) n -> p c n", p=128)
        for c in range(3):
            load_cast(qcb[:, c, :], qcv[:, c, :], 256, "qcb")
        tiles += [("ctx", 0), ("ctx", 1)]
    units = [(ti, kind, t, h) for ti, (kind, t) in enumerate(tiles) for h in range(6)]
    loaded = set()
    def tile_load(ti, kind, t):
        if ti in loaded or kind != "main":
            return
        loaded.add(ti)
        wb = ti % 2
        P.dma("gpsimd", kb[wb][:], kwT[t].rearrange("(c p) n -> p c n", p=128), writes=[("kb", wb)], chan=f"kb{wb}", max_dma_last_dim=4096)
        P.dma("gpsimd", vb[wb][:], vw[t], writes=[("vb", wb)], chan=f"vb{wb}", max_dma_last_dim=4096)
    def info(u):
        ti, kind, t, h = units[u]
        return ti, kind, t, h, u % 2, u % 4, ti % 2, h // 2, (h % 2) * 64, u % 8
    def stA(u):
        ti, kind, t, h, b, b3, wb, c, p0, b8 = info(u)
        tile_load(ti, kind, t)
        if kind == "main":
            qs = qb[p0:p0 + 64, c, t * 128:(t + 1) * 128]; qres = "qb"
        else:
            qs = qcb[p0:p0 + 64, c, t * 128:(t + 1) * 128]; qres = "qcb"
        P.mm(psB[b][:, 64:320], qs, kcb[p0:p0 + 64, c, :], True, True, reads=[qres, "kcb"], writes=[("psB", b)])
        if kind == "main":
            P.mm(psA[b][:, 0:512], qs, kb[wb][p0:p0 + 64, c, 0:512], True, True, reads=[qres, ("kb", wb)], writes=[("psA", b)])
            P.mm(psB[b][:, 0:64], qs, kb[wb][p0:p0 + 64, c, 512:576], True, True, reads=[qres, ("kb", wb)], writes=[("psB", b)])
        P.act(S[b3][:, 0:256], psB[b][:, 64:320], AF.Copy, reads=[("psB", b)], writes=[("Sc", b3)], scale=0.125)
    def stB(u):
        ti, kind, t, h, b, b3, wb, c, p0, b8 = info(u)
        W = 832 if kind == "main" else 256
        if kind == "main":
            ty = {0: 0, 1: 1, 14: 3, 15: 4}.get(t, 2)
            P.stt(S[b3][:, 256:768], psA[b][:, 0:512], 0.125, TB[:, ty, h, 0:512], ALU.mult, ALU.add,
                  reads=[("psA", b), ("TB", ty)], writes=[("Sw", b3)])
            P.stt(S[b3][:, 768:832], psB[b][:, 0:64], 0.125, TB[:, ty, h, 512:576], ALU.mult, ALU.add,
                  reads=[("psB", b), ("TB", ty)], writes=[("Sw2", b3)])
        P.op("vector", lambda e: e.tensor_reduce(out=mx[b8][:], in_=S[b3][:, 0:W], axis=AX.X, op=ALU.max, negate=True),
             reads=[("Sc", b3), ("Sw", b3), ("Sw2", b3)], writes=[("mx", b8)])
        P.act(Pb[b3][:, 0:W], S[b3][:, 0:W], AF.Exp, reads=[("Sc", b3), ("Sw", b3), ("Sw2", b3), ("mx", b8)], writes=[("Pb", b3), ("ssum", b8)],
              bias=mx[b8][:], scale=1.0, accum_out=ssum[b8][:])
    def stC(u):
        ti, kind, t, h, b, b3, wb, c, p0, b8 = info(u)
        nblk = 7 if kind == "main" else 2
        for kbk in range(nblk):
            kw = 64 if kbk == 6 else 128
            P.op("tensor", lambda e, kbk=kbk, kw=kw: e.transpose(psT[b][0:kw, kbk * 128:(kbk + 1) * 128], Pb[b3][:, kbk * 128:kbk * 128 + kw], idb[:]),
                 reads=[("Pb", b3), "idb"], writes=[("psT", b)], nosync_same=True)
        P.copy(PT[b3][:, 0:nblk * 128], psT[b][:, 0:nblk * 128], reads=[("psT", b)], writes=[("PT", b3)], eng="scalar")
    def stD(u):
        ti, kind, t, h, b, b3, wb, c, p0, b8 = info(u)
        ob = ti % 2
        nblk = 7 if kind == "main" else 2
        for kbk in range(nblk):
            kw = 64 if kbk == 6 else 128
            if kbk < 2:
                rhs = vcb[:, kbk, h * 64:(h + 1) * 64]; rres = "vcb"
            else:
                rhs = vb[wb][0:kw, kbk - 2, h * 64:(h + 1) * 64]; rres = ("vb", wb)
            P.mm(psO[ob][:, h * 64:(h + 1) * 64], PT[b3][0:kw, kbk * 128:(kbk + 1) * 128], rhs, kbk == 0, kbk == nblk - 1,
                 reads=[("PT", b3), rres], writes=[("psO", ob)])
        P.op("vector", lambda e: e.reciprocal(out=rinv[b8][:], in_=ssum[b8][:]), reads=[("ssum", b8)], writes=[("rinv", b8)])
        P.ts(Osb[ob][:, h * 64:(h + 1) * 64], psO[ob][:, h * 64:(h + 1) * 64], rinv[b8][:], None, ALU.mult,
             reads=[("psO", ob), ("rinv", b8)], writes=[("Osb", ob)])
        if h == 5:
            dst = Y[t * 128:(t + 1) * 128, :] if kind == "main" else Yc[t * 128:(t + 1) * 128, :]
            P.dma("gpsimd", dst, Osb[ob][:], reads=[("Osb", ob)], chan="st")
    NU = len(units)
    LC, LD = 3, 5
    for step in range(NU + LD):
        if step < NU: stA(step)
        if 0 <= step - 1 < NU: stB(step - 1)
        if 0 <= step - LC < NU: stC(step - LC)
        if 0 <= step - LD < NU: stD(step - LD)
    return P.finish()

def na_tile_geometry(r0):
    R = 128
    rs0 = int(np.clip(r0 - 4, 0, R - 8)); rs1 = int(np.clip(r0 + 1 - 4, 0, R - 8))
    return rs0, rs1

def na_tables(rpb, q):
    cols = np.arange(64); cs = np.clip(cols - 8, 0, 48)
    kc = np.arange(64)
    inwin = (kc[None, :] >= cs[:, None]) & (kc[None, :] < cs[:, None] + 16)
    dc = np.clip(kc[None, :] - cols[:, None] + 15, 0, 30)
    types = [32 * q, 32 * q + 2, 32 * q + 16, 32 * q + 28, 32 * q + 30]
    tbraw = np.zeros((5, 128, 6, 9, 64), np.float32); mask = np.zeros((5, 128, 9, 64), np.float32)
    for ti, r0 in enumerate(types):
        rs0, rs1 = na_tile_geometry(r0)
        for half, (r, rs) in enumerate(((r0, rs0), (r0 + 1, rs1))):
            for slot in range(9):
                krow = rs0 + slot
                valid = (krow >= rs) and (krow < rs + 8)
                dr = int(np.clip(krow - r + 7, 0, 14))
                g = rpb[:, dr][:, dc]
                tbraw[ti, half * 64:(half + 1) * 64, :, slot, :] = g.transpose(1, 0, 2)
                m = np.where(inwin & valid, 0.0, NEG).astype(np.float32)
                mask[ti, half * 64:(half + 1) * 64, slot, :] = m
    return tbraw.reshape(5, 128, 6, 576), mask.reshape(5, 128, 576)

def na_windows(k_b, v_b, q):
    kp = np.concatenate([k_b, np.zeros((64 * 16, 384), k_b.dtype)], 0); vp = np.concatenate([v_b, np.zeros((64 * 16, 384), v_b.dtype)], 0)
    kwT = np.zeros((16, 384, 576), k_b.dtype); vw = np.zeros((16, 128, 5, 384), v_b.dtype)
    for t in range(16):
        r0 = 32 * q + 2 * t
        rs0, _ = na_tile_geometry(r0)
        kwT[t] = kp[rs0 * 64:(rs0 + 9) * 64].T
        for j in range(4):
            vw[t, :, j, :] = vp[(rs0 + 2 * j) * 64:(rs0 + 2 * j + 2) * 64]
        vw[t, 0:64, 4, :] = vp[(rs0 + 8) * 64:(rs0 + 9) * 64]
    return kwT, vw

NCH = 1056
PI = math.pi

class A:
    def __init__(self, P): self.P = P
    @staticmethod
    def nm(*aps): return [a.tensor.name for a in aps if hasattr(a, "tensor")]
    def tt(self, o, a, b, op, eng="vector"): self.P.tt(o, a, b, op, reads=self.nm(a, b), writes=self.nm(o), eng=eng)
    def ts(self, o, a, s1, op0, s2=None, op1=None, eng="vector"):
        self.P.ts(o, a, s1, s2, op0, op1, reads=self.nm(a, s1, s2), writes=self.nm(o), eng=eng)
    def stt(self, o, a, s, b, op0, op1): self.P.stt(o, a, s, b, op0, op1, reads=self.nm(a, s, b), writes=self.nm(o))
    def act(self, o, a, f, **kw): self.P.act(o, a, f, reads=self.nm(a, *[v for v in kw.values()]), writes=self.nm(o), **kw)
    def copy(self, o, a, eng="vector"): self.P.copy(o, a, reads=self.nm(a), writes=self.nm(o), eng=eng)
    def memset(self, o, v, eng="vector"): self.P.memset(o, v, writes=self.nm(o), eng=eng)
    def mm(self, o, l, r, st, sp): self.P.mm(o, l, r, st, sp, reads=self.nm(l, r), writes=self.nm(o))
    def dma_in(self, o, src, chan): self.P.dma("sync", o, src, writes=self.nm(o), chan=chan)
    def dma_out(self, dst, a, chan="st"): self.P.dma("gpsimd", dst, a, reads=self.nm(a), chan=chan)
    def scan(self, o, d0, d1, init):
        self.P.op("vector", lambda e: e.tensor_tensor_scan(out=o, data0=d0, data1=d1, initial=init, op0=ALU.mult, op1=ALU.add),
                  reads=self.nm(d0, d1, init), writes=self.nm(o))
    def recip(self, o, a): self.P.op("vector", lambda e: e.reciprocal(out=o, in_=a), reads=self.nm(a), writes=self.nm(o))
    def transpose(self, o, a, ident):
        self.P.op("tensor", lambda e: e.transpose(o, a, ident), reads=self.nm(a, ident), writes=self.nm(o), nosync_same=True)
    def cmul_s(self, o_re, o_im, a_re, a_im, s_re, s_im, s_imn):
        self.ts(o_re, a_re, s_re, ALU.mult)
        self.stt(o_re, a_im, s_imn, o_re, ALU.mult, ALU.add)
        self.ts(o_im, a_re, s_im, ALU.mult)
        self.stt(o_im, a_im, s_re, o_im, ALU.mult, ALU.add)

def build_ssm():
    P = Prog(); a = A(P)
    d_in = {}
    for nm_, shp in (("are", [128, 6]), ("aim", [128, 6]), ("ldt", [128, 6]), ("Bre", [128, 96]), ("Bim", [128, 96]),
                     ("Cre", [128, 96]), ("Cim", [128, 96]), ("maskF", [128, 128]), ("maskB", [128, 128]), ("sgn", [128, 1]),
                     ("ident", [128, 128])):
        d_in[nm_] = P.dram_in(nm_, shp)
    Ddiag = P.dram_in("Ddiag", [6, 128, 128]); U = P.dram_in("U", [6, 128, NCH]); Yg = P.dram_out("Yg", [6, 128, NCH])
    s = {}
    for nm_, ap in d_in.items():
        shp = list(ap.shape)
        s[nm_] = P.sbuf("s_" + nm_, shp)
        a.dma_in(s[nm_][:], ap, "ld0")
    def T(name, shape): return P.sbuf(name, shape)
    dt = T("dt", [128, 6]); x = T("x", [128, 6]); th = T("th", [128, 6]); er = T("er", [128, 6]); m = T("m", [128, 6])
    y2 = T("y2", [128, 6]); sn = T("sn", [128, 6]); cs = T("cs", [128, 6]); lbr = T("lbr", [128, 6]); lbi = T("lbi", [128, 6])
    n2 = T("n2", [128, 6]); t1 = T("t1", [128, 6]); t2 = T("t2", [128, 6]); am1 = T("am1", [128, 6])
    qr = T("qr", [128, 6]); qi = T("qi", [128, 6]); qin = T("qin", [128, 6])
    a.act(dt[:], s["ldt"][:], AF.Exp)
    a.tt(x[:], s["are"][:], dt[:], ALU.mult); a.tt(th[:], s["aim"][:], dt[:], ALU.mult)
    a.act(er[:], x[:], AF.Exp)
    for _ in range(4):
        a.ts(m[:], th[:], PI, ALU.is_gt)
        a.stt(th[:], m[:], -2 * PI, th[:], ALU.mult, ALU.add)
    a.ts(y2[:], th[:], PI / 2, ALU.add)
    a.ts(m[:], y2[:], PI, ALU.is_gt)
    a.stt(y2[:], m[:], -2 * PI, y2[:], ALU.mult, ALU.add)
    a.act(sn[:], th[:], AF.Sin); a.act(cs[:], y2[:], AF.Sin)
    a.tt(lbr[:], er[:], cs[:], ALU.mult); a.tt(lbi[:], er[:], sn[:], ALU.mult)
    a.tt(n2[:], s["are"][:], s["are"][:], ALU.mult); a.tt(t1[:], s["aim"][:], s["aim"][:], ALU.mult); a.tt(n2[:], n2[:], t1[:], ALU.add)
    a.recip(n2[:], n2[:])
    a.ts(am1[:], lbr[:], -1.0, ALU.add)
    a.tt(t1[:], am1[:], s["are"][:], ALU.mult); a.tt(t2[:], lbi[:], s["aim"][:], ALU.mult); a.tt(t1[:], t1[:], t2[:], ALU.add)
    a.tt(qr[:], t1[:], n2[:], ALU.mult)
    a.tt(t1[:], lbi[:], s["are"][:], ALU.mult); a.tt(t2[:], am1[:], s["aim"][:], ALU.mult); a.tt(t1[:], t1[:], t2[:], ALU.subtract)
    a.tt(qi[:], t1[:], n2[:], ALU.mult)
    a.ts(qin[:], qi[:], -1.0, ALU.mult)
    Lr = T("Lr", [128, 6, 9]); Li = T("Li", [128, 6, 9]); Vr = T("Vr", [128, 6, 8]); Vi = T("Vi", [128, 6, 8])
    Rr = T("Rr", [128, 6, 9]); Ri = T("Ri", [128, 6, 9])
    e2 = T("e2", [128, 6]); ivr = T("ivr", [128, 6]); ivi = T("ivi", [128, 6])
    a.memset(Lr[:, :, 0], 1.0); a.memset(Li[:, :, 0], 0.0); a.memset(Vr[:, :, 0], 1.0); a.memset(Vi[:, :, 0], 0.0)
    a.act(e2[:], x[:], AF.Exp, scale=-2.0)
    a.tt(ivr[:], lbr[:], e2[:], ALU.mult); a.tt(ivi[:], lbi[:], e2[:], ALU.mult); a.ts(ivi[:], ivi[:], -1.0, ALU.mult)
    def cmul_t(o_r, o_i, p_r, p_i, q_r, q_i):
        a.tt(t1[:], p_r, q_r, ALU.mult); a.tt(t2[:], p_i, q_i, ALU.mult); a.tt(o_r, t1[:], t2[:], ALU.subtract)
        a.tt(t1[:], p_r, q_i, ALU.mult); a.tt(t2[:], p_i, q_r, ALU.mult); a.tt(o_i, t1[:], t2[:], ALU.add)
    for k in range(8):
        cmul_t(Lr[:, :, k + 1], Li[:, :, k + 1], Lr[:, :, k], Li[:, :, k], lbr[:], lbi[:])
    for k in range(7):
        cmul_t(Vr[:, :, k + 1], Vi[:, :, k + 1], Vr[:, :, k], Vi[:, :, k], ivr[:], ivi[:])
    for k in range(9):
        a.copy(Rr[:, :, k], Lr[:, :, 8 - k], eng="gpsimd"); a.copy(Ri[:, :, k], Li[:, :, 8 - k], eng="gpsimd")
    tabs = {}
    for nm_, (lo_r, lo_i, hi_r, hi_i) in dict(
            XL=(Vr[0:64, :, 0:8], Vi[0:64, :, 0:8], Lr[64:128, :, 0:8], Li[64:128, :, 0:8]),
            YL=(Lr[0:64, :, 0:8], Li[0:64, :, 0:8], Vr[64:128, :, 0:8], Vi[64:128, :, 0:8]),
            SL=(Rr[0:64, :, 1:9], Ri[0:64, :, 1:9], Lr[64:128, :, 0:8], Li[64:128, :, 0:8]),
            OL=(Lr[0:64, :, 1:9], Li[0:64, :, 1:9], Rr[64:128, :, 0:8], Ri[64:128, :, 0:8])).items():
        tr = T(nm_ + "r", [128, 6, 8]); ti = T(nm_ + "i", [128, 6, 8]); tn = T(nm_ + "n", [128, 6, 8])
        a.copy(tr[0:64], lo_r); a.copy(ti[0:64], lo_i); a.copy(tr[64:128], hi_r); a.copy(ti[64:128], hi_i)
        a.ts(tn[:], ti[:], -1.0, ALU.mult)
        tabs[nm_] = (tr, ti, tn)
    rho8 = T("rho8", [128, 6]); c8 = T("c8", [128, 6]); s8 = T("s8", [128, 6]); e8 = T("e8", [128, 6])
    a.act(rho8[:], x[:], AF.Exp, scale=8.0); a.act(e8[:], x[:], AF.Exp, scale=-8.0)
    a.tt(c8[:], Lr[:, :, 8], e8[:], ALU.mult); a.tt(s8[:], Li[:, :, 8], e8[:], ALU.mult)
    a.ts(s8[:], s8[:], s["sgn"][:, 0:1], ALU.mult)
    onesT = T("onesT", [128, NCH]); a.memset(onesT[:], 1.0, eng="gpsimd")
    Bbr = T("Bbr", [128, 16]); Bbi = T("Bbi", [128, 16])
    Xr = T("Xr", [128, 8, 16]); Xi = T("Xi", [128, 8, 16]); Yr = T("Yr", [128, 8, 16]); Yin = T("Yin", [128, 8, 16])
    Wtr = T("Wtr", [128, 8, 16]); Wti = T("Wti", [128, 8, 16]); Wor = T("Wor", [128, 8, 16]); Woin = T("Woin", [128, 8, 16])
    Wsr = T("Wsr", [128, 128]); Wsi = T("Wsi", [128, 128]); Msb = T("Msb", [128, 128]); Mtmp = T("Mtmp", [128, 128]); Dd = T("Dd", [128, 128])
    Us = [T(f"Us{i}", [128, NCH]) for i in range(2)]
    Sre = T("Sre", [128, NCH]); Sim = T("Sim", [128, NCH]); Spr = T("Spr", [128, NCH]); Spi = T("Spi", [128, NCH])
    Gre = T("Gre", [128, NCH]); Gim = T("Gim", [128, NCH]); Hor = T("Hor", [128, NCH]); Hoi = T("Hoi", [128, NCH])
    Hir = T("Hir", [128, NCH]); Hii = T("Hii", [128, NCH])
    Tr = T("Tr", [128, NCH + 1]); Ti = T("Ti", [128, NCH + 1]); rhoT = T("rhoT", [128, NCH])
    w1 = T("w1", [128, NCH]); w2 = T("w2", [128, NCH])
    mult = T("mult", [128, 11, 3]); ini = T("ini", [128, 4]); Ysb = T("Ysb", [128, NCH])
    ps = [P.psum(f"ps{i}") for i in range(8)]
    BLK = [(0, 512), (512, 512), (1024, NCH - 1024)]
    for gi in range(6):
        ub = gi % 2
        a.dma_in(Us[ub][:], U[gi], f"u{ub}")
        a.dma_in(Dd[:], Ddiag[gi], "dd")
        g16 = slice(gi * 16, gi * 16 + 16)
        a.cmul_s(Bbr[:], Bbi[:], s["Bre"][:, g16], s["Bim"][:, g16], qr[:, gi:gi + 1], qi[:, gi:gi + 1], qin[:, gi:gi + 1])
        XL, YL, SL, OL = tabs["XL"], tabs["YL"], tabs["SL"], tabs["OL"]
        for k in range(8):
            sc = lambda tb: (tb[0][:, gi, k:k + 1], tb[1][:, gi, k:k + 1], tb[2][:, gi, k:k + 1])
            a.cmul_s(Xr[:, k, :], Xi[:, k, :], Bbr[:], Bbi[:], *sc(XL))
            a.cmul_s(Wtr[:, k, :], Wti[:, k, :], Bbr[:], Bbi[:], *sc(SL))
            a.cmul_s(Yr[:, k, :], Yin[:, k, :], s["Cre"][:, g16], s["Cim"][:, g16], *sc(YL))
            a.cmul_s(Wor[:, k, :], Woin[:, k, :], s["Cre"][:, g16], s["Cim"][:, g16], *sc(OL))
        a.ts(Yin[:], Yin[:], -1.0, ALU.mult); a.ts(Woin[:], Woin[:], -1.0, ALU.mult)
        f2 = lambda t_: t_[:].rearrange("p a b -> p (a b)")
        for half, pb in ((0, 6), (1, 7)):
            rows = slice(half * 64, half * 64 + 64)
            a.mm(ps[pb][:, 0:128], f2(Xr)[rows], f2(Yr)[rows], True, False)
            a.mm(ps[pb][:, 0:128], f2(Xi)[rows], f2(Yin)[rows], False, True)
        a.tt(Msb[:], ps[6][:, 0:128], s["maskF"][:], ALU.mult)
        a.tt(Mtmp[:], ps[7][:, 0:128], s["maskB"][:], ALU.mult)
        a.tt(Msb[:], Msb[:], Mtmp[:], ALU.add, eng="gpsimd"); a.tt(Msb[:], Msb[:], Dd[:], ALU.add, eng="gpsimd")
        a.transpose(ps[6][:, 128:256], f2(Wtr), s["ident"][:]); a.transpose(ps[7][:, 128:256], f2(Wti), s["ident"][:])
        a.copy(Wsr[:], ps[6][:, 128:256], eng="scalar"); a.copy(Wsi[:], ps[7][:, 128:256], eng="scalar")
        for bi, (c0, cn) in enumerate(BLK):
            a.mm(ps[bi][:, 0:cn], Wsr[:], Us[ub][:, c0:c0 + cn], True, True)
            a.mm(ps[3 + bi][:, 0:cn], Wsi[:], Us[ub][:, c0:c0 + cn], True, True)
            a.copy(Sre[:, c0:c0 + cn], ps[bi][:, 0:cn], eng="scalar"); a.copy(Sim[:, c0:c0 + cn], ps[3 + bi][:, 0:cn], eng="scalar")
        a.memset(Tr[:, 0:1], 1.0); a.memset(Ti[:, 0:1], 0.0)
        a.copy(mult[:, 0, 0:1], c8[:, gi:gi + 1]); a.copy(mult[:, 0, 1:2], s8[:, gi:gi + 1])
        a.ts(mult[:, 0, 2:3], mult[:, 0, 1:2], -1.0, ALU.mult)
        for k in range(1, 11):
            a.tt(ini[:, 0:1], mult[:, k - 1, 0:1], mult[:, k - 1, 0:1], ALU.mult); a.tt(ini[:, 1:2], mult[:, k - 1, 1:2], mult[:, k - 1, 1:2], ALU.mult)
            a.tt(mult[:, k, 0:1], ini[:, 0:1], ini[:, 1:2], ALU.subtract)
            a.tt(ini[:, 0:1], mult[:, k - 1, 0:1], mult[:, k - 1, 1:2], ALU.mult)
            a.ts(mult[:, k, 1:2], ini[:, 0:1], 2.0, ALU.mult); a.ts(mult[:, k, 2:3], ini[:, 0:1], -2.0, ALU.mult)
        for k in range(11):
            n = 1 << k
            cnt = min(n, NCH + 1 - n)
            a.cmul_s(Tr[:, n:n + cnt], Ti[:, n:n + cnt], Tr[:, 0:cnt], Ti[:, 0:cnt], mult[:, k, 0:1], mult[:, k, 1:2], mult[:, k, 2:3])
        a.ts(rhoT[:], onesT[:], rho8[:, gi:gi + 1], ALU.mult, eng="gpsimd")
        a.tt(w1[:], Sre[:], Tr[:, 0:NCH], ALU.mult); a.tt(w2[:], Sim[:], Ti[:, 0:NCH], ALU.mult, eng="gpsimd")
        a.tt(Spr[:], w1[:], w2[:], ALU.subtract)
        a.tt(w1[:], Sre[:], Ti[:, 0:NCH], ALU.mult); a.tt(w2[:], Sim[:], Tr[:, 0:NCH], ALU.mult, eng="gpsimd")
        a.tt(Spi[:], w1[:], w2[:], ALU.add)
        for (Gx, Sx) in ((Gre, Spr), (Gim, Spi)):
            a.scan(Gx[0:64, :], rhoT[0:64, :], Sx[0:64, :], 0.0)
            a.scan(Gx[64:128, 0:32][:, ::-1], rhoT[64:128, 0:32], Sx[64:128, 0:32][:, ::-1], 0.0)
        lo = slice(64, 128)
        a.tt(ini[lo, 0:1], Gre[lo, 0:1], Tr[lo, NCH:NCH + 1], ALU.mult); a.tt(ini[lo, 1:2], Gim[lo, 0:1], Ti[lo, NCH:NCH + 1], ALU.mult)
        a.tt(ini[lo, 2:3], ini[lo, 0:1], ini[lo, 1:2], ALU.subtract)
        a.tt(ini[lo, 0:1], Gre[lo, 0:1], Ti[lo, NCH:NCH + 1], ALU.mult); a.tt(ini[lo, 1:2], Gim[lo, 0:1], Tr[lo, NCH:NCH + 1], ALU.mult)
        a.tt(ini[lo, 3:4], ini[lo, 0:1], ini[lo, 1:2], ALU.add)
        a.scan(Gre[lo, 32:NCH][:, ::-1], rhoT[lo, 32:NCH], Spr[lo, 32:NCH][:, ::-1], ini[lo, 2:3])
        a.scan(Gim[lo, 32:NCH][:, ::-1], rhoT[lo, 32:NCH], Spi[lo, 32:NCH][:, ::-1], ini[lo, 3:4])
        a.tt(w1[:], Gre[:], Tr[:, 0:NCH], ALU.mult); a.tt(w2[:], Gim[:], Ti[:, 0:NCH], ALU.mult, eng="gpsimd")
        a.tt(Hor[:], w1[:], w2[:], ALU.add)
        a.tt(w1[:], Gim[:], Tr[:, 0:NCH], ALU.mult); a.tt(w2[:], Gre[:], Ti[:, 0:NCH], ALU.mult, eng="gpsimd")
        a.tt(Hoi[:], w1[:], w2[:], ALU.subtract)
        for (Hi_, Ho_, Gx) in ((Hir, Hor, Gre), (Hii, Hoi, Gim)):
            a.copy(Hi_[0:64, 1:NCH], Ho_[0:64, 0:NCH - 1], eng="scalar"); a.memset(Hi_[0:64, 0:1], 0.0)
            a.copy(Hi_[lo, 0:NCH - 1], Ho_[lo, 1:NCH], eng="scalar"); a.memset(Hi_[lo, 31:32], 0.0)
            a.copy(Hi_[lo, NCH - 1:NCH], Gx[lo, 0:1])
        for bi, (c0, cn) in enumerate(BLK):
            a.mm(ps[bi][:, 0:cn], Msb[:], Us[ub][:, c0:c0 + cn], True, False)
            a.mm(ps[bi][:, 0:cn], f2(Wor), Hir[:, c0:c0 + cn], False, False)
            a.mm(ps[bi][:, 0:cn], f2(Woin), Hii[:, c0:c0 + cn], False, True)
            a.copy(Ysb[:, c0:c0 + cn], ps[bi][:, 0:cn], eng="scalar")
        a.dma_out(Yg[gi], Ysb[:])
    return P.finish()

def ssm_inputs(inp, l, j4, u_b, uc_b):
    gs = np.arange(6 * j4, 6 * j4 + 6)
    def rows(arr):
        return np.ascontiguousarray(arr[:, gs, :].transpose(0, 2, 1).reshape(128, 6))
    are = rows(inp["ssm_a_re"][l]); aim = rows(inp["ssm_a_im"][l])
    ldt = np.ascontiguousarray(np.repeat(inp["ssm_log_dt"][l][:, gs][:, None, :], 64, axis=1).reshape(128, 6))
    def rowsB(arr):
        return np.ascontiguousarray(arr[:, gs].transpose(0, 2, 1, 3).reshape(128, 96))
    def rowsC(arr):
        return np.ascontiguousarray(arr[:, gs].transpose(0, 3, 1, 2).reshape(128, 96))
    s_ = np.arange(8)
    mF = (s_[None, :] >= s_[:, None]).astype(np.float32)
    maskF = np.kron(mF, np.ones((16, 16), np.float32)); maskB = np.kron(mF.T, np.ones((16, 16), np.float32))
    sgn = np.concatenate([-np.ones((64, 1), np.float32), np.ones((64, 1), np.float32)], 0)
    dsk = inp["ssm_d"][l]
    Dd = np.zeros((6, 128, 128), np.float32)
    for gi, g in enumerate(gs):
        dd = np.zeros((8, 16, 8, 16), np.float32)
        for t in range(8):
            dd[t, np.arange(16), t, np.arange(16)] = dsk[16 * g:16 * g + 16]
        Dd[gi] = dd.reshape(128, 128)
    seq = np.concatenate([uc_b, u_b], 0)
    U = np.zeros((6, 128, NCH), np.float32)
    for gi, g in enumerate(gs):
        U[gi] = seq[:, 16 * g:16 * g + 16].reshape(NCH, 128).T
    return dict(are=are, aim=aim, ldt=ldt, Bre=rowsB(inp["ssm_b_re"][l]), Bim=rowsB(inp["ssm_b_im"][l]),
                Cre=rowsC(inp["ssm_c_re"][l]), Cim=rowsC(inp["ssm_c_im"][l]), maskF=maskF, maskB=maskB, sgn=sgn,
                ident=np.eye(128, dtype=np.float32), Ddiag=Dd, U=U)

def ssm_unpack(Yg):
    return np.ascontiguousarray(Yg.transpose(2, 1, 0).reshape(NCH, 8, 16, 6).transpose(0, 1, 3, 2).reshape(NCH * 8, 96))


_PROGS = {}
def _prog(name, fn):
    if name not in _PROGS:
        _PROGS[name] = fn()
    return _PROGS[name]

def _run(nc, maps):
    res = run_bass_kernel_spmd(nc, maps, core_ids=list(range(8)))
    return res.results

def kernel(x, c, ctx, c_ctx, w_mod, b_mod, g_pre_mix, g_post_mix, w_in, ssm_a_re, ssm_a_im, ssm_log_dt, ssm_b_re, ssm_b_im,
           ssm_c_re, ssm_c_im, ssm_d, w_glu, w_fourier, na_rpb, w_out, g_pre_ffn, g_post_ffn, w_ffn_gate, w_ffn_up, w_ffn_down):
    f32 = lambda a: np.ascontiguousarray(np.asarray(a, dtype=np.float32))
    inp = dict(ssm_a_re=f32(ssm_a_re), ssm_a_im=f32(ssm_a_im), ssm_log_dt=f32(ssm_log_dt), ssm_b_re=f32(ssm_b_re), ssm_b_im=f32(ssm_b_im),
               ssm_c_re=f32(ssm_c_re), ssm_c_im=f32(ssm_c_im), ssm_d=f32(ssm_d))
    x = f32(x); c = f32(c); ctx = f32(ctx); c_ctx = f32(c_ctx); w_mod = f32(w_mod); b_mod = f32(b_mod)
    w_in = f32(w_in); w_glu = f32(w_glu); w_fourier = f32(w_fourier); na_rpb = f32(na_rpb); w_out = f32(w_out)
    g_pre_mix = f32(g_pre_mix); g_post_mix = f32(g_post_mix); g_pre_ffn = f32(g_pre_ffn); g_post_ffn = f32(g_post_ffn)
    w_ffn_gate = f32(w_ffn_gate); w_ffn_up = f32(w_ffn_up); w_ffn_down = f32(w_ffn_down)
    DEPTH = 2
    cores = [(k // 4, k % 4) for k in range(8)]
    cTs = [np.ascontiguousarray(np.concatenate([colT(c[b], 8), colT(c_ctx, 8)], axis=1)) for b in range(2)]
    xT = [np.ascontiguousarray(np.concatenate([x[b, q * 2048:(q + 1) * 2048].T, ctx[b].T], axis=1)) for (b, q) in cores]
    KF = fnet_consts(); permm = perm_matrix(); ident = np.eye(128, dtype=np.float32)
    ropes = [rope_tables(q * 2048, 2048) for q in range(4)]
    for l in range(DEPTH):
        maps = []
        for k, (b, q) in enumerate(cores):
            maps.append(dict(xT=xT[k], w_in=w_in[l], w_mod=np.ascontiguousarray(w_mod[l][:, 0:2048]), b_modT=colT(b_mod[l][0:2048], 16),
                             g_preT=colT(g_pre_mix[l], 8), cT=cTs[b], cos=ropes[q][0], sin=ropes[q][1], perm=permm))
        res = _run(_prog("l1", build_l1), maps)
        hfull = [np.concatenate([res[k]["hT"], res[k]["hTb"]], 0) for k in range(8)]
        h_lat = [np.concatenate([hfull[4 * b + q][:, 0:2048].T for q in range(4)], 0) for b in range(2)]
        h_ctx = [np.ascontiguousarray(hfull[4 * b][:, 2048:2304].T) for b in range(2)]
        del hfull
        del res
        maps = [ssm_inputs(inp, l, j4, h_lat[b][:, 0:384], h_ctx[b][:, 0:384]) for (b, j4) in cores]
        res = _run(_prog("ssm", build_ssm), maps)
        ys = [[ssm_unpack(res[4 * b + j4]["Yg"]) for j4 in range(4)] for b in range(2)]
        ysT = [np.ascontiguousarray(np.concatenate(ys[b], 1).T) for b in range(2)]
        del res, ys
        maps = []
        for (b, g) in cores:
            m = dict(z=np.ascontiguousarray(h_lat[b][:, 384 + 64 * g:448 + 64 * g].reshape(128, 4096)),
                     zc=np.ascontiguousarray(h_ctx[b][:, 384 + 64 * g:448 + 64 * g].reshape(2, 128, 64).transpose(1, 0, 2)))
            m.update(KF); maps.append(m)
        res = _run(_prog("fnet", lambda: build_fnet(True)), maps)
        mxT = [np.concatenate([res[4 * b + g]["R"] for g in range(4)], 0) for b in range(2)]
        mxcT = [np.concatenate([res[4 * b + g]["Rc"] for g in range(4)], 0) for b in range(2)]
        del res
        maps = []
        for (b, q) in cores:
            kwT, vw = na_windows(h_lat[b][:, 1024:1408], h_lat[b][:, 1408:1792], q)
            tbraw, mask = na_tables(na_rpb[l], q)
            maps.append(dict(qT=np.ascontiguousarray(h_lat[b][q * 2048:(q + 1) * 2048, 640:1024].T), kwT=kwT, vw=vw,
                             kcT=np.ascontiguousarray(h_ctx[b][:, 1024:1408].T),
                             vc=np.ascontiguousarray(h_ctx[b][:, 1408:1792].reshape(2, 128, 384).transpose(1, 0, 2)),
                             tbraw=tbraw, mask=mask, ident=ident, qcT=np.ascontiguousarray(h_ctx[b][:, 640:1024].T)))
        res = _run(_prog("na", lambda: build_na(True)), maps)
        naT = [np.ascontiguousarray(np.concatenate([res[4 * b + q]["Y"] for q in range(4)], 0).T) for b in range(2)]
        nacT = [np.ascontiguousarray(res[4 * b]["Yc"].T) for b in range(2)]
        del res, h_lat
        maps = []
        for k, (b, q) in enumerate(cores):
            sl = slice(q * 2048, (q + 1) * 2048)
            maps.append(dict(xT=xT[k], ysT=np.ascontiguousarray(np.concatenate([ysT[b][:, 256 + q * 2048:256 + (q + 1) * 2048], ysT[b][:, 0:256]], 1)),
                             mxT=np.ascontiguousarray(np.concatenate([mxT[b][:, sl], mxcT[b]], 1)),
                             naT=np.ascontiguousarray(np.concatenate([naT[b][:, sl], nacT[b]], 1)),
                             w_mod=np.ascontiguousarray(w_mod[l][:, 2048:3072]), b_modT=colT(b_mod[l][2048:3072], 8), cT=cTs[b],
                             g_postT=colT(g_post_mix[l], 8), w_glu=w_glu[l], w_fourier=w_fourier[l], w_out=w_out[l]))
        res = _run(_prog("l3a", lambda: build_l3a(True)), maps)
        xT = [res[k]["xoT"] for k in range(8)]
        del res
        maps = []
        for k, (b, q) in enumerate(cores):
            maps.append(dict(xT=xT[k], w_mod=np.ascontiguousarray(w_mod[l][:, 3072:6144]), b_modT=colT(b_mod[l][3072:6144], 24), cT=cTs[b],
                             g_preT=colT(g_pre_ffn[l], 8), g_postT=colT(g_post_ffn[l], 8),
                             w_gate=w_ffn_gate[l], w_up=w_ffn_up[l], w_down=w_ffn_down[l]))
        res = _run(_prog("l3b", lambda: build_l3b(True)), maps)
        xT = [np.ascontiguousarray(res[k]["xoT"]) for k in range(8)]
        del res
    out = np.empty((2, 8192, 1024), np.float32)
    for k, (b, q) in enumerate(cores):
        out[b, q * 2048:(q + 1) * 2048] = xT[k][:, 0:2048].T
    return out
```

```python
import math
import numpy as np
from contextlib import ExitStack
import concourse.bass as bass
import concourse.mybir as mybir
from concourse.bass_utils import run_bass_kernel_spmd


F32 = mybir.dt.float32
BF16 = mybir.dt.bfloat16
ALU = mybir.AluOpType
AF = mybir.ActivationFunctionType
AX = mybir.AxisListType

COMPUTE = ("tensor", "vector", "scalar", "gpsimd")


class Prog:
    def __init__(self):
        self.nc = bass.Bass("TRN2", target_bir_lowering=False)
        self.ops = []
        self.stack = ExitStack()
        self.ndram = 0

    def dram_in(self, name, shape, dtype=F32):
        return self.nc.dram_tensor(name, list(shape), dtype, kind="ExternalInput").ap()

    def dram_out(self, name, shape, dtype=F32):
        return self.nc.dram_tensor(name, list(shape), dtype, kind="ExternalOutput").ap()

    def sbuf(self, name, shape, dtype=F32):
        return self.stack.enter_context(self.nc.sbuf_tensor(name, list(shape), dtype))

    def psum(self, name, shape=(128, 512), dtype=F32):
        return self.stack.enter_context(self.nc.psum_tensor(name, list(shape), dtype))

    def op(self, eng, fn, reads=(), writes=(), chan=None, nosync_same=False, inc=True):
        self.ops.append(dict(eng=eng, fn=fn, reads=tuple(reads), writes=tuple(writes),
                             chan=chan, nosync_same=nosync_same, inc=inc))

    def dma(self, eng, out, in_, reads=(), writes=(), chan="ld", **kw):
        self.op(eng, lambda e: e.dma_start(out=out, in_=in_, **kw), reads, writes, chan=chan)

    def mm(self, out, lhsT, rhs, start, stop, reads=(), writes=()):
        self.op("tensor", lambda e: e.matmul(out, lhsT, rhs, start=start, stop=stop),
                reads, writes, nosync_same=True, inc=True)

    def act(self, out, in_, func, reads=(), writes=(), **kw):
        self.op("scalar", lambda e: e.activation(out=out, in_=in_, func=func, **kw), reads, writes)

    def tt(self, out, in0, in1, op, reads=(), writes=(), eng="vector"):
        self.op(eng, lambda e: e.tensor_tensor(out=out, in0=in0, in1=in1, op=op), reads, writes)

    def ts(self, out, in0, s1, s2, op0, op1=None, reads=(), writes=(), eng="vector"):
        if op1 is None:
            self.op(eng, lambda e: e.tensor_scalar(out=out, in0=in0, scalar1=s1, scalar2=None, op0=op0),
                    reads, writes)
        else:
            self.op(eng, lambda e: e.tensor_scalar(out=out, in0=in0, scalar1=s1, scalar2=s2, op0=op0, op1=op1),
                    reads, writes)

    def stt(self, out, in0, scalar, in1, op0, op1, reads=(), writes=()):
        self.op("vector", lambda e: e.scalar_tensor_tensor(out=out, in0=in0, scalar=scalar, in1=in1,
                                                            op0=op0, op1=op1), reads, writes)

    def copy(self, out, in_, reads=(), writes=(), eng="vector"):
        if eng == "scalar":
            self.op(eng, lambda e: e.copy(out=out, in_=in_), reads, writes)
        else:
            self.op(eng, lambda e: e.tensor_copy(out=out, in_=in_), reads, writes)

    def memset(self, ap, val, writes=(), eng="vector"):
        self.op(eng, lambda e: e.memset(ap, val), (), writes)

    def finish(self):
        nc = self.nc
        ops = self.ops
        engines = []
        for o in ops:
            if o["eng"] not in engines:
                engines.append(o["eng"])
        chans = []
        for o in ops:
            if o["chan"] is not None and o["chan"] not in chans:
                chans.append(o["chan"])
        sems = {}
        for e in engines:
            sems[("e", e)] = self.stack.enter_context(nc.semaphore("s_" + e))
        for c in chans:
            sems[("c", c)] = self.stack.enter_context(nc.semaphore("c_" + c))
        def plan_pass():
            viol = set()
            eng_count = {e: 0 for e in engines}
            chan_count = {c: 0 for c in chans}
            last_writer = {}
            readers = {}
            known = {e: {} for e in engines}
            plan = {e: [] for e in engines}
            done = []
            for i, o in enumerate(ops):
                e = o["eng"]
                deps = set()
                for r in o["reads"]:
                    if r in last_writer:
                        deps.add(last_writer[r])
                for w in o["writes"]:
                    if w in last_writer:
                        deps.add(last_writer[w])
                    for rd in readers.get(w, ()):
                        deps.add(rd)
                need = {}
                for d in deps:
                    od = ops[d]
                    if od["chan"] is not None:
                        key = ("c", od["chan"])
                        val = 16 * chan_count[od["chan"]]
                    else:
                        if od["eng"] == e and (o["nosync_same"] and od["nosync_same"]):
                            continue
                        key = ("e", od["eng"])
                        val = done[d][1]
                        if val > eng_count[od["eng"]]:
                            viol.add(d)
                    if val > need.get(key, 0):
                        need[key] = val
                waits = []
                for key, val in need.items():
                    if known[e].get(key, 0) >= val:
                        continue
                    known[e][key] = val
                    waits.append((key, val))
                if o["chan"] is not None:
                    chan_count[o["chan"]] += 1
                    done.append((("c", o["chan"]), 16 * chan_count[o["chan"]]))
                    inc = (("c", o["chan"]), 16)
                elif not o["inc"]:
                    done.append((("e", e), eng_count[e] + 1))
                    inc = None
                else:
                    eng_count[e] += 1
                    done.append((("e", e), eng_count[e]))
                    inc = (("e", e), 1)
                plan[e].append((waits, o["fn"], inc))
                for r in o["reads"]:
                    readers.setdefault(r, []).append(i)
                for w in o["writes"]:
                    last_writer[w] = i
                    readers[w] = []
            return viol, plan, chan_count
        while True:
            viol, plan, chan_count = plan_pass()
            if not viol:
                break
            for d in viol:
                ops[d]["inc"] = True
        final_waits = {e: [] for e in engines}
        chan_eng = {}
        for o in ops:
            if o["chan"] is not None:
                chan_eng[o["chan"]] = o["eng"]
        for c, e in chan_eng.items():
            final_waits[e].append((("c", c), 16 * chan_count[c]))

        semv = {k: 0 for k in sems}
        ptr = {e: 0 for e in engines}
        progressed = True
        while progressed:
            progressed = False
            for e in engines:
                while ptr[e] < len(plan[e]):
                    waits, _fn, inc = plan[e][ptr[e]]
                    if any(semv[k] < v for k, v in waits):
                        break
                    if inc is not None:
                        semv[inc[0]] += inc[1]
                    ptr[e] += 1
                    progressed = True
        stuck = {e: (ptr[e], len(plan[e])) for e in engines if ptr[e] < len(plan[e])}
        if stuck:
            det = {e: [(k, v, semv[k]) for k, v in plan[e][ptr[e]][0] if semv[k] < v] for e in stuck}
            raise RuntimeError(f"sync plan deadlocks: {stuck} waiting on {det}")

        with nc.Block() as block:
            def make(e):
                def body(eng):
                    for waits, fn, inc in plan[e]:
                        for key, val in waits:
                            eng.wait_ge(sems[key], val)
                        ins = fn(eng)
                        if inc is not None:
                            ins.then_inc(sems[inc[0]], inc[1])
                    for key, val in final_waits[e]:
                        eng.wait_ge(sems[key], val)
                return body
            for e in engines:
                getattr(block, e)(make(e))
        self.stack.close()
        return nc

GRID_W = 64
def rope_tables(tok0, n):
    t = np.arange(tok0, tok0 + n); row = (t // GRID_W).astype(np.float32); col = (t % GRID_W).astype(np.float32)
    quarter = 16
    freqs = (10000.0 ** (-np.arange(quarter, dtype=np.float32) / quarter)).astype(np.float32)
    cos = np.zeros((64, n), np.float32); sin = np.zeros((64, n), np.float32)
    for d in range(64):
        pos = row if d < 32 else col
        dd = d % 32
        f = freqs[dd % 16]
        ang = (pos * f).astype(np.float32)
        cos[d] = np.cos(ang); s = np.sin(ang)
        sin[d] = -s if dd < 16 else s
    return np.concatenate([cos, cos], 0), np.concatenate([sin, sin], 0)
def perm_matrix():
    Pm = np.zeros((128, 128), np.float32)
    for m in range(128):
        dd = m % 32
        k = m + 16 if dd < 16 else m - 16
        Pm[k, m] = 1.0
    return Pm
def colT(v, n):
    return np.ascontiguousarray(v.reshape(n, 128).T)

EPS = 1e-6

def get_stage(P):
    if not hasattr(P, "_stage"):
        P._stage_n = getattr(P, "_stage_n", 3)
        P._stage = [P.sbuf(f"stage{i}", [128, 1024]) for i in range(P._stage_n)]
        P._stage_i = 0
    return P._stage

def emit_mod(P, w_mod, b_modT, cT, nct, ps_mod, tag="m"):
    ncols = nct * 128
    c_s = P.sbuf(tag + "c_s", [128, 16]); sc_s = P.sbuf(tag + "sc_s", [128, 16])
    bm_s = P.sbuf(tag + "bm_s", [128, nct]); modv = P.sbuf(tag + "modv", [128, 2 * nct]); modc = P.sbuf(tag + "modc", [128, 2 * nct])
    modrow = [P.sbuf(f"{tag}modrow{i}", [2, 512]) for i in range(2)]
    scr = P.nc.dram_tensor(tag + "_modscr", [2, ncols], F32, kind="Internal").ap()
    st = get_stage(P)
    P.dma("sync", c_s[:], cT, writes=[tag + "c_s"], chan="ld0")
    P.dma("sync", bm_s[:], b_modT, writes=[tag + "bm_s"], chan="ld0")
    P.act(sc_s[:], c_s[:], AF.Silu, reads=[tag + "c_s"], writes=[tag + "sc_s"])
    wmv = w_mod.rearrange("(kc p) n -> p kc n", p=128)
    for pc in range(ncols // 512):
        for i in range(4):
            b = P._stage_i % P._stage_n; P._stage_i += 1
            P.dma("sync", st[b][:].rearrange("p (k n) -> p k n", k=2), wmv[:, 2 * i:2 * i + 2, pc * 512:(pc + 1) * 512],
                  writes=[("stage", b)], chan=f"stage{b}")
            for k2 in range(2):
                kc = 2 * i + k2
                P.mm(ps_mod[0:2, 0:512], sc_s[:, kc:16:8], st[b][:, k2 * 512:(k2 + 1) * 512], kc == 0, kc == 7,
                     reads=[("stage", b), tag + "sc_s"], writes=["ps_mod"])
        P.copy(modrow[pc % 2][:], ps_mod[0:2, 0:512], reads=["ps_mod"], writes=[(tag + "modrow", pc % 2)], eng="scalar")
        P.dma("sync", scr[:, pc * 512:(pc + 1) * 512], modrow[pc % 2][:], reads=[(tag + "modrow", pc % 2)], writes=[tag + "scr"], chan="modw")
    P.dma("sync", modc[:].rearrange("p (j t) -> p j t", j=2), scr.rearrange("j (t p) -> p j t", p=128),
          reads=[tag + "scr"], writes=[tag + "modc"], chan="modr", allow_slow_non_contiguous=True)
    for j in range(2):
        P.tt(modv[:, j * nct:(j + 1) * nct], modc[:, j * nct:(j + 1) * nct], bm_s[:], ALU.add,
             reads=[tag + "modc", tag + "bm_s"], writes=[tag + "modv"])
    return modv

def load_cast(P, w_dram, w_bf, nk, ncols, tag, piece=1024):
    wv = w_dram.rearrange("(kc p) n -> p kc n", p=128)
    for kc in range(nk):
        P.dma("gpsimd", w_bf[:, kc, :], wv[:, kc, :], writes=[(tag, kc)], chan="wld_" + tag, max_dma_last_dim=4096)

def emit_rstd(P, src, nk, n, sqb, ones, ps_ss, sd, rstd, src_res, tag=""):
    P.act(sqb[:, 0:nk, 0:n], src[:, 0:nk, 0:n], AF.Square, reads=src_res, writes=["sqb"])
    for kc in range(nk):
        P.mm(ps_ss[:, 0:n], ones[:], sqb[:, kc, 0:n], kc == 0, kc == nk - 1, reads=["sqb", "ones"], writes=["ps_ss"])
    P.act(sd[:, 0:n], ps_ss[:, 0:n], AF.Sqrt, reads=["ps_ss"], writes=["sd" + tag], scale=1.0 / (128 * nk), bias=EPS)
    P.op("vector", lambda e: e.reciprocal(out=rstd[:, 0:n], in_=sd[:, 0:n]), reads=["sd" + tag], writes=["rstd" + tag])

def build_l3b(with_ctx, N=256):
    NT = 2304 if with_ctx else 2048
    P = Prog(); P._stage_n = 2
    xT = P.dram_in("xT", [1024, NT])
    w_mod = P.dram_in("w_mod", [1024, 3072]); b_modT = P.dram_in("b_modT", [128, 24]); cT = P.dram_in("cT", [128, 16])
    g_preT = P.dram_in("g_preT", [128, 8]); g_postT = P.dram_in("g_postT", [128, 8])
    w_gate = P.dram_in("w_gate", [1024, 2816]); w_up = P.dram_in("w_up", [1024, 2816]); w_down = P.dram_in("w_down", [2816, 1024])
    xoT = P.dram_out("xoT", [1024, NT])
    wg = P.sbuf("wg", [128, 8, 2816], BF16); wu = P.sbuf("wu", [128, 8, 2816], BF16); wd = P.sbuf("wd", [128, 22, 1024], BF16)
    xs = [P.sbuf(f"xs{i}", [128, 8, N]) for i in range(2)]
    sqb = P.sbuf("sqb", [128, 8, N], BF16)
    tt_ = [P.sbuf(f"tt{i}", [128, N]) for i in range(2)]; xn = [P.sbuf(f"xn{i}", [128, 8, N], BF16) for i in range(2)]
    sd2 = P.sbuf("sd2", [128, N]); rstd2 = P.sbuf("rstd2", [128, N])
    hmid = P.sbuf("hmid", [128, 22, N], BF16)
    sg = [P.sbuf(f"sg{i}", [128, N]) for i in range(2)]
    o2 = P.sbuf("o2", [128, 8, N]); tmp = [P.sbuf(f"tmp{i}", [128, N]) for i in range(2)]
    xo = [P.sbuf(f"xo{i}", [128, N]) for i in range(2)]
    ones = P.sbuf("ones", [128, 128], BF16)
    sd = P.sbuf("sd", [128, N]); rstd = P.sbuf("rstd", [128, N])
    gp_s = P.sbuf("gp_s", [128, 8]); gq_s = P.sbuf("gq_s", [128, 8])
    Av = P.sbuf("Av", [128, 16]); Gv = P.sbuf("Gv", [128, 16])
    ps_mod = P.psum("ps_mod"); ps_ss = P.psum("ps_ss")
    psg = [P.psum(f"psg{i}") for i in range(2)]; psu = [P.psum(f"psu{i}") for i in range(2)]; pso = [P.psum(f"pso{i}") for i in range(2)]
    P.memset(ones[:], 1.0, writes=["ones"])
    P.dma("sync", gp_s[:], g_preT, writes=["gp_s"], chan="ld0")
    P.dma("sync", gq_s[:], g_postT, writes=["gq_s"], chan="ld0")
    modv = emit_mod(P, w_mod, b_modT, cT, 24, ps_mod)
    for j in range(2):
        P.stt(Av[:, j * 8:(j + 1) * 8], modv[:, j * 24 + 8: j * 24 + 16], 1.0, gp_s[:], ALU.add, ALU.mult,
              reads=["mmodv", "gp_s"], writes=["Av"])
        P.tt(Gv[:, j * 8:(j + 1) * 8], modv[:, j * 24 + 16: j * 24 + 24], gq_s[:], ALU.mult,
             reads=["mmodv", "gq_s"], writes=["Gv"])
    load_cast(P, w_gate, wg, 8, 2816, "wg")
    load_cast(P, w_up, wu, 8, 2816, "wu")
    xv = xT.rearrange("(kc p) t -> p kc t", p=128); xov = xoT.rearrange("(kc p) t -> p kc t", p=128)
    slabs = list(range(0, NT, N)); n = N
    cnt = dict(ng=0, no=0, nt=0)
    def pre(si):
        t0 = slabs[si]; b = si % 2; j = 0 if t0 < 2048 else 1
        P.dma("sync", xs[b][:], xv[:, :, t0:t0 + n], writes=[("xs", b)], chan=f"xs{b}")
        emit_rstd(P, xs[b], 8, n, sqb, ones, ps_ss, sd, rstd, [("xs", b)])
        for kc in range(8):
            P.tt(tt_[kc % 2][:], xs[b][:, kc, :], rstd[:], ALU.mult, reads=[("xs", b), "rstd"], writes=[("tt", kc % 2)])
            P.act(xn[b][:, kc, :], tt_[kc % 2][:], AF.Identity, reads=[("tt", kc % 2), "Av", "mmodv"], writes=[("xn", b, kc)],
                  scale=Av[:, j * 8 + kc: j * 8 + kc + 1], bias=modv[:, j * 24 + kc: j * 24 + kc + 1])
    def gu(si):
        b = si % 2
        for jj in range(22):
            pb = cnt["ng"] % 2; cnt["ng"] += 1
            for kc in range(8):
                P.mm(psg[pb][:, 0:n], wg[:, kc, jj * 128:(jj + 1) * 128], xn[b][:, kc, :], kc == 0, kc == 7,
                     reads=[("wg", kc), ("xn", b, kc)], writes=[("psg", pb)])
            for kc in range(8):
                P.mm(psu[pb][:, 0:n], wu[:, kc, jj * 128:(jj + 1) * 128], xn[b][:, kc, :], kc == 0, kc == 7,
                     reads=[("wu", kc), ("xn", b, kc)], writes=[("psu", pb)])
            P.act(sg[pb][:], psg[pb][:, 0:n], AF.Silu, reads=[("psg", pb)], writes=[("sg", pb)])
            P.tt(hmid[:, jj, :], sg[pb][:], psu[pb][:, 0:n], ALU.mult, reads=[("sg", pb), ("psu", pb)], writes=[("hmid", jj)])
    def dn(si):
        for m in range(8):
            pb = cnt["no"] % 2; cnt["no"] += 1
            for jj in range(22):
                P.mm(pso[pb][:, 0:n], wd[:, jj, m * 128:(m + 1) * 128], hmid[:, jj, :], jj == 0, jj == 21,
                     reads=[("wd", jj), ("hmid", jj)], writes=[("pso", pb)])
            P.copy(o2[:, m, :], pso[pb][:, 0:n], reads=[("pso", pb)], writes=[("o2", m)], eng="scalar")
    def post(si):
        t0 = slabs[si]; b = si % 2; j = 0 if t0 < 2048 else 1
        emit_rstd(P, o2, 8, n, sqb, ones, ps_ss, sd2, rstd2, [("o2", m) for m in range(8)], tag="2")
        for m in range(8):
            P.stt(o2[:, m, :], o2[:, m, :], Gv[:, j * 8 + m: j * 8 + m + 1], rstd2[:], ALU.mult, ALU.mult,
                  reads=[("o2", m), "Gv", "rstd2"], writes=[("o2", m)])
            P.tt(o2[:, m, :], xs[b][:, m, :], o2[:, m, :], ALU.add, reads=[("xs", b), ("o2", m)], writes=[("o2", m)], eng="gpsimd")
        P.dma("gpsimd", xov[:, :, t0:t0 + n], o2[:], reads=[("o2", m) for m in range(8)], chan="st")
    pre(0)
    for si in range(len(slabs)):
        gu(si)
        if si == 0:
            load_cast(P, w_down, wd, 22, 1024, "wd")
        if si + 1 < len(slabs):
            pre(si + 1)
        dn(si)
        post(si)
    return P.finish()

def build_l3a(with_ctx, N=512):
    NT = 2304 if with_ctx else 2048
    P = Prog()
    xT = P.dram_in("xT", [1024, NT]); ysT = P.dram_in("ysT", [384, NT]); mxT = P.dram_in("mxT", [256, NT]); naT = P.dram_in("naT", [384, NT])
    w_mod = P.dram_in("w_mod", [1024, 1024]); b_modT = P.dram_in("b_modT", [128, 8]); cT = P.dram_in("cT", [128, 16])
    g_postT = P.dram_in("g_postT", [128, 8])
    w_glu = P.dram_in("w_glu", [384, 384]); w_fourier = P.dram_in("w_fourier", [256, 256]); w_out = P.dram_in("w_out", [1024, 1024])
    xoT = P.dram_out("xoT", [1024, NT])
    wglu = P.sbuf("wglu", [128, 3, 384], BF16); wf = P.sbuf("wf", [128, 2, 256], BF16); wo = P.sbuf("wo", [128, 8, 1024], BF16)
    xs = [P.sbuf(f"xs{i}", [128, 8, N]) for i in range(2)]
    ys = [P.sbuf(f"ys{i}", [128, 3, N]) for i in range(2)]
    mx = [P.sbuf(f"mx{i}", [128, 2, N]) for i in range(2)]
    na = [P.sbuf(f"na{i}", [128, 3, N]) for i in range(2)]
    sq = P.sbuf("sq", [128, 3, N]); t1 = P.sbuf("t1", [128, 3, N]); sgm = P.sbuf("sgm", [128, 3, N])
    zf = P.sbuf("zf", [128, 3, N]); zb = P.sbuf("zb", [128, 3, N], BF16); mxb = P.sbuf("mxb", [128, 2, N], BF16)
    sg2 = [P.sbuf(f"sg2{i}", [128, N]) for i in range(2)]
    cat = P.sbuf("cat", [128, 8, N], BF16)
    sqb = P.sbuf("sqb", [128, 8, N], BF16)
    o2 = P.sbuf("o2", [128, 8, N]); tmp = [P.sbuf(f"tmp{i}", [128, N]) for i in range(2)]
    xo = [P.sbuf(f"xo{i}", [128, N]) for i in range(2)]
    ones = P.sbuf("ones", [128, 128], BF16)
    sd = P.sbuf("sd", [128, N]); rstd = P.sbuf("rstd", [128, N])
    gq_s = P.sbuf("gq_s", [128, 8]); Gv = P.sbuf("Gv", [128, 16])
    ps_mod = P.psum("ps_mod"); ps_ss = P.psum("ps_ss")
    psa = [P.psum(f"psa{i}") for i in range(3)]; pso = [P.psum(f"pso{i}") for i in range(3)]
    P.memset(ones[:], 1.0, writes=["ones"])
    P.dma("sync", gq_s[:], g_postT, writes=["gq_s"], chan="ld0")
    modv = emit_mod(P, w_mod, b_modT, cT, 8, ps_mod)
    for j in range(2):
        P.tt(Gv[:, j * 8:(j + 1) * 8], modv[:, j * 8:(j + 1) * 8], gq_s[:], ALU.mult, reads=["mmodv", "gq_s"], writes=["Gv"])
    load_cast(P, w_glu, wglu, 3, 384, "wglu")
    load_cast(P, w_fourier, wf, 2, 256, "wf")
    load_cast(P, w_out, wo, 8, 1024, "wo")
    xv = xT.rearrange("(kc p) t -> p kc t", p=128); xov = xoT.rearrange("(kc p) t -> p kc t", p=128)
    ysv = ysT.rearrange("(kc p) t -> p kc t", p=128); mxv = mxT.rearrange("(kc p) t -> p kc t", p=128); nav = naT.rearrange("(kc p) t -> p kc t", p=128)
    na_ = 0; no = 0; nt = 0
    for si, t0 in enumerate(range(0, NT, N)):
        n = min(N, NT - t0); b = si % 2; j = 0 if t0 < 2048 else 1
        P.dma("sync", xs[b][:, :, 0:n], xv[:, :, t0:t0 + n], writes=[("xs", b)], chan=f"xs{b}")
        P.dma("sync", ys[b][:, :, 0:n], ysv[:, :, t0:t0 + n], writes=[("ys", b)], chan=f"ys{b}")
        P.dma("sync", mx[b][:, :, 0:n], mxv[:, :, t0:t0 + n], writes=[("mx", b)], chan=f"mx{b}")
        P.dma("sync", na[b][:, :, 0:n], nav[:, :, t0:t0 + n], writes=[("na", b)], chan=f"na{b}")
        Y = ys[b][:, :, 0:n]
        P.tt(sq[:, :, 0:n], Y, Y, ALU.mult, reads=[("ys", b)], writes=["sq"], eng="gpsimd")
        P.ts(t1[:, :, 0:n], sq[:, :, 0:n], 0.044715, 1.0, ALU.mult, ALU.add, reads=["sq"], writes=["t1"])
        P.tt(sq[:, :, 0:n], t1[:, :, 0:n], Y, ALU.mult, reads=["t1", ("ys", b)], writes=["sq"])
        P.act(sgm[:, :, 0:n], sq[:, :, 0:n], AF.Sigmoid, reads=["sq"], writes=["sgm"], scale=1.5957691216057308)
        P.tt(zf[:, :, 0:n], Y, sgm[:, :, 0:n], ALU.mult, reads=[("ys", b), "sgm"], writes=["zf"])
        P.copy(zb[:, :, 0:n], zf[:, :, 0:n], reads=["zf"], writes=["zb"], eng="gpsimd")
        for m in range(3):
            pb = na_ % 3; na_ += 1
            for kc in range(3):
                P.mm(psa[pb][:, 0:n], wglu[:, kc, m * 128:(m + 1) * 128], zb[:, kc, 0:n], kc == 0, kc == 2,
                     reads=[("wglu", kc), "zb"], writes=[("psa", pb)])
            P.act(sg2[m % 2][:, 0:n], psa[pb][:, 0:n], AF.Sigmoid, reads=[("psa", pb)], writes=[("sg2", m % 2)])
            P.tt(cat[:, m, 0:n], zf[:, m, 0:n], sg2[m % 2][:, 0:n], ALU.mult, reads=["zf", ("sg2", m % 2)], writes=[("cat", m)])
        P.copy(mxb[:, :, 0:n], mx[b][:, :, 0:n], reads=[("mx", b)], writes=["mxb"], eng="gpsimd")
        for m in range(2):
            pb = na_ % 3; na_ += 1
            for kc in range(2):
                P.mm(psa[pb][:, 0:n], wf[:, kc, m * 128:(m + 1) * 128], mxb[:, kc, 0:n], kc == 0, kc == 1,
                     reads=[("wf", kc), "mxb"], writes=[("psa", pb)])
            P.copy(cat[:, 3 + m, 0:n], psa[pb][:, 0:n], reads=[("psa", pb)], writes=[("cat", 3 + m)], eng="scalar")
        P.copy(cat[:, 5:8, 0:n], na[b][:, :, 0:n], reads=[("na", b)], writes=[("cat", 5), ("cat", 6), ("cat", 7)], eng="gpsimd")
        for m in range(8):
            pb = no % 3; no += 1
            for kc in range(8):
                P.mm(pso[pb][:, 0:n], wo[:, kc, m * 128:(m + 1) * 128], cat[:, kc, 0:n], kc == 0, kc == 7,
                     reads=[("wo", kc), ("cat", kc)], writes=[("pso", pb)])
            P.copy(o2[:, m, 0:n], pso[pb][:, 0:n], reads=[("pso", pb)], writes=[("o2", m)], eng="scalar")
        emit_rstd(P, o2, 8, n, sqb, ones, ps_ss, sd, rstd, [("o2", m) for m in range(8)])
        for m in range(8):
            P.stt(o2[:, m, 0:n], o2[:, m, 0:n], Gv[:, j * 8 + m: j * 8 + m + 1], rstd[:, 0:n], ALU.mult, ALU.mult,
                  reads=[("o2", m), "Gv", "rstd"], writes=[("o2", m)])
            P.tt(o2[:, m, 0:n], xs[b][:, m, 0:n], o2[:, m, 0:n], ALU.add, reads=[("xs", b), ("o2", m)], writes=[("o2", m)], eng="gpsimd")
        P.dma("gpsimd", xov[:, :, t0:t0 + n], o2[:, :, 0:n], reads=[("o2", m) for m in range(8)], chan="st")
    return P.finish()

EPS = 1e-6
NT = 2304
SLABS = [(0, 512), (512, 512), (1024, 512), (1536, 512), (2048, 256)]

def build_l1():
    P = Prog(); P._stage_n = 4
    xT = P.dram_in("xT", [1024, NT])
    w_in = P.dram_in("w_in", [1024, 1792])
    w_mod = P.dram_in("w_mod", [1024, 2048])
    b_modT = P.dram_in("b_modT", [128, 16])
    g_preT = P.dram_in("g_preT", [128, 8])
    cT = P.dram_in("cT", [128, 16])
    cos = P.dram_in("cos", [128, 2048]); sin = P.dram_in("sin", [128, 2048])
    perm = P.dram_in("perm", [128, 128])
    hT = P.dram_out("hT", [640, NT])
    hTb = P.dram_out("hTb", [1152, NT])

    xs = [P.sbuf(f"xs{i}", [128, 8, 512]) for i in range(2)]
    sqb = P.sbuf("sqb", [128, 8, 512], BF16)
    tt_ = [P.sbuf(f"tt{i}", [128, 512]) for i in range(2)]
    xn = [P.sbuf(f"xn{i}", [128, 8, 512], BF16) for i in range(2)]
    w_bf = P.sbuf("w_bf", [128, 8, 1792], BF16)
    ones = P.sbuf("ones", [128, 128], BF16)
    sd = P.sbuf("sd", [128, 512]); rstd = P.sbuf("rstd", [128, 512])
    ho = [P.sbuf(f"ho{i}", [128, 512]) for i in range(4)]
    hob = [P.sbuf(f"hob{i}", [128, 512]) for i in range(4)]
    r1 = [P.sbuf(f"r1{i}", [128, 512]) for i in range(2)]
    r2 = [P.sbuf(f"r2{i}", [128, 512]) for i in range(2)]
    cos_s = P.sbuf("cos_s", [128, 2048]); sin_s = P.sbuf("sin_s", [128, 2048])
    perm_s = P.sbuf("perm_s", [128, 128])
    gp_s = P.sbuf("gp_s", [128, 8]); Av = P.sbuf("Av", [128, 16])
    ps_mod = P.psum("ps_mod"); ps_ss = P.psum("ps_ss")
    psm = [P.psum(f"psm{i}") for i in range(4)]
    psr = [P.psum(f"psr{i}") for i in range(2)]

    P.dma("sync", gp_s[:], g_preT, writes=["gp_s"], chan="ld0")
    P.dma("sync", cos_s[:], cos, writes=["cos_s"], chan="ld0")
    P.dma("sync", sin_s[:], sin, writes=["sin_s"], chan="ld0")
    P.dma("sync", perm_s[:], perm, writes=["perm_s"], chan="ld0")
    P.memset(ones[:], 1.0, writes=["ones"])
    modv = emit_mod(P, w_mod, b_modT, cT, 16, ps_mod)
    for j in range(2):
        P.stt(Av[:, j * 8:(j + 1) * 8], modv[:, j * 16 + 8: j * 16 + 16], 1.0, gp_s[:], ALU.add, ALU.mult,
              reads=["mmodv", "gp_s"], writes=["Av"])
    load_cast(P, w_in, w_bf, 8, 1792, "w_bf", piece=1024)
    xv = xT.rearrange("(kc p) t -> p kc t", p=128)
    cnt = dict(nho=0, nr=0, nps=0)
    def pre(si):
        t0, n = SLABS[si]; b = si % 2; j = 0 if si < 4 else 1
        P.dma("sync", xs[b][:, :, 0:n], xv[:, :, t0:t0 + n], writes=[("xs", b)], chan=f"xs{b}")
        P.act(sqb[:, :, 0:n], xs[b][:, :, 0:n], AF.Square, reads=[("xs", b)], writes=["sqb"])
        for kc in range(8):
            P.mm(ps_ss[:, 0:n], ones[:], sqb[:, kc, 0:n], kc == 0, kc == 7, reads=["sqb", "ones"], writes=["ps_ss"])
        P.act(sd[:, 0:n], ps_ss[:, 0:n], AF.Sqrt, reads=["ps_ss"], writes=["sd"], scale=1.0 / 1024, bias=EPS)
        P.op("vector", lambda e: e.reciprocal(out=rstd[:, 0:n], in_=sd[:, 0:n]), reads=["sd"], writes=["rstd"])
        for kc in range(8):
            P.tt(tt_[kc % 2][:, 0:n], xs[b][:, kc, 0:n], rstd[:, 0:n], ALU.mult, reads=[("xs", b), "rstd"], writes=[("tt", kc % 2)])
            P.act(xn[b][:, kc, 0:n], tt_[kc % 2][:, 0:n], AF.Identity, reads=[("tt", kc % 2), "Av", "mmodv"], writes=[("xn", b, kc)],
                  scale=Av[:, j * 8 + kc: j * 8 + kc + 1], bias=modv[:, j * 16 + kc: j * 16 + kc + 1])
    pending = []
    def flush():
        while pending:
            pending.pop(0)()
    def main(si):
        t0, n = SLABS[si]; b = si % 2
        for m in range(14):
            pb = cnt["nps"] % 4; cnt["nps"] += 1
            for kc in range(8):
                P.mm(psm[pb][:, 0:n], w_bf[:, kc, m * 128:(m + 1) * 128], xn[b][:, kc, 0:n], kc == 0, kc == 7,
                     reads=[("w_bf", kc), ("xn", b, kc)], writes=[("psm", pb)])
            flush()
            hb = cnt["nho"] % 4; cnt["nho"] += 1
            if m < 5:
                P.copy(ho[hb][:, 0:n], psm[pb][:, 0:n], reads=[("psm", pb)], writes=[("ho", hb)], eng="scalar")
                P.dma("gpsimd", hT[m * 128:(m + 1) * 128, t0:t0 + n], ho[hb][:, 0:n], reads=[("ho", hb)], chan=f"sto{hb}")
            elif m <= 10 and si < 4:
                rb = cnt["nr"] % 2; cnt["nr"] += 1
                P.copy(ho[hb][:, 0:n], psm[pb][:, 0:n], reads=[("psm", pb)], writes=[("ho", hb)], eng="scalar")
                def rope(hb=hb, rb=rb, m=m, t0=t0, n=n):
                    P.mm(psr[rb][:, 0:n], perm_s[:], ho[hb][:, 0:n], True, True, reads=["perm_s", ("ho", hb)], writes=[("psr", rb)])
                    P.tt(r1[rb][:, 0:n], ho[hb][:, 0:n], cos_s[:, t0:t0 + n], ALU.mult, reads=[("ho", hb), "cos_s"], writes=[("r1", rb)])
                    P.tt(r2[rb][:, 0:n], psr[rb][:, 0:n], sin_s[:, t0:t0 + n], ALU.mult, reads=[("psr", rb), "sin_s"], writes=[("r2", rb)])
                    P.tt(hob[hb][:, 0:n], r1[rb][:, 0:n], r2[rb][:, 0:n], ALU.add, reads=[("r1", rb), ("r2", rb)], writes=[("hob", hb)], eng="gpsimd")
                    P.dma("gpsimd", hTb[(m - 5) * 128:(m - 4) * 128, t0:t0 + n], hob[hb][:, 0:n], reads=[("hob", hb)], chan=f"stb{hb}")
                pending.append(rope)
            else:
                P.copy(hob[hb][:, 0:n], psm[pb][:, 0:n], reads=[("psm", pb)], writes=[("hob", hb)], eng="scalar")
                P.dma("gpsimd", hTb[(m - 5) * 128:(m - 4) * 128, t0:t0 + n], hob[hb][:, 0:n], reads=[("hob", hb)], chan=f"stb{hb}")
    pre(0)
    for si in range(len(SLABS)):
        if si + 1 < len(SLABS):
            pre(si + 1)
        main(si)
    flush()
    return P.finish()


def fnet_consts():
    n1 = np.arange(128); a = 2 * np.pi * np.outer(n1, n1) / 128
    cs128 = np.concatenate([np.cos(a), -np.sin(a)], 1).astype(np.float32)
    n2 = np.arange(64); a = 2 * np.pi * np.outer(n2, n2) / 64
    C64, S64 = np.cos(a), np.sin(a)
    fb1 = np.concatenate([C64, -S64], 1).astype(np.float32); fb2 = np.concatenate([S64, C64], 1).astype(np.float32)
    a = 2 * np.pi * np.outer(n2, n1) / 8192
    tw = np.concatenate([np.cos(a), np.sin(a)], 1).astype(np.float32)
    fc = (np.concatenate([C64, S64], 1) / np.sqrt(8192 * 64)).astype(np.float32)
    n = np.arange(256); a = 2 * np.pi * np.outer(n, n) / 256
    cs = np.concatenate([np.cos(a), -np.sin(a)], 1)
    cs256 = np.ascontiguousarray(cs.reshape(2, 128, 512).transpose(1, 0, 2)).astype(np.float32)
    fcc = (np.concatenate([C64, S64], 1) / np.sqrt(256 * 64)).astype(np.float32)
    return dict(cs128=cs128, fb1=fb1, fb2=fb2, tw=tw, fc=fc, cs256=cs256, fcc=fcc)

def build_fnet(with_ctx):
    P = Prog()
    z = P.dram_in("z", [128, 4096]); cs128 = P.dram_in("cs128", [128, 256])
    fb1 = P.dram_in("fb1", [64, 128]); fb2 = P.dram_in("fb2", [64, 128]); tw = P.dram_in("tw", [64, 256]); fc = P.dram_in("fc", [64, 128])
    R = P.dram_out("R", [64, 8192])
    zs = P.sbuf("zs", [128, 64, 64]); cs_s = P.sbuf("cs_s", [128, 256])
    fb1_s = P.sbuf("fb1_s", [64, 128], BF16); fb2_s = P.sbuf("fb2_s", [64, 128], BF16); tw_s = P.sbuf("tw_s", [64, 256]); fc_s = P.sbuf("fc_s", [64, 128], BF16)
    Ych = [P.sbuf(f"Ych{i}", [64, 8, 256]) for i in range(2)]
    ta = P.sbuf("ta", [64, 8, 128]); tb = P.sbuf("tb", [64, 8, 128]); tc = P.sbuf("tc", [64, 8, 128]); td = P.sbuf("td", [64, 8, 128])
    Yp = P.sbuf("Yp", [64, 64, 256], BF16); X1 = P.sbuf("X1", [64, 2, 8192], BF16)
    Rs = [P.sbuf(f"Rs{i}", [64, 512]) for i in range(2)]
    psA = [P.psum(f"psA{i}") for i in range(3)]; psB = [P.psum(f"psB{i}") for i in range(3)]; psC = [P.psum(f"psC{i}") for i in range(2)]
    P.dma("sync", zs[:].rearrange("p a b -> p (a b)"), z, writes=["zs"], chan="ld0")
    for (s, d, nm) in ((cs_s, cs128, "cs_s"), (tw_s, tw, "tw_s")):
        P.dma("sync", s[:], d, writes=[nm], chan="ld1")
    for (s, d, nm) in ((fb1_s, fb1, "fb1_s"), (fb2_s, fb2, "fb2_s"), (fc_s, fc, "fc_s")):
        P.dma("gpsimd", s[:], d, writes=[nm], chan="ldc")
    twr = tw_s[:, 0:128].rearrange("p (o k) -> p o k", o=1).to_broadcast([64, 8, 128])
    tws = tw_s[:, 128:256].rearrange("p (o k) -> p o k", o=1).to_broadcast([64, 8, 128])
    na_ = 0
    for ch in range(8):
        cb = ch % 2
        for pair in range(4):
            pb = na_ % 3; na_ += 1
            for h in range(2):
                c = ch * 8 + pair * 2 + h
                P.mm(psA[pb][0:64, h * 256:(h + 1) * 256], zs[:, :, c], cs_s[:], True, True, reads=["zs", "cs_s"], writes=[("psA", pb)])
            P.copy(Ych[cb][:, pair * 2:pair * 2 + 2, :].rearrange("p a b -> p (a b)"), psA[pb][0:64, :], reads=[("psA", pb)],
                   writes=[("Ych", cb)], eng="scalar")
        Yr = Ych[cb][:, :, 0:128]; Yi = Ych[cb][:, :, 128:256]; c0 = ch * 8
        P.tt(ta[:], Yr, twr, ALU.mult, reads=[("Ych", cb), "tw_s"], writes=["ta"])
        P.tt(tb[:], Yi, tws, ALU.mult, reads=[("Ych", cb), "tw_s"], writes=["tb"], eng="gpsimd")
        P.tt(Yp[:, c0:c0 + 8, 0:128], ta[:], tb[:], ALU.add, reads=["ta", "tb"], writes=[("Yp", ch)])
        P.tt(tc[:], Yi, twr, ALU.mult, reads=[("Ych", cb), "tw_s"], writes=["tc"], eng="gpsimd")
        P.tt(td[:], Yr, tws, ALU.mult, reads=[("Ych", cb), "tw_s"], writes=["td"])
        P.tt(Yp[:, c0:c0 + 8, 128:256], tc[:], td[:], ALU.subtract, reads=["tc", "td"], writes=[("Yp", ch)], eng="gpsimd")
    allYp = [("Yp", ch) for ch in range(8)]
    X1v = X1[:].rearrange("p c (k2 k1) -> p c k2 k1", k1=128)
    for g in range(32):
        pb = g % 3
        for q in range(4):
            k1 = 4 * g + q
            P.mm(psB[pb][0:64, q * 128:(q + 1) * 128], Yp[:, :, k1], fb1_s[:], True, False, reads=allYp + ["fb1_s"], writes=[("psB", pb)])
            P.mm(psB[pb][0:64, q * 128:(q + 1) * 128], Yp[:, :, 128 + k1], fb2_s[:], False, True, reads=allYp + ["fb2_s"], writes=[("psB", pb)])
        pv = psB[pb][0:64, :].rearrange("p (q c k) -> p c k q", q=4, c=2)
        for comp in range(2):
            P.copy(X1v[:, comp, :, 4 * g:4 * g + 4], pv[:, comp, :, :], reads=[("psB", pb)], writes=[("X1", g)],
                   eng=("scalar" if comp == 0 else "vector"))
    allX1 = [("X1", g) for g in range(32)]
    for blk in range(16):
        pb = blk % 2
        P.mm(psC[pb][0:64, :], fc_s[:, 0:64], X1[:, 0, blk * 512:(blk + 1) * 512], True, False, reads=allX1 + ["fc_s"], writes=[("psC", pb)])
        P.mm(psC[pb][0:64, :], fc_s[:, 64:128], X1[:, 1, blk * 512:(blk + 1) * 512], False, True, reads=allX1 + ["fc_s"], writes=[("psC", pb)])
        P.copy(Rs[pb][:], psC[pb][0:64, :], reads=[("psC", pb)], writes=[("Rs", pb)], eng="scalar")
        P.dma("gpsimd", R[:, blk * 512:(blk + 1) * 512], Rs[pb][:], reads=[("Rs", pb)], chan=f"st{pb}")
    if with_ctx:
        zc = P.dram_in("zc", [128, 2, 64]); cs256 = P.dram_in("cs256", [128, 2, 512]); fcc = P.dram_in("fcc", [64, 128])
        Rc = P.dram_out("Rc", [64, 256])
        zc_s = P.sbuf("zc_s", [128, 2, 64]); c2_s = P.sbuf("c2_s", [128, 2, 512]); fcc_s = P.sbuf("fcc_s", [64, 128])
        Pc = P.sbuf("Pc", [64, 512]); Rc_s = P.sbuf("Rc_s", [64, 256])
        P.dma("sync", zc_s[:], zc, writes=["zc_s"], chan="ld2"); P.dma("sync", c2_s[:], cs256, writes=["c2_s"], chan="ld2")
        P.dma("sync", fcc_s[:], fcc, writes=["fcc_s"], chan="ld2")
        for t in range(2):
            P.mm(psA[0][0:64, :], zc_s[:, t, :], c2_s[:, t, :], t == 0, t == 1, reads=["zc_s", "c2_s"], writes=[("psA", 0)])
        P.copy(Pc[:], psA[0][0:64, :], reads=[("psA", 0)], writes=["Pc"])
        P.mm(psA[1][0:64, 0:256], fcc_s[:, 0:64], Pc[:, 0:256], True, False, reads=["Pc", "fcc_s"], writes=[("psA", 1)])
        P.mm(psA[1][0:64, 0:256], fcc_s[:, 64:128], Pc[:, 256:512], False, True, reads=["Pc", "fcc_s"], writes=[("psA", 1)])
        P.copy(Rc_s[:], psA[1][0:64, 0:256], reads=[("psA", 1)], writes=["Rc_s"])
        P.dma("gpsimd", Rc, Rc_s[:], reads=["Rc_s"], chan="st")
    return P.finish()

NEG = -30000.0

def build_na(with_ctx):
    P = Prog()
    qT = P.dram_in("qT", [384, 2048]); kwT = P.dram_in("kwT", [16, 384, 576]); vw = P.dram_in("vw", [16, 128, 5, 384])
    kcT = P.dram_in("kcT", [384, 256]); vc = P.dram_in("vc", [128, 2, 384])
    tbraw = P.dram_in("tbraw", [5, 128, 6, 576]); mask = P.dram_in("mask", [5, 128, 576]); ident = P.dram_in("ident", [128, 128])
    Y = P.dram_out("Y", [2048, 384])
    qb = P.sbuf("qb", [128, 3, 2048], BF16); kcb = P.sbuf("kcb", [128, 3, 256], BF16); vcb = P.sbuf("vcb", [128, 2, 384], BF16)
    TB = P.sbuf("TB", [128, 5, 6, 576]); mk = P.sbuf("mk", [128, 5, 576])
    idb = P.sbuf("idb", [128, 128], BF16)
    kb = [P.sbuf(f"kb{i}", [128, 3, 576], BF16) for i in range(2)]; vb = [P.sbuf(f"vb{i}", [128, 5, 384], BF16) for i in range(2)]
    S = [P.sbuf(f"S{i}", [128, 832]) for i in range(4)]; Pb = [P.sbuf(f"Pb{i}", [128, 832], BF16) for i in range(4)]
    PT = [P.sbuf(f"PT{i}", [128, 896], BF16) for i in range(4)]
    Osb = [P.sbuf(f"Osb{i}", [128, 384]) for i in range(2)]
    mx = [P.sbuf(f"mx{i}", [128, 1]) for i in range(8)]; ssum = [P.sbuf(f"ssum{i}", [128, 1]) for i in range(8)]
    rinv = [P.sbuf(f"rinv{i}", [128, 1]) for i in range(8)]
    psA = [P.psum(f"psA{i}") for i in range(2)]; psB = [P.psum(f"psB{i}") for i in range(2)]
    psT = [P.psum(f"psT{i}", [128, 1024], BF16) for i in range(2)]; psO = [P.psum(f"psO{i}") for i in range(2)]
    si = [0]
    def load_cast(dst, src_ap, n, dres):
        P.dma("gpsimd", dst, src_ap, writes=[dres], chan="ldc", max_dma_last_dim=4096)
    qv = qT.rearrange("(c p) n -> p c n", p=128)
    for c in range(3):
        load_cast(qb[:, c, :], qv[:, c, :], 2048, "qb")
    kcv = kcT.rearrange("(c p) n -> p c n", p=128)
    for c in range(3):
        load_cast(kcb[:, c, :], kcv[:, c, :], 256, "kcb")
    load_cast(vcb[:].rearrange("p a b -> p (a b)"), vc.rearrange("p a b -> p (a b)"), 768, "vcb")
    load_cast(idb[:], ident, 128, "idb")
    for ty in range(5):
        P.dma("sync", TB[:, ty, :, :], tbraw[ty], writes=[("TB", ty)], chan="ld0")
    P.dma("sync", mk[:], mask.rearrange("t p n -> p t n"), writes=["mk"], chan="ld0")
    for ty in range(5):
        mb = mk[:, ty, :].rearrange("p (o n) -> p o n", o=1).to_broadcast([128, 6, 576])
        P.tt(TB[:, ty, :, :], TB[:, ty, :, :], mb, ALU.add, reads=[("TB", ty), "mk"], writes=[("TB", ty)])
    tiles = [("main", t) for t in range(16)]
    if with_ctx:
        qcT = P.dram_in("qcT", [384, 256]); Yc = P.dram_out("Yc", [256, 384])
        qcb = P.sbuf("qcb", [128, 3, 256], BF16)
        qcv = qcT.rearrange("(c p) n -> p c n", p=128)
        for c in range(3):
            load_cast(qcb[:, c, :], qcv[:, c, :], 256, "qcb")
        tiles += [("ctx", 0), ("ctx", 1)]
    units = [(ti, kind, t, h) for ti, (kind, t) in enumerate(tiles) for h in range(6)]
    loaded = set()
    def tile_load(ti, kind, t):
        if ti in loaded or kind != "main":
            return
        loaded.add(ti)
        wb = ti % 2
        P.dma("gpsimd", kb[wb][:], kwT[t].rearrange("(c p) n -> p c n", p=128), writes=[("kb", wb)], chan=f"kb{wb}", max_dma_last_dim=4096)
        P.dma("gpsimd", vb[wb][:], vw[t], writes=[("vb", wb)], chan=f"vb{wb}", max_dma_last_dim=4096)
    def info(u):
        ti, kind, t, h = units[u]
        return ti, kind, t, h, u % 2, u % 4, ti % 2, h // 2, (h % 2) * 64, u % 8
    def stA(u):
        ti, kind, t, h, b, b3, wb, c, p0, b8 = info(u)
        tile_load(ti, kind, t)
        if kind == "main":
            qs = qb[p0:p0 + 64, c, t * 128:(t + 1) * 128]; qres = "qb"
        else:
            qs = qcb[p0:p0 + 64, c, t * 128:(t + 1) * 128]; qres = "qcb"
        P.mm(psB[b][:, 64:320], qs, kcb[p0:p0 + 64, c, :], True, True, reads=[qres, "kcb"], writes=[("psB", b)])
        if kind == "main":
            P.mm(psA[b][:, 0:512], qs, kb[wb][p0:p0 + 64, c, 0:512], True, True, reads=[qres, ("kb", wb)], writes=[("psA", b)])
            P.mm(psB[b][:, 0:64], qs, kb[wb][p0:p0 + 64, c, 512:576], True, True, reads=[qres, ("kb", wb)], writes=[("psB", b)])
        P.act(S[b3][:, 0:256], psB[b][:, 64:320], AF.Copy, reads=[("psB", b)], writes=[("Sc", b3)], scale=0.125)
    def stB(u):
        ti, kind, t, h, b, b3, wb, c, p0, b8 = info(u)
        W = 832 if kind == "main" else 256
        if kind == "main":
            ty = {0: 0, 1: 1, 14: 3, 15: 4}.get(t, 2)
            P.stt(S[b3][:, 256:768], psA[b][:, 0:512], 0.125, TB[:, ty, h, 0:512], ALU.mult, ALU.add,
                  reads=[("psA", b), ("TB", ty)], writes=[("Sw", b3)])
            P.stt(S[b3][:, 768:832], psB[b][:, 0:64], 0.125, TB[:, ty, h, 512:576], ALU.mult, ALU.add,
                  reads=[("psB", b), ("TB", ty)], writes=[("Sw2", b3)])
        P.op("vector", lambda e: e.tensor_reduce(out=mx[b8][:], in_=S[b3][:, 0:W], axis=AX.X, op=ALU.max, negate=True),
             reads=[("Sc", b3), ("Sw", b3), ("Sw2", b3)], writes=[("mx", b8)])
        P.act(Pb[b3][:, 0:W], S[b3][:, 0:W], AF.Exp, reads=[("Sc", b3), ("Sw", b3), ("Sw2", b3), ("mx", b8)], writes=[("Pb", b3), ("ssum", b8)],
              bias=mx[b8][:], scale=1.0, accum_out=ssum[b8][:])
    def stC(u):
        ti, kind, t, h, b, b3, wb, c, p0, b8 = info(u)
        nblk = 7 if kind == "main" else 2
        for kbk in range(nblk):
            kw = 64 if kbk == 6 else 128
            P.op("tensor", lambda e, kbk=kbk, kw=kw: e.transpose(psT[b][0:kw, kbk * 128:(kbk + 1) * 128], Pb[b3][:, kbk * 128:kbk * 128 + kw], idb[:]),
                 reads=[("Pb", b3), "idb"], writes=[("psT", b)], nosync_same=True)
        P.copy(PT[b3][:, 0:nblk * 128], psT[b][:, 0:nblk * 128], reads=[("psT", b)], writes=[("PT", b3)], eng="scalar")
    def stD(u):
        ti, kind, t, h, b, b3, wb, c, p0, b8 = info(u)
        ob = ti % 2
        nblk = 7 if kind == "main" else 2
        for kbk in range(nblk):
            kw = 64 if kbk == 6 else 128
            if kbk < 2:
                rhs = vcb[:, kbk, h * 64:(h + 1) * 64]; rres = "vcb"
            else:
                rhs = vb[wb][0:kw, kbk - 2, h * 64:(h + 1) * 64]; rres = ("vb", wb)
            P.mm(psO[ob][:, h * 64:(h + 1) * 64], PT[b3][0:kw, kbk * 128:(kbk + 1) * 128], rhs, kbk == 0, kbk == nblk - 1,
                 reads=[("PT", b3), rres], writes=[("psO", ob)])
        P.op("vector", lambda e: e.reciprocal(out=rinv[b8][:], in_=ssum[b8][:]), reads=[("ssum", b8)], writes=[("rinv", b8)])
        P.ts(Osb[ob][:, h * 64:(h + 1) * 64], psO[ob][:, h * 64:(h + 1) * 64], rinv[b8][:], None, ALU.mult,
             reads=[("psO", ob), ("rinv", b8)], writes=[("Osb", ob)])
        if h == 5:
            dst = Y[t * 128:(t + 1) * 128, :] if kind == "main" else Yc[t * 128:(t + 1) * 128, :]
            P.dma("gpsimd", dst, Osb[ob][:], reads=[("Osb", ob)], chan=f"st{ob}")
    NU = len(units)
    LC, LD = 3, 5
    for step in range(NU + LD):
        if step < NU: stA(step)
        if 0 <= step - 1 < NU: stB(step - 1)
        if 0 <= step - LC < NU: stC(step - LC)
        if 0 <= step - LD < NU: stD(step - LD)
    return P.finish()

def na_tile_geometry(r0):
    R = 128
    rs0 = int(np.clip(r0 - 4, 0, R - 8)); rs1 = int(np.clip(r0 + 1 - 4, 0, R - 8))
    return rs0, rs1

def na_tables(rpb, q):
    cols = np.arange(64); cs = np.clip(cols - 8, 0, 48)
    kc = np.arange(64)
    inwin = (kc[None, :] >= cs[:, None]) & (kc[None, :] < cs[:, None] + 16)
    dc = np.clip(kc[None, :] - cols[:, None] + 15, 0, 30)
    types = [32 * q, 32 * q + 2, 32 * q + 16, 32 * q + 28, 32 * q + 30]
    tbraw = np.zeros((5, 128, 6, 9, 64), np.float32); mask = np.zeros((5, 128, 9, 64), np.float32)
    for ti, r0 in enumerate(types):
        rs0, rs1 = na_tile_geometry(r0)
        for half, (r, rs) in enumerate(((r0, rs0), (r0 + 1, rs1))):
            for slot in range(9):
                krow = rs0 + slot
                valid = (krow >= rs) and (krow < rs + 8)
                dr = int(np.clip(krow - r + 7, 0, 14))
                g = rpb[:, dr][:, dc]
                tbraw[ti, half * 64:(half + 1) * 64, :, slot, :] = g.transpose(1, 0, 2)
                m = np.where(inwin & valid, 0.0, NEG).astype(np.float32)
                mask[ti, half * 64:(half + 1) * 64, slot, :] = m
    return tbraw.reshape(5, 128, 6, 576), mask.reshape(5, 128, 576)

def na_windows(k_b, v_b, q):
    kp = np.concatenate([k_b, np.zeros((64 * 16, 384), k_b.dtype)], 0); vp = np.concatenate([v_b, np.zeros((64 * 16, 384), v_b.dtype)], 0)
    kwT = np.zeros((16, 384, 576), k_b.dtype); vw = np.zeros((16, 128, 5, 384), v_b.dtype)
    for t in range(16):
        r0 = 32 * q + 2 * t
        rs0, _ = na_tile_geometry(r0)
        kwT[t] = kp[rs0 * 64:(rs0 + 9) * 64].T
        for j in range(4):
            vw[t, :, j, :] = vp[(rs0 + 2 * j) * 64:(rs0 + 2 * j + 2) * 64]
        vw[t, 0:64, 4, :] = vp[(rs0 + 8) * 64:(rs0 + 9) * 64]
    return kwT, vw

NCH = 1056
PI = math.pi

class A:
    def __init__(self, P): self.P = P
    @staticmethod
    def nm(*aps): return [a.tensor.name for a in aps if hasattr(a, "tensor")]
    def tt(self, o, a, b, op, eng="vector"): self.P.tt(o, a, b, op, reads=self.nm(a, b), writes=self.nm(o), eng=eng)
    def ts(self, o, a, s1, op0, s2=None, op1=None, eng="vector"):
        self.P.ts(o, a, s1, s2, op0, op1, reads=self.nm(a, s1, s2), writes=self.nm(o), eng=eng)
    def stt(self, o, a, s, b, op0, op1): self.P.stt(o, a, s, b, op0, op1, reads=self.nm(a, s, b), writes=self.nm(o))
    def act(self, o, a, f, **kw): self.P.act(o, a, f, reads=self.nm(a, *[v for v in kw.values()]), writes=self.nm(o), **kw)
    def copy(self, o, a, eng="vector"): self.P.copy(o, a, reads=self.nm(a), writes=self.nm(o), eng=eng)
    def memset(self, o, v, eng="vector"): self.P.memset(o, v, writes=self.nm(o), eng=eng)
    def mm(self, o, l, r, st, sp): self.P.mm(o, l, r, st, sp, reads=self.nm(l, r), writes=self.nm(o))
    def dma_in(self, o, src, chan): self.P.dma("sync", o, src, writes=self.nm(o), chan=chan)
    def dma_out(self, dst, a, chan="st"): self.P.dma("gpsimd", dst, a, reads=self.nm(a), chan=chan)
    def scan(self, o, d0, d1, init):
        self.P.op("vector", lambda e: e.tensor_tensor_scan(out=o, data0=d0, data1=d1, initial=init, op0=ALU.mult, op1=ALU.add),
                  reads=self.nm(d0, d1, init), writes=self.nm(o))
    def recip(self, o, a): self.P.op("vector", lambda e: e.reciprocal(out=o, in_=a), reads=self.nm(a), writes=self.nm(o))
    def transpose(self, o, a, ident):
        self.P.op("tensor", lambda e: e.transpose(o, a, ident), reads=self.nm(a, ident), writes=self.nm(o), nosync_same=True)
    def cmul_s(self, o_re, o_im, a_re, a_im, s_re, s_im, s_imn):
        self.ts(o_re, a_re, s_re, ALU.mult)
        self.stt(o_re, a_im, s_imn, o_re, ALU.mult, ALU.add)
        self.ts(o_im, a_re, s_im, ALU.mult)
        self.stt(o_im, a_im, s_re, o_im, ALU.mult, ALU.add)

def build_ssm():
    P = Prog(); a = A(P)
    d_in = {}
    for nm_, shp in (("are", [128, 6]), ("aim", [128, 6]), ("ldt", [128, 6]), ("Bre", [128, 96]), ("Bim", [128, 96]),
                     ("Cre", [128, 96]), ("Cim", [128, 96]), ("maskF", [128, 128]), ("maskB", [128, 128]), ("sgn", [128, 1]),
                     ("ident", [128, 128])):
        d_in[nm_] = P.dram_in(nm_, shp)
    Ddiag = P.dram_in("Ddiag", [6, 128, 128]); U = P.dram_in("U", [6, 128, NCH]); Yg = P.dram_out("Yg", [6, 128, NCH])
    s = {}
    for nm_, ap in d_in.items():
        shp = list(ap.shape)
        s[nm_] = P.sbuf("s_" + nm_, shp)
        a.dma_in(s[nm_][:], ap, "ld0")
    def T(name, shape): return P.sbuf(name, shape)
    dt = T("dt", [128, 6]); x = T("x", [128, 6]); th = T("th", [128, 6]); er = T("er", [128, 6]); m = T("m", [128, 6])
    y2 = T("y2", [128, 6]); sn = T("sn", [128, 6]); cs = T("cs", [128, 6]); lbr = T("lbr", [128, 6]); lbi = T("lbi", [128, 6])
    n2 = T("n2", [128, 6]); t1 = T("t1", [128, 6]); t2 = T("t2", [128, 6]); am1 = T("am1", [128, 6])
    qr = T("qr", [128, 6]); qi = T("qi", [128, 6]); qin = T("qin", [128, 6])
    a.act(dt[:], s["ldt"][:], AF.Exp)
    a.tt(x[:], s["are"][:], dt[:], ALU.mult); a.tt(th[:], s["aim"][:], dt[:], ALU.mult)
    a.act(er[:], x[:], AF.Exp)
    for _ in range(4):
        a.ts(m[:], th[:], PI, ALU.is_gt)
        a.stt(th[:], m[:], -2 * PI, th[:], ALU.mult, ALU.add)
    a.ts(y2[:], th[:], PI / 2, ALU.add)
    a.ts(m[:], y2[:], PI, ALU.is_gt)
    a.stt(y2[:], m[:], -2 * PI, y2[:], ALU.mult, ALU.add)
    a.act(sn[:], th[:], AF.Sin); a.act(cs[:], y2[:], AF.Sin)
    a.tt(lbr[:], er[:], cs[:], ALU.mult); a.tt(lbi[:], er[:], sn[:], ALU.mult)
    a.tt(n2[:], s["are"][:], s["are"][:], ALU.mult); a.tt(t1[:], s["aim"][:], s["aim"][:], ALU.mult); a.tt(n2[:], n2[:], t1[:], ALU.add)
    a.recip(n2[:], n2[:])
    a.ts(am1[:], lbr[:], -1.0, ALU.add)
    a.tt(t1[:], am1[:], s["are"][:], ALU.mult); a.tt(t2[:], lbi[:], s["aim"][:], ALU.mult); a.tt(t1[:], t1[:], t2[:], ALU.add)
    a.tt(qr[:], t1[:], n2[:], ALU.mult)
    a.tt(t1[:], lbi[:], s["are"][:], ALU.mult); a.tt(t2[:], am1[:], s["aim"][:], ALU.mult); a.tt(t1[:], t1[:], t2[:], ALU.subtract)
    a.tt(qi[:], t1[:], n2[:], ALU.mult)
    a.ts(qin[:], qi[:], -1.0, ALU.mult)
    Lr = T("Lr", [128, 6, 9]); Li = T("Li", [128, 6, 9]); Vr = T("Vr", [128, 6, 8]); Vi = T("Vi", [128, 6, 8])
    Rr = T("Rr", [128, 6, 9]); Ri = T("Ri", [128, 6, 9])
    e2 = T("e2", [128, 6]); ivr = T("ivr", [128, 6]); ivi = T("ivi", [128, 6])
    a.memset(Lr[:, :, 0], 1.0); a.memset(Li[:, :, 0], 0.0); a.memset(Vr[:, :, 0], 1.0); a.memset(Vi[:, :, 0], 0.0)
    a.act(e2[:], x[:], AF.Exp, scale=-2.0)
    a.tt(ivr[:], lbr[:], e2[:], ALU.mult); a.tt(ivi[:], lbi[:], e2[:], ALU.mult); a.ts(ivi[:], ivi[:], -1.0, ALU.mult)
    def cmul_t(o_r, o_i, p_r, p_i, q_r, q_i):
        a.tt(t1[:], p_r, q_r, ALU.mult); a.tt(t2[:], p_i, q_i, ALU.mult); a.tt(o_r, t1[:], t2[:], ALU.subtract)
        a.tt(t1[:], p_r, q_i, ALU.mult); a.tt(t2[:], p_i, q_r, ALU.mult); a.tt(o_i, t1[:], t2[:], ALU.add)
    for k in range(8):
        cmul_t(Lr[:, :, k + 1], Li[:, :, k + 1], Lr[:, :, k], Li[:, :, k], lbr[:], lbi[:])
    for k in range(7):
        cmul_t(Vr[:, :, k + 1], Vi[:, :, k + 1], Vr[:, :, k], Vi[:, :, k], ivr[:], ivi[:])
    for k in range(9):
        a.copy(Rr[:, :, k], Lr[:, :, 8 - k], eng="gpsimd"); a.copy(Ri[:, :, k], Li[:, :, 8 - k], eng="gpsimd")
    tabs = {}
    for nm_, (lo_r, lo_i, hi_r, hi_i) in dict(
            XL=(Vr[0:64, :, 0:8], Vi[0:64, :, 0:8], Lr[64:128, :, 0:8], Li[64:128, :, 0:8]),
            YL=(Lr[0:64, :, 0:8], Li[0:64, :, 0:8], Vr[64:128, :, 0:8], Vi[64:128, :, 0:8]),
            SL=(Rr[0:64, :, 1:9], Ri[0:64, :, 1:9], Lr[64:128, :, 0:8], Li[64:128, :, 0:8]),
            OL=(Lr[0:64, :, 1:9], Li[0:64, :, 1:9], Rr[64:128, :, 0:8], Ri[64:128, :, 0:8])).items():
        tr = T(nm_ + "r", [128, 6, 8]); ti = T(nm_ + "i", [128, 6, 8]); tn = T(nm_ + "n", [128, 6, 8])
        a.copy(tr[0:64], lo_r); a.copy(ti[0:64], lo_i); a.copy(tr[64:128], hi_r); a.copy(ti[64:128], hi_i)
        a.ts(tn[:], ti[:], -1.0, ALU.mult)
        tabs[nm_] = (tr, ti, tn)
    rho8 = T("rho8", [128, 6]); c8 = T("c8", [128, 6]); s8 = T("s8", [128, 6]); e8 = T("e8", [128, 6])
    a.act(rho8[:], x[:], AF.Exp, scale=8.0); a.act(e8[:], x[:], AF.Exp, scale=-8.0)
    a.tt(c8[:], Lr[:, :, 8], e8[:], ALU.mult); a.tt(s8[:], Li[:, :, 8], e8[:], ALU.mult)
    a.ts(s8[:], s8[:], s["sgn"][:, 0:1], ALU.mult)
    onesT = T("onesT", [128, NCH]); a.memset(onesT[:], 1.0, eng="gpsimd")
    Bbr = T("Bbr", [128, 16]); Bbi = T("Bbi", [128, 16])
    Xr = T("Xr", [128, 8, 16]); Xi = T("Xi", [128, 8, 16]); Yr = T("Yr", [128, 8, 16]); Yin = T("Yin", [128, 8, 16])
    Wtr = T("Wtr", [128, 8, 16]); Wti = T("Wti", [128, 8, 16]); Wor = T("Wor", [128, 8, 16]); Woin = T("Woin", [128, 8, 16])
    Wsr = T("Wsr", [128, 128]); Wsi = T("Wsi", [128, 128]); Msb = T("Msb", [128, 128]); Mtmp = T("Mtmp", [128, 128]); Dd = T("Dd", [128, 128])
    Us = [T(f"Us{i}", [128, NCH]) for i in range(2)]
    Sre = T("Sre", [128, NCH]); Sim = T("Sim", [128, NCH]); Spr = T("Spr", [128, NCH]); Spi = T("Spi", [128, NCH])
    Gre = T("Gre", [128, NCH]); Gim = T("Gim", [128, NCH]); Hor = T("Hor", [128, NCH]); Hoi = T("Hoi", [128, NCH])
    Hir = T("Hir", [128, NCH]); Hii = T("Hii", [128, NCH])
    Tr = T("Tr", [128, NCH + 1]); Ti = T("Ti", [128, NCH + 1]); rhoT = T("rhoT", [128, NCH])
    w1 = T("w1", [128, NCH]); w2 = T("w2", [128, NCH])
    mult = T("mult", [128, 11, 3]); ini = T("ini", [128, 4]); Ysb = T("Ysb", [128, NCH])
    ps = [P.psum(f"ps{i}") for i in range(8)]
    BLK = [(0, 512), (512, 512), (1024, NCH - 1024)]
    for gi in range(6):
        ub = gi % 2
        a.dma_in(Us[ub][:], U[gi], f"u{ub}")
        a.dma_in(Dd[:], Ddiag[gi], "dd")
        g16 = slice(gi * 16, gi * 16 + 16)
        a.cmul_s(Bbr[:], Bbi[:], s["Bre"][:, g16], s["Bim"][:, g16], qr[:, gi:gi + 1], qi[:, gi:gi + 1], qin[:, gi:gi + 1])
        XL, YL, SL, OL = tabs["XL"], tabs["YL"], tabs["SL"], tabs["OL"]
        for k in range(8):
            sc = lambda tb: (tb[0][:, gi, k:k + 1], tb[1][:, gi, k:k + 1], tb[2][:, gi, k:k + 1])
            a.cmul_s(Xr[:, k, :], Xi[:, k, :], Bbr[:], Bbi[:], *sc(XL))
            a.cmul_s(Wtr[:, k, :], Wti[:, k, :], Bbr[:], Bbi[:], *sc(SL))
            a.cmul_s(Yr[:, k, :], Yin[:, k, :], s["Cre"][:, g16], s["Cim"][:, g16], *sc(YL))
            a.cmul_s(Wor[:, k, :], Woin[:, k, :], s["Cre"][:, g16], s["Cim"][:, g16], *sc(OL))
        a.ts(Yin[:], Yin[:], -1.0, ALU.mult); a.ts(Woin[:], Woin[:], -1.0, ALU.mult)
        f2 = lambda t_: t_[:].rearrange("p a b -> p (a b)")
        for half, pb in ((0, 6), (1, 7)):
            rows = slice(half * 64, half * 64 + 64)
            a.mm(ps[pb][:, 0:128], f2(Xr)[rows], f2(Yr)[rows], True, False)
            a.mm(ps[pb][:, 0:128], f2(Xi)[rows], f2(Yin)[rows], False, True)
        a.tt(Msb[:], ps[6][:, 0:128], s["maskF"][:], ALU.mult)
        a.tt(Mtmp[:], ps[7][:, 0:128], s["maskB"][:], ALU.mult)
        a.tt(Msb[:], Msb[:], Mtmp[:], ALU.add, eng="gpsimd"); a.tt(Msb[:], Msb[:], Dd[:], ALU.add, eng="gpsimd")
        a.transpose(ps[6][:, 128:256], f2(Wtr), s["ident"][:]); a.transpose(ps[7][:, 128:256], f2(Wti), s["ident"][:])
        a.copy(Wsr[:], ps[6][:, 128:256], eng="scalar"); a.copy(Wsi[:], ps[7][:, 128:256], eng="scalar")
        for bi, (c0, cn) in enumerate(BLK):
            a.mm(ps[bi][:, 0:cn], Wsr[:], Us[ub][:, c0:c0 + cn], True, True)
            a.mm(ps[3 + bi][:, 0:cn], Wsi[:], Us[ub][:, c0:c0 + cn], True, True)
            a.copy(Sre[:, c0:c0 + cn], ps[bi][:, 0:cn], eng="scalar"); a.copy(Sim[:, c0:c0 + cn], ps[3 + bi][:, 0:cn], eng="scalar")
        a.memset(Tr[:, 0:1], 1.0); a.memset(Ti[:, 0:1], 0.0)
        a.copy(mult[:, 0, 0:1], c8[:, gi:gi + 1]); a.copy(mult[:, 0, 1:2], s8[:, gi:gi + 1])
        a.ts(mult[:, 0, 2:3], mult[:, 0, 1:2], -1.0, ALU.mult)
        for k in range(1, 11):
            a.tt(ini[:, 0:1], mult[:, k - 1, 0:1], mult[:, k - 1, 0:1], ALU.mult); a.tt(ini[:, 1:2], mult[:, k - 1, 1:2], mult[:, k - 1, 1:2], ALU.mult)
            a.tt(mult[:, k, 0:1], ini[:, 0:1], ini[:, 1:2], ALU.subtract)
            a.tt(ini[:, 0:1], mult[:, k - 1, 0:1], mult[:, k - 1, 1:2], ALU.mult)
            a.ts(mult[:, k, 1:2], ini[:, 0:1], 2.0, ALU.mult); a.ts(mult[:, k, 2:3], ini[:, 0:1], -2.0, ALU.mult)
        for k in range(11):
            n = 1 << k
            cnt = min(n, NCH + 1 - n)
            a.cmul_s(Tr[:, n:n + cnt], Ti[:, n:n + cnt], Tr[:, 0:cnt], Ti[:, 0:cnt], mult[:, k, 0:1], mult[:, k, 1:2], mult[:, k, 2:3])
        a.ts(rhoT[:], onesT[:], rho8[:, gi:gi + 1], ALU.mult, eng="gpsimd")
        a.tt(w1[:], Sre[:], Tr[:, 0:NCH], ALU.mult); a.tt(w2[:], Sim[:], Ti[:, 0:NCH], ALU.mult, eng="gpsimd")
        a.tt(Spr[:], w1[:], w2[:], ALU.subtract)
        a.tt(w1[:], Sre[:], Ti[:, 0:NCH], ALU.mult); a.tt(w2[:], Sim[:], Tr[:, 0:NCH], ALU.mult, eng="gpsimd")
        a.tt(Spi[:], w1[:], w2[:], ALU.add)
        for (Gx, Sx) in ((Gre, Spr), (Gim, Spi)):
            a.scan(Gx[0:64, :], rhoT[0:64, :], Sx[0:64, :], 0.0)
            a.scan(Gx[64:128, 0:32][:, ::-1], rhoT[64:128, 0:32], Sx[64:128, 0:32][:, ::-1], 0.0)
        lo = slice(64, 128)
        a.tt(ini[lo, 0:1], Gre[lo, 0:1], Tr[lo, NCH:NCH + 1], ALU.mult); a.tt(ini[lo, 1:2], Gim[lo, 0:1], Ti[lo, NCH:NCH + 1], ALU.mult)
        a.tt(ini[lo, 2:3], ini[lo, 0:1], ini[lo, 1:2], ALU.subtract)
        a.tt(ini[lo, 0:1], Gre[lo, 0:1], Ti[lo, NCH:NCH + 1], ALU.mult); a.tt(ini[lo, 1:2], Gim[lo, 0:1], Tr[lo, NCH:NCH + 1], ALU.mult)
        a.tt(ini[lo, 3:4], ini[lo, 0:1], ini[lo, 1:2], ALU.add)
        a.scan(Gre[lo, 32:NCH][:, ::-1], rhoT[lo, 32:NCH], Spr[lo, 32:NCH][:, ::-1], ini[lo, 2:3])
        a.scan(Gim[lo, 32:NCH][:, ::-1], rhoT[lo, 32:NCH], Spi[lo, 32:NCH][:, ::-1], ini[lo, 3:4])
        a.tt(w1[:], Gre[:], Tr[:, 0:NCH], ALU.mult); a.tt(w2[:], Gim[:], Ti[:, 0:NCH], ALU.mult, eng="gpsimd")
        a.tt(Hor[:], w1[:], w2[:], ALU.add)
        a.tt(w1[:], Gim[:], Tr[:, 0:NCH], ALU.mult); a.tt(w2[:], Gre[:], Ti[:, 0:NCH], ALU.mult, eng="gpsimd")
        a.tt(Hoi[:], w1[:], w2[:], ALU.subtract)
        for (Hi_, Ho_, Gx) in ((Hir, Hor, Gre), (Hii, Hoi, Gim)):
            a.copy(Hi_[0:64, 1:NCH], Ho_[0:64, 0:NCH - 1], eng="scalar"); a.memset(Hi_[0:64, 0:1], 0.0)
            a.copy(Hi_[lo, 0:NCH - 1], Ho_[lo, 1:NCH], eng="scalar"); a.memset(Hi_[lo, 31:32], 0.0)
            a.copy(Hi_[lo, NCH - 1:NCH], Gx[lo, 0:1])
        for bi, (c0, cn) in enumerate(BLK):
            a.mm(ps[bi][:, 0:cn], Msb[:], Us[ub][:, c0:c0 + cn], True, False)
            a.mm(ps[bi][:, 0:cn], f2(Wor), Hir[:, c0:c0 + cn], False, False)
            a.mm(ps[bi][:, 0:cn], f2(Woin), Hii[:, c0:c0 + cn], False, True)
            a.copy(Ysb[:, c0:c0 + cn], ps[bi][:, 0:cn], eng="scalar")
        a.dma_out(Yg[gi], Ysb[:])
    return P.finish()

def ssm_inputs(inp, l, j4, u_b, uc_b):
    gs = np.arange(6 * j4, 6 * j4 + 6)
    def rows(arr):
        return np.ascontiguousarray(arr[:, gs, :].transpose(0, 2, 1).reshape(128, 6))
    are = rows(inp["ssm_a_re"][l]); aim = rows(inp["ssm_a_im"][l])
    ldt = np.ascontiguousarray(np.repeat(inp["ssm_log_dt"][l][:, gs][:, None, :], 64, axis=1).reshape(128, 6))
    def rowsB(arr):
        return np.ascontiguousarray(arr[:, gs].transpose(0, 2, 1, 3).reshape(128, 96))
    def rowsC(arr):
        return np.ascontiguousarray(arr[:, gs].transpose(0, 3, 1, 2).reshape(128, 96))
    s_ = np.arange(8)
    mF = (s_[None, :] >= s_[:, None]).astype(np.float32)
    maskF = np.kron(mF, np.ones((16, 16), np.float32)); maskB = np.kron(mF.T, np.ones((16, 16), np.float32))
    sgn = np.concatenate([-np.ones((64, 1), np.float32), np.ones((64, 1), np.float32)], 0)
    dsk = inp["ssm_d"][l]
    Dd = np.zeros((6, 128, 128), np.float32)
    for gi, g in enumerate(gs):
        dd = np.zeros((8, 16, 8, 16), np.float32)
        for t in range(8):
            dd[t, np.arange(16), t, np.arange(16)] = dsk[16 * g:16 * g + 16]
        Dd[gi] = dd.reshape(128, 128)
    seq = np.concatenate([uc_b, u_b], 0)
    U = np.zeros((6, 128, NCH), np.float32)
    for gi, g in enumerate(gs):
        U[gi] = seq[:, 16 * g:16 * g + 16].reshape(NCH, 128).T
    return dict(are=are, aim=aim, ldt=ldt, Bre=rowsB(inp["ssm_b_re"][l]), Bim=rowsB(inp["ssm_b_im"][l]),
                Cre=rowsC(inp["ssm_c_re"][l]), Cim=rowsC(inp["ssm_c_im"][l]), maskF=maskF, maskB=maskB, sgn=sgn,
                ident=np.eye(128, dtype=np.float32), Ddiag=Dd, U=U)

def ssm_unpack(Yg):
    return np.ascontiguousarray(Yg.transpose(2, 1, 0).reshape(NCH, 8, 16, 6).transpose(0, 1, 3, 2).reshape(NCH * 8, 96))


_PROGS = {}
def _prog(name, fn):
    if name not in _PROGS:
        _PROGS[name] = fn()
    return _PROGS[name]

def _run(nc, maps):
    res = run_bass_kernel_spmd(nc, maps, core_ids=list(range(8)))
    return res.results

def kernel(x, c, ctx, c_ctx, w_mod, b_mod, g_pre_mix, g_post_mix, w_in, ssm_a_re, ssm_a_im, ssm_log_dt, ssm_b_re, ssm_b_im,
           ssm_c_re, ssm_c_im, ssm_d, w_glu, w_fourier, na_rpb, w_out, g_pre_ffn, g_post_ffn, w_ffn_gate, w_ffn_up, w_ffn_down):
    f32 = lambda a: np.ascontiguousarray(np.asarray(a, dtype=np.float32))
    inp = dict(ssm_a_re=f32(ssm_a_re), ssm_a_im=f32(ssm_a_im), ssm_log_dt=f32(ssm_log_dt), ssm_b_re=f32(ssm_b_re), ssm_b_im=f32(ssm_b_im),
               ssm_c_re=f32(ssm_c_re), ssm_c_im=f32(ssm_c_im), ssm_d=f32(ssm_d))
    x = f32(x); c = f32(c); ctx = f32(ctx); c_ctx = f32(c_ctx); w_mod = f32(w_mod); b_mod = f32(b_mod)
    w_in = f32(w_in); w_glu = f32(w_glu); w_fourier = f32(w_fourier); na_rpb = f32(na_rpb); w_out = f32(w_out)
    g_pre_mix = f32(g_pre_mix); g_post_mix = f32(g_post_mix); g_pre_ffn = f32(g_pre_ffn); g_post_ffn = f32(g_post_ffn)
    w_ffn_gate = f32(w_ffn_gate); w_ffn_up = f32(w_ffn_up); w_ffn_down = f32(w_ffn_down)
    DEPTH = 2
    cores = [(k // 4, k % 4) for k in range(8)]
    cTs = [np.ascontiguousarray(np.concatenate([colT(c[b], 8), colT(c_ctx, 8)], axis=1)) for b in range(2)]
    xT = [np.ascontiguousarray(np.concatenate([x[b, q * 2048:(q + 1) * 2048].T, ctx[b].T], axis=1)) for (b, q) in cores]
    KF = fnet_consts(); permm = perm_matrix(); ident = np.eye(128, dtype=np.float32)
    ropes = [rope_tables(q * 2048, 2048) for q in range(4)]
    for l in range(DEPTH):
        maps = []
        for k, (b, q) in enumerate(cores):
            maps.append(dict(xT=xT[k], w_in=w_in[l], w_mod=np.ascontiguousarray(w_mod[l][:, 0:2048]), b_modT=colT(b_mod[l][0:2048], 16),
                             g_preT=colT(g_pre_mix[l], 8), cT=cTs[b], cos=ropes[q][0], sin=ropes[q][1], perm=permm))
        res = _run(_prog("l1", build_l1), maps)
        hfull = [np.concatenate([res[k]["hT"], res[k]["hTb"]], 0) for k in range(8)]
        h_lat = [np.concatenate([hfull[4 * b + q][:, 0:2048].T for q in range(4)], 0) for b in range(2)]
        h_ctx = [np.ascontiguousarray(hfull[4 * b][:, 2048:2304].T) for b in range(2)]
        del hfull
        del res
        maps = [ssm_inputs(inp, l, j4, h_lat[b][:, 0:384], h_ctx[b][:, 0:384]) for (b, j4) in cores]
        res = _run(_prog("ssm", build_ssm), maps)
        ys = [[ssm_unpack(res[4 * b + j4]["Yg"]) for j4 in range(4)] for b in range(2)]
        ysT = [np.ascontiguousarray(np.concatenate(ys[b], 1).T) for b in range(2)]
        del res, ys
        maps = []
        for (b, g) in cores:
            m = dict(z=np.ascontiguousarray(h_lat[b][:, 384 + 64 * g:448 + 64 * g].reshape(128, 4096)),
                     zc=np.ascontiguousarray(h_ctx[b][:, 384 + 64 * g:448 + 64 * g].reshape(2, 128, 64).transpose(1, 0, 2)))
            m.update(KF); maps.append(m)
        res = _run(_prog("fnet", lambda: build_fnet(True)), maps)
        mxT = [np.concatenate([res[4 * b + g]["R"] for g in range(4)], 0) for b in range(2)]
        mxcT = [np.concatenate([res[4 * b + g]["Rc"] for g in range(4)], 0) for b in range(2)]
        del res
        maps = []
        for (b, q) in cores:
            kwT, vw = na_windows(h_lat[b][:, 1024:1408], h_lat[b][:, 1408:1792], q)
            tbraw, mask = na_tables(na_rpb[l], q)
            maps.append(dict(qT=np.ascontiguousarray(h_lat[b][q * 2048:(q + 1) * 2048, 640:1024].T), kwT=kwT, vw=vw,
                             kcT=np.ascontiguousarray(h_ctx[b][:, 1024:1408].T),
                             vc=np.ascontiguousarray(h_ctx[b][:, 1408:1792].reshape(2, 128, 384).transpose(1, 0, 2)),
                             tbraw=tbraw, mask=mask, ident=ident, qcT=np.ascontiguousarray(h_ctx[b][:, 640:1024].T)))
        res = _run(_prog("na", lambda: build_na(True)), maps)
        naT = [np.ascontiguousarray(np.concatenate([res[4 * b + q]["Y"] for q in range(4)], 0).T) for b in range(2)]
        nacT = [np.ascontiguousarray(res[4 * b]["Yc"].T) for b in range(2)]
        del res, h_lat
        maps = []
        for k, (b, q) in enumerate(cores):
            sl = slice(q * 2048, (q + 1) * 2048)
            maps.append(dict(xT=xT[k], ysT=np.ascontiguousarray(np.concatenate([ysT[b][:, 256 + q * 2048:256 + (q + 1) * 2048], ysT[b][:, 0:256]], 1)),
                             mxT=np.ascontiguousarray(np.concatenate([mxT[b][:, sl], mxcT[b]], 1)),
                             naT=np.ascontiguousarray(np.concatenate([naT[b][:, sl], nacT[b]], 1)),
                             w_mod=np.ascontiguousarray(w_mod[l][:, 2048:3072]), b_modT=colT(b_mod[l][2048:3072], 8), cT=cTs[b],
                             g_postT=colT(g_post_mix[l], 8), w_glu=w_glu[l], w_fourier=w_fourier[l], w_out=w_out[l]))
        res = _run(_prog("l3a", lambda: build_l3a(True)), maps)
        xT = [res[k]["xoT"] for k in range(8)]
        del res
        maps = []
        for k, (b, q) in enumerate(cores):
            maps.append(dict(xT=xT[k], w_mod=np.ascontiguousarray(w_mod[l][:, 3072:6144]), b_modT=colT(b_mod[l][3072:6144], 24), cT=cTs[b],
                             g_preT=colT(g_pre_ffn[l], 8), g_postT=colT(g_post_ffn[l], 8),
                             w_gate=w_ffn_gate[l], w_up=w_ffn_up[l], w_down=w_ffn_down[l]))
        res = _run(_prog("l3b", lambda: build_l3b(True)), maps)
        xT = [np.ascontiguousarray(res[k]["xoT"]) for k in range(8)]
        del res
    out = np.empty((2, 8192, 1024), np.float32)
    for k, (b, q) in enumerate(cores):
        out[b, q * 2048:(q + 1) * 2048] = xT[k][:, 0:2048].T
    return out
```

```python
import math
import numpy as np
from contextlib import ExitStack
import concourse.bass as bass
import concourse.mybir as mybir
from concourse.bass_utils import run_bass_kernel_spmd


F32 = mybir.dt.float32
BF16 = mybir.dt.bfloat16
ALU = mybir.AluOpType
AF = mybir.ActivationFunctionType
AX = mybir.AxisListType

COMPUTE = ("tensor", "vector", "scalar", "gpsimd")


class Prog:
    def __init__(self):
        self.nc = bass.Bass("TRN2", target_bir_lowering=False)
        self.ops = []
        self.stack = ExitStack()
        self.ndram = 0

    def dram_in(self, name, shape, dtype=F32):
        return self.nc.dram_tensor(name, list(shape), dtype, kind="ExternalInput").ap()

    def dram_out(self, name, shape, dtype=F32):
        return self.nc.dram_tensor(name, list(shape), dtype, kind="ExternalOutput").ap()

    def sbuf(self, name, shape, dtype=F32):
        return self.stack.enter_context(self.nc.sbuf_tensor(name, list(shape), dtype))

    def psum(self, name, shape=(128, 512), dtype=F32):
        return self.stack.enter_context(self.nc.psum_tensor(name, list(shape), dtype))

    def op(self, eng, fn, reads=(), writes=(), chan=None, nosync_same=False, inc=True):
        self.ops.append(dict(eng=eng, fn=fn, reads=tuple(reads), writes=tuple(writes),
                             chan=chan, nosync_same=nosync_same, inc=inc))

    def dma(self, eng, out, in_, reads=(), writes=(), chan="ld", **kw):
        self.op(eng, lambda e: e.dma_start(out=out, in_=in_, **kw), reads, writes, chan=chan)

    def mm(self, out, lhsT, rhs, start, stop, reads=(), writes=()):
        self.op("tensor", lambda e: e.matmul(out, lhsT, rhs, start=start, stop=stop),
                reads, writes, nosync_same=True, inc=True)

    def act(self, out, in_, func, reads=(), writes=(), **kw):
        self.op("scalar", lambda e: e.activation(out=out, in_=in_, func=func, **kw), reads, writes)

    def tt(self, out, in0, in1, op, reads=(), writes=(), eng="vector"):
        self.op(eng, lambda e: e.tensor_tensor(out=out, in0=in0, in1=in1, op=op), reads, writes)

    def ts(self, out, in0, s1, s2, op0, op1=None, reads=(), writes=(), eng="vector"):
        if op1 is None:
            self.op(eng, lambda e: e.tensor_scalar(out=out, in0=in0, scalar1=s1, scalar2=None, op0=op0),
                    reads, writes)
        else:
            self.op(eng, lambda e: e.tensor_scalar(out=out, in0=in0, scalar1=s1, scalar2=s2, op0=op0, op1=op1),
                    reads, writes)

    def stt(self, out, in0, scalar, in1, op0, op1, reads=(), writes=()):
        self.op("vector", lambda e: e.scalar_tensor_tensor(out=out, in0=in0, scalar=scalar, in1=in1,
                                                            op0=op0, op1=op1), reads, writes)

    def copy(self, out, in_, reads=(), writes=(), eng="vector"):
        if eng == "scalar":
            self.op(eng, lambda e: e.copy(out=out, in_=in_), reads, writes)
        else:
            self.op(eng, lambda e: e.tensor_copy(out=out, in_=in_), reads, writes)

    def memset(self, ap, val, writes=(), eng="vector"):
        self.op(eng, lambda e: e.memset(ap, val), (), writes)

    def finish(self):
        nc = self.nc
        ops = self.ops
        engines = []
        for o in ops:
            if o["eng"] not in engines:
                engines.append(o["eng"])
        chans = []
        for o in ops:
            if o["chan"] is not None and o["chan"] not in chans:
                chans.append(o["chan"])
        sems = {}
        for e in engines:
            sems[("e", e)] = self.stack.enter_context(nc.semaphore("s_" + e))
        for c in chans:
            sems[("c", c)] = self.stack.enter_context(nc.semaphore("c_" + c))
        def plan_pass():
            viol = set()
            eng_count = {e: 0 for e in engines}
            chan_count = {c: 0 for c in chans}
            last_writer = {}
            readers = {}
            known = {e: {} for e in engines}
            plan = {e: [] for e in engines}
            done = []
            for i, o in enumerate(ops):
                e = o["eng"]
                deps = set()
                for r in o["reads"]:
                    if r in last_writer:
                        deps.add(last_writer[r])
                for w in o["writes"]:
                    if w in last_writer:
                        deps.add(last_writer[w])
                    for rd in readers.get(w, ()):
                        deps.add(rd)
                need = {}
                for d in deps:
                    od = ops[d]
                    if od["chan"] is not None:
                        key = ("c", od["chan"])
                        val = 16 * chan_count[od["chan"]]
                    else:
                        if od["eng"] == e and (o["nosync_same"] and od["nosync_same"]):
                            continue
                        key = ("e", od["eng"])
                        val = done[d][1]
                        if val > eng_count[od["eng"]]:
                            viol.add(d)
                    if val > need.get(key, 0):
                        need[key] = val
                waits = []
                for key, val in need.items():
                    if known[e].get(key, 0) >= val:
                        continue
                    known[e][key] = val
                    waits.append((key, val))
                if o["chan"] is not None:
                    chan_count[o["chan"]] += 1
                    done.append((("c", o["chan"]), 16 * chan_count[o["chan"]]))
                    inc = (("c", o["chan"]), 16)
                elif not o["inc"]:
                    done.append((("e", e), eng_count[e] + 1))
                    inc = None
                else:
                    eng_count[e] += 1
                    done.append((("e", e), eng_count[e]))
                    inc = (("e", e), 1)
                plan[e].append((waits, o["fn"], inc))
                for r in o["reads"]:
                    readers.setdefault(r, []).append(i)
                for w in o["writes"]:
                    last_writer[w] = i
                    readers[w] = []
            return viol, plan, chan_count
        while True:
            viol, plan, chan_count = plan_pass()
            if not viol:
                break
            for d in viol:
                ops[d]["inc"] = True
        final_waits = {e: [] for e in engines}
        chan_eng = {}
        for o in ops:
            if o["chan"] is not None:
                chan_eng[o["chan"]] = o["eng"]
        for c, e in chan_eng.items():
            final_waits[e].append((("c", c), 16 * chan_count[c]))

        semv = {k: 0 for k in sems}
        ptr = {e: 0 for e in engines}
        progressed = True
        while progressed:
            progressed = False
            for e in engines:
                while ptr[e] < len(plan[e]):
                    waits, _fn, inc = plan[e][ptr[e]]
                    if any(semv[k] < v for k, v in waits):
                        break
                    if inc is not None:
                        semv[inc[0]] += inc[1]
                    ptr[e] += 1
                    progressed = True
        stuck = {e: (ptr[e], len(plan[e])) for e in engines if ptr[e] < len(plan[e])}
        if stuck:
            det = {e: [(k, v, semv[k]) for k, v in plan[e][ptr[e]][0] if semv[k] < v] for e in stuck}
            raise RuntimeError(f"sync plan deadlocks: {stuck} waiting on {det}")

        with nc.Block() as block:
            def make(e):
                def body(eng):
                    for waits, fn, inc in plan[e]:
                        for key, val in waits:
                            eng.wait_ge(sems[key], val)
                        ins = fn(eng)
                        if inc is not None:
                            ins.then_inc(sems[inc[0]], inc[1])
                    for key, val in final_waits[e]:
                        eng.wait_ge(sems[key], val)
                return body
            for e in engines:
                getattr(block, e)(make(e))
        self.stack.close()
        return nc

GRID_W = 64
def rope_tables(tok0, n):
    t = np.arange(tok0, tok0 + n); row = (t // GRID_W).astype(np.float32); col = (t % GRID_W).astype(np.float32)
    quarter = 16
    freqs = (10000.0 ** (-np.arange(quarter, dtype=np.float32) / quarter)).astype(np.float32)
    cos = np.zeros((64, n), np.float32); sin = np.zeros((64, n), np.float32)
    for d in range(64):
        pos = row if d < 32 else col
        dd = d % 32
        f = freqs[dd % 16]
        ang = (pos * f).astype(np.float32)
        cos[d] = np.cos(ang); s = np.sin(ang)
        sin[d] = -s if dd < 16 else s
    return np.concatenate([cos, cos], 0), np.concatenate([sin, sin], 0)
def perm_matrix():
    Pm = np.zeros((128, 128), np.float32)
    for m in range(128):
        dd = m % 32
        k = m + 16 if dd < 16 else m - 16
        Pm[k, m] = 1.0
    return Pm
def colT(v, n):
    return np.ascontiguousarray(v.reshape(n, 128).T)

EPS = 1e-6

def get_stage(P):
    if not hasattr(P, "_stage"):
        P._stage_n = getattr(P, "_stage_n", 3)
        P._stage = [P.sbuf(f"stage{i}", [128, 1024]) for i in range(P._stage_n)]
        P._stage_i = 0
    return P._stage

def emit_mod(P, w_mod, b_modT, cT, nct, ps_mod, tag="m"):
    ncols = nct * 128
    c_s = P.sbuf(tag + "c_s", [128, 16]); sc_s = P.sbuf(tag + "sc_s", [128, 16])
    bm_s = P.sbuf(tag + "bm_s", [128, nct]); modv = P.sbuf(tag + "modv", [128, 2 * nct]); modc = P.sbuf(tag + "modc", [128, 2 * nct])
    modrow = [P.sbuf(f"{tag}modrow{i}", [2, 512]) for i in range(2)]
    scr = P.nc.dram_tensor(tag + "_modscr", [2, ncols], F32, kind="Internal").ap()
    st = get_stage(P)
    P.dma("sync", c_s[:], cT, writes=[tag + "c_s"], chan="ld0")
    P.dma("sync", bm_s[:], b_modT, writes=[tag + "bm_s"], chan="ld0")
    P.act(sc_s[:], c_s[:], AF.Silu, reads=[tag + "c_s"], writes=[tag + "sc_s"])
    wmv = w_mod.rearrange("(kc p) n -> p kc n", p=128)
    for pc in range(ncols // 512):
        for i in range(4):
            b = P._stage_i % P._stage_n; P._stage_i += 1
            P.dma("sync", st[b][:].rearrange("p (k n) -> p k n", k=2), wmv[:, 2 * i:2 * i + 2, pc * 512:(pc + 1) * 512],
                  writes=[("stage", b)], chan=f"stage{b}")
            for k2 in range(2):
                kc = 2 * i + k2
                P.mm(ps_mod[0:2, 0:512], sc_s[:, kc:16:8], st[b][:, k2 * 512:(k2 + 1) * 512], kc == 0, kc == 7,
                     reads=[("stage", b), tag + "sc_s"], writes=["ps_mod"])
        P.copy(modrow[pc % 2][:], ps_mod[0:2, 0:512], reads=["ps_mod"], writes=[(tag + "modrow", pc % 2)], eng="scalar")
        P.dma("sync", scr[:, pc * 512:(pc + 1) * 512], modrow[pc % 2][:], reads=[(tag + "modrow", pc % 2)], writes=[tag + "scr"], chan="modw")
    P.dma("sync", modc[:].rearrange("p (j t) -> p j t", j=2), scr.rearrange("j (t p) -> p j t", p=128),
          reads=[tag + "scr"], writes=[tag + "modc"], chan="modr", allow_slow_non_contiguous=True)
    for j in range(2):
        P.tt(modv[:, j * nct:(j + 1) * nct], modc[:, j * nct:(j + 1) * nct], bm_s[:], ALU.add,
             reads=[tag + "modc", tag + "bm_s"], writes=[tag + "modv"])
    return modv

def load_cast(P, w_dram, w_bf, nk, ncols, tag, piece=1024):
    wv = w_dram.rearrange("(kc p) n -> p kc n", p=128)
    for kc in range(nk):
        P.dma("gpsimd", w_bf[:, kc, :], wv[:, kc, :], writes=[(tag, kc)], chan="wld_" + tag, max_dma_last_dim=4096)

def emit_rstd(P, src, nk, n, sqb, ones, ps_ss, sd, rstd, src_res, tag=""):
    P.act(sqb[:, 0:nk, 0:n], src[:, 0:nk, 0:n], AF.Square, reads=src_res, writes=["sqb"])
    for kc in range(nk):
        P.mm(ps_ss[:, 0:n], ones[:], sqb[:, kc, 0:n], kc == 0, kc == nk - 1, reads=["sqb", "ones"], writes=["ps_ss"])
    P.act(sd[:, 0:n], ps_ss[:, 0:n], AF.Sqrt, reads=["ps_ss"], writes=["sd" + tag], scale=1.0 / (128 * nk), bias=EPS)
    P.op("vector", lambda e: e.reciprocal(out=rstd[:, 0:n], in_=sd[:, 0:n]), reads=["sd" + tag], writes=["rstd" + tag])

def build_l3b(with_ctx, N=256):
    NT = 2304 if with_ctx else 2048
    P = Prog(); P._stage_n = 2
    xT = P.dram_in("xT", [1024, NT])
    w_mod = P.dram_in("w_mod", [1024, 3072]); b_modT = P.dram_in("b_modT", [128, 24]); cT = P.dram_in("cT", [128, 16])
    g_preT = P.dram_in("g_preT", [128, 8]); g_postT = P.dram_in("g_postT", [128, 8])
    w_gate = P.dram_in("w_gate", [1024, 2816]); w_up = P.dram_in("w_up", [1024, 2816]); w_down = P.dram_in("w_down", [2816, 1024])
    xoT = P.dram_out("xoT", [1024, NT])
    wg = P.sbuf("wg", [128, 8, 2816], BF16); wu = P.sbuf("wu", [128, 8, 2816], BF16); wd = P.sbuf("wd", [128, 22, 1024], BF16)
    xs = [P.sbuf(f"xs{i}", [128, 8, N]) for i in range(2)]
    sqb = P.sbuf("sqb", [128, 8, N], BF16)
    tt_ = [P.sbuf(f"tt{i}", [128, N]) for i in range(2)]; xn = [P.sbuf(f"xn{i}", [128, 8, N], BF16) for i in range(2)]
    sd2 = P.sbuf("sd2", [128, N]); rstd2 = P.sbuf("rstd2", [128, N])
    hmid = P.sbuf("hmid", [128, 22, N], BF16)
    sg = [P.sbuf(f"sg{i}", [128, N]) for i in range(2)]
    o2 = P.sbuf("o2", [128, 8, N]); tmp = [P.sbuf(f"tmp{i}", [128, N]) for i in range(2)]
    xo = [P.sbuf(f"xo{i}", [128, N]) for i in range(2)]
    ones = P.sbuf("ones", [128, 128], BF16)
    sd = P.sbuf("sd", [128, N]); rstd = P.sbuf("rstd", [128, N])
    gp_s = P.sbuf("gp_s", [128, 8]); gq_s = P.sbuf("gq_s", [128, 8])
    Av = P.sbuf("Av", [128, 16]); Gv = P.sbuf("Gv", [128, 16])
    ps_mod = P.psum("ps_mod"); ps_ss = P.psum("ps_ss")
    psg = [P.psum(f"psg{i}") for i in range(2)]; psu = [P.psum(f"psu{i}") for i in range(2)]; pso = [P.psum(f"pso{i}") for i in range(2)]
    P.memset(ones[:], 1.0, writes=["ones"])
    P.dma("sync", gp_s[:], g_preT, writes=["gp_s"], chan="ld0")
    P.dma("sync", gq_s[:], g_postT, writes=["gq_s"], chan="ld0")
    modv = emit_mod(P, w_mod, b_modT, cT, 24, ps_mod)
    for j in range(2):
        P.stt(Av[:, j * 8:(j + 1) * 8], modv[:, j * 24 + 8: j * 24 + 16], 1.0, gp_s[:], ALU.add, ALU.mult,
              reads=["mmodv", "gp_s"], writes=["Av"])
        P.tt(Gv[:, j * 8:(j + 1) * 8], modv[:, j * 24 + 16: j * 24 + 24], gq_s[:], ALU.mult,
             reads=["mmodv", "gq_s"], writes=["Gv"])
    load_cast(P, w_gate, wg, 8, 2816, "wg")
    load_cast(P, w_up, wu, 8, 2816, "wu")
    xv = xT.rearrange("(kc p) t -> p kc t", p=128); xov = xoT.rearrange("(kc p) t -> p kc t", p=128)
    slabs = list(range(0, NT, N)); n = N
    cnt = dict(ng=0, no=0, nt=0)
    def pre(si):
        t0 = slabs[si]; b = si % 2; j = 0 if t0 < 2048 else 1
        P.dma("sync", xs[b][:], xv[:, :, t0:t0 + n], writes=[("xs", b)], chan=f"xs{b}")
        emit_rstd(P, xs[b], 8, n, sqb, ones, ps_ss, sd, rstd, [("xs", b)])
        for kc in range(8):
            P.tt(tt_[kc % 2][:], xs[b][:, kc, :], rstd[:], ALU.mult, reads=[("xs", b), "rstd"], writes=[("tt", kc % 2)])
            P.act(xn[b][:, kc, :], tt_[kc % 2][:], AF.Identity, reads=[("tt", kc % 2), "Av", "mmodv"], writes=[("xn", b, kc)],
                  scale=Av[:, j * 8 + kc: j * 8 + kc + 1], bias=modv[:, j * 24 + kc: j * 24 + kc + 1])
    def gu(si):
        b = si % 2
        for jj in range(22):
            pb = cnt["ng"] % 2; cnt["ng"] += 1
            for kc in range(8):
                P.mm(psg[pb][:, 0:n], wg[:, kc, jj * 128:(jj + 1) * 128], xn[b][:, kc, :], kc == 0, kc == 7,
                     reads=[("wg", kc), ("xn", b, kc)], writes=[("psg", pb)])
            for kc in range(8):
                P.mm(psu[pb][:, 0:n], wu[:, kc, jj * 128:(jj + 1) * 128], xn[b][:, kc, :], kc == 0, kc == 7,
                     reads=[("wu", kc), ("xn", b, kc)], writes=[("psu", pb)])
            P.act(sg[pb][:], psg[pb][:, 0:n], AF.Silu, reads=[("psg", pb)], writes=[("sg", pb)])
            P.tt(hmid[:, jj, :], sg[pb][:], psu[pb][:, 0:n], ALU.mult, reads=[("sg", pb), ("psu", pb)], writes=[("hmid", jj)])
    def dn(si):
        for m in range(8):
            pb = cnt["no"] % 2; cnt["no"] += 1
            for jj in range(22):
                P.mm(pso[pb][:, 0:n], wd[:, jj, m * 128:(m + 1) * 128], hmid[:, jj, :], jj == 0, jj == 21,
                     reads=[("wd", jj), ("hmid", jj)], writes=[("pso", pb)])
            P.copy(o2[:, m, :], pso[pb][:, 0:n], reads=[("pso", pb)], writes=[("o2", m)], eng="scalar")
    def post(si):
        t0 = slabs[si]; b = si % 2; j = 0 if t0 < 2048 else 1
        emit_rstd(P, o2, 8, n, sqb, ones, ps_ss, sd2, rstd2, [("o2", m) for m in range(8)], tag="2")
        for m in range(8):
            P.stt(o2[:, m, :], o2[:, m, :], Gv[:, j * 8 + m: j * 8 + m + 1], rstd2[:], ALU.mult, ALU.mult,
                  reads=[("o2", m), "Gv", "rstd2"], writes=[("o2", m)])
            P.tt(o2[:, m, :], xs[b][:, m, :], o2[:, m, :], ALU.add, reads=[("xs", b), ("o2", m)], writes=[("o2", m)], eng="gpsimd")
        P.dma("gpsimd", xov[:, :, t0:t0 + n], o2[:], reads=[("o2", m) for m in range(8)], chan="st")
    pre(0)
    for si in range(len(slabs)):
        gu(si)
        if si == 0:
            load_cast(P, w_down, wd, 22, 1024, "wd")
        if si + 1 < len(slabs):
            pre(si + 1)
        dn(si)
        post(si)
    return P.finish()

def build_l3a(with_ctx, N=512):
    NT = 2304 if with_ctx else 2048
    P = Prog()
    xT = P.dram_in("xT", [1024, NT]); ysT = P.dram_in("ysT", [384, NT]); mxT = P.dram_in("mxT", [256, NT]); naT = P.dram_in("naT", [384, NT])
    w_mod = P.dram_in("w_mod", [1024, 1024]); b_modT = P.dram_in("b_modT", [128, 8]); cT = P.dram_in("cT", [128, 16])
    g_postT = P.dram_in("g_postT", [128, 8])
    w_glu = P.dram_in("w_glu", [384, 384]); w_fourier = P.dram_in("w_fourier", [256, 256]); w_out = P.dram_in("w_out", [1024, 1024])
    xoT = P.dram_out("xoT", [1024, NT])
    wglu = P.sbuf("wglu", [128, 3, 384], BF16); wf = P.sbuf("wf", [128, 2, 256], BF16); wo = P.sbuf("wo", [128, 8, 1024], BF16)
    xs = [P.sbuf(f"xs{i}", [128, 8, N]) for i in range(2)]
    ys = [P.sbuf(f"ys{i}", [128, 3, N]) for i in range(2)]
    mx = [P.sbuf(f"mx{i}", [128, 2, N]) for i in range(2)]
    na = [P.sbuf(f"na{i}", [128, 3, N]) for i in range(2)]
    sq = P.sbuf("sq", [128, 3, N]); t1 = P.sbuf("t1", [128, 3, N]); sgm = P.sbuf("sgm", [128, 3, N])
    zf = P.sbuf("zf", [128, 3, N]); zb = P.sbuf("zb", [128, 3, N], BF16); mxb = P.sbuf("mxb", [128, 2, N], BF16)
    sg2 = [P.sbuf(f"sg2{i}", [128, N]) for i in range(2)]
    cat = P.sbuf("cat", [128, 8, N], BF16)
    sqb = P.sbuf("sqb", [128, 8, N], BF16)
    o2 = P.sbuf("o2", [128, 8, N]); tmp = [P.sbuf(f"tmp{i}", [128, N]) for i in range(2)]
    xo = [P.sbuf(f"xo{i}", [128, N]) for i in range(2)]
    ones = P.sbuf("ones", [128, 128], BF16)
    sd = P.sbuf("sd", [128, N]); rstd = P.sbuf("rstd", [128, N])
    gq_s = P.sbuf("gq_s", [128, 8]); Gv = P.sbuf("Gv", [128, 16])
    ps_mod = P.psum("ps_mod"); ps_ss = P.psum("ps_ss")
    psa = [P.psum(f"psa{i}") for i in range(3)]; pso = [P.psum(f"pso{i}") for i in range(3)]
    P.memset(ones[:], 1.0, writes=["ones"])
    P.dma("sync", gq_s[:], g_postT, writes=["gq_s"], chan="ld0")
    modv = emit_mod(P, w_mod, b_modT, cT, 8, ps_mod)
    for j in range(2):
        P.tt(Gv[:, j * 8:(j + 1) * 8], modv[:, j * 8:(j + 1) * 8], gq_s[:], ALU.mult, reads=["mmodv", "gq_s"], writes=["Gv"])
    load_cast(P, w_glu, wglu, 3, 384, "wglu")
    load_cast(P, w_fourier, wf, 2, 256, "wf")
    load_cast(P, w_out, wo, 8, 1024, "wo")
    xv = xT.rearrange("(kc p) t -> p kc t", p=128); xov = xoT.rearrange("(kc p) t -> p kc t", p=128)
    ysv = ysT.rearrange("(kc p) t -> p kc t", p=128); mxv = mxT.rearrange("(kc p) t -> p kc t", p=128); nav = naT.rearrange("(kc p) t -> p kc t", p=128)
    na_ = 0; no = 0; nt = 0
    for si, t0 in enumerate(range(0, NT, N)):
        n = min(N, NT - t0); b = si % 2; j = 0 if t0 < 2048 else 1
        P.dma("sync", xs[b][:, :, 0:n], xv[:, :, t0:t0 + n], writes=[("xs", b)], chan=f"xs{b}")
        P.dma("sync", ys[b][:, :, 0:n], ysv[:, :, t0:t0 + n], writes=[("ys", b)], chan=f"ys{b}")
        P.dma("sync", mx[b][:, :, 0:n], mxv[:, :, t0:t0 + n], writes=[("mx", b)], chan=f"mx{b}")
        P.dma("sync", na[b][:, :, 0:n], nav[:, :, t0:t0 + n], writes=[("na", b)], chan=f"na{b}")
        Y = ys[b][:, :, 0:n]
        P.tt(sq[:, :, 0:n], Y, Y, ALU.mult, reads=[("ys", b)], writes=["sq"], eng="gpsimd")
        P.ts(t1[:, :, 0:n], sq[:, :, 0:n], 0.044715, 1.0, ALU.mult, ALU.add, reads=["sq"], writes=["t1"])
        P.tt(sq[:, :, 0:n], t1[:, :, 0:n], Y, ALU.mult, reads=["t1", ("ys", b)], writes=["sq"])
        P.act(sgm[:, :, 0:n], sq[:, :, 0:n], AF.Sigmoid, reads=["sq"], writes=["sgm"], scale=1.5957691216057308)
        P.tt(zf[:, :, 0:n], Y, sgm[:, :, 0:n], ALU.mult, reads=[("ys", b), "sgm"], writes=["zf"])
        P.copy(zb[:, :, 0:n], zf[:, :, 0:n], reads=["zf"], writes=["zb"], eng="gpsimd")
        for m in range(3):
            pb = na_ % 3; na_ += 1
            for kc in range(3):
                P.mm(psa[pb][:, 0:n], wglu[:, kc, m * 128:(m + 1) * 128], zb[:, kc, 0:n], kc == 0, kc == 2,
                     reads=[("wglu", kc), "zb"], writes=[("psa", pb)])
            P.act(sg2[m % 2][:, 0:n], psa[pb][:, 0:n], AF.Sigmoid, reads=[("psa", pb)], writes=[("sg2", m % 2)])
            P.tt(cat[:, m, 0:n], zf[:, m, 0:n], sg2[m % 2][:, 0:n], ALU.mult, reads=["zf", ("sg2", m % 2)], writes=[("cat", m)])
        P.copy(mxb[:, :, 0:n], mx[b][:, :, 0:n], reads=[("mx", b)], writes=["mxb"], eng="gpsimd")
        for m in range(2):
            pb = na_ % 3; na_ += 1
            for kc in range(2):
                P.mm(psa[pb][:, 0:n], wf[:, kc, m * 128:(m + 1) * 128], mxb[:, kc, 0:n], kc == 0, kc == 1,
                     reads=[("wf", kc), "mxb"], writes=[("psa", pb)])
            P.copy(cat[:, 3 + m, 0:n], psa[pb][:, 0:n], reads=[("psa", pb)], writes=[("cat", 3 + m)], eng="scalar")
        P.copy(cat[:, 5:8, 0:n], na[b][:, :, 0:n], reads=[("na", b)], writes=[("cat", 5), ("cat", 6), ("cat", 7)], eng="gpsimd")
        for m in range(8):
            pb = no % 3; no += 1
            for kc in range(8):
                P.mm(pso[pb][:, 0:n], wo[:, kc, m * 128:(m + 1) * 128], cat[:, kc, 0:n], kc == 0, kc == 7,
                     reads=[("wo", kc), ("cat", kc)], writes=[("pso", pb)])
            P.copy(o2[:, m, 0:n], pso[pb][:, 0:n], reads=[("pso", pb)], writes=[("o2", m)], eng="scalar")
        emit_rstd(P, o2, 8, n, sqb, ones, ps_ss, sd, rstd, [("o2", m) for m in range(8)])
        for m in range(8):
            P.stt(o2[:, m, 0:n], o2[:, m, 0:n], Gv[:, j * 8 + m: j * 8 + m + 1], rstd[:, 0:n], ALU.mult, ALU.mult,
                  reads=[("o2", m), "Gv", "rstd"], writes=[("o2", m)])
            P.tt(o2[:, m, 0:n], xs[b][:, m, 0:n], o2[:, m, 0:n], ALU.add, reads=[("xs", b), ("o2", m)], writes=[("o2", m)], eng="gpsimd")
        P.dma("gpsimd", xov[:, :, t0:t0 + n], o2[:, :, 0:n], reads=[("o2", m) for m in range(8)], chan="st")
    return P.finish()

EPS = 1e-6
NT = 2304
SLABS = [(0, 512), (512, 512), (1024, 512), (1536, 512), (2048, 256)]

def build_l1():
    P = Prog(); P._stage_n = 4
    xT = P.dram_in("xT", [1024, NT])
    w_in = P.dram_in("w_in", [1024, 1792])
    w_mod = P.dram_in("w_mod", [1024, 2048])
    b_modT = P.dram_in("b_modT", [128, 16])
    g_preT = P.dram_in("g_preT", [128, 8])
    cT = P.dram_in("cT", [128, 16])
    cos = P.dram_in("cos", [128, 2048]); sin = P.dram_in("sin", [128, 2048])
    perm = P.dram_in("perm", [128, 128])
    hT = P.dram_out("hT", [640, NT])
    hTb = P.dram_out("hTb", [1152, NT])

    xs = [P.sbuf(f"xs{i}", [128, 8, 512]) for i in range(2)]
    sqb = P.sbuf("sqb", [128, 8, 512], BF16)
    tt_ = [P.sbuf(f"tt{i}", [128, 512]) for i in range(2)]
    xn = [P.sbuf(f"xn{i}", [128, 8, 512], BF16) for i in range(2)]
    w_bf = P.sbuf("w_bf", [128, 8, 1792], BF16)
    ones = P.sbuf("ones", [128, 128], BF16)
    sd = P.sbuf("sd", [128, 512]); rstd = P.sbuf("rstd", [128, 512])
    ho = [P.sbuf(f"ho{i}", [128, 512]) for i in range(4)]
    hob = [P.sbuf(f"hob{i}", [128, 512]) for i in range(4)]
    r1 = [P.sbuf(f"r1{i}", [128, 512]) for i in range(2)]
    r2 = [P.sbuf(f"r2{i}", [128, 512]) for i in range(2)]
    cos_s = P.sbuf("cos_s", [128, 2048]); sin_s = P.sbuf("sin_s", [128, 2048])
    perm_s = P.sbuf("perm_s", [128, 128])
    gp_s = P.sbuf("gp_s", [128, 8]); Av = P.sbuf("Av", [128, 16])
    ps_mod = P.psum("ps_mod"); ps_ss = P.psum("ps_ss")
    psm = [P.psum(f"psm{i}") for i in range(4)]
    psr = [P.psum(f"psr{i}") for i in range(2)]

    P.dma("sync", gp_s[:], g_preT, writes=["gp_s"], chan="ld0")
    P.dma("sync", cos_s[:], cos, writes=["cos_s"], chan="ld0")
    P.dma("sync", sin_s[:], sin, writes=["sin_s"], chan="ld0")
    P.dma("sync", perm_s[:], perm, writes=["perm_s"], chan="ld0")
    P.memset(ones[:], 1.0, writes=["ones"])
    modv = emit_mod(P, w_mod, b_modT, cT, 16, ps_mod)
    for j in range(2):
        P.stt(Av[:, j * 8:(j + 1) * 8], modv[:, j * 16 + 8: j * 16 + 16], 1.0, gp_s[:], ALU.add, ALU.mult,
              reads=["mmodv", "gp_s"], writes=["Av"])
    load_cast(P, w_in, w_bf, 8, 1792, "w_bf", piece=1024)
    xv = xT.rearrange("(kc p) t -> p kc t", p=128)
    cnt = dict(nho=0, nr=0, nps=0)
    def pre(si):
        t0, n = SLABS[si]; b = si % 2; j = 0 if si < 4 else 1
        P.dma("sync", xs[b][:, :, 0:n], xv[:, :, t0:t0 + n], writes=[("xs", b)], chan=f"xs{b}")
        P.act(sqb[:, :, 0:n], xs[b][:, :, 0:n], AF.Square, reads=[("xs", b)], writes=["sqb"])
        for kc in range(8):
            P.mm(ps_ss[:, 0:n], ones[:], sqb[:, kc, 0:n], kc == 0, kc == 7, reads=["sqb", "ones"], writes=["ps_ss"])
        P.act(sd[:, 0:n], ps_ss[:, 0:n], AF.Sqrt, reads=["ps_ss"], writes=["sd"], scale=1.0 / 1024, bias=EPS)
        P.op("vector", lambda e: e.reciprocal(out=rstd[:, 0:n], in_=sd[:, 0:n]), reads=["sd"], writes=["rstd"])
        for kc in range(8):
            P.tt(tt_[kc % 2][:, 0:n], xs[b][:, kc, 0:n], rstd[:, 0:n], ALU.mult, reads=[("xs", b), "rstd"], writes=[("tt", kc % 2)])
            P.act(xn[b][:, kc, 0:n], tt_[kc % 2][:, 0:n], AF.Identity, reads=[("tt", kc % 2), "Av", "mmodv"], writes=[("xn", b, kc)],
                  scale=Av[:, j * 8 + kc: j * 8 + kc + 1], bias=modv[:, j * 16 + kc: j * 16 + kc + 1])
    pending = []
    def flush():
        while pending:
            pending.pop(0)()
    def main(si):
        t0, n = SLABS[si]; b = si % 2
        for m in range(14):
            pb = cnt["nps"] % 4; cnt["nps"] += 1
            for kc in range(8):
                P.mm(psm[pb][:, 0:n], w_bf[:, kc, m * 128:(m + 1) * 128], xn[b][:, kc, 0:n], kc == 0, kc == 7,
                     reads=[("w_bf", kc), ("xn", b, kc)], writes=[("psm", pb)])
            flush()
            hb = cnt["nho"] % 4; cnt["nho"] += 1
            if m < 5:
                P.copy(ho[hb][:, 0:n], psm[pb][:, 0:n], reads=[("psm", pb)], writes=[("ho", hb)], eng="scalar")
                P.dma("gpsimd", hT[m * 128:(m + 1) * 128, t0:t0 + n], ho[hb][:, 0:n], reads=[("ho", hb)], chan=f"sto{hb}")
            elif m <= 10 and si < 4:
                rb = cnt["nr"] % 2; cnt["nr"] += 1
                P.copy(ho[hb][:, 0:n], psm[pb][:, 0:n], reads=[("psm", pb)], writes=[("ho", hb)], eng="scalar")
                def rope(hb=hb, rb=rb, m=m, t0=t0, n=n):
                    P.mm(psr[rb][:, 0:n], perm_s[:], ho[hb][:, 0:n], True, True, reads=["perm_s", ("ho", hb)], writes=[("psr", rb)])
                    P.tt(r1[rb][:, 0:n], ho[hb][:, 0:n], cos_s[:, t0:t0 + n], ALU.mult, reads=[("ho", hb), "cos_s"], writes=[("r1", rb)])
                    P.tt(r2[rb][:, 0:n], psr[rb][:, 0:n], sin_s[:, t0:t0 + n], ALU.mult, reads=[("psr", rb), "sin_s"], writes=[("r2", rb)])
                    P.tt(hob[hb][:, 0:n], r1[rb][:, 0:n], r2[rb][:, 0:n], ALU.add, reads=[("r1", rb), ("r2", rb)], writes=[("hob", hb)], eng="gpsimd")
                    P.dma("gpsimd", hTb[(m - 5) * 128:(m - 4) * 128, t0:t0 + n], hob[hb][:, 0:n], reads=[("hob", hb)], chan=f"stb{hb}")
                pending.append(rope)
            else:
                P.copy(hob[hb][:, 0:n], psm[pb][:, 0:n], reads=[("psm", pb)], writes=[("hob", hb)], eng="scalar")
                P.dma("gpsimd", hTb[(m - 5) * 128:(m - 4) * 128, t0:t0 + n], hob[hb][:, 0:n], reads=[("hob", hb)], chan=f"stb{hb}")
    pre(0)
    for si in range(len(SLABS)):
        if si + 1 < len(SLABS):
            pre(si + 1)
        main(si)
    flush()
    return P.finish()


def fnet_consts():
    n1 = np.arange(128); a = 2 * np.pi * np.outer(n1, n1) / 128
    cs128 = np.concatenate([np.cos(a), -np.sin(a)], 1).astype(np.float32)
    n2 = np.arange(64); a = 2 * np.pi * np.outer(n2, n2) / 64
    C64, S64 = np.cos(a), np.sin(a)
    fb1 = np.concatenate([C64, -S64], 1).astype(np.float32); fb2 = np.concatenate([S64, C64], 1).astype(np.float32)
    a = 2 * np.pi * np.outer(n2, n1) / 8192
    tw = np.concatenate([np.cos(a), np.sin(a)], 1).astype(np.float32)
    fc = (np.concatenate([C64, S64], 1) / np.sqrt(8192 * 64)).astype(np.float32)
    n = np.arange(256); a = 2 * np.pi * np.outer(n, n) / 256
    cs = np.concatenate([np.cos(a), -np.sin(a)], 1)
    cs256 = np.ascontiguousarray(cs.reshape(2, 128, 512).transpose(1, 0, 2)).astype(np.float32)
    fcc = (np.concatenate([C64, S64], 1) / np.sqrt(256 * 64)).astype(np.float32)
    return dict(cs128=cs128, fb1=fb1, fb2=fb2, tw=tw, fc=fc, cs256=cs256, fcc=fcc)

def build_fnet(with_ctx):
    P = Prog()
    z = P.dram_in("z", [128, 4096]); cs128 = P.dram_in("cs128", [128, 256])
    fb1 = P.dram_in("fb1", [64, 128]); fb2 = P.dram_in("fb2", [64, 128]); tw = P.dram_in("tw", [64, 256]); fc = P.dram_in("fc", [64, 128])
    R = P.dram_out("R", [64, 8192])
    zs = P.sbuf("zs", [128, 64, 64]); cs_s = P.sbuf("cs_s", [128, 256])
    fb1_s = P.sbuf("fb1_s", [64, 128], BF16); fb2_s = P.sbuf("fb2_s", [64, 128], BF16); tw_s = P.sbuf("tw_s", [64, 256]); fc_s = P.sbuf("fc_s", [64, 128], BF16)
    Ych = [P.sbuf(f"Ych{i}", [64, 8, 256]) for i in range(2)]
    ta = P.sbuf("ta", [64, 8, 128]); tb = P.sbuf("tb", [64, 8, 128]); tc = P.sbuf("tc", [64, 8, 128]); td = P.sbuf("td", [64, 8, 128])
    Yp = P.sbuf("Yp", [64, 64, 256], BF16); X1 = P.sbuf("X1", [64, 2, 8192], BF16)
    Rs = [P.sbuf(f"Rs{i}", [64, 512]) for i in range(2)]
    psA = [P.psum(f"psA{i}") for i in range(3)]; psB = [P.psum(f"psB{i}") for i in range(3)]; psC = [P.psum(f"psC{i}") for i in range(2)]
    P.dma("sync", zs[:].rearrange("p a b -> p (a b)"), z, writes=["zs"], chan="ld0")
    for (s, d, nm) in ((cs_s, cs128, "cs_s"), (tw_s, tw, "tw_s")):
        P.dma("sync", s[:], d, writes=[nm], chan="ld1")
    for (s, d, nm) in ((fb1_s, fb1, "fb1_s"), (fb2_s, fb2, "fb2_s"), (fc_s, fc, "fc_s")):
        P.dma("gpsimd", s[:], d, writes=[nm], chan="ldc")
    twr = tw_s[:, 0:128].rearrange("p (o k) -> p o k", o=1).to_broadcast([64, 8, 128])
    tws = tw_s[:, 128:256].rearrange("p (o k) -> p o k", o=1).to_broadcast([64, 8, 128])
    na_ = 0
    for ch in range(8):
        cb = ch % 2
        for pair in range(4):
            pb = na_ % 3; na_ += 1
            for h in range(2):
                c = ch * 8 + pair * 2 + h
                P.mm(psA[pb][0:64, h * 256:(h + 1) * 256], zs[:, :, c], cs_s[:], True, True, reads=["zs", "cs_s"], writes=[("psA", pb)])
            P.copy(Ych[cb][:, pair * 2:pair * 2 + 2, :].rearrange("p a b -> p (a b)"), psA[pb][0:64, :], reads=[("psA", pb)],
                   writes=[("Ych", cb)], eng="scalar")
        Yr = Ych[cb][:, :, 0:128]; Yi = Ych[cb][:, :, 128:256]; c0 = ch * 8
        P.tt(ta[:], Yr, twr, ALU.mult, reads=[("Ych", cb), "tw_s"], writes=["ta"])
        P.tt(tb[:], Yi, tws, ALU.mult, reads=[("Ych", cb), "tw_s"], writes=["tb"], eng="gpsimd")
        P.tt(Yp[:, c0:c0 + 8, 0:128], ta[:], tb[:], ALU.add, reads=["ta", "tb"], writes=[("Yp", ch)])
        P.tt(tc[:], Yi, twr, ALU.mult, reads=[("Ych", cb), "tw_s"], writes=["tc"], eng="gpsimd")
        P.tt(td[:], Yr, tws, ALU.mult, reads=[("Ych", cb), "tw_s"], writes=["td"])
        P.tt(Yp[:, c0:c0 + 8, 128:256], tc[:], td[:], ALU.subtract, reads=["tc", "td"], writes=[("Yp", ch)], eng="gpsimd")
    allYp = [("Yp", ch) for ch in range(8)]
    X1v = X1[:].rearrange("p c (k2 k1) -> p c k2 k1", k1=128)
    for g in range(32):
        pb = g % 3
        for q in range(4):
            k1 = 4 * g + q
            P.mm(psB[pb][0:64, q * 128:(q + 1) * 128], Yp[:, :, k1], fb1_s[:], True, False, reads=allYp + ["fb1_s"], writes=[("psB", pb)])
            P.mm(psB[pb][0:64, q * 128:(q + 1) * 128], Yp[:, :, 128 + k1], fb2_s[:], False, True, reads=allYp + ["fb2_s"], writes=[("psB", pb)])
        pv = psB[pb][0:64, :].rearrange("p (q c k) -> p c k q", q=4, c=2)
        for comp in range(2):
            P.copy(X1v[:, comp, :, 4 * g:4 * g + 4], pv[:, comp, :, :], reads=[("psB", pb)], writes=[("X1", g)],
                   eng=("scalar" if comp == 0 else "vector"))
    allX1 = [("X1", g) for g in range(32)]
    for blk in range(16):
        pb = blk % 2
        P.mm(psC[pb][0:64, :], fc_s[:, 0:64], X1[:, 0, blk * 512:(blk + 1) * 512], True, False, reads=allX1 + ["fc_s"], writes=[("psC", pb)])
        P.mm(psC[pb][0:64, :], fc_s[:, 64:128], X1[:, 1, blk * 512:(blk + 1) * 512], False, True, reads=allX1 + ["fc_s"], writes=[("psC", pb)])
        P.copy(Rs[pb][:], psC[pb][0:64, :], reads=[("psC", pb)], writes=[("Rs", pb)], eng="scalar")
        P.dma("gpsimd", R[:, blk * 512:(blk + 1) * 512], Rs[pb][:], reads=[("Rs", pb)], chan=f"st{pb}")
    if with_ctx:
        zc = P.dram_in("zc", [128, 2, 64]); cs256 = P.dram_in("cs256", [128, 2, 512]); fcc = P.dram_in("fcc", [64, 128])
        Rc = P.dram_out("Rc", [64, 256])
        zc_s = P.sbuf("zc_s", [128, 2, 64]); c2_s = P.sbuf("c2_s", [128, 2, 512]); fcc_s = P.sbuf("fcc_s", [64, 128])
        Pc = P.sbuf("Pc", [64, 512]); Rc_s = P.sbuf("Rc_s", [64, 256])
        P.dma("sync", zc_s[:], zc, writes=["zc_s"], chan="ld2"); P.dma("sync", c2_s[:], cs256, writes=["c2_s"], chan="ld2")
        P.dma("sync", fcc_s[:], fcc, writes=["fcc_s"], chan="ld2")
        for t in range(2):
            P.mm(psA[0][0:64, :], zc_s[:, t, :], c2_s[:, t, :], t == 0, t == 1, reads=["zc_s", "c2_s"], writes=[("psA", 0)])
        P.copy(Pc[:], psA[0][0:64, :], reads=[("psA", 0)], writes=["Pc"])
        P.mm(psA[1][0:64, 0:256], fcc_s[:, 0:64], Pc[:, 0:256], True, False, reads=["Pc", "fcc_s"], writes=[("psA", 1)])
        P.mm(psA[1][0:64, 0:256], fcc_s[:, 64:128], Pc[:, 256:512], False, True, reads=["Pc", "fcc_s"], writes=[("psA", 1)])
        P.copy(Rc_s[:], psA[1][0:64, 0:256], reads=[("psA", 1)], writes=["Rc_s"])
        P.dma("gpsimd", Rc, Rc_s[:], reads=["Rc_s"], chan="st")
    return P.finish()

NEG = -30000.0

def build_na(with_ctx):
    P = Prog()
    qT = P.dram_in("qT", [384, 2048]); kwT = P.dram_in("kwT", [16, 384, 576]); vw = P.dram_in("vw", [16, 128, 5, 384])
    kcT = P.dram_in("kcT", [384, 256]); vc = P.dram_in("vc", [128, 2, 384])
    tbraw = P.dram_in("tbraw", [5, 128, 6, 576]); mask = P.dram_in("mask", [5, 128, 576]); ident = P.dram_in("ident", [128, 128])
    Y = P.dram_out("Y", [2048, 384])
    qb = P.sbuf("qb", [128, 3, 2048], BF16); kcb = P.sbuf("kcb", [128, 3, 256], BF16); vcb = P.sbuf("vcb", [128, 2, 384], BF16)
    TB = P.sbuf("TB", [128, 5, 6, 576]); mk = P.sbuf("mk", [128, 5, 576])
    idb = P.sbuf("idb", [128, 128], BF16)
    kb = [P.sbuf(f"kb{i}", [128, 3, 576], BF16) for i in range(2)]; vb = [P.sbuf(f"vb{i}", [128, 5, 384], BF16) for i in range(2)]
    S = [P.sbuf(f"S{i}", [128, 832]) for i in range(4)]; Pb = [P.sbuf(f"Pb{i}", [128, 832], BF16) for i in range(4)]
    PT = [P.sbuf(f"PT{i}", [128, 896], BF16) for i in range(4)]
    Osb = [P.sbuf(f"Osb{i}", [128, 384]) for i in range(2)]
    mx = [P.sbuf(f"mx{i}", [128, 1]) for i in range(8)]; ssum = [P.sbuf(f"ssum{i}", [128, 1]) for i in range(8)]
    rinv = [P.sbuf(f"rinv{i}", [128, 1]) for i in range(8)]
    psA = [P.psum(f"psA{i}") for i in range(2)]; psB = [P.psum(f"psB{i}") for i in range(2)]
    psT = [P.psum(f"psT{i}", [128, 1024], BF16) for i in range(2)]; psO = [P.psum(f"psO{i}") for i in range(2)]
    si = [0]
    def load_cast(dst, src_ap, n, dres):
        P.dma("gpsimd", dst, src_ap, writes=[dres], chan="ldc", max_dma_last_dim=4096)
    qv = qT.rearrange("(c p) n -> p c n", p=128)
    for c in range(3):
        load_cast(qb[:, c, :], qv[:, c, :], 2048, "qb")
    kcv = kcT.rearrange("(c p) n -> p c n", p=128)
    for c in range(3):
        load_cast(kcb[:, c, :], kcv[:, c, :], 256, "kcb")
    load_cast(vcb[:].rearrange("p a b -> p (a b)"), vc.rearrange("p a b -> p (a b)"), 768, "vcb")
    load_cast(idb[:], ident, 128, "idb")
    for ty in range(5):
        P.dma("sync", TB[:, ty, :, :], tbraw[ty], writes=[("TB", ty)], chan="ld0")
    P.dma("sync", mk[:], mask.rearrange("t p n -> p t n"), writes=["mk"], chan="ld0")
    for ty in range(5):
        mb = mk[:, ty, :].rearrange("p (o n) -> p o n", o=1).to_broadcast([128, 6, 576])
        P.tt(TB[:, ty, :, :], TB[:, ty, :, :], mb, ALU.add, reads=[("TB", ty), "mk"], writes=[("TB", ty)])
    tiles = [("main", t) for t in range(16)]
    if with_ctx:
        qcT = P.dram_in("qcT", [384, 256]); Yc = P.dram_out("Yc", [256, 384])
        qcb = P.sbuf("qcb", [128, 3, 256], BF16)
        qcv = qcT.rearrange("(c p) n -> p c n", p=128)
        for c in range(3):
            load_cast(qcb[:, c, :], qcv[:, c, :], 256, "qcb")
        tiles += [("ctx", 0), ("ctx", 1)]
    units = [(ti, kind, t, h) for ti, (kind, t) in enumerate(tiles) for h in range(6)]
    loaded = set()
    def tile_load(ti, kind, t):
        if ti in loaded or kind != "main":
            return
        loaded.add(ti)
        wb = ti % 2
        P.dma("gpsimd", kb[wb][:], kwT[t].rearrange("(c p) n -> p c n", p=128), writes=[("kb", wb)], chan=f"kb{wb}", max_dma_last_dim=4096)
        P.dma("gpsimd", vb[wb][:], vw[t], writes=[("vb", wb)], chan=f"vb{wb}", max_dma_last_dim=4096)
    def info(u):
        ti, kind, t, h = units[u]
        return ti, kind, t, h, u % 2, u % 4, ti % 2, h // 2, (h % 2) * 64, u % 8
    def stA(u):
        ti, kind, t, h, b, b3, wb, c, p0, b8 = info(u)
        tile_load(ti, kind, t)
        if kind == "main":
            qs = qb[p0:p0 + 64, c, t * 128:(t + 1) * 128]; qres = "qb"
        else:
            qs = qcb[p0:p0 + 64, c, t * 128:(t + 1) * 128]; qres = "qcb"
        P.mm(psB[b][:, 64:320], qs, kcb[p0:p0 + 64, c, :], True, True, reads=[qres, "kcb"], writes=[("psB", b)])
        if kind == "main":
            P.mm(psA[b][:, 0:512], qs, kb[wb][p0:p0 + 64, c, 0:512], True, True, reads=[qres, ("kb", wb)], writes=[("psA", b)])
            P.mm(psB[b][:, 0:64], qs, kb[wb][p0:p0 + 64, c, 512:576], True, True, reads=[qres, ("kb", wb)], writes=[("psB", b)])
        P.act(S[b3][:, 0:256], psB[b][:, 64:320], AF.Copy, reads=[("psB", b)], writes=[("Sc", b3)], scale=0.125)
    def stB(u):
        ti, kind, t, h, b, b3, wb, c, p0, b8 = info(u)
        W = 832 if kind == "main" else 256
        if kind == "main":
            ty = {0: 0, 1: 1, 14: 3, 15: 4}.get(t, 2)
            P.stt(S[b3][:, 256:768], psA[b][:, 0:512], 0.125, TB[:, ty, h, 0:512], ALU.mult, ALU.add,
                  reads=[("psA", b), ("TB", ty)], writes=[("Sw", b3)])
            P.stt(S[b3][:, 768:832], psB[b][:, 0:64], 0.125, TB[:, ty, h, 512:576], ALU.mult, ALU.add,
                  reads=[("psB", b), ("TB", ty)], writes=[("Sw2", b3)])
        P.op("vector", lambda e: e.tensor_reduce(out=mx[b8][:], in_=S[b3][:, 0:W], axis=AX.X, op=ALU.max, negate=True),
             reads=[("Sc", b3), ("Sw", b3), ("Sw2", b3)], writes=[("mx", b8)])
        P.act(Pb[b3][:, 0:W], S[b3][:, 0:W], AF.Exp, reads=[("Sc", b3), ("Sw", b3), ("Sw2", b3), ("mx", b8)], writes=[("Pb", b3), ("ssum", b8)],
              bias=mx[b8][:], scale=1.0, accum_out=ssum[b8][:])
    def stC(u):
        ti, kind, t, h, b, b3, wb, c, p0, b8 = info(u)
        nblk = 7 if kind == "main" else 2
        for kbk in range(nblk):
            kw = 64 if kbk == 6 else 128
            P.op("tensor", lambda e, kbk=kbk, kw=kw: e.transpose(psT[b][0:kw, kbk * 128:(kbk + 1) * 128], Pb[b3][:, kbk * 128:kbk * 128 + kw], idb[:]),
                 reads=[("Pb", b3), "idb"], writes=[("psT", b)], nosync_same=True)
        P.copy(PT[b3][:, 0:nblk * 128], psT[b][:, 0:nblk * 128], reads=[("psT", b)], writes=[("PT", b3)], eng="scalar")
    def stD(u):
        ti, kind, t, h, b, b3, wb, c, p0, b8 = info(u)
        ob = ti % 2
        nblk = 7 if kind == "main" else 2
        for kbk in range(nblk):
            kw = 64 if kbk == 6 else 128
            if kbk < 2:
                rhs = vcb[:, kbk, h * 64:(h + 1) * 64]; rres = "vcb"
            else:
                rhs = vb[wb][0:kw, kbk - 2, h * 64:(h + 1) * 64]; rres = ("vb", wb)
            P.mm(psO[ob][:, h * 64:(h + 1) * 64], PT[b3][0:kw, kbk * 128:(kbk + 1) * 128], rhs, kbk == 0, kbk == nblk - 1,
                 reads=[("PT", b3), rres], writes=[("psO", ob)])
        P.op("vector", lambda e: e.reciprocal(out=rinv[b8][:], in_=ssum[b8][:]), reads=[("ssum", b8)], writes=[("rinv", b8)])
        P.ts(Osb[ob][:, h * 64:(h + 1) * 64], psO[ob][:, h * 64:(h + 1) * 64], rinv[b8][:], None, ALU.mult,
             reads=[("psO", ob), ("rinv", b8)], writes=[("Osb", ob)])
        if h == 5:
            dst = Y[t * 128:(t + 1) * 128, :] if kind == "main" else Yc[t * 128:(t + 1) * 128, :]
            P.dma("gpsimd", dst, Osb[ob][:], reads=[("Osb", ob)], chan=f"st{ob}")
    NU = len(units)
    LC, LD = 3, 5
    for step in range(NU + LD):
        if step < NU: stA(step)
        if 0 <= step - 1 < NU: stB(step - 1)
        if 0 <= step - LC < NU: stC(step - LC)
        if 0 <= step - LD < NU: stD(step - LD)
    return P.finish()

def na_tile_geometry(r0):
    R = 128
    rs0 = int(np.clip(r0 - 4, 0, R - 8)); rs1 = int(np.clip(r0 + 1 - 4, 0, R - 8))
    return rs0, rs1

def na_tables(rpb, q):
    cols = np.arange(64); cs = np.clip(cols - 8, 0, 48)
    kc = np.arange(64)
    inwin = (kc[None, :] >= cs[:, None]) & (kc[None, :] < cs[:, None] + 16)
    dc = np.clip(kc[None, :] - cols[:, None] + 15, 0, 30)
    types = [32 * q, 32 * q + 2, 32 * q + 16, 32 * q + 28, 32 * q + 30]
    tbraw = np.zeros((5, 128, 6, 9, 64), np.float32); mask = np.zeros((5, 128, 9, 64), np.float32)
    for ti, r0 in enumerate(types):
        rs0, rs1 = na_tile_geometry(r0)
        for half, (r, rs) in enumerate(((r0, rs0), (r0 + 1, rs1))):
            for slot in range(9):
                krow = rs0 + slot
                valid = (krow >= rs) and (krow < rs + 8)
                dr = int(np.clip(krow - r + 7, 0, 14))
                g = rpb[:, dr][:, dc]
                tbraw[ti, half * 64:(half + 1) * 64, :, slot, :] = g.transpose(1, 0, 2)
                m = np.where(inwin & valid, 0.0, NEG).astype(np.float32)
                mask[ti, half * 64:(half + 1) * 64, slot, :] = m
    return tbraw.reshape(5, 128, 6, 576), mask.reshape(5, 128, 576)

def na_windows(k_b, v_b, q):
    kp = np.concatenate([k_b, np.zeros((64 * 16, 384), k_b.dtype)], 0); vp = np.concatenate([v_b, np.zeros((64 * 16, 384), v_b.dtype)], 0)
    kwT = np.zeros((16, 384, 576), k_b.dtype); vw = np.zeros((16, 128, 5, 384), v_b.dtype)
    for t in range(16):
        r0 = 32 * q + 2 * t
        rs0, _ = na_tile_geometry(r0)
        kwT[t] = kp[rs0 * 64:(rs0 + 9) * 64].T
        for j in range(4):
            vw[t, :, j, :] = vp[(rs0 + 2 * j) * 64:(rs0 + 2 * j + 2) * 64]
        vw[t, 0:64, 4, :] = vp[(rs0 + 8) * 64:(rs0 + 9) * 64]
    return kwT, vw

NCH = 1056
PI = math.pi

class A:
    def __init__(self, P): self.P = P
    @staticmethod
    def nm(*aps): return [a.tensor.name for a in aps if hasattr(a, "tensor")]
    def tt(self, o, a, b, op, eng="vector"): self.P.tt(o, a, b, op, reads=self.nm(a, b), writes=self.nm(o), eng=eng)
    def ts(self, o, a, s1, op0, s2=None, op1=None, eng="vector"):
        self.P.ts(o, a, s1, s2, op0, op1, reads=self.nm(a, s1, s2), writes=self.nm(o), eng=eng)
    def stt(self, o, a, s, b, op0, op1): self.P.stt(o, a, s, b, op0, op1, reads=self.nm(a, s, b), writes=self.nm(o))
    def act(self, o, a, f, **kw): self.P.act(o, a, f, reads=self.nm(a, *[v for v in kw.values()]), writes=self.nm(o), **kw)
    def copy(self, o, a, eng="vector"): self.P.copy(o, a, reads=self.nm(a), writes=self.nm(o), eng=eng)
    def memset(self, o, v, eng="vector"): self.P.memset(o, v, writes=self.nm(o), eng=eng)
    def mm(self, o, l, r, st, sp): self.P.mm(o, l, r, st, sp, reads=self.nm(l, r), writes=self.nm(o))
    def dma_in(self, o, src, chan): self.P.dma("sync", o, src, writes=self.nm(o), chan=chan)
    def dma_out(self, dst, a, chan="st"): self.P.dma("gpsimd", dst, a, reads=self.nm(a), chan=chan)
    def scan(self, o, d0, d1, init):
        self.P.op("vector", lambda e: e.tensor_tensor_scan(out=o, data0=d0, data1=d1, initial=init, op0=ALU.mult, op1=ALU.add),
                  reads=self.nm(d0, d1, init), writes=self.nm(o))
    def recip(self, o, a): self.P.op("vector", lambda e: e.reciprocal(out=o, in_=a), reads=self.nm(a), writes=self.nm(o))
    def transpose(self, o, a, ident):
        self.P.op("tensor", lambda e: e.transpose(o, a, ident), reads=self.nm(a, ident), writes=self.nm(o), nosync_same=True)
    def cmul_s(self, o_re, o_im, a_re, a_im, s_re, s_im, s_imn):
        self.ts(o_re, a_re, s_re, ALU.mult)
        self.stt(o_re, a_im, s_imn, o_re, ALU.mult, ALU.add)
        self.ts(o_im, a_re, s_im, ALU.mult)
        self.stt(o_im, a_im, s_re, o_im, ALU.mult, ALU.add)

def build_ssm():
    P = Prog(); a = A(P)
    d_in = {}
    for nm_, shp in (("are", [128, 6]), ("aim", [128, 6]), ("ldt", [128, 6]), ("Bre", [128, 96]), ("Bim", [128, 96]),
                     ("Cre", [128, 96]), ("Cim", [128, 96]), ("maskF", [128, 128]), ("maskB", [128, 128]), ("sgn", [128, 1]),
                     ("ident", [128, 128])):
        d_in[nm_] = P.dram_in(nm_, shp)
    Ddiag = P.dram_in("Ddiag", [6, 128, 128]); U = P.dram_in("U", [6, 128, NCH]); Yg = P.dram_out("Yg", [6, 128, NCH])
    s = {}
    for nm_, ap in d_in.items():
        shp = list(ap.shape)
        s[nm_] = P.sbuf("s_" + nm_, shp)
        a.dma_in(s[nm_][:], ap, "ld0")
    def T(name, shape): return P.sbuf(name, shape)
    dt = T("dt", [128, 6]); x = T("x", [128, 6]); th = T("th", [128, 6]); er = T("er", [128, 6]); m = T("m", [128, 6])
    y2 = T("y2", [128, 6]); sn = T("sn", [128, 6]); cs = T("cs", [128, 6]); lbr = T("lbr", [128, 6]); lbi = T("lbi", [128, 6])
    n2 = T("n2", [128, 6]); t1 = T("t1", [128, 6]); t2 = T("t2", [128, 6]); am1 = T("am1", [128, 6])
    qr = T("qr", [128, 6]); qi = T("qi", [128, 6]); qin = T("qin", [128, 6])
    a.act(dt[:], s["ldt"][:], AF.Exp)
    a.tt(x[:], s["are"][:], dt[:], ALU.mult); a.tt(th[:], s["aim"][:], dt[:], ALU.mult)
    a.act(er[:], x[:], AF.Exp)
    for _ in range(4):
        a.ts(m[:], th[:], PI, ALU.is_gt)
        a.stt(th[:], m[:], -2 * PI, th[:], ALU.mult, ALU.add)
    a.ts(y2[:], th[:], PI / 2, ALU.add)
    a.ts(m[:], y2[:], PI, ALU.is_gt)
    a.stt(y2[:], m[:], -2 * PI, y2[:], ALU.mult, ALU.add)
    a.act(sn[:], th[:], AF.Sin); a.act(cs[:], y2[:], AF.Sin)
    a.tt(lbr[:], er[:], cs[:], ALU.mult); a.tt(lbi[:], er[:], sn[:], ALU.mult)
    a.tt(n2[:], s["are"][:], s["are"][:], ALU.mult); a.tt(t1[:], s["aim"][:], s["aim"][:], ALU.mult); a.tt(n2[:], n2[:], t1[:], ALU.add)
    a.recip(n2[:], n2[:])
    a.ts(am1[:], lbr[:], -1.0, ALU.add)
    a.tt(t1[:], am1[:], s["are"][:], ALU.mult); a.tt(t2[:], lbi[:], s["aim"][:], ALU.mult); a.tt(t1[:], t1[:], t2[:], ALU.add)
    a.tt(qr[:], t1[:], n2[:], ALU.mult)
    a.tt(t1[:], lbi[:], s["are"][:], ALU.mult); a.tt(t2[:], am1[:], s["aim"][:], ALU.mult); a.tt(t1[:], t1[:], t2[:], ALU.subtract)
    a.tt(qi[:], t1[:], n2[:], ALU.mult)
    a.ts(qin[:], qi[:], -1.0, ALU.mult)
    Lr = T("Lr", [128, 6, 9]); Li = T("Li", [128, 6, 9]); Vr = T("Vr", [128, 6, 8]); Vi = T("Vi", [128, 6, 8])
    Rr = T("Rr", [128, 6, 9]); Ri = T("Ri", [128, 6, 9])
    e2 = T("e2", [128, 6]); ivr = T("ivr", [128, 6]); ivi = T("ivi", [128, 6])
    a.memset(Lr[:, :, 0], 1.0); a.memset(Li[:, :, 0], 0.0); a.memset(Vr[:, :, 0], 1.0); a.memset(Vi[:, :, 0], 0.0)
    a.act(e2[:], x[:], AF.Exp, scale=-2.0)
    a.tt(ivr[:], lbr[:], e2[:], ALU.mult); a.tt(ivi[:], lbi[:], e2[:], ALU.mult); a.ts(ivi[:], ivi[:], -1.0, ALU.mult)
    def cmul_t(o_r, o_i, p_r, p_i, q_r, q_i):
        a.tt(t1[:], p_r, q_r, ALU.mult); a.tt(t2[:], p_i, q_i, ALU.mult); a.tt(o_r, t1[:], t2[:], ALU.subtract)
        a.tt(t1[:], p_r, q_i, ALU.mult); a.tt(t2[:], p_i, q_r, ALU.mult); a.tt(o_i, t1[:], t2[:], ALU.add)
    for k in range(8):
        cmul_t(Lr[:, :, k + 1], Li[:, :, k + 1], Lr[:, :, k], Li[:, :, k], lbr[:], lbi[:])
    for k in range(7):
        cmul_t(Vr[:, :, k + 1], Vi[:, :, k + 1], Vr[:, :, k], Vi[:, :, k], ivr[:], ivi[:])
    for k in range(9):
        a.copy(Rr[:, :, k], Lr[:, :, 8 - k], eng="gpsimd"); a.copy(Ri[:, :, k], Li[:, :, 8 - k], eng="gpsimd")
    tabs = {}
    for nm_, (lo_r, lo_i, hi_r, hi_i) in dict(
            XL=(Vr[0:64, :, 0:8], Vi[0:64, :, 0:8], Lr[64:128, :, 0:8], Li[64:128, :, 0:8]),
            YL=(Lr[0:64, :, 0:8], Li[0:64, :, 0:8], Vr[64:128, :, 0:8], Vi[64:128, :, 0:8]),
            SL=(Rr[0:64, :, 1:9], Ri[0:64, :, 1:9], Lr[64:128, :, 0:8], Li[64:128, :, 0:8]),
            OL=(Lr[0:64, :, 1:9], Li[0:64, :, 1:9], Rr[64:128, :, 0:8], Ri[64:128, :, 0:8])).items():
        tr = T(nm_ + "r", [128, 6, 8]); ti = T(nm_ + "i", [128, 6, 8]); tn = T(nm_ + "n", [128, 6, 8])
        a.copy(tr[0:64], lo_r); a.copy(ti[0:64], lo_i); a.copy(tr[64:128], hi_r); a.copy(ti[64:128], hi_i)
        a.ts(tn[:], ti[:], -1.0, ALU.mult)
        tabs[nm_] = (tr, ti, tn)
    rho8 = T("rho8", [128, 6]); c8 = T("c8", [128, 6]); s8 = T("s8", [128, 6]); e8 = T("e8", [128, 6])
    a.act(rho8[:], x[:], AF.Exp, scale=8.0); a.act(e8[:], x[:], AF.Exp, scale=-8.0)
    a.tt(c8[:], Lr[:, :, 8], e8[:], ALU.mult); a.tt(s8[:], Li[:, :, 8], e8[:], ALU.mult)
    a.ts(s8[:], s8[:], s["sgn"][:, 0:1], ALU.mult)
    onesT = T("onesT", [128, NCH]); a.memset(onesT[:], 1.0, eng="gpsimd")
    def bc_j(ap):
        return ap.rearrange("p g (o j) -> p g o j", o=1).to_broadcast([128, 6, 8, 16])
    def bc_k(ap):
        return ap.rearrange("p g (k o) -> p g k o", o=1).to_broadcast([128, 6, 8, 16])
    Bre_v = s["Bre"][:].rearrange("p (g j) -> p g j", j=16); Bim_v = s["Bim"][:].rearrange("p (g j) -> p g j", j=16)
    Cre_v = s["Cre"][:].rearrange("p (g j) -> p g j", j=16); Cim_v = s["Cim"][:].rearrange("p (g j) -> p g j", j=16)
    Bbr_a = T("Bbr_a", [128, 6, 16]); Bbi_a = T("Bbi_a", [128, 6, 16])
    u1 = T("u1", [128, 6, 16]); u2 = T("u2", [128, 6, 16])
    qr_b = qr[:].rearrange("p (g o) -> p g o", o=1).to_broadcast([128, 6, 16]); qi_b = qi[:].rearrange("p (g o) -> p g o", o=1).to_broadcast([128, 6, 16])
    a.tt(u1[:], Bre_v, qr_b, ALU.mult); a.tt(u2[:], Bim_v, qi_b, ALU.mult, eng="gpsimd"); a.tt(Bbr_a[:], u1[:], u2[:], ALU.subtract)
    a.tt(u1[:], Bre_v, qi_b, ALU.mult); a.tt(u2[:], Bim_v, qr_b, ALU.mult, eng="gpsimd"); a.tt(Bbi_a[:], u1[:], u2[:], ALU.add)
    v1 = T("v1", [128, 6, 8, 16]); v2 = T("v2", [128, 6, 8, 16])
    def ctab(name, Ar, Ai, tb, neg_im):
        o_r = T(name + "r_a", [128, 6, 8, 16]); o_i = T(name + "i_a", [128, 6, 8, 16])
        Sr, Si = tb[0][:], tb[1][:]
        a.tt(v1[:], bc_j(Ar), bc_k(Sr), ALU.mult); a.tt(v2[:], bc_j(Ai), bc_k(Si), ALU.mult, eng="gpsimd")
        a.tt(o_r[:], v1[:], v2[:], ALU.subtract)
        a.tt(v1[:], bc_j(Ar), bc_k(Si), ALU.mult); a.tt(v2[:], bc_j(Ai), bc_k(Sr), ALU.mult, eng="gpsimd")
        a.tt(o_i[:], v1[:], v2[:], ALU.add)
        if neg_im:
            a.ts(o_i[:], o_i[:], -1.0, ALU.mult)
        return o_r, o_i
    Xr_a, Xi_a = ctab("X", Bbr_a[:], Bbi_a[:], tabs["XL"], False)
    Wtr_a, Wti_a = ctab("Wt", Bbr_a[:], Bbi_a[:], tabs["SL"], False)
    Yr_a, Yin_a = ctab("Y", Cre_v, Cim_v, tabs["YL"], True)
    Wor_a, Woin_a = ctab("Wo", Cre_v, Cim_v, tabs["OL"], True)
    Wsr = T("Wsr", [128, 128]); Wsi = T("Wsi", [128, 128]); Msb = T("Msb", [128, 128]); Mtmp = T("Mtmp", [128, 128]); Dd = T("Dd", [128, 128])
    Us = [T(f"Us{i}", [128, NCH]) for i in range(2)]
    Sre = T("Sre", [128, NCH]); Sim = T("Sim", [128, NCH]); Spr = T("Spr", [128, NCH]); Spi = T("Spi", [128, NCH])
    Gre = T("Gre", [128, NCH]); Gim = T("Gim", [128, NCH]); Hor = T("Hor", [128, NCH]); Hoi = T("Hoi", [128, NCH])
    Hir = T("Hir", [128, NCH]); Hii = T("Hii", [128, NCH])
    Tr = T("Tr", [128, NCH + 1]); Ti = T("Ti", [128, NCH + 1]); rhoT = T("rhoT", [128, NCH])
    w1 = T("w1", [128, NCH]); w2 = T("w2", [128, NCH])
    mult = T("mult", [128, 11, 3]); ini = T("ini", [128, 4]); Ysb = T("Ysb", [128, NCH])
    ps = [P.psum(f"ps{i}") for i in range(8)]
    BLK = [(0, 512), (512, 512), (1024, NCH - 1024)]
    for gi in range(6):
        ub = gi % 2
        a.dma_in(Us[ub][:], U[gi], f"u{ub}")
        a.dma_in(Dd[:], Ddiag[gi], "dd")
        Xr, Xi, Yr, Yin = Xr_a[:, gi], Xi_a[:, gi], Yr_a[:, gi], Yin_a[:, gi]
        Wtr, Wti, Wor, Woin = Wtr_a[:, gi], Wti_a[:, gi], Wor_a[:, gi], Woin_a[:, gi]
        f2 = lambda t_: t_.rearrange("p a b -> p (a b)")
        for half, pb in ((0, 6), (1, 7)):
            rows = slice(half * 64, half * 64 + 64)
            a.mm(ps[pb][:, 0:128], f2(Xr)[rows], f2(Yr)[rows], True, False)
            a.mm(ps[pb][:, 0:128], f2(Xi)[rows], f2(Yin)[rows], False, True)
        a.tt(Msb[:], ps[6][:, 0:128], s["maskF"][:], ALU.mult)
        a.tt(Mtmp[:], ps[7][:, 0:128], s["maskB"][:], ALU.mult)
        a.tt(Msb[:], Msb[:], Mtmp[:], ALU.add, eng="gpsimd"); a.tt(Msb[:], Msb[:], Dd[:], ALU.add, eng="gpsimd")
        a.transpose(ps[6][:, 128:256], f2(Wtr), s["ident"][:]); a.transpose(ps[7][:, 128:256], f2(Wti), s["ident"][:])
        a.copy(Wsr[:], ps[6][:, 128:256], eng="scalar"); a.copy(Wsi[:], ps[7][:, 128:256], eng="scalar")
        for bi, (c0, cn) in enumerate(BLK):
            a.mm(ps[bi][:, 0:cn], Wsr[:], Us[ub][:, c0:c0 + cn], True, True)
            a.mm(ps[3 + bi][:, 0:cn], Wsi[:], Us[ub][:, c0:c0 + cn], True, True)
            a.copy(Sre[:, c0:c0 + cn], ps[bi][:, 0:cn], eng="scalar"); a.copy(Sim[:, c0:c0 + cn], ps[3 + bi][:, 0:cn], eng="scalar")
        a.memset(Tr[:, 0:1], 1.0); a.memset(Ti[:, 0:1], 0.0)
        a.copy(mult[:, 0, 0:1], c8[:, gi:gi + 1]); a.copy(mult[:, 0, 1:2], s8[:, gi:gi + 1])
        a.ts(mult[:, 0, 2:3], mult[:, 0, 1:2], -1.0, ALU.mult)
        for k in range(1, 11):
            a.tt(ini[:, 0:1], mult[:, k - 1, 0:1], mult[:, k - 1, 0:1], ALU.mult); a.tt(ini[:, 1:2], mult[:, k - 1, 1:2], mult[:, k - 1, 1:2], ALU.mult)
            a.tt(mult[:, k, 0:1], ini[:, 0:1], ini[:, 1:2], ALU.subtract)
            a.tt(ini[:, 0:1], mult[:, k - 1, 0:1], mult[:, k - 1, 1:2], ALU.mult)
            a.ts(mult[:, k, 1:2], ini[:, 0:1], 2.0, ALU.mult); a.ts(mult[:, k, 2:3], ini[:, 0:1], -2.0, ALU.mult)
        for k in range(11):
            n = 1 << k
            cnt = min(n, NCH + 1 - n)
            a.cmul_s(Tr[:, n:n + cnt], Ti[:, n:n + cnt], Tr[:, 0:cnt], Ti[:, 0:cnt], mult[:, k, 0:1], mult[:, k, 1:2], mult[:, k, 2:3])
        a.ts(rhoT[:], onesT[:], rho8[:, gi:gi + 1], ALU.mult, eng="gpsimd")
        a.tt(w1[:], Sre[:], Tr[:, 0:NCH], ALU.mult); a.tt(w2[:], Sim[:], Ti[:, 0:NCH], ALU.mult, eng="gpsimd")
        a.tt(Spr[:], w1[:], w2[:], ALU.subtract)
        a.tt(w1[:], Sre[:], Ti[:, 0:NCH], ALU.mult); a.tt(w2[:], Sim[:], Tr[:, 0:NCH], ALU.mult, eng="gpsimd")
        a.tt(Spi[:], w1[:], w2[:], ALU.add)
        for (Gx, Sx) in ((Gre, Spr), (Gim, Spi)):
            a.scan(Gx[0:64, :], rhoT[0:64, :], Sx[0:64, :], 0.0)
            a.scan(Gx[64:128, 0:32][:, ::-1], rhoT[64:128, 0:32], Sx[64:128, 0:32][:, ::-1], 0.0)
        lo = slice(64, 128)
        a.tt(ini[lo, 0:1], Gre[lo, 0:1], Tr[lo, NCH:NCH + 1], ALU.mult); a.tt(ini[lo, 1:2], Gim[lo, 0:1], Ti[lo, NCH:NCH + 1], ALU.mult)
        a.tt(ini[lo, 2:3], ini[lo, 0:1], ini[lo, 1:2], ALU.subtract)
        a.tt(ini[lo, 0:1], Gre[lo, 0:1], Ti[lo, NCH:NCH + 1], ALU.mult); a.tt(ini[lo, 1:2], Gim[lo, 0:1], Tr[lo, NCH:NCH + 1], ALU.mult)
        a.tt(ini[lo, 3:4], ini[lo, 0:1], ini[lo, 1:2], ALU.add)
        a.scan(Gre[lo, 32:NCH][:, ::-1], rhoT[lo, 32:NCH], Spr[lo, 32:NCH][:, ::-1], ini[lo, 2:3])
        a.scan(Gim[lo, 32:NCH][:, ::-1], rhoT[lo, 32:NCH], Spi[lo, 32:NCH][:, ::-1], ini[lo, 3:4])
        a.tt(w1[:], Gre[:], Tr[:, 0:NCH], ALU.mult); a.tt(w2[:], Gim[:], Ti[:, 0:NCH], ALU.mult, eng="gpsimd")
        a.tt(Hor[:], w1[:], w2[:], ALU.add)
        a.tt(w1[:], Gim[:], Tr[:, 0:NCH], ALU.mult); a.tt(w2[:], Gre[:], Ti[:, 0:NCH], ALU.mult, eng="gpsimd")
        a.tt(Hoi[:], w1[:], w2[:], ALU.subtract)
        for (Hi_, Ho_, Gx) in ((Hir, Hor, Gre), (Hii, Hoi, Gim)):
            a.copy(Hi_[0:64, 1:NCH], Ho_[0:64, 0:NCH - 1], eng="scalar"); a.memset(Hi_[0:64, 0:1], 0.0)
            a.copy(Hi_[lo, 0:NCH - 1], Ho_[lo, 1:NCH], eng="scalar"); a.memset(Hi_[lo, 31:32], 0.0)
            a.copy(Hi_[lo, NCH - 1:NCH], Gx[lo, 0:1])
        for bi, (c0, cn) in enumerate(BLK):
            a.mm(ps[bi][:, 0:cn], Msb[:], Us[ub][:, c0:c0 + cn], True, False)
            a.mm(ps[bi][:, 0:cn], f2(Wor), Hir[:, c0:c0 + cn], False, False)
            a.mm(ps[bi][:, 0:cn], f2(Woin), Hii[:, c0:c0 + cn], False, True)
            a.copy(Ysb[:, c0:c0 + cn], ps[bi][:, 0:cn], eng="scalar")
        a.dma_out(Yg[gi], Ysb[:])
    return P.finish()

def ssm_inputs(inp, l, j4, u_b, uc_b):
    gs = np.arange(6 * j4, 6 * j4 + 6)
    def rows(arr):
        return np.ascontiguousarray(arr[:, gs, :].transpose(0, 2, 1).reshape(128, 6))
    are = rows(inp["ssm_a_re"][l]); aim = rows(inp["ssm_a_im"][l])
    ldt = np.ascontiguousarray(np.repeat(inp["ssm_log_dt"][l][:, gs][:, None, :], 64, axis=1).reshape(128, 6))
    def rowsB(arr):
        return np.ascontiguousarray(arr[:, gs].transpose(0, 2, 1, 3).reshape(128, 96))
    def rowsC(arr):
        return np.ascontiguousarray(arr[:, gs].transpose(0, 3, 1, 2).reshape(128, 96))
    s_ = np.arange(8)
    mF = (s_[None, :] >= s_[:, None]).astype(np.float32)
    maskF = np.kron(mF, np.ones((16, 16), np.float32)); maskB = np.kron(mF.T, np.ones((16, 16), np.float32))
    sgn = np.concatenate([-np.ones((64, 1), np.float32), np.ones((64, 1), np.float32)], 0)
    dsk = inp["ssm_d"][l]
    Dd = np.zeros((6, 128, 128), np.float32)
    for gi, g in enumerate(gs):
        dd = np.zeros((8, 16, 8, 16), np.float32)
        for t in range(8):
            dd[t, np.arange(16), t, np.arange(16)] = dsk[16 * g:16 * g + 16]
        Dd[gi] = dd.reshape(128, 128)
    seq = np.concatenate([uc_b, u_b], 0)
    U = np.zeros((6, 128, NCH), np.float32)
    for gi, g in enumerate(gs):
        U[gi] = seq[:, 16 * g:16 * g + 16].reshape(NCH, 128).T
    return dict(are=are, aim=aim, ldt=ldt, Bre=rowsB(inp["ssm_b_re"][l]), Bim=rowsB(inp["ssm_b_im"][l]),
                Cre=rowsC(inp["ssm_c_re"][l]), Cim=rowsC(inp["ssm_c_im"][l]), maskF=maskF, maskB=maskB, sgn=sgn,
                ident=np.eye(128, dtype=np.float32), Ddiag=Dd, U=U)

def ssm_unpack(Yg):
    return np.ascontiguousarray(Yg.transpose(2, 1, 0).reshape(NCH, 8, 16, 6).transpose(0, 1, 3, 2).reshape(NCH * 8, 96))


_PROGS = {}
def _prog(name, fn):
    if name not in _PROGS:
        _PROGS[name] = fn()
    return _PROGS[name]

def _run(nc, maps):
    res = run_bass_kernel_spmd(nc, maps, core_ids=list(range(8)))
    return res.results

def kernel(x, c, ctx, c_ctx, w_mod, b_mod, g_pre_mix, g_post_mix, w_in, ssm_a_re, ssm_a_im, ssm_log_dt, ssm_b_re, ssm_b_im,
           ssm_c_re, ssm_c_im, ssm_d, w_glu, w_fourier, na_rpb, w_out, g_pre_ffn, g_post_ffn, w_ffn_gate, w_ffn_up, w_ffn_down):
    f32 = lambda a: np.ascontiguousarray(np.asarray(a, dtype=np.float32))
    inp = dict(ssm_a_re=f32(ssm_a_re), ssm_a_im=f32(ssm_a_im), ssm_log_dt=f32(ssm_log_dt), ssm_b_re=f32(ssm_b_re), ssm_b_im=f32(ssm_b_im),
               ssm_c_re=f32(ssm_c_re), ssm_c_im=f32(ssm_c_im), ssm_d=f32(ssm_d))
    x = f32(x); c = f32(c); ctx = f32(ctx); c_ctx = f32(c_ctx); w_mod = f32(w_mod); b_mod = f32(b_mod)
    w_in = f32(w_in); w_glu = f32(w_glu); w_fourier = f32(w_fourier); na_rpb = f32(na_rpb); w_out = f32(w_out)
    g_pre_mix = f32(g_pre_mix); g_post_mix = f32(g_post_mix); g_pre_ffn = f32(g_pre_ffn); g_post_ffn = f32(g_post_ffn)
    w_ffn_gate = f32(w_ffn_gate); w_ffn_up = f32(w_ffn_up); w_ffn_down = f32(w_ffn_down)
    DEPTH = 2
    cores = [(k // 4, k % 4) for k in range(8)]
    cTs = [np.ascontiguousarray(np.concatenate([colT(c[b], 8), colT(c_ctx, 8)], axis=1)) for b in range(2)]
    xT = [np.ascontiguousarray(np.concatenate([x[b, q * 2048:(q + 1) * 2048].T, ctx[b].T], axis=1)) for (b, q) in cores]
    KF = fnet_consts(); permm = perm_matrix(); ident = np.eye(128, dtype=np.float32)
    ropes = [rope_tables(q * 2048, 2048) for q in range(4)]
    for l in range(DEPTH):
        maps = []
        for k, (b, q) in enumerate(cores):
            maps.append(dict(xT=xT[k], w_in=w_in[l], w_mod=np.ascontiguousarray(w_mod[l][:, 0:2048]), b_modT=colT(b_mod[l][0:2048], 16),
                             g_preT=colT(g_pre_mix[l], 8), cT=cTs[b], cos=ropes[q][0], sin=ropes[q][1], perm=permm))
        res = _run(_prog("l1", build_l1), maps)
        hfull = [np.concatenate([res[k]["hT"], res[k]["hTb"]], 0) for k in range(8)]
        h_lat = [np.concatenate([hfull[4 * b + q][:, 0:2048].T for q in range(4)], 0) for b in range(2)]
        h_ctx = [np.ascontiguousarray(hfull[4 * b][:, 2048:2304].T) for b in range(2)]
        del hfull
        del res
        maps = [ssm_inputs(inp, l, j4, h_lat[b][:, 0:384], h_ctx[b][:, 0:384]) for (b, j4) in cores]
        res = _run(_prog("ssm", build_ssm), maps)
        ys = [[ssm_unpack(res[4 * b + j4]["Yg"]) for j4 in range(4)] for b in range(2)]
        ysT = [np.ascontiguousarray(np.concatenate(ys[b], 1).T) for b in range(2)]
        del res, ys
        maps = []
        for (b, g) in cores:
            m = dict(z=np.ascontiguousarray(h_lat[b][:, 384 + 64 * g:448 + 64 * g].reshape(128, 4096)),
                     zc=np.ascontiguousarray(h_ctx[b][:, 384 + 64 * g:448 + 64 * g].reshape(2, 128, 64).transpose(1, 0, 2)))
            m.update(KF); maps.append(m)
        res = _run(_prog("fnet", lambda: build_fnet(True)), maps)
        mxT = [np.concatenate([res[4 * b + g]["R"] for g in range(4)], 0) for b in range(2)]
        mxcT = [np.concatenate([res[4 * b + g]["Rc"] for g in range(4)], 0) for b in range(2)]
        del res
        maps = []
        for (b, q) in cores:
            kwT, vw = na_windows(h_lat[b][:, 1024:1408], h_lat[b][:, 1408:1792], q)
            tbraw, mask = na_tables(na_rpb[l], q)
            maps.append(dict(qT=np.ascontiguousarray(h_lat[b][q * 2048:(q + 1) * 2048, 640:1024].T), kwT=kwT, vw=vw,
                             kcT=np.ascontiguousarray(h_ctx[b][:, 1024:1408].T),
                             vc=np.ascontiguousarray(h_ctx[b][:, 1408:1792].reshape(2, 128, 384).transpose(1, 0, 2)),
                             tbraw=tbraw, mask=mask, ident=ident, qcT=np.ascontiguousarray(h_ctx[b][:, 640:1024].T)))
        res = _run(_prog("na", lambda: build_na(True)), maps)
        naT = [np.ascontiguousarray(np.concatenate([res[4 * b + q]["Y"] for q in range(4)], 0).T) for b in range(2)]
        nacT = [np.ascontiguousarray(res[4 * b]["Yc"].T) for b in range(2)]
        del res, h_lat
        maps = []
        for k, (b, q) in enumerate(cores):
            sl = slice(q * 2048, (q + 1) * 2048)
            maps.append(dict(xT=xT[k], ysT=np.ascontiguousarray(np.concatenate([ysT[b][:, 256 + q * 2048:256 + (q + 1) * 2048], ysT[b][:, 0:256]], 1)),
                             mxT=np.ascontiguousarray(np.concatenate([mxT[b][:, sl], mxcT[b]], 1)),
                             naT=np.ascontiguousarray(np.concatenate([naT[b][:, sl], nacT[b]], 1)),
                             w_mod=np.ascontiguousarray(w_mod[l][:, 2048:3072]), b_modT=colT(b_mod[l][2048:3072], 8), cT=cTs[b],
                             g_postT=colT(g_post_mix[l], 8), w_glu=w_glu[l], w_fourier=w_fourier[l], w_out=w_out[l]))
        res = _run(_prog("l3a", lambda: build_l3a(True)), maps)
        xT = [res[k]["xoT"] for k in range(8)]
        del res
        maps = []
        for k, (b, q) in enumerate(cores):
            maps.append(dict(xT=xT[k], w_mod=np.ascontiguousarray(w_mod[l][:, 3072:6144]), b_modT=colT(b_mod[l][3072:6144], 24), cT=cTs[b],
                             g_preT=colT(g_pre_ffn[l], 8), g_postT=colT(g_post_ffn[l], 8),
                             w_gate=w_ffn_gate[l], w_up=w_ffn_up[l], w_down=w_ffn_down[l]))
        res = _run(_prog("l3b", lambda: build_l3b(True)), maps)
        xT = [np.ascontiguousarray(res[k]["xoT"]) for k in range(8)]
        del res
    out = np.empty((2, 8192, 1024), np.float32)
    for k, (b, q) in enumerate(cores):
        out[b, q * 2048:(q + 1) * 2048] = xT[k][:, 0:2048].T
    return out
```

```python
import math
import numpy as np
from contextlib import ExitStack
import concourse.bass as bass
import concourse.mybir as mybir
from concourse.bass_utils import run_bass_kernel_spmd


F32 = mybir.dt.float32
BF16 = mybir.dt.bfloat16
ALU = mybir.AluOpType
AF = mybir.ActivationFunctionType
AX = mybir.AxisListType

COMPUTE = ("tensor", "vector", "scalar", "gpsimd")


class Prog:
    def __init__(self):
        self.nc = bass.Bass("TRN2", target_bir_lowering=False)
        self.ops = []
        self.stack = ExitStack()
        self.ndram = 0

    def dram_in(self, name, shape, dtype=F32):
        return self.nc.dram_tensor(name, list(shape), dtype, kind="ExternalInput").ap()

    def dram_out(self, name, shape, dtype=F32):
        return self.nc.dram_tensor(name, list(shape), dtype, kind="ExternalOutput").ap()

    def sbuf(self, name, shape, dtype=F32):
        return self.stack.enter_context(self.nc.sbuf_tensor(name, list(shape), dtype))

    def psum(self, name, shape=(128, 512), dtype=F32):
        return self.stack.enter_context(self.nc.psum_tensor(name, list(shape), dtype))

    def op(self, eng, fn, reads=(), writes=(), chan=None, nosync_same=False, inc=True):
        self.ops.append(dict(eng=eng, fn=fn, reads=tuple(reads), writes=tuple(writes),
                             chan=chan, nosync_same=nosync_same, inc=inc))

    def dma(self, eng, out, in_, reads=(), writes=(), chan="ld", **kw):
        self.op(eng, lambda e: e.dma_start(out=out, in_=in_, **kw), reads, writes, chan=chan)

    def mm(self, out, lhsT, rhs, start, stop, reads=(), writes=()):
        self.op("tensor", lambda e: e.matmul(out, lhsT, rhs, start=start, stop=stop),
                reads, writes, nosync_same=True, inc=True)

    def act(self, out, in_, func, reads=(), writes=(), **kw):
        self.op("scalar", lambda e: e.activation(out=out, in_=in_, func=func, **kw), reads, writes)

    def tt(self, out, in0, in1, op, reads=(), writes=(), eng="vector"):
        self.op(eng, lambda e: e.tensor_tensor(out=out, in0=in0, in1=in1, op=op), reads, writes)

    def ts(self, out, in0, s1, s2, op0, op1=None, reads=(), writes=(), eng="vector"):
        if op1 is None:
            self.op(eng, lambda e: e.tensor_scalar(out=out, in0=in0, scalar1=s1, scalar2=None, op0=op0),
                    reads, writes)
        else:
            self.op(eng, lambda e: e.tensor_scalar(out=out, in0=in0, scalar1=s1, scalar2=s2, op0=op0, op1=op1),
                    reads, writes)

    def stt(self, out, in0, scalar, in1, op0, op1, reads=(), writes=()):
        self.op("vector", lambda e: e.scalar_tensor_tensor(out=out, in0=in0, scalar=scalar, in1=in1,
                                                            op0=op0, op1=op1), reads, writes)

    def copy(self, out, in_, reads=(), writes=(), eng="vector"):
        if eng == "scalar":
            self.op(eng, lambda e: e.copy(out=out, in_=in_), reads, writes)
        else:
            self.op(eng, lambda e: e.tensor_copy(out=out, in_=in_), reads, writes)

    def memset(self, ap, val, writes=(), eng="vector"):
        self.op(eng, lambda e: e.memset(ap, val), (), writes)

    def finish(self):
        nc = self.nc
        ops = self.ops
        engines = []
        for o in ops:
            if o["eng"] not in engines:
                engines.append(o["eng"])
        chans = []
        for o in ops:
            if o["chan"] is not None and o["chan"] not in chans:
                chans.append(o["chan"])
        sems = {}
        for e in engines:
            sems[("e", e)] = self.stack.enter_context(nc.semaphore("s_" + e))
        for c in chans:
            sems[("c", c)] = self.stack.enter_context(nc.semaphore("c_" + c))
        def plan_pass():
            viol = set()
            eng_count = {e: 0 for e in engines}
            chan_count = {c: 0 for c in chans}
            last_writer = {}
            readers = {}
            known = {e: {} for e in engines}
            plan = {e: [] for e in engines}
            done = []
            for i, o in enumerate(ops):
                e = o["eng"]
                deps = set()
                for r in o["reads"]:
                    if r in last_writer:
                        deps.add(last_writer[r])
                for w in o["writes"]:
                    if w in last_writer:
                        deps.add(last_writer[w])
                    for rd in readers.get(w, ()):
                        deps.add(rd)
                need = {}
                for d in deps:
                    od = ops[d]
                    if od["chan"] is not None:
                        key = ("c", od["chan"])
                        val = 16 * chan_count[od["chan"]]
                    else:
                        if od["eng"] == e and (o["nosync_same"] and od["nosync_same"]):
                            continue
                        key = ("e", od["eng"])
                        val = done[d][1]
                        if val > eng_count[od["eng"]]:
                            viol.add(d)
                    if val > need.get(key, 0):
                        need[key] = val
                waits = []
                for key, val in need.items():
                    if known[e].get(key, 0) >= val:
                        continue
                    known[e][key] = val
                    waits.append((key, val))
                if o["chan"] is not None:
                    chan_count[o["chan"]] += 1
                    done.append((("c", o["chan"]), 16 * chan_count[o["chan"]]))
                    inc = (("c", o["chan"]), 16)
                elif not o["inc"]:
                    done.append((("e", e), eng_count[e] + 1))
                    inc = None
                else:
                    eng_count[e] += 1
                    done.append((("e", e), eng_count[e]))
                    inc = (("e", e), 1)
                plan[e].append((waits, o["fn"], inc))
                for r in o["reads"]:
                    readers.setdefault(r, []).append(i)
                for w in o["writes"]:
                    last_writer[w] = i
                    readers[w] = []
            return viol, plan, chan_count
        while True:
            viol, plan, chan_count = plan_pass()
            if not viol:
                break
            for d in viol:
                ops[d]["inc"] = True
        final_waits = {e: [] for e in engines}
        chan_eng = {}
        for o in ops:
            if o["chan"] is not None:
                chan_eng[o["chan"]] = o["eng"]
        for c, e in chan_eng.items():
            final_waits[e].append((("c", c), 16 * chan_count[c]))

        semv = {k: 0 for k in sems}
        ptr = {e: 0 for e in engines}
        progressed = True
        while progressed:
            progressed = False
            for e in engines:
                while ptr[e] < len(plan[e]):
                    waits, _fn, inc = plan[e][ptr[e]]
                    if any(semv[k] < v for k, v in waits):
                        break
                    if inc is not None:
                        semv[inc[0]] += inc[1]
                    ptr[e] += 1
                    progressed = True
        stuck = {e: (ptr[e], len(plan[e])) for e in engines if ptr[e] < len(plan[e])}
        if stuck:
            det = {e: [(k, v, semv[k]) for k, v in plan[e][ptr[e]][0] if semv[k] < v] for e in stuck}
            raise RuntimeError(f"sync plan deadlocks: {stuck} waiting on {det}")

        with nc.Block() as block:
            def make(e):
                def body(eng):
                    for waits, fn, inc in plan[e]:
                        for key, val in waits:
                            eng.wait_ge(sems[key], val)
                        ins = fn(eng)
                        if inc is not None:
                            ins.then_inc(sems[inc[0]], inc[1])
                    for key, val in final_waits[e]:
                        eng.wait_ge(sems[key], val)
                return body
            for e in engines:
                getattr(block, e)(make(e))
        self.stack.close()
        return nc

GRID_W = 64
def rope_tables(tok0, n):
    t = np.arange(tok0, tok0 + n); row = (t // GRID_W).astype(np.float32); col = (t % GRID_W).astype(np.float32)
    quarter = 16
    freqs = (10000.0 ** (-np.arange(quarter, dtype=np.float32) / quarter)).astype(np.float32)
    cos = np.zeros((64, n), np.float32); sin = np.zeros((64, n), np.float32)
    for d in range(64):
        pos = row if d < 32 else col
        dd = d % 32
        f = freqs[dd % 16]
        ang = (pos * f).astype(np.float32)
        cos[d] = np.cos(ang); s = np.sin(ang)
        sin[d] = -s if dd < 16 else s
    return np.concatenate([cos, cos], 0), np.concatenate([sin, sin], 0)
def perm_matrix():
    Pm = np.zeros((128, 128), np.float32)
    for m in range(128):
        dd = m % 32
        k = m + 16 if dd < 16 else m - 16
        Pm[k, m] = 1.0
    return Pm
def colT(v, n):
    return np.ascontiguousarray(v.reshape(n, 128).T)

EPS = 1e-6

def get_stage(P):
    if not hasattr(P, "_stage"):
        P._stage_n = getattr(P, "_stage_n", 3)
        P._stage = [P.sbuf(f"stage{i}", [128, 1024]) for i in range(P._stage_n)]
        P._stage_i = 0
    return P._stage

def emit_mod(P, w_mod, b_modT, cT, nct, ps_mod, tag="m"):
    ncols = nct * 128
    c_s = P.sbuf(tag + "c_s", [128, 16])
    bm_s = P.sbuf(tag + "bm_s", [128, nct]); modv = P.sbuf(tag + "modv", [128, 2 * nct]); modc = P.sbuf(tag + "modc", [128, 2 * nct])
    modrow = [P.sbuf(f"{tag}modrow{i}", [2, 512]) for i in range(2)]
    scr = P.nc.dram_tensor(tag + "_modscr", [2, ncols], F32, kind="Internal").ap()
    stb = [P.sbuf(f"{tag}stb{i}", [128, 2, 512], BF16) for i in range(3)]
    sc_b = P.sbuf(tag + "sc_b", [128, 16], BF16)
    P.dma("sync", c_s[:], cT, writes=[tag + "c_s"], chan="ld0")
    P.dma("sync", bm_s[:], b_modT, writes=[tag + "bm_s"], chan="ld0")
    P.act(sc_b[:], c_s[:], AF.Silu, reads=[tag + "c_s"], writes=[tag + "sc_s"])
    wmv = w_mod.rearrange("(kc p) n -> p kc n", p=128)
    nst = 0
    for pc in range(ncols // 512):
        for i in range(4):
            b = nst % 3; nst += 1
            P.dma("gpsimd", stb[b][:], wmv[:, 2 * i:2 * i + 2, pc * 512:(pc + 1) * 512],
                  writes=[(tag + "stb", b)], chan=f"{tag}stb{b}", max_dma_last_dim=2048)
            for k2 in range(2):
                kc = 2 * i + k2
                P.mm(ps_mod[0:2, 0:512], sc_b[:, kc:16:8], stb[b][:, k2, :], kc == 0, kc == 7,
                     reads=[(tag + "stb", b), tag + "sc_s"], writes=["ps_mod"])
        P.copy(modrow[pc % 2][:], ps_mod[0:2, 0:512], reads=["ps_mod"], writes=[(tag + "modrow", pc % 2)], eng="scalar")
        P.dma("sync", scr[:, pc * 512:(pc + 1) * 512], modrow[pc % 2][:], reads=[(tag + "modrow", pc % 2)], writes=[tag + "scr"], chan="modw")
    P.dma("sync", modc[:].rearrange("p (j t) -> p j t", j=2), scr.rearrange("j (t p) -> p j t", p=128),
          reads=[tag + "scr"], writes=[tag + "modc"], chan="modr", allow_slow_non_contiguous=True)
    for j in range(2):
        P.tt(modv[:, j * nct:(j + 1) * nct], modc[:, j * nct:(j + 1) * nct], bm_s[:], ALU.add,
             reads=[tag + "modc", tag + "bm_s"], writes=[tag + "modv"])
    return modv

def load_cast(P, w_dram, w_bf, nk, ncols, tag, piece=1024):
    wv = w_dram.rearrange("(kc p) n -> p kc n", p=128)
    for kc in range(nk):
        P.dma("gpsimd", w_bf[:, kc, :], wv[:, kc, :], writes=[(tag, kc)], chan="wld_" + tag, max_dma_last_dim=4096)

def emit_rstd(P, src, nk, n, sqb, ones, ps_ss, sd, rstd, src_res, tag=""):
    P.act(sqb[:, 0:nk, 0:n], src[:, 0:nk, 0:n], AF.Square, reads=src_res, writes=["sqb"])
    for kc in range(nk):
        P.mm(ps_ss[:, 0:n], ones[:], sqb[:, kc, 0:n], kc == 0, kc == nk - 1, reads=["sqb", "ones"], writes=["ps_ss"])
    P.act(sd[:, 0:n], ps_ss[:, 0:n], AF.Sqrt, reads=["ps_ss"], writes=["sd" + tag], scale=1.0 / (128 * nk), bias=EPS)
    P.op("vector", lambda e: e.reciprocal(out=rstd[:, 0:n], in_=sd[:, 0:n]), reads=["sd" + tag], writes=["rstd" + tag])

def build_l3b(with_ctx, N=256):
    NT = 2304 if with_ctx else 2048
    P = Prog(); P._stage_n = 2
    xT = P.dram_in("xT", [1024, NT])
    w_mod = P.dram_in("w_mod", [1024, 3072]); b_modT = P.dram_in("b_modT", [128, 24]); cT = P.dram_in("cT", [128, 16])
    g_preT = P.dram_in("g_preT", [128, 8]); g_postT = P.dram_in("g_postT", [128, 8])
    w_gate = P.dram_in("w_gate", [1024, 2816]); w_up = P.dram_in("w_up", [1024, 2816]); w_down = P.dram_in("w_down", [2816, 1024])
    xoT = P.dram_out("xoT", [1024, NT])
    wg = P.sbuf("wg", [128, 8, 2816], BF16); wu = P.sbuf("wu", [128, 8, 2816], BF16); wd = P.sbuf("wd", [128, 22, 1024], BF16)
    xs = [P.sbuf(f"xs{i}", [128, 8, N]) for i in range(2)]
    sqb = P.sbuf("sqb", [128, 8, N], BF16)
    tt_ = [P.sbuf(f"tt{i}", [128, N]) for i in range(2)]; xn = [P.sbuf(f"xn{i}", [128, 8, N], BF16) for i in range(2)]
    sd2 = P.sbuf("sd2", [128, N]); rstd2 = P.sbuf("rstd2", [128, N])
    hmid = P.sbuf("hmid", [128, 22, N], BF16)
    sg = [P.sbuf(f"sg{i}", [128, N]) for i in range(2)]
    o2 = P.sbuf("o2", [128, 8, N]); tmp = [P.sbuf(f"tmp{i}", [128, N]) for i in range(2)]
    xo = [P.sbuf(f"xo{i}", [128, N]) for i in range(2)]
    ones = P.sbuf("ones", [128, 128], BF16)
    sd = P.sbuf("sd", [128, N]); rstd = P.sbuf("rstd", [128, N])
    gp_s = P.sbuf("gp_s", [128, 8]); gq_s = P.sbuf("gq_s", [128, 8])
    Av = P.sbuf("Av", [128, 16]); Gv = P.sbuf("Gv", [128, 16])
    ps_mod = P.psum("ps_mod"); ps_ss = P.psum("ps_ss")
    psg = [P.psum(f"psg{i}") for i in range(2)]; psu = [P.psum(f"psu{i}") for i in range(2)]; pso = [P.psum(f"pso{i}") for i in range(2)]
    P.memset(ones[:], 1.0, writes=["ones"])
    P.dma("sync", gp_s[:], g_preT, writes=["gp_s"], chan="ld0")
    P.dma("sync", gq_s[:], g_postT, writes=["gq_s"], chan="ld0")
    modv = emit_mod(P, w_mod, b_modT, cT, 24, ps_mod)
    for j in range(2):
        P.stt(Av[:, j * 8:(j + 1) * 8], modv[:, j * 24 + 8: j * 24 + 16], 1.0, gp_s[:], ALU.add, ALU.mult,
              reads=["mmodv", "gp_s"], writes=["Av"])
        P.tt(Gv[:, j * 8:(j + 1) * 8], modv[:, j * 24 + 16: j * 24 + 24], gq_s[:], ALU.mult,
             reads=["mmodv", "gq_s"], writes=["Gv"])
    load_cast(P, w_gate, wg, 8, 2816, "wg")
    load_cast(P, w_up, wu, 8, 2816, "wu")
    xv = xT.rearrange("(kc p) t -> p kc t", p=128); xov = xoT.rearrange("(kc p) t -> p kc t", p=128)
    slabs = list(range(0, NT, N)); n = N
    cnt = dict(ng=0, no=0, nt=0)
    def pre(si):
        t0 = slabs[si]; b = si % 2; j = 0 if t0 < 2048 else 1
        P.dma("sync", xs[b][:], xv[:, :, t0:t0 + n], writes=[("xs", b)], chan=f"xs{b}")
        emit_rstd(P, xs[b], 8, n, sqb, ones, ps_ss, sd, rstd, [("xs", b)])
        for kc in range(8):
            P.tt(tt_[kc % 2][:], xs[b][:, kc, :], rstd[:], ALU.mult, reads=[("xs", b), "rstd"], writes=[("tt", kc % 2)])
            P.act(xn[b][:, kc, :], tt_[kc % 2][:], AF.Identity, reads=[("tt", kc % 2), "Av", "mmodv"], writes=[("xn", b, kc)],
                  scale=Av[:, j * 8 + kc: j * 8 + kc + 1], bias=modv[:, j * 24 + kc: j * 24 + kc + 1])
    def gu(si):
        b = si % 2
        for jj in range(22):
            pb = cnt["ng"] % 2; cnt["ng"] += 1
            for kc in range(8):
                P.mm(psg[pb][:, 0:n], wg[:, kc, jj * 128:(jj + 1) * 128], xn[b][:, kc, :], kc == 0, kc == 7,
                     reads=[("wg", kc), ("xn", b, kc)], writes=[("psg", pb)])
            for kc in range(8):
                P.mm(psu[pb][:, 0:n], wu[:, kc, jj * 128:(jj + 1) * 128], xn[b][:, kc, :], kc == 0, kc == 7,
                     reads=[("wu", kc), ("xn", b, kc)], writes=[("psu", pb)])
            P.act(sg[pb][:], psg[pb][:, 0:n], AF.Silu, reads=[("psg", pb)], writes=[("sg", pb)])
            P.tt(hmid[:, jj, :], sg[pb][:], psu[pb][:, 0:n], ALU.mult, reads=[("sg", pb), ("psu", pb)], writes=[("hmid", jj)])
    def dn(si):
        for m in range(8):
            pb = cnt["no"] % 2; cnt["no"] += 1
            for jj in range(22):
                P.mm(pso[pb][:, 0:n], wd[:, jj, m * 128:(m + 1) * 128], hmid[:, jj, :], jj == 0, jj == 21,
                     reads=[("wd", jj), ("hmid", jj)], writes=[("pso", pb)])
            P.copy(o2[:, m, :], pso[pb][:, 0:n], reads=[("pso", pb)], writes=[("o2", m)], eng="scalar")
    def post(si):
        t0 = slabs[si]; b = si % 2; j = 0 if t0 < 2048 else 1
        emit_rstd(P, o2, 8, n, sqb, ones, ps_ss, sd2, rstd2, [("o2", m) for m in range(8)], tag="2")
        for m in range(8):
            P.stt(o2[:, m, :], o2[:, m, :], Gv[:, j * 8 + m: j * 8 + m + 1], rstd2[:], ALU.mult, ALU.mult,
                  reads=[("o2", m), "Gv", "rstd2"], writes=[("o2", m)])
            P.tt(o2[:, m, :], xs[b][:, m, :], o2[:, m, :], ALU.add, reads=[("xs", b), ("o2", m)], writes=[("o2", m)], eng="gpsimd")
        P.dma("gpsimd", xov[:, :, t0:t0 + n], o2[:], reads=[("o2", m) for m in range(8)], chan="st")
    pre(0)
    for si in range(len(slabs)):
        gu(si)
        if si == 0:
            load_cast(P, w_down, wd, 22, 1024, "wd")
        if si + 1 < len(slabs):
            pre(si + 1)
        dn(si)
        post(si)
    return P.finish()

def build_l3a(with_ctx, N=512):
    NT = 2304 if with_ctx else 2048
    P = Prog()
    xT = P.dram_in("xT", [1024, NT]); ysT = P.dram_in("ysT", [384, NT]); mxT = P.dram_in("mxT", [256, NT]); naT = P.dram_in("naT", [384, NT])
    w_mod = P.dram_in("w_mod", [1024, 1024]); b_modT = P.dram_in("b_modT", [128, 8]); cT = P.dram_in("cT", [128, 16])
    g_postT = P.dram_in("g_postT", [128, 8])
    w_glu = P.dram_in("w_glu", [384, 384]); w_fourier = P.dram_in("w_fourier", [256, 256]); w_out = P.dram_in("w_out", [1024, 1024])
    xoT = P.dram_out("xoT", [1024, NT])
    wglu = P.sbuf("wglu", [128, 3, 384], BF16); wf = P.sbuf("wf", [128, 2, 256], BF16); wo = P.sbuf("wo", [128, 8, 1024], BF16)
    xs = [P.sbuf(f"xs{i}", [128, 8, N]) for i in range(2)]
    ys = [P.sbuf(f"ys{i}", [128, 3, N]) for i in range(2)]
    mx = [P.sbuf(f"mx{i}", [128, 2, N]) for i in range(2)]
    na = [P.sbuf(f"na{i}", [128, 3, N]) for i in range(2)]
    sq = P.sbuf("sq", [128, 3, N]); t1 = P.sbuf("t1", [128, 3, N]); sgm = P.sbuf("sgm", [128, 3, N])
    zf = P.sbuf("zf", [128, 3, N]); zb = P.sbuf("zb", [128, 3, N], BF16); mxb = P.sbuf("mxb", [128, 2, N], BF16)
    sg2 = [P.sbuf(f"sg2{i}", [128, N]) for i in range(2)]
    cat = P.sbuf("cat", [128, 8, N], BF16)
    sqb = P.sbuf("sqb", [128, 8, N], BF16)
    o2 = P.sbuf("o2", [128, 8, N]); tmp = [P.sbuf(f"tmp{i}", [128, N]) for i in range(2)]
    xo = [P.sbuf(f"xo{i}", [128, N]) for i in range(2)]
    ones = P.sbuf("ones", [128, 128], BF16)
    sd = P.sbuf("sd", [128, N]); rstd = P.sbuf("rstd", [128, N])
    gq_s = P.sbuf("gq_s", [128, 8]); Gv = P.sbuf("Gv", [128, 16])
    ps_mod = P.psum("ps_mod"); ps_ss = P.psum("ps_ss")
    psa = [P.psum(f"psa{i}") for i in range(3)]; pso = [P.psum(f"pso{i}") for i in range(3)]
    P.memset(ones[:], 1.0, writes=["ones"])
    P.dma("sync", gq_s[:], g_postT, writes=["gq_s"], chan="ld0")
    modv = emit_mod(P, w_mod, b_modT, cT, 8, ps_mod)
    for j in range(2):
        P.tt(Gv[:, j * 8:(j + 1) * 8], modv[:, j * 8:(j + 1) * 8], gq_s[:], ALU.mult, reads=["mmodv", "gq_s"], writes=["Gv"])
    load_cast(P, w_glu, wglu, 3, 384, "wglu")
    load_cast(P, w_fourier, wf, 2, 256, "wf")
    load_cast(P, w_out, wo, 8, 1024, "wo")
    xv = xT.rearrange("(kc p) t -> p kc t", p=128); xov = xoT.rearrange("(kc p) t -> p kc t", p=128)
    ysv = ysT.rearrange("(kc p) t -> p kc t", p=128); mxv = mxT.rearrange("(kc p) t -> p kc t", p=128); nav = naT.rearrange("(kc p) t -> p kc t", p=128)
    na_ = 0; no = 0; nt = 0
    for si, t0 in enumerate(range(0, NT, N)):
        n = min(N, NT - t0); b = si % 2; j = 0 if t0 < 2048 else 1
        P.dma("sync", xs[b][:, :, 0:n], xv[:, :, t0:t0 + n], writes=[("xs", b)], chan=f"xs{b}")
        P.dma("sync", ys[b][:, :, 0:n], ysv[:, :, t0:t0 + n], writes=[("ys", b)], chan=f"ys{b}")
        P.dma("sync", mx[b][:, :, 0:n], mxv[:, :, t0:t0 + n], writes=[("mx", b)], chan=f"mx{b}")
        P.dma("sync", na[b][:, :, 0:n], nav[:, :, t0:t0 + n], writes=[("na", b)], chan=f"na{b}")
        Y = ys[b][:, :, 0:n]
        P.tt(sq[:, :, 0:n], Y, Y, ALU.mult, reads=[("ys", b)], writes=["sq"], eng="gpsimd")
        P.ts(t1[:, :, 0:n], sq[:, :, 0:n], 0.044715, 1.0, ALU.mult, ALU.add, reads=["sq"], writes=["t1"])
        P.tt(sq[:, :, 0:n], t1[:, :, 0:n], Y, ALU.mult, reads=["t1", ("ys", b)], writes=["sq"])
        P.act(sgm[:, :, 0:n], sq[:, :, 0:n], AF.Sigmoid, reads=["sq"], writes=["sgm"], scale=1.5957691216057308)
        P.tt(zf[:, :, 0:n], Y, sgm[:, :, 0:n], ALU.mult, reads=[("ys", b), "sgm"], writes=["zf"])
        P.copy(zb[:, :, 0:n], zf[:, :, 0:n], reads=["zf"], writes=["zb"], eng="gpsimd")
        for m in range(3):
            pb = na_ % 3; na_ += 1
            for kc in range(3):
                P.mm(psa[pb][:, 0:n], wglu[:, kc, m * 128:(m + 1) * 128], zb[:, kc, 0:n], kc == 0, kc == 2,
                     reads=[("wglu", kc), "zb"], writes=[("psa", pb)])
            P.act(sg2[m % 2][:, 0:n], psa[pb][:, 0:n], AF.Sigmoid, reads=[("psa", pb)], writes=[("sg2", m % 2)])
            P.tt(cat[:, m, 0:n], zf[:, m, 0:n], sg2[m % 2][:, 0:n], ALU.mult, reads=["zf", ("sg2", m % 2)], writes=[("cat", m)])
        P.copy(mxb[:, :, 0:n], mx[b][:, :, 0:n], reads=[("mx", b)], writes=["mxb"], eng="gpsimd")
        for m in range(2):
            pb = na_ % 3; na_ += 1
            for kc in range(2):
                P.mm(psa[pb][:, 0:n], wf[:, kc, m * 128:(m + 1) * 128], mxb[:, kc, 0:n], kc == 0, kc == 1,
                     reads=[("wf", kc), "mxb"], writes=[("psa", pb)])
            P.copy(cat[:, 3 + m, 0:n], psa[pb][:, 0:n], reads=[("psa", pb)], writes=[("cat", 3 + m)], eng="scalar")
        P.copy(cat[:, 5:8, 0:n], na[b][:, :, 0:n], reads=[("na", b)], writes=[("cat", 5), ("cat", 6), ("cat", 7)], eng="gpsimd")
        for m in range(8):
            pb = no % 3; no += 1
            for kc in range(8):
                P.mm(pso[pb][:, 0:n], wo[:, kc, m * 128:(m + 1) * 128], cat[:, kc, 0:n], kc == 0, kc == 7,
                     reads=[("wo", kc), ("cat", kc)], writes=[("pso", pb)])
            P.copy(o2[:, m, 0:n], pso[pb][:, 0:n], reads=[("pso", pb)], writes=[("o2", m)], eng="scalar")
        emit_rstd(P, o2, 8, n, sqb, ones, ps_ss, sd, rstd, [("o2", m) for m in range(8)])
        for m in range(8):
            P.stt(o2[:, m, 0:n], o2[:, m, 0:n], Gv[:, j * 8 + m: j * 8 + m + 1], rstd[:, 0:n], ALU.mult, ALU.mult,
                  reads=[("o2", m), "Gv", "rstd"], writes=[("o2", m)])
            P.tt(o2[:, m, 0:n], xs[b][:, m, 0:n], o2[:, m, 0:n], ALU.add, reads=[("xs", b), ("o2", m)], writes=[("o2", m)], eng="gpsimd")
        P.dma("gpsimd", xov[:, :, t0:t0 + n], o2[:, :, 0:n], reads=[("o2", m) for m in range(8)], chan="st")
    return P.finish()

EPS = 1e-6
NT = 2304
SLABS = [(0, 512), (512, 512), (1024, 512), (1536, 512), (2048, 256)]

def build_l1():
    P = Prog(); P._stage_n = 4
    xT = P.dram_in("xT", [1024, NT])
    w_in = P.dram_in("w_in", [1024, 1792])
    w_mod = P.dram_in("w_mod", [1024, 2048])
    b_modT = P.dram_in("b_modT", [128, 16])
    g_preT = P.dram_in("g_preT", [128, 8])
    cT = P.dram_in("cT", [128, 16])
    cos = P.dram_in("cos", [128, 2048]); sin = P.dram_in("sin", [128, 2048])
    perm = P.dram_in("perm", [128, 128])
    hT = P.dram_out("hT", [640, NT])
    hTb = P.dram_out("hTb", [1152, NT])

    xs = [P.sbuf(f"xs{i}", [128, 8, 512]) for i in range(2)]
    sqb = P.sbuf("sqb", [128, 8, 512], BF16)
    tt_ = [P.sbuf(f"tt{i}", [128, 512]) for i in range(2)]
    xn = [P.sbuf(f"xn{i}", [128, 8, 512], BF16) for i in range(2)]
    w_bf = P.sbuf("w_bf", [128, 8, 1792], BF16)
    ones = P.sbuf("ones", [128, 128], BF16)
    sd = P.sbuf("sd", [128, 512]); rstd = P.sbuf("rstd", [128, 512])
    ho = [P.sbuf(f"ho{i}", [128, 512]) for i in range(4)]
    hob = [P.sbuf(f"hob{i}", [128, 512]) for i in range(4)]
    r1 = [P.sbuf(f"r1{i}", [128, 512]) for i in range(2)]
    r2 = [P.sbuf(f"r2{i}", [128, 512]) for i in range(2)]
    cos_s = P.sbuf("cos_s", [128, 2048]); sin_s = P.sbuf("sin_s", [128, 2048])
    perm_s = P.sbuf("perm_s", [128, 128])
    gp_s = P.sbuf("gp_s", [128, 8]); Av = P.sbuf("Av", [128, 16])
    ps_mod = P.psum("ps_mod"); ps_ss = P.psum("ps_ss")
    psm = [P.psum(f"psm{i}") for i in range(4)]
    psr = [P.psum(f"psr{i}") for i in range(2)]

    P.dma("sync", gp_s[:], g_preT, writes=["gp_s"], chan="ld0")
    P.dma("sync", cos_s[:], cos, writes=["cos_s"], chan="ld0")
    P.dma("sync", sin_s[:], sin, writes=["sin_s"], chan="ld0")
    P.dma("sync", perm_s[:], perm, writes=["perm_s"], chan="ld0")
    P.memset(ones[:], 1.0, writes=["ones"])
    modv = emit_mod(P, w_mod, b_modT, cT, 16, ps_mod)
    for j in range(2):
        P.stt(Av[:, j * 8:(j + 1) * 8], modv[:, j * 16 + 8: j * 16 + 16], 1.0, gp_s[:], ALU.add, ALU.mult,
              reads=["mmodv", "gp_s"], writes=["Av"])
    load_cast(P, w_in, w_bf, 8, 1792, "w_bf", piece=1024)
    xv = xT.rearrange("(kc p) t -> p kc t", p=128)
    cnt = dict(nho=0, nr=0, nps=0)
    def pre(si):
        t0, n = SLABS[si]; b = si % 2; j = 0 if si < 4 else 1
        P.dma("sync", xs[b][:, :, 0:n], xv[:, :, t0:t0 + n], writes=[("xs", b)], chan=f"xs{b}")
        P.act(sqb[:, :, 0:n], xs[b][:, :, 0:n], AF.Square, reads=[("xs", b)], writes=["sqb"])
        for kc in range(8):
            P.mm(ps_ss[:, 0:n], ones[:], sqb[:, kc, 0:n], kc == 0, kc == 7, reads=["sqb", "ones"], writes=["ps_ss"])
        P.act(sd[:, 0:n], ps_ss[:, 0:n], AF.Sqrt, reads=["ps_ss"], writes=["sd"], scale=1.0 / 1024, bias=EPS)
        P.op("vector", lambda e: e.reciprocal(out=rstd[:, 0:n], in_=sd[:, 0:n]), reads=["sd"], writes=["rstd"])
        for kc in range(8):
            P.tt(tt_[kc % 2][:, 0:n], xs[b][:, kc, 0:n], rstd[:, 0:n], ALU.mult, reads=[("xs", b), "rstd"], writes=[("tt", kc % 2)])
            P.act(xn[b][:, kc, 0:n], tt_[kc % 2][:, 0:n], AF.Identity, reads=[("tt", kc % 2), "Av", "mmodv"], writes=[("xn", b, kc)],
                  scale=Av[:, j * 8 + kc: j * 8 + kc + 1], bias=modv[:, j * 16 + kc: j * 16 + kc + 1])
    pending = []
    def flush():
        while pending:
            pending.pop(0)()
    def main(si):
        t0, n = SLABS[si]; b = si % 2
        for m in range(14):
            pb = cnt["nps"] % 4; cnt["nps"] += 1
            for kc in range(8):
                P.mm(psm[pb][:, 0:n], w_bf[:, kc, m * 128:(m + 1) * 128], xn[b][:, kc, 0:n], kc == 0, kc == 7,
                     reads=[("w_bf", kc), ("xn", b, kc)], writes=[("psm", pb)])
            flush()
            hb = cnt["nho"] % 4; cnt["nho"] += 1
            if m < 5:
                P.copy(ho[hb][:, 0:n], psm[pb][:, 0:n], reads=[("psm", pb)], writes=[("ho", hb)], eng="scalar")
                P.dma("gpsimd", hT[m * 128:(m + 1) * 128, t0:t0 + n], ho[hb][:, 0:n], reads=[("ho", hb)], chan=f"sto{hb}")
            elif m <= 10 and si < 4:
                rb = cnt["nr"] % 2; cnt["nr"] += 1
                P.copy(ho[hb][:, 0:n], psm[pb][:, 0:n], reads=[("psm", pb)], writes=[("ho", hb)], eng="scalar")
                def rope(hb=hb, rb=rb, m=m, t0=t0, n=n):
                    P.mm(psr[rb][:, 0:n], perm_s[:], ho[hb][:, 0:n], True, True, reads=["perm_s", ("ho", hb)], writes=[("psr", rb)])
                    P.tt(r1[rb][:, 0:n], ho[hb][:, 0:n], cos_s[:, t0:t0 + n], ALU.mult, reads=[("ho", hb), "cos_s"], writes=[("r1", rb)])
                    P.tt(r2[rb][:, 0:n], psr[rb][:, 0:n], sin_s[:, t0:t0 + n], ALU.mult, reads=[("psr", rb), "sin_s"], writes=[("r2", rb)])
                    P.tt(hob[hb][:, 0:n], r1[rb][:, 0:n], r2[rb][:, 0:n], ALU.add, reads=[("r1", rb), ("r2", rb)], writes=[("hob", hb)], eng="gpsimd")
                    P.dma("gpsimd", hTb[(m - 5) * 128:(m - 4) * 128, t0:t0 + n], hob[hb][:, 0:n], reads=[("hob", hb)], chan=f"stb{hb}")
                pending.append(rope)
            else:
                P.copy(hob[hb][:, 0:n], psm[pb][:, 0:n], reads=[("psm", pb)], writes=[("hob", hb)], eng="scalar")
                P.dma("gpsimd", hTb[(m - 5) * 128:(m - 4) * 128, t0:t0 + n], hob[hb][:, 0:n], reads=[("hob", hb)], chan=f"stb{hb}")
    pre(0)
    for si in range(len(SLABS)):
        if si + 1 < len(SLABS):
            pre(si + 1)
        main(si)
    flush()
    return P.finish()


def fnet_consts():
    n1 = np.arange(128); a = 2 * np.pi * np.outer(n1, n1) / 128
    cs128 = np.concatenate([np.cos(a), -np.sin(a)], 1).astype(np.float32)
    n2 = np.arange(64); a = 2 * np.pi * np.outer(n2, n2) / 64
    C64, S64 = np.cos(a), np.sin(a)
    fb1 = np.concatenate([C64, -S64], 1).astype(np.float32); fb2 = np.concatenate([S64, C64], 1).astype(np.float32)
    a = 2 * np.pi * np.outer(n2, n1) / 8192
    tw = np.concatenate([np.cos(a), np.sin(a)], 1).astype(np.float32)
    fc = (np.concatenate([C64, S64], 1) / np.sqrt(8192 * 64)).astype(np.float32)
    n = np.arange(256); a = 2 * np.pi * np.outer(n, n) / 256
    cs = np.concatenate([np.cos(a), -np.sin(a)], 1)
    cs256 = np.ascontiguousarray(cs.reshape(2, 128, 512).transpose(1, 0, 2)).astype(np.float32)
    fcc = (np.concatenate([C64, S64], 1) / np.sqrt(256 * 64)).astype(np.float32)
    return dict(cs128=cs128, fb1=fb1, fb2=fb2, tw=tw, fc=fc, cs256=cs256, fcc=fcc)

def build_fnet(with_ctx):
    P = Prog()
    z = P.dram_in("z", [128, 4096]); cs128 = P.dram_in("cs128", [128, 256])
    fb1 = P.dram_in("fb1", [64, 128]); fb2 = P.dram_in("fb2", [64, 128]); tw = P.dram_in("tw", [64, 256]); fc = P.dram_in("fc", [64, 128])
    R = P.dram_out("R", [64, 8192])
    zs = P.sbuf("zs", [128, 64, 64]); cs_s = P.sbuf("cs_s", [128, 256])
    fb1_s = P.sbuf("fb1_s", [64, 128], BF16); fb2_s = P.sbuf("fb2_s", [64, 128], BF16); tw_s = P.sbuf("tw_s", [64, 256]); fc_s = P.sbuf("fc_s", [64, 128], BF16)
    Ych = [P.sbuf(f"Ych{i}", [64, 8, 256]) for i in range(2)]
    ta = P.sbuf("ta", [64, 8, 128]); tb = P.sbuf("tb", [64, 8, 128]); tc = P.sbuf("tc", [64, 8, 128]); td = P.sbuf("td", [64, 8, 128])
    Yp = P.sbuf("Yp", [64, 64, 256], BF16); X1 = P.sbuf("X1", [64, 2, 8192], BF16)
    Rs = [P.sbuf(f"Rs{i}", [64, 512]) for i in range(2)]
    psA = [P.psum(f"psA{i}") for i in range(3)]; psB = [P.psum(f"psB{i}") for i in range(3)]; psC = [P.psum(f"psC{i}") for i in range(2)]
    P.dma("sync", zs[:].rearrange("p a b -> p (a b)"), z, writes=["zs"], chan="ld0")
    for (s, d, nm) in ((cs_s, cs128, "cs_s"), (tw_s, tw, "tw_s")):
        P.dma("sync", s[:], d, writes=[nm], chan="ld1")
    for (s, d, nm) in ((fb1_s, fb1, "fb1_s"), (fb2_s, fb2, "fb2_s"), (fc_s, fc, "fc_s")):
        P.dma("gpsimd", s[:], d, writes=[nm], chan="ldc")
    twr = tw_s[:, 0:128].rearrange("p (o k) -> p o k", o=1).to_broadcast([64, 8, 128])
    tws = tw_s[:, 128:256].rearrange("p (o k) -> p o k", o=1).to_broadcast([64, 8, 128])
    na_ = 0
    for ch in range(8):
        cb = ch % 2
        for pair in range(4):
            pb = na_ % 3; na_ += 1
            for h in range(2):
                c = ch * 8 + pair * 2 + h
                P.mm(psA[pb][0:64, h * 256:(h + 1) * 256], zs[:, :, c], cs_s[:], True, True, reads=["zs", "cs_s"], writes=[("psA", pb)])
            P.copy(Ych[cb][:, pair * 2:pair * 2 + 2, :].rearrange("p a b -> p (a b)"), psA[pb][0:64, :], reads=[("psA", pb)],
                   writes=[("Ych", cb)], eng="scalar")
        Yr = Ych[cb][:, :, 0:128]; Yi = Ych[cb][:, :, 128:256]; c0 = ch * 8
        P.tt(ta[:], Yr, twr, ALU.mult, reads=[("Ych", cb), "tw_s"], writes=["ta"])
        P.tt(tb[:], Yi, tws, ALU.mult, reads=[("Ych", cb), "tw_s"], writes=["tb"], eng="gpsimd")
        P.tt(Yp[:, c0:c0 + 8, 0:128], ta[:], tb[:], ALU.add, reads=["ta", "tb"], writes=[("Yp", ch)])
        P.tt(tc[:], Yi, twr, ALU.mult, reads=[("Ych", cb), "tw_s"], writes=["tc"], eng="gpsimd")
        P.tt(td[:], Yr, tws, ALU.mult, reads=[("Ych", cb), "tw_s"], writes=["td"])
        P.tt(Yp[:, c0:c0 + 8, 128:256], tc[:], td[:], ALU.subtract, reads=["tc", "td"], writes=[("Yp", ch)], eng="gpsimd")
    allYp = [("Yp", ch) for ch in range(8)]
    X1v = X1[:].rearrange("p c (k2 k1) -> p c k2 k1", k1=128)
    for g in range(32):
        pb = g % 3
        for q in range(4):
            k1 = 4 * g + q
            P.mm(psB[pb][0:64, q * 128:(q + 1) * 128], Yp[:, :, k1], fb1_s[:], True, False, reads=allYp + ["fb1_s"], writes=[("psB", pb)])
            P.mm(psB[pb][0:64, q * 128:(q + 1) * 128], Yp[:, :, 128 + k1], fb2_s[:], False, True, reads=allYp + ["fb2_s"], writes=[("psB", pb)])
        pv = psB[pb][0:64, :].rearrange("p (q c k) -> p c k q", q=4, c=2)
        for comp in range(2):
            P.copy(X1v[:, comp, :, 4 * g:4 * g + 4], pv[:, comp, :, :], reads=[("psB", pb)], writes=[("X1", g)],
                   eng=("scalar" if comp == 0 else "vector"))
    allX1 = [("X1", g) for g in range(32)]
    for blk in range(16):
        pb = blk % 2
        P.mm(psC[pb][0:64, :], fc_s[:, 0:64], X1[:, 0, blk * 512:(blk + 1) * 512], True, False, reads=allX1 + ["fc_s"], writes=[("psC", pb)])
        P.mm(psC[pb][0:64, :], fc_s[:, 64:128], X1[:, 1, blk * 512:(blk + 1) * 512], False, True, reads=allX1 + ["fc_s"], writes=[("psC", pb)])
        P.copy(Rs[pb][:], psC[pb][0:64, :], reads=[("psC", pb)], writes=[("Rs", pb)], eng="scalar")
        P.dma("gpsimd", R[:, blk * 512:(blk + 1) * 512], Rs[pb][:], reads=[("Rs", pb)], chan=f"st{pb}")
    if with_ctx:
        zc = P.dram_in("zc", [128, 2, 64]); cs256 = P.dram_in("cs256", [128, 2, 512]); fcc = P.dram_in("fcc", [64, 128])
        Rc = P.dram_out("Rc", [64, 256])
        zc_s = P.sbuf("zc_s", [128, 2, 64]); c2_s = P.sbuf("c2_s", [128, 2, 512]); fcc_s = P.sbuf("fcc_s", [64, 128])
        Pc = P.sbuf("Pc", [64, 512]); Rc_s = P.sbuf("Rc_s", [64, 256])
        P.dma("sync", zc_s[:], zc, writes=["zc_s"], chan="ld2"); P.dma("sync", c2_s[:], cs256, writes=["c2_s"], chan="ld2")
        P.dma("sync", fcc_s[:], fcc, writes=["fcc_s"], chan="ld2")
        for t in range(2):
            P.mm(psA[0][0:64, :], zc_s[:, t, :], c2_s[:, t, :], t == 0, t == 1, reads=["zc_s", "c2_s"], writes=[("psA", 0)])
        P.copy(Pc[:], psA[0][0:64, :], reads=[("psA", 0)], writes=["Pc"])
        P.mm(psA[1][0:64, 0:256], fcc_s[:, 0:64], Pc[:, 0:256], True, False, reads=["Pc", "fcc_s"], writes=[("psA", 1)])
        P.mm(psA[1][0:64, 0:256], fcc_s[:, 64:128], Pc[:, 256:512], False, True, reads=["Pc", "fcc_s"], writes=[("psA", 1)])
        P.copy(Rc_s[:], psA[1][0:64, 0:256], reads=[("psA", 1)], writes=["Rc_s"])
        P.dma("gpsimd", Rc, Rc_s[:], reads=["Rc_s"], chan="st")
    return P.finish()

NEG = -30000.0

def build_na(with_ctx):
    P = Prog()
    qT = P.dram_in("qT", [384, 2048]); kwT = P.dram_in("kwT", [16, 384, 576]); vw = P.dram_in("vw", [16, 128, 5, 384])
    kcT = P.dram_in("kcT", [384, 256]); vc = P.dram_in("vc", [128, 2, 384])
    tbraw = P.dram_in("tbraw", [5, 128, 6, 576]); mask = P.dram_in("mask", [5, 128, 576]); ident = P.dram_in("ident", [128, 128])
    Y = P.dram_out("Y", [2048, 384])
    qb = P.sbuf("qb", [128, 3, 2048], BF16); kcb = P.sbuf("kcb", [128, 3, 256], BF16); vcb = P.sbuf("vcb", [128, 2, 384], BF16)
    TB = P.sbuf("TB", [128, 5, 6, 576]); mk = P.sbuf("mk", [128, 5, 576])
    idb = P.sbuf("idb", [128, 128], BF16)
    kb = [P.sbuf(f"kb{i}", [128, 3, 576], BF16) for i in range(2)]; vb = [P.sbuf(f"vb{i}", [128, 5, 384], BF16) for i in range(2)]
    S = [P.sbuf(f"S{i}", [128, 832]) for i in range(4)]; Pb = [P.sbuf(f"Pb{i}", [128, 832], BF16) for i in range(4)]
    PT = [P.sbuf(f"PT{i}", [128, 896], BF16) for i in range(4)]
    Osb = [P.sbuf(f"Osb{i}", [128, 384]) for i in range(2)]
    mx = [P.sbuf(f"mx{i}", [128, 1]) for i in range(8)]; ssum = [P.sbuf(f"ssum{i}", [128, 1]) for i in range(8)]
    rinv = [P.sbuf(f"rinv{i}", [128, 1]) for i in range(8)]
    psA = [P.psum(f"psA{i}") for i in range(2)]; psB = [P.psum(f"psB{i}") for i in range(2)]
    psT = [P.psum(f"psT{i}", [128, 1024], BF16) for i in range(2)]; psO = [P.psum(f"psO{i}") for i in range(2)]
    si = [0]
    def load_cast(dst, src_ap, n, dres):
        P.dma("gpsimd", dst, src_ap, writes=[dres], chan="ldc", max_dma_last_dim=4096)
    qv = qT.rearrange("(c p) n -> p c n", p=128)
    for c in range(3):
        load_cast(qb[:, c, :], qv[:, c, :], 2048, "qb")
    kcv = kcT.rearrange("(c p) n -> p c n", p=128)
    for c in range(3):
        load_cast(kcb[:, c, :], kcv[:, c, :], 256, "kcb")
    load_cast(vcb[:].rearrange("p a b -> p (a b)"), vc.rearrange("p a b -> p (a b)"), 768, "vcb")
    load_cast(idb[:], ident, 128, "idb")
    for ty in range(5):
        P.dma("sync", TB[:, ty, :, :], tbraw[ty], writes=[("TB", ty)], chan="ld0")
    P.dma("sync", mk[:], mask.rearrange("t p n -> p t n"), writes=["mk"], chan="ld0")
    for ty in range(5):
        mb = mk[:, ty, :].rearrange("p (o n) -> p o n", o=1).to_broadcast([128, 6, 576])
        P.tt(TB[:, ty, :, :], TB[:, ty, :, :], mb, ALU.add, reads=[("TB", ty), "mk"], writes=[("TB", ty)])
    tiles = [("main", t) for t in range(16)]
    if with_ctx:
        qcT = P.dram_in("qcT", [384, 256]); Yc = P.dram_out("Yc", [256, 384])
        qcb = P.sbuf("qcb", [128, 3, 256], BF16)
        qcv = qcT.rearrange("(c p) n -> p c n", p=128)
        for c in range(3):
            load_cast(qcb[:, c, :], qcv[:, c, :], 256, "qcb")
        tiles += [("ctx", 0), ("ctx", 1)]
    units = [(ti, kind, t, h) for ti, (kind, t) in enumerate(tiles) for h in range(6)]
    loaded = set()
    def tile_load(ti, kind, t):
        if ti in loaded or kind != "main":
            return
        loaded.add(ti)
        wb = ti % 2
        P.dma("gpsimd", kb[wb][:], kwT[t].rearrange("(c p) n -> p c n", p=128), writes=[("kb", wb)], chan=f"kb{wb}", max_dma_last_dim=4096)
        P.dma("gpsimd", vb[wb][:], vw[t], writes=[("vb", wb)], chan=f"vb{wb}", max_dma_last_dim=4096)
    def info(u):
        ti, kind, t, h = units[u]
        return ti, kind, t, h, u % 2, u % 4, ti % 2, h // 2, (h % 2) * 64, u % 8
    def stA(u):
        ti, kind, t, h, b, b3, wb, c, p0, b8 = info(u)
        tile_load(ti, kind, t)
        if kind == "main":
            qs = qb[p0:p0 + 64, c, t * 128:(t + 1) * 128]; qres = "qb"
        else:
            qs = qcb[p0:p0 + 64, c, t * 128:(t + 1) * 128]; qres = "qcb"
        P.mm(psB[b][:, 64:320], qs, kcb[p0:p0 + 64, c, :], True, True, reads=[qres, "kcb"], writes=[("psB", b)])
        if kind == "main":
            P.mm(psA[b][:, 0:512], qs, kb[wb][p0:p0 + 64, c, 0:512], True, True, reads=[qres, ("kb", wb)], writes=[("psA", b)])
            P.mm(psB[b][:, 0:64], qs, kb[wb][p0:p0 + 64, c, 512:576], True, True, reads=[qres, ("kb", wb)], writes=[("psB", b)])
        P.act(S[b3][:, 0:256], psB[b][:, 64:320], AF.Copy, reads=[("psB", b)], writes=[("Sc", b3)], scale=0.125)
    def stB(u):
        ti, kind, t, h, b, b3, wb, c, p0, b8 = info(u)
        W = 832 if kind == "main" else 256
        if kind == "main":
            ty = {0: 0, 1: 1, 14: 3, 15: 4}.get(t, 2)
            P.stt(S[b3][:, 256:768], psA[b][:, 0:512], 0.125, TB[:, ty, h, 0:512], ALU.mult, ALU.add,
                  reads=[("psA", b), ("TB", ty)], writes=[("Sw", b3)])
            P.stt(S[b3][:, 768:832], psB[b][:, 0:64], 0.125, TB[:, ty, h, 512:576], ALU.mult, ALU.add,
                  reads=[("psB", b), ("TB", ty)], writes=[("Sw2", b3)])
        P.op("vector", lambda e: e.tensor_reduce(out=mx[b8][:], in_=S[b3][:, 0:W], axis=AX.X, op=ALU.max, negate=True),
             reads=[("Sc", b3), ("Sw", b3), ("Sw2", b3)], writes=[("mx", b8)])
        P.act(Pb[b3][:, 0:W], S[b3][:, 0:W], AF.Exp, reads=[("Sc", b3), ("Sw", b3), ("Sw2", b3), ("mx", b8)], writes=[("Pb", b3), ("ssum", b8)],
              bias=mx[b8][:], scale=1.0, accum_out=ssum[b8][:])
    def stC(u):
        ti, kind, t, h, b, b3, wb, c, p0, b8 = info(u)
        nblk = 7 if kind == "main" else 2
        for kbk in range(nblk):
            kw = 64 if kbk == 6 else 128
            P.op("tensor", lambda e, kbk=kbk, kw=kw: e.transpose(psT[b][0:kw, kbk * 128:(kbk + 1) * 128], Pb[b3][:, kbk * 128:kbk * 128 + kw], idb[:]),
                 reads=[("Pb", b3), "idb"], writes=[("psT", b)], nosync_same=True)
        P.copy(PT[b3][:, 0:nblk * 128], psT[b][:, 0:nblk * 128], reads=[("psT", b)], writes=[("PT", b3)], eng="scalar")
    def stD(u):
        ti, kind, t, h, b, b3, wb, c, p0, b8 = info(u)
        ob = ti % 2
        nblk = 7 if kind == "main" else 2
        for kbk in range(nblk):
            kw = 64 if kbk == 6 else 128
            if kbk < 2:
                rhs = vcb[:, kbk, h * 64:(h + 1) * 64]; rres = "vcb"
            else:
                rhs = vb[wb][0:kw, kbk - 2, h * 64:(h + 1) * 64]; rres = ("vb", wb)
            P.mm(psO[ob][:, h * 64:(h + 1) * 64], PT[b3][0:kw, kbk * 128:(kbk + 1) * 128], rhs, kbk == 0, kbk == nblk - 1,
                 reads=[("PT", b3), rres], writes=[("psO", ob)])
        P.op("vector", lambda e: e.reciprocal(out=rinv[b8][:], in_=ssum[b8][:]), reads=[("ssum", b8)], writes=[("rinv", b8)])
        P.ts(Osb[ob][:, h * 64:(h + 1) * 64], psO[ob][:, h * 64:(h + 1) * 64], rinv[b8][:], None, ALU.mult,
             reads=[("psO", ob), ("rinv", b8)], writes=[("Osb", ob)])
        if h == 5:
            dst = Y[t * 128:(t + 1) * 128, :] if kind == "main" else Yc[t * 128:(t + 1) * 128, :]
            P.dma("gpsimd", dst, Osb[ob][:], reads=[("Osb", ob)], chan=f"st{ob}")
    NU = len(units)
    LC, LD = 3, 5
    for step in range(NU + LD):
        if step < NU: stA(step)
        if 0 <= step - 1 < NU: stB(step - 1)
        if 0 <= step - LC < NU: stC(step - LC)
        if 0 <= step - LD < NU: stD(step - LD)
    return P.finish()

def na_tile_geometry(r0):
    R = 128
    rs0 = int(np.clip(r0 - 4, 0, R - 8)); rs1 = int(np.clip(r0 + 1 - 4, 0, R - 8))
    return rs0, rs1

def na_tables(rpb, q):
    cols = np.arange(64); cs = np.clip(cols - 8, 0, 48)
    kc = np.arange(64)
    inwin = (kc[None, :] >= cs[:, None]) & (kc[None, :] < cs[:, None] + 16)
    dc = np.clip(kc[None, :] - cols[:, None] + 15, 0, 30)
    types = [32 * q, 32 * q + 2, 32 * q + 16, 32 * q + 28, 32 * q + 30]
    tbraw = np.zeros((5, 128, 6, 9, 64), np.float32); mask = np.zeros((5, 128, 9, 64), np.float32)
    for ti, r0 in enumerate(types):
        rs0, rs1 = na_tile_geometry(r0)
        for half, (r, rs) in enumerate(((r0, rs0), (r0 + 1, rs1))):
            for slot in range(9):
                krow = rs0 + slot
                valid = (krow >= rs) and (krow < rs + 8)
                dr = int(np.clip(krow - r + 7, 0, 14))
                g = rpb[:, dr][:, dc]
                tbraw[ti, half * 64:(half + 1) * 64, :, slot, :] = g.transpose(1, 0, 2)
                m = np.where(inwin & valid, 0.0, NEG).astype(np.float32)
                mask[ti, half * 64:(half + 1) * 64, slot, :] = m
    return tbraw.reshape(5, 128, 6, 576), mask.reshape(5, 128, 576)

def na_windows(k_b, v_b, q):
    kp = np.concatenate([k_b, np.zeros((64 * 16, 384), k_b.dtype)], 0); vp = np.concatenate([v_b, np.zeros((64 * 16, 384), v_b.dtype)], 0)
    kwT = np.zeros((16, 384, 576), k_b.dtype); vw = np.zeros((16, 128, 5, 384), v_b.dtype)
    for t in range(16):
        r0 = 32 * q + 2 * t
        rs0, _ = na_tile_geometry(r0)
        kwT[t] = kp[rs0 * 64:(rs0 + 9) * 64].T
        for j in range(4):
            vw[t, :, j, :] = vp[(rs0 + 2 * j) * 64:(rs0 + 2 * j + 2) * 64]
        vw[t, 0:64, 4, :] = vp[(rs0 + 8) * 64:(rs0 + 9) * 64]
    return kwT, vw

NCH = 1056
PI = math.pi

class A:
    def __init__(self, P): self.P = P
    @staticmethod
    def nm(*aps): return [a.tensor.name for a in aps if hasattr(a, "tensor")]
    def tt(self, o, a, b, op, eng="vector"): self.P.tt(o, a, b, op, reads=self.nm(a, b), writes=self.nm(o), eng=eng)
    def ts(self, o, a, s1, op0, s2=None, op1=None, eng="vector"):
        self.P.ts(o, a, s1, s2, op0, op1, reads=self.nm(a, s1, s2), writes=self.nm(o), eng=eng)
    def stt(self, o, a, s, b, op0, op1): self.P.stt(o, a, s, b, op0, op1, reads=self.nm(a, s, b), writes=self.nm(o))
    def act(self, o, a, f, **kw): self.P.act(o, a, f, reads=self.nm(a, *[v for v in kw.values()]), writes=self.nm(o), **kw)
    def copy(self, o, a, eng="vector"): self.P.copy(o, a, reads=self.nm(a), writes=self.nm(o), eng=eng)
    def memset(self, o, v, eng="vector"): self.P.memset(o, v, writes=self.nm(o), eng=eng)
    def mm(self, o, l, r, st, sp): self.P.mm(o, l, r, st, sp, reads=self.nm(l, r), writes=self.nm(o))
    def dma_in(self, o, src, chan): self.P.dma("sync", o, src, writes=self.nm(o), chan=chan)
    def dma_out(self, dst, a, chan="st"): self.P.dma("gpsimd", dst, a, reads=self.nm(a), chan=chan)
    def scan(self, o, d0, d1, init):
        self.P.op("vector", lambda e: e.tensor_tensor_scan(out=o, data0=d0, data1=d1, initial=init, op0=ALU.mult, op1=ALU.add),
                  reads=self.nm(d0, d1, init), writes=self.nm(o))
    def recip(self, o, a): self.P.op("vector", lambda e: e.reciprocal(out=o, in_=a), reads=self.nm(a), writes=self.nm(o))
    def transpose(self, o, a, ident):
        self.P.op("tensor", lambda e: e.transpose(o, a, ident), reads=self.nm(a, ident), writes=self.nm(o), nosync_same=True)
    def cmul_s(self, o_re, o_im, a_re, a_im, s_re, s_im, s_imn):
        self.ts(o_re, a_re, s_re, ALU.mult)
        self.stt(o_re, a_im, s_imn, o_re, ALU.mult, ALU.add)
        self.ts(o_im, a_re, s_im, ALU.mult)
        self.stt(o_im, a_im, s_re, o_im, ALU.mult, ALU.add)

def build_ssm():
    P = Prog(); a = A(P)
    d_in = {}
    for nm_, shp in (("are", [128, 6]), ("aim", [128, 6]), ("ldt", [128, 6]), ("Bre", [128, 96]), ("Bim", [128, 96]),
                     ("Cre", [128, 96]), ("Cim", [128, 96]), ("maskF", [128, 128]), ("maskB", [128, 128]), ("sgn", [128, 1]),
                     ("ident", [128, 128])):
        d_in[nm_] = P.dram_in(nm_, shp)
    Ddiag = P.dram_in("Ddiag", [6, 128, 128]); U = P.dram_in("U", [6, 128, NCH]); Yg = P.dram_out("Yg", [6, 128, NCH])
    s = {}
    for nm_, ap in d_in.items():
        shp = list(ap.shape)
        s[nm_] = P.sbuf("s_" + nm_, shp)
        a.dma_in(s[nm_][:], ap, "ld0")
    def T(name, shape): return P.sbuf(name, shape)
    dt = T("dt", [128, 6]); x = T("x", [128, 6]); th = T("th", [128, 6]); er = T("er", [128, 6]); m = T("m", [128, 6])
    y2 = T("y2", [128, 6]); sn = T("sn", [128, 6]); cs = T("cs", [128, 6]); lbr = T("lbr", [128, 6]); lbi = T("lbi", [128, 6])
    n2 = T("n2", [128, 6]); t1 = T("t1", [128, 6]); t2 = T("t2", [128, 6]); am1 = T("am1", [128, 6])
    qr = T("qr", [128, 6]); qi = T("qi", [128, 6]); qin = T("qin", [128, 6])
    a.act(dt[:], s["ldt"][:], AF.Exp)
    a.tt(x[:], s["are"][:], dt[:], ALU.mult); a.tt(th[:], s["aim"][:], dt[:], ALU.mult)
    a.act(er[:], x[:], AF.Exp)
    for _ in range(4):
        a.ts(m[:], th[:], PI, ALU.is_gt)
        a.stt(th[:], m[:], -2 * PI, th[:], ALU.mult, ALU.add)
    a.ts(y2[:], th[:], PI / 2, ALU.add)
    a.ts(m[:], y2[:], PI, ALU.is_gt)
    a.stt(y2[:], m[:], -2 * PI, y2[:], ALU.mult, ALU.add)
    a.act(sn[:], th[:], AF.Sin); a.act(cs[:], y2[:], AF.Sin)
    a.tt(lbr[:], er[:], cs[:], ALU.mult); a.tt(lbi[:], er[:], sn[:], ALU.mult)
    a.tt(n2[:], s["are"][:], s["are"][:], ALU.mult); a.tt(t1[:], s["aim"][:], s["aim"][:], ALU.mult); a.tt(n2[:], n2[:], t1[:], ALU.add)
    a.recip(n2[:], n2[:])
    a.ts(am1[:], lbr[:], -1.0, ALU.add)
    a.tt(t1[:], am1[:], s["are"][:], ALU.mult); a.tt(t2[:], lbi[:], s["aim"][:], ALU.mult); a.tt(t1[:], t1[:], t2[:], ALU.add)
    a.tt(qr[:], t1[:], n2[:], ALU.mult)
    a.tt(t1[:], lbi[:], s["are"][:], ALU.mult); a.tt(t2[:], am1[:], s["aim"][:], ALU.mult); a.tt(t1[:], t1[:], t2[:], ALU.subtract)
    a.tt(qi[:], t1[:], n2[:], ALU.mult)
    a.ts(qin[:], qi[:], -1.0, ALU.mult)
    Lr = T("Lr", [128, 6, 9]); Li = T("Li", [128, 6, 9]); Vr = T("Vr", [128, 6, 8]); Vi = T("Vi", [128, 6, 8])
    Rr = T("Rr", [128, 6, 9]); Ri = T("Ri", [128, 6, 9])
    e2 = T("e2", [128, 6]); ivr = T("ivr", [128, 6]); ivi = T("ivi", [128, 6])
    a.memset(Lr[:, :, 0], 1.0); a.memset(Li[:, :, 0], 0.0); a.memset(Vr[:, :, 0], 1.0); a.memset(Vi[:, :, 0], 0.0)
    a.act(e2[:], x[:], AF.Exp, scale=-2.0)
    a.tt(ivr[:], lbr[:], e2[:], ALU.mult); a.tt(ivi[:], lbi[:], e2[:], ALU.mult); a.ts(ivi[:], ivi[:], -1.0, ALU.mult)
    def cmul_t(o_r, o_i, p_r, p_i, q_r, q_i):
        a.tt(t1[:], p_r, q_r, ALU.mult); a.tt(t2[:], p_i, q_i, ALU.mult); a.tt(o_r, t1[:], t2[:], ALU.subtract)
        a.tt(t1[:], p_r, q_i, ALU.mult); a.tt(t2[:], p_i, q_r, ALU.mult); a.tt(o_i, t1[:], t2[:], ALU.add)
    for k in range(8):
        cmul_t(Lr[:, :, k + 1], Li[:, :, k + 1], Lr[:, :, k], Li[:, :, k], lbr[:], lbi[:])
    for k in range(7):
        cmul_t(Vr[:, :, k + 1], Vi[:, :, k + 1], Vr[:, :, k], Vi[:, :, k], ivr[:], ivi[:])
    for k in range(9):
        a.copy(Rr[:, :, k], Lr[:, :, 8 - k], eng="gpsimd"); a.copy(Ri[:, :, k], Li[:, :, 8 - k], eng="gpsimd")
    tabs = {}
    for nm_, (lo_r, lo_i, hi_r, hi_i) in dict(
            XL=(Vr[0:64, :, 0:8], Vi[0:64, :, 0:8], Lr[64:128, :, 0:8], Li[64:128, :, 0:8]),
            YL=(Lr[0:64, :, 0:8], Li[0:64, :, 0:8], Vr[64:128, :, 0:8], Vi[64:128, :, 0:8]),
            SL=(Rr[0:64, :, 1:9], Ri[0:64, :, 1:9], Lr[64:128, :, 0:8], Li[64:128, :, 0:8]),
            OL=(Lr[0:64, :, 1:9], Li[0:64, :, 1:9], Rr[64:128, :, 0:8], Ri[64:128, :, 0:8])).items():
        tr = T(nm_ + "r", [128, 6, 8]); ti = T(nm_ + "i", [128, 6, 8]); tn = T(nm_ + "n", [128, 6, 8])
        a.copy(tr[0:64], lo_r); a.copy(ti[0:64], lo_i); a.copy(tr[64:128], hi_r); a.copy(ti[64:128], hi_i)
        a.ts(tn[:], ti[:], -1.0, ALU.mult)
        tabs[nm_] = (tr, ti, tn)
    rho8 = T("rho8", [128, 6]); c8 = T("c8", [128, 6]); s8 = T("s8", [128, 6]); e8 = T("e8", [128, 6])
    a.act(rho8[:], x[:], AF.Exp, scale=8.0); a.act(e8[:], x[:], AF.Exp, scale=-8.0)
    a.tt(c8[:], Lr[:, :, 8], e8[:], ALU.mult); a.tt(s8[:], Li[:, :, 8], e8[:], ALU.mult)
    a.ts(s8[:], s8[:], s["sgn"][:, 0:1], ALU.mult)
    onesT = T("onesT", [128, NCH]); a.memset(onesT[:], 1.0, eng="gpsimd")
    def bc_j(ap):
        return ap.rearrange("p g (o j) -> p g o j", o=1).to_broadcast([128, 6, 8, 16])
    def bc_k(ap):
        return ap.rearrange("p g (k o) -> p g k o", o=1).to_broadcast([128, 6, 8, 16])
    Bre_v = s["Bre"][:].rearrange("p (g j) -> p g j", j=16); Bim_v = s["Bim"][:].rearrange("p (g j) -> p g j", j=16)
    Cre_v = s["Cre"][:].rearrange("p (g j) -> p g j", j=16); Cim_v = s["Cim"][:].rearrange("p (g j) -> p g j", j=16)
    Bbr_a = T("Bbr_a", [128, 6, 16]); Bbi_a = T("Bbi_a", [128, 6, 16])
    u1 = T("u1", [128, 6, 16]); u2 = T("u2", [128, 6, 16])
    qr_b = qr[:].rearrange("p (g o) -> p g o", o=1).to_broadcast([128, 6, 16]); qi_b = qi[:].rearrange("p (g o) -> p g o", o=1).to_broadcast([128, 6, 16])
    a.tt(u1[:], Bre_v, qr_b, ALU.mult); a.tt(u2[:], Bim_v, qi_b, ALU.mult, eng="gpsimd"); a.tt(Bbr_a[:], u1[:], u2[:], ALU.subtract)
    a.tt(u1[:], Bre_v, qi_b, ALU.mult); a.tt(u2[:], Bim_v, qr_b, ALU.mult, eng="gpsimd"); a.tt(Bbi_a[:], u1[:], u2[:], ALU.add)
    v1 = T("v1", [128, 6, 8, 16]); v2 = T("v2", [128, 6, 8, 16])
    def ctab(name, Ar, Ai, tb, neg_im):
        o_r = T(name + "r_a", [128, 6, 8, 16]); o_i = T(name + "i_a", [128, 6, 8, 16])
        Sr, Si = tb[0][:], tb[1][:]
        a.tt(v1[:], bc_j(Ar), bc_k(Sr), ALU.mult); a.tt(v2[:], bc_j(Ai), bc_k(Si), ALU.mult, eng="gpsimd")
        a.tt(o_r[:], v1[:], v2[:], ALU.subtract)
        a.tt(v1[:], bc_j(Ar), bc_k(Si), ALU.mult); a.tt(v2[:], bc_j(Ai), bc_k(Sr), ALU.mult, eng="gpsimd")
        a.tt(o_i[:], v1[:], v2[:], ALU.add)
        if neg_im:
            a.ts(o_i[:], o_i[:], -1.0, ALU.mult)
        return o_r, o_i
    Xr_a, Xi_a = ctab("X", Bbr_a[:], Bbi_a[:], tabs["XL"], False)
    Wtr_a, Wti_a = ctab("Wt", Bbr_a[:], Bbi_a[:], tabs["SL"], False)
    Yr_a, Yin_a = ctab("Y", Cre_v, Cim_v, tabs["YL"], True)
    Wor_a, Woin_a = ctab("Wo", Cre_v, Cim_v, tabs["OL"], True)
    Wsr = T("Wsr", [128, 128]); Wsi = T("Wsi", [128, 128]); Msb = T("Msb", [128, 128]); Mtmp = T("Mtmp", [128, 128]); Dd = T("Dd", [128, 128])
    Us = [T(f"Us{i}", [128, NCH]) for i in range(2)]
    Sre = T("Sre", [128, NCH]); Sim = T("Sim", [128, NCH]); Spr = T("Spr", [128, NCH]); Spi = T("Spi", [128, NCH])
    Gre = T("Gre", [128, NCH]); Gim = T("Gim", [128, NCH]); Hor = T("Hor", [128, NCH]); Hoi = T("Hoi", [128, NCH])
    Hir = T("Hir", [128, NCH]); Hii = T("Hii", [128, NCH])
    Tr = T("Tr", [128, NCH + 1]); Ti = T("Ti", [128, NCH + 1]); rhoT = T("rhoT", [128, NCH])
    w1 = T("w1", [128, NCH]); w2 = T("w2", [128, NCH])
    mult = T("mult", [128, 11, 3]); ini = T("ini", [128, 4]); Ysb = T("Ysb", [128, NCH])
    ps = [P.psum(f"ps{i}") for i in range(8)]
    BLK = [(0, 512), (512, 512), (1024, NCH - 1024)]
    for gi in range(6):
        ub = gi % 2
        a.dma_in(Us[ub][:], U[gi], f"u{ub}")
        a.dma_in(Dd[:], Ddiag[gi], "dd")
        Xr, Xi, Yr, Yin = Xr_a[:, gi], Xi_a[:, gi], Yr_a[:, gi], Yin_a[:, gi]
        Wtr, Wti, Wor, Woin = Wtr_a[:, gi], Wti_a[:, gi], Wor_a[:, gi], Woin_a[:, gi]
        f2 = lambda t_: t_.rearrange("p a b -> p (a b)")
        for half, pb in ((0, 6), (1, 7)):
            rows = slice(half * 64, half * 64 + 64)
            a.mm(ps[pb][:, 0:128], f2(Xr)[rows], f2(Yr)[rows], True, False)
            a.mm(ps[pb][:, 0:128], f2(Xi)[rows], f2(Yin)[rows], False, True)
        a.tt(Msb[:], ps[6][:, 0:128], s["maskF"][:], ALU.mult)
        a.tt(Mtmp[:], ps[7][:, 0:128], s["maskB"][:], ALU.mult)
        a.tt(Msb[:], Msb[:], Mtmp[:], ALU.add, eng="gpsimd"); a.tt(Msb[:], Msb[:], Dd[:], ALU.add, eng="gpsimd")
        a.transpose(ps[6][:, 128:256], f2(Wtr), s["ident"][:]); a.transpose(ps[7][:, 128:256], f2(Wti), s["ident"][:])
        a.copy(Wsr[:], ps[6][:, 128:256], eng="scalar"); a.copy(Wsi[:], ps[7][:, 128:256], eng="scalar")
        for bi, (c0, cn) in enumerate(BLK):
            a.mm(ps[bi][:, 0:cn], Wsr[:], Us[ub][:, c0:c0 + cn], True, True)
            a.mm(ps[3 + bi][:, 0:cn], Wsi[:], Us[ub][:, c0:c0 + cn], True, True)
            a.copy(Sre[:, c0:c0 + cn], ps[bi][:, 0:cn], eng="scalar"); a.copy(Sim[:, c0:c0 + cn], ps[3 + bi][:, 0:cn], eng="scalar")
        a.memset(Tr[:, 0:1], 1.0); a.memset(Ti[:, 0:1], 0.0)
        a.copy(mult[:, 0, 0:1], c8[:, gi:gi + 1]); a.copy(mult[:, 0, 1:2], s8[:, gi:gi + 1])
        a.ts(mult[:, 0, 2:3], mult[:, 0, 1:2], -1.0, ALU.mult)
        for k in range(1, 11):
            a.tt(ini[:, 0:1], mult[:, k - 1, 0:1], mult[:, k - 1, 0:1], ALU.mult); a.tt(ini[:, 1:2], mult[:, k - 1, 1:2], mult[:, k - 1, 1:2], ALU.mult)
            a.tt(mult[:, k, 0:1], ini[:, 0:1], ini[:, 1:2], ALU.subtract)
            a.tt(ini[:, 0:1], mult[:, k - 1, 0:1], mult[:, k - 1, 1:2], ALU.mult)
            a.ts(mult[:, k, 1:2], ini[:, 0:1], 2.0, ALU.mult); a.ts(mult[:, k, 2:3], ini[:, 0:1], -2.0, ALU.mult)
        for k in range(11):
            n = 1 << k
            cnt = min(n, NCH + 1 - n)
            a.cmul_s(Tr[:, n:n + cnt], Ti[:, n:n + cnt], Tr[:, 0:cnt], Ti[:, 0:cnt], mult[:, k, 0:1], mult[:, k, 1:2], mult[:, k, 2:3])
        a.ts(rhoT[:], onesT[:], rho8[:, gi:gi + 1], ALU.mult, eng="gpsimd")
        a.tt(w1[:], Sre[:], Tr[:, 0:NCH], ALU.mult); a.tt(w2[:], Sim[:], Ti[:, 0:NCH], ALU.mult, eng="gpsimd")
        a.tt(Spr[:], w1[:], w2[:], ALU.subtract)
        a.tt(w1[:], Sre[:], Ti[:, 0:NCH], ALU.mult); a.tt(w2[:], Sim[:], Tr[:, 0:NCH], ALU.mult, eng="gpsimd")
        a.tt(Spi[:], w1[:], w2[:], ALU.add)
        for (Gx, Sx) in ((Gre, Spr), (Gim, Spi)):
            a.scan(Gx[0:64, :], rhoT[0:64, :], Sx[0:64, :], 0.0)
            a.scan(Gx[64:128, 0:32][:, ::-1], rhoT[64:128, 0:32], Sx[64:128, 0:32][:, ::-1], 0.0)
        lo = slice(64, 128)
        a.tt(ini[lo, 0:1], Gre[lo, 0:1], Tr[lo, NCH:NCH + 1], ALU.mult); a.tt(ini[lo, 1:2], Gim[lo, 0:1], Ti[lo, NCH:NCH + 1], ALU.mult)
        a.tt(ini[lo, 2:3], ini[lo, 0:1], ini[lo, 1:2], ALU.subtract)
        a.tt(ini[lo, 0:1], Gre[lo, 0:1], Ti[lo, NCH:NCH + 1], ALU.mult); a.tt(ini[lo, 1:2], Gim[lo, 0:1], Tr[lo, NCH:NCH + 1], ALU.mult)
        a.tt(ini[lo, 3:4], ini[lo, 0:1], ini[lo, 1:2], ALU.add)
        a.scan(Gre[lo, 32:NCH][:, ::-1], rhoT[lo, 32:NCH], Spr[lo, 32:NCH][:, ::-1], ini[lo, 2:3])
        a.scan(Gim[lo, 32:NCH][:, ::-1], rhoT[lo, 32:NCH], Spi[lo, 32:NCH][:, ::-1], ini[lo, 3:4])
        a.tt(w1[:], Gre[:], Tr[:, 0:NCH], ALU.mult); a.tt(w2[:], Gim[:], Ti[:, 0:NCH], ALU.mult, eng="gpsimd")
        a.tt(Hor[:], w1[:], w2[:], ALU.add)
        a.tt(w1[:], Gim[:], Tr[:, 0:NCH], ALU.mult); a.tt(w2[:], Gre[:], Ti[:, 0:NCH], ALU.mult, eng="gpsimd")
        a.tt(Hoi[:], w1[:], w2[:], ALU.subtract)
        for (Hi_, Ho_, Gx) in ((Hir, Hor, Gre), (Hii, Hoi, Gim)):
            a.copy(Hi_[0:64, 1:NCH], Ho_[0:64, 0:NCH - 1], eng="scalar"); a.memset(Hi_[0:64, 0:1], 0.0)
            a.copy(Hi_[lo, 0:NCH - 1], Ho_[lo, 1:NCH], eng="scalar"); a.memset(Hi_[lo, 31:32], 0.0)
            a.copy(Hi_[lo, NCH - 1:NCH], Gx[lo, 0:1])
        for bi, (c0, cn) in enumerate(BLK):
            a.mm(ps[bi][:, 0:cn], Msb[:], Us[ub][:, c0:c0 + cn], True, False)
            a.mm(ps[bi][:, 0:cn], f2(Wor), Hir[:, c0:c0 + cn], False, False)
            a.mm(ps[bi][:, 0:cn], f2(Woin), Hii[:, c0:c0 + cn], False, True)
            a.copy(Ysb[:, c0:c0 + cn], ps[bi][:, 0:cn], eng="scalar")
        a.dma_out(Yg[gi], Ysb[:])
    return P.finish()

def ssm_inputs(inp, l, j4, u_b, uc_b):
    gs = np.arange(6 * j4, 6 * j4 + 6)
    def rows(arr):
        return np.ascontiguousarray(arr[:, gs, :].transpose(0, 2, 1).reshape(128, 6))
    are = rows(inp["ssm_a_re"][l]); aim = rows(inp["ssm_a_im"][l])
    ldt = np.ascontiguousarray(np.repeat(inp["ssm_log_dt"][l][:, gs][:, None, :], 64, axis=1).reshape(128, 6))
    def rowsB(arr):
        return np.ascontiguousarray(arr[:, gs].transpose(0, 2, 1, 3).reshape(128, 96))
    def rowsC(arr):
        return np.ascontiguousarray(arr[:, gs].transpose(0, 3, 1, 2).reshape(128, 96))
    s_ = np.arange(8)
    mF = (s_[None, :] >= s_[:, None]).astype(np.float32)
    maskF = np.kron(mF, np.ones((16, 16), np.float32)); maskB = np.kron(mF.T, np.ones((16, 16), np.float32))
    sgn = np.concatenate([-np.ones((64, 1), np.float32), np.ones((64, 1), np.float32)], 0)
    dsk = inp["ssm_d"][l]
    Dd = np.zeros((6, 128, 128), np.float32)
    for gi, g in enumerate(gs):
        dd = np.zeros((8, 16, 8, 16), np.float32)
        for t in range(8):
            dd[t, np.arange(16), t, np.arange(16)] = dsk[16 * g:16 * g + 16]
        Dd[gi] = dd.reshape(128, 128)
    seq = np.concatenate([uc_b, u_b], 0)
    U = np.zeros((6, 128, NCH), np.float32)
    for gi, g in enumerate(gs):
        U[gi] = seq[:, 16 * g:16 * g + 16].reshape(NCH, 128).T
    return dict(are=are, aim=aim, ldt=ldt, Bre=rowsB(inp["ssm_b_re"][l]), Bim=rowsB(inp["ssm_b_im"][l]),
                Cre=rowsC(inp["ssm_c_re"][l]), Cim=rowsC(inp["ssm_c_im"][l]), maskF=maskF, maskB=maskB, sgn=sgn,
                ident=np.eye(128, dtype=np.float32), Ddiag=Dd, U=U)

def ssm_unpack(Yg):
    return np.ascontiguousarray(Yg.transpose(2, 1, 0).reshape(NCH, 8, 16, 6).transpose(0, 1, 3, 2).reshape(NCH * 8, 96))


_PROGS = {}
def _prog(name, fn):
    if name not in _PROGS:
        _PROGS[name] = fn()
    return _PROGS[name]

def _run(nc, maps):
    res = run_bass_kernel_spmd(nc, maps, core_ids=list(range(8)))
    return res.results

def kernel(x, c, ctx, c_ctx, w_mod, b_mod, g_pre_mix, g_post_mix, w_in, ssm_a_re, ssm_a_im, ssm_log_dt, ssm_b_re, ssm_b_im,
           ssm_c_re, ssm_c_im, ssm_d, w_glu, w_fourier, na_rpb, w_out, g_pre_ffn, g_post_ffn, w_ffn_gate, w_ffn_up, w_ffn_down):
    f32 = lambda a: np.ascontiguousarray(np.asarray(a, dtype=np.float32))
    inp = dict(ssm_a_re=f32(ssm_a_re), ssm_a_im=f32(ssm_a_im), ssm_log_dt=f32(ssm_log_dt), ssm_b_re=f32(ssm_b_re), ssm_b_im=f32(ssm_b_im),
               ssm_c_re=f32(ssm_c_re), ssm_c_im=f32(ssm_c_im), ssm_d=f32(ssm_d))
    x = f32(x); c = f32(c); ctx = f32(ctx); c_ctx = f32(c_ctx); w_mod = f32(w_mod); b_mod = f32(b_mod)
    w_in = f32(w_in); w_glu = f32(w_glu); w_fourier = f32(w_fourier); na_rpb = f32(na_rpb); w_out = f32(w_out)
    g_pre_mix = f32(g_pre_mix); g_post_mix = f32(g_post_mix); g_pre_ffn = f32(g_pre_ffn); g_post_ffn = f32(g_post_ffn)
    w_ffn_gate = f32(w_ffn_gate); w_ffn_up = f32(w_ffn_up); w_ffn_down = f32(w_ffn_down)
    DEPTH = 2
    cores = [(k // 4, k % 4) for k in range(8)]
    cTs = [np.ascontiguousarray(np.concatenate([colT(c[b], 8), colT(c_ctx, 8)], axis=1)) for b in range(2)]
    xT = [np.ascontiguousarray(np.concatenate([x[b, q * 2048:(q + 1) * 2048].T, ctx[b].T], axis=1)) for (b, q) in cores]
    KF = fnet_consts(); permm = perm_matrix(); ident = np.eye(128, dtype=np.float32)
    ropes = [rope_tables(q * 2048, 2048) for q in range(4)]
    for l in range(DEPTH):
        maps = []
        for k, (b, q) in enumerate(cores):
            maps.append(dict(xT=xT[k], w_in=w_in[l], w_mod=np.ascontiguousarray(w_mod[l][:, 0:2048]), b_modT=colT(b_mod[l][0:2048], 16),
                             g_preT=colT(g_pre_mix[l], 8), cT=cTs[b], cos=ropes[q][0], sin=ropes[q][1], perm=permm))
        res = _run(_prog("l1", build_l1), maps)
        hfull = [np.concatenate([res[k]["hT"], res[k]["hTb"]], 0) for k in range(8)]
        h_lat = [np.concatenate([hfull[4 * b + q][:, 0:2048].T for q in range(4)], 0) for b in range(2)]
        h_ctx = [np.ascontiguousarray(hfull[4 * b][:, 2048:2304].T) for b in range(2)]
        del hfull
        del res
        maps = [ssm_inputs(inp, l, j4, h_lat[b][:, 0:384], h_ctx[b][:, 0:384]) for (b, j4) in cores]
        res = _run(_prog("ssm", build_ssm), maps)
        ys = [[ssm_unpack(res[4 * b + j4]["Yg"]) for j4 in range(4)] for b in range(2)]
        ysT = [np.ascontiguousarray(np.concatenate(ys[b], 1).T) for b in range(2)]
        del res, ys
        maps = []
        for (b, g) in cores:
            m = dict(z=np.ascontiguousarray(h_lat[b][:, 384 + 64 * g:448 + 64 * g].reshape(128, 4096)),
                     zc=np.ascontiguousarray(h_ctx[b][:, 384 + 64 * g:448 + 64 * g].reshape(2, 128, 64).transpose(1, 0, 2)))
            m.update(KF); maps.append(m)
        res = _run(_prog("fnet", lambda: build_fnet(True)), maps)
        mxT = [np.concatenate([res[4 * b + g]["R"] for g in range(4)], 0) for b in range(2)]
        mxcT = [np.concatenate([res[4 * b + g]["Rc"] for g in range(4)], 0) for b in range(2)]
        del res
        maps = []
        for (b, q) in cores:
            kwT, vw = na_windows(h_lat[b][:, 1024:1408], h_lat[b][:, 1408:1792], q)
            tbraw, mask = na_tables(na_rpb[l], q)
            maps.append(dict(qT=np.ascontiguousarray(h_lat[b][q * 2048:(q + 1) * 2048, 640:1024].T), kwT=kwT, vw=vw,
                             kcT=np.ascontiguousarray(h_ctx[b][:, 1024:1408].T),
                             vc=np.ascontiguousarray(h_ctx[b][:, 1408:1792].reshape(2, 128, 384).transpose(1, 0, 2)),
                             tbraw=tbraw, mask=mask, ident=ident, qcT=np.ascontiguousarray(h_ctx[b][:, 640:1024].T)))
        res = _run(_prog("na", lambda: build_na(True)), maps)
        naT = [np.ascontiguousarray(np.concatenate([res[4 * b + q]["Y"] for q in range(4)], 0).T) for b in range(2)]
        nacT = [np.ascontiguousarray(res[4 * b]["Yc"].T) for b in range(2)]
        del res, h_lat
        maps = []
        for k, (b, q) in enumerate(cores):
            sl = slice(q * 2048, (q + 1) * 2048)
            maps.append(dict(xT=xT[k], ysT=np.ascontiguousarray(np.concatenate([ysT[b][:, 256 + q * 2048:256 + (q + 1) * 2048], ysT[b][:, 0:256]], 1)),
                             mxT=np.ascontiguousarray(np.concatenate([mxT[b][:, sl], mxcT[b]], 1)),
                             naT=np.ascontiguousarray(np.concatenate([naT[b][:, sl], nacT[b]], 1)),
                             w_mod=np.ascontiguousarray(w_mod[l][:, 2048:3072]), b_modT=colT(b_mod[l][2048:3072], 8), cT=cTs[b],
                             g_postT=colT(g_post_mix[l], 8), w_glu=w_glu[l], w_fourier=w_fourier[l], w_out=w_out[l]))
        res = _run(_prog("l3a", lambda: build_l3a(True)), maps)
        xT = [res[k]["xoT"] for k in range(8)]
        del res
        maps = []
        for k, (b, q) in enumerate(cores):
            maps.append(dict(xT=xT[k], w_mod=np.ascontiguousarray(w_mod[l][:, 3072:6144]), b_modT=colT(b_mod[l][3072:6144], 24), cT=cTs[b],
                             g_preT=colT(g_pre_ffn[l], 8), g_postT=colT(g_post_ffn[l], 8),
                             w_gate=w_ffn_gate[l], w_up=w_ffn_up[l], w_down=w_ffn_down[l]))
        res = _run(_prog("l3b", lambda: build_l3b(True)), maps)
        xT = [np.ascontiguousarray(res[k]["xoT"]) for k in range(8)]
        del res
    out = np.empty((2, 8192, 1024), np.float32)
    for k, (b, q) in enumerate(cores):
        out[b, q * 2048:(q + 1) * 2048] = xT[k][:, 0:2048].T
    return out
```

```python
import math
import numpy as np
from contextlib import ExitStack
import concourse.bass as bass
import concourse.mybir as mybir
from concourse.bass_utils import run_bass_kernel_spmd


F32 = mybir.dt.float32
BF16 = mybir.dt.bfloat16
ALU = mybir.AluOpType
AF = mybir.ActivationFunctionType
AX = mybir.AxisListType

COMPUTE = ("tensor", "vector", "scalar", "gpsimd")


class Prog:
    def __init__(self):
        self.nc = bass.Bass("TRN2", target_bir_lowering=False)
        self.ops = []
        self.stack = ExitStack()
        self.ndram = 0

    def dram_in(self, name, shape, dtype=F32):
        return self.nc.dram_tensor(name, list(shape), dtype, kind="ExternalInput").ap()

    def dram_out(self, name, shape, dtype=F32):
        return self.nc.dram_tensor(name, list(shape), dtype, kind="ExternalOutput").ap()

    def sbuf(self, name, shape, dtype=F32):
        return self.stack.enter_context(self.nc.sbuf_tensor(name, list(shape), dtype))

    def psum(self, name, shape=(128, 512), dtype=F32):
        return self.stack.enter_context(self.nc.psum_tensor(name, list(shape), dtype))

    def op(self, eng, fn, reads=(), writes=(), chan=None, nosync_same=False, inc=True):
        self.ops.append(dict(eng=eng, fn=fn, reads=tuple(reads), writes=tuple(writes),
                             chan=chan, nosync_same=nosync_same, inc=inc))

    def dma(self, eng, out, in_, reads=(), writes=(), chan="ld", **kw):
        self.op(eng, lambda e: e.dma_start(out=out, in_=in_, **kw), reads, writes, chan=chan)

    def mm(self, out, lhsT, rhs, start, stop, reads=(), writes=()):
        self.op("tensor", lambda e: e.matmul(out, lhsT, rhs, start=start, stop=stop),
                reads, writes, nosync_same=True, inc=True)

    def act(self, out, in_, func, reads=(), writes=(), **kw):
        self.op("scalar", lambda e: e.activation(out=out, in_=in_, func=func, **kw), reads, writes)

    def tt(self, out, in0, in1, op, reads=(), writes=(), eng="vector"):
        self.op(eng, lambda e: e.tensor_tensor(out=out, in0=in0, in1=in1, op=op), reads, writes)

    def ts(self, out, in0, s1, s2, op0, op1=None, reads=(), writes=(), eng="vector"):
        if op1 is None:
            self.op(eng, lambda e: e.tensor_scalar(out=out, in0=in0, scalar1=s1, scalar2=None, op0=op0),
                    reads, writes)
        else:
            self.op(eng, lambda e: e.tensor_scalar(out=out, in0=in0, scalar1=s1, scalar2=s2, op0=op0, op1=op1),
                    reads, writes)

    def stt(self, out, in0, scalar, in1, op0, op1, reads=(), writes=()):
        self.op("vector", lambda e: e.scalar_tensor_tensor(out=out, in0=in0, scalar=scalar, in1=in1,
                                                            op0=op0, op1=op1), reads, writes)

    def copy(self, out, in_, reads=(), writes=(), eng="vector"):
        if eng == "scalar":
            self.op(eng, lambda e: e.copy(out=out, in_=in_), reads, writes)
        else:
            self.op(eng, lambda e: e.tensor_copy(out=out, in_=in_), reads, writes)

    def memset(self, ap, val, writes=(), eng="vector"):
        self.op(eng, lambda e: e.memset(ap, val), (), writes)

    def finish(self):
        nc = self.nc
        ops = self.ops
        engines = []
        for o in ops:
            if o["eng"] not in engines:
                engines.append(o["eng"])
        chans = []
        for o in ops:
            if o["chan"] is not None and o["chan"] not in chans:
                chans.append(o["chan"])
        sems = {}
        for e in engines:
            sems[("e", e)] = self.stack.enter_context(nc.semaphore("s_" + e))
        for c in chans:
            sems[("c", c)] = self.stack.enter_context(nc.semaphore("c_" + c))
        def plan_pass():
            viol = set()
            eng_count = {e: 0 for e in engines}
            chan_count = {c: 0 for c in chans}
            last_writer = {}
            readers = {}
            known = {e: {} for e in engines}
            plan = {e: [] for e in engines}
            done = []
            for i, o in enumerate(ops):
                e = o["eng"]
                deps = set()
                for r in o["reads"]:
                    if r in last_writer:
                        deps.add(last_writer[r])
                for w in o["writes"]:
                    if w in last_writer:
                        deps.add(last_writer[w])
                    for rd in readers.get(w, ()):
                        deps.add(rd)
                need = {}
                for d in deps:
                    od = ops[d]
                    if od["chan"] is not None:
                        key = ("c", od["chan"])
                        val = 16 * chan_count[od["chan"]]
                    else:
                        if od["eng"] == e and (o["nosync_same"] and od["nosync_same"]):
                            continue
                        key = ("e", od["eng"])
                        val = done[d][1]
                        if val > eng_count[od["eng"]]:
                            viol.add(d)
                    if val > need.get(key, 0):
                        need[key] = val
                waits = []
                for key, val in need.items():
                    if known[e].get(key, 0) >= val:
                        continue
                    known[e][key] = val
                    waits.append((key, val))
                if o["chan"] is not None:
                    chan_count[o["chan"]] += 1
                    done.append((("c", o["chan"]), 16 * chan_count[o["chan"]]))
                    inc = (("c", o["chan"]), 16)
                elif not o["inc"]:
                    done.append((("e", e), eng_count[e] + 1))
                    inc = None
                else:
                    eng_count[e] += 1
                    done.append((("e", e), eng_count[e]))
                    inc = (("e", e), 1)
                plan[e].append((waits, o["fn"], inc))
                for r in o["reads"]:
                    readers.setdefault(r, []).append(i)
                for w in o["writes"]:
                    last_writer[w] = i
                    readers[w] = []
            return viol, plan, chan_count
        while True:
            viol, plan, chan_count = plan_pass()
            if not viol:
                break
            for d in viol:
                ops[d]["inc"] = True
        final_waits = {e: [] for e in engines}
        chan_eng = {}
        for o in ops:
            if o["chan"] is not None:
                chan_eng[o["chan"]] = o["eng"]
        for c, e in chan_eng.items():
            final_waits[e].append((("c", c), 16 * chan_count[c]))

        semv = {k: 0 for k in sems}
        ptr = {e: 0 for e in engines}
        progressed = True
        while progressed:
            progressed = False
            for e in engines:
                while ptr[e] < len(plan[e]):
                    waits, _fn, inc = plan[e][ptr[e]]
                    if any(semv[k] < v for k, v in waits):
                        break
                    if inc is not None:
                        semv[inc[0]] += inc[1]
                    ptr[e] += 1
                    progressed = True
        stuck = {e: (ptr[e], len(plan[e])) for e in engines if ptr[e] < len(plan[e])}
        if stuck:
            det = {e: [(k, v, semv[k]) for k, v in plan[e][ptr[e]][0] if semv[k] < v] for e in stuck}
            raise RuntimeError(f"sync plan deadlocks: {stuck} waiting on {det}")

        with nc.Block() as block:
            def make(e):
                def body(eng):
                    for waits, fn, inc in plan[e]:
                        for key, val in waits:
                            eng.wait_ge(sems[key], val)
                        ins = fn(eng)
                        if inc is not None:
                            ins.then_inc(sems[inc[0]], inc[1])
                    for key, val in final_waits[e]:
                        eng.wait_ge(sems[key], val)
                return body
            for e in engines:
                getattr(block, e)(make(e))
        self.stack.close()
        return nc

GRID_W = 64
def rope_tables(tok0, n):
    t = np.arange(tok0, tok0 + n); row = (t // GRID_W).astype(np.float32); col = (t % GRID_W).astype(np.float32)
    quarter = 16
    freqs = (10000.0 ** (-np.arange(quarter, dtype=np.float32) / quarter)).astype(np.float32)
    cos = np.zeros((64, n), np.float32); sin = np.zeros((64, n), np.float32)
    for d in range(64):
        pos = row if d < 32 else col
        dd = d % 32
        f = freqs[dd % 16]
        ang = (pos * f).astype(np.float32)
        cos[d] = np.cos(ang); s = np.sin(ang)
        sin[d] = -s if dd < 16 else s
    return np.concatenate([cos, cos], 0), np.concatenate([sin, sin], 0)
def perm_matrix():
    Pm = np.zeros((128, 128), np.float32)
    for m in range(128):
        dd = m % 32
        k = m + 16 if dd < 16 else m - 16
        Pm[k, m] = 1.0
    return Pm
def colT(v, n):
    return np.ascontiguousarray(v.reshape(n, 128).T)

EPS = 1e-6

def get_stage(P):
    if not hasattr(P, "_stage"):
        P._stage_n = getattr(P, "_stage_n", 3)
        P._stage = [P.sbuf(f"stage{i}", [128, 1024]) for i in range(P._stage_n)]
        P._stage_i = 0
    return P._stage

def emit_mod(P, w_mod, b_modT, cT, nct, ps_mod, tag="m"):
    ncols = nct * 128
    c_s = P.sbuf(tag + "c_s", [128, 16])
    bm_s = P.sbuf(tag + "bm_s", [128, nct]); modv = P.sbuf(tag + "modv", [128, 2 * nct]); modc = P.sbuf(tag + "modc", [128, 2 * nct])
    modrow = [P.sbuf(f"{tag}modrow{i}", [2, 512]) for i in range(2)]
    scr = P.nc.dram_tensor(tag + "_modscr", [2, ncols], F32, kind="Internal").ap()
    stb = [P.sbuf(f"{tag}stb{i}", [128, 2, 512], BF16) for i in range(3)]
    sc_b = P.sbuf(tag + "sc_b", [128, 16], BF16)
    P.dma("sync", c_s[:], cT, writes=[tag + "c_s"], chan="ld0")
    P.dma("sync", bm_s[:], b_modT, writes=[tag + "bm_s"], chan="ld0")
    P.act(sc_b[:], c_s[:], AF.Silu, reads=[tag + "c_s"], writes=[tag + "sc_s"])
    wmv = w_mod.rearrange("(kc p) n -> p kc n", p=128)
    nst = 0
    for pc in range(ncols // 512):
        for i in range(4):
            b = nst % 3; nst += 1
            P.dma("gpsimd", stb[b][:], wmv[:, 2 * i:2 * i + 2, pc * 512:(pc + 1) * 512],
                  writes=[(tag + "stb", b)], chan=f"{tag}stb{b}", max_dma_last_dim=2048)
            for k2 in range(2):
                kc = 2 * i + k2
                P.mm(ps_mod[0:2, 0:512], sc_b[:, kc:16:8], stb[b][:, k2, :], kc == 0, kc == 7,
                     reads=[(tag + "stb", b), tag + "sc_s"], writes=["ps_mod"])
        P.copy(modrow[pc % 2][:], ps_mod[0:2, 0:512], reads=["ps_mod"], writes=[(tag + "modrow", pc % 2)], eng="scalar")
        P.dma("sync", scr[:, pc * 512:(pc + 1) * 512], modrow[pc % 2][:], reads=[(tag + "modrow", pc % 2)], writes=[tag + "scr"], chan="modw")
    P.dma("sync", modc[:].rearrange("p (j t) -> p j t", j=2), scr.rearrange("j (t p) -> p j t", p=128),
          reads=[tag + "scr"], writes=[tag + "modc"], chan="modr", allow_slow_non_contiguous=True)
    for j in range(2):
        P.tt(modv[:, j * nct:(j + 1) * nct], modc[:, j * nct:(j + 1) * nct], bm_s[:], ALU.add,
             reads=[tag + "modc", tag + "bm_s"], writes=[tag + "modv"])
    return modv

def load_cast(P, w_dram, w_bf, nk, ncols, tag, piece=1024):
    wv = w_dram.rearrange("(kc p) n -> p kc n", p=128)
    for kc in range(nk):
        P.dma("gpsimd", w_bf[:, kc, :], wv[:, kc, :], writes=[(tag, kc)], chan="wld_" + tag, max_dma_last_dim=4096)

def emit_rstd(P, src, nk, n, sqb, ones, ps_ss, sd, rstd, src_res, tag=""):
    P.act(sqb[:, 0:nk, 0:n], src[:, 0:nk, 0:n], AF.Square, reads=src_res, writes=["sqb"])
    for kc in range(nk):
        P.mm(ps_ss[:, 0:n], ones[:], sqb[:, kc, 0:n], kc == 0, kc == nk - 1, reads=["sqb", "ones"], writes=["ps_ss"])
    P.act(sd[:, 0:n], ps_ss[:, 0:n], AF.Sqrt, reads=["ps_ss"], writes=["sd" + tag], scale=1.0 / (128 * nk), bias=EPS)
    P.op("vector", lambda e: e.reciprocal(out=rstd[:, 0:n], in_=sd[:, 0:n]), reads=["sd" + tag], writes=["rstd" + tag])

def build_l3b(with_ctx, N=256):
    NT = 2304 if with_ctx else 2048
    P = Prog(); P._stage_n = 2
    xT = P.dram_in("xT", [1024, NT])
    w_mod = P.dram_in("w_mod", [1024, 3072]); b_modT = P.dram_in("b_modT", [128, 24]); cT = P.dram_in("cT", [128, 16])
    g_preT = P.dram_in("g_preT", [128, 8]); g_postT = P.dram_in("g_postT", [128, 8])
    w_gate = P.dram_in("w_gate", [1024, 2816]); w_up = P.dram_in("w_up", [1024, 2816]); w_down = P.dram_in("w_down", [2816, 1024])
    xoT = P.dram_out("xoT", [1024, NT])
    wg = P.sbuf("wg", [128, 8, 2816], BF16); wu = P.sbuf("wu", [128, 8, 2816], BF16); wd = P.sbuf("wd", [128, 22, 1024], BF16)
    xs = [P.sbuf(f"xs{i}", [128, 8, N]) for i in range(2)]
    sqb = P.sbuf("sqb", [128, 8, N], BF16)
    tt_ = [P.sbuf(f"tt{i}", [128, N]) for i in range(2)]; xn = [P.sbuf(f"xn{i}", [128, 8, N], BF16) for i in range(2)]
    sd2 = P.sbuf("sd2", [128, N]); rstd2 = P.sbuf("rstd2", [128, N])
    hmid = P.sbuf("hmid", [128, 22, N], BF16)
    sg = [P.sbuf(f"sg{i}", [128, N]) for i in range(2)]
    o2 = P.sbuf("o2", [128, 8, N]); tmp = [P.sbuf(f"tmp{i}", [128, N]) for i in range(2)]
    xo = [P.sbuf(f"xo{i}", [128, N]) for i in range(2)]
    ones = P.sbuf("ones", [128, 128], BF16)
    sd = P.sbuf("sd", [128, N]); rstd = P.sbuf("rstd", [128, N])
    gp_s = P.sbuf("gp_s", [128, 8]); gq_s = P.sbuf("gq_s", [128, 8])
    Av = P.sbuf("Av", [128, 16]); Gv = P.sbuf("Gv", [128, 16])
    ps_mod = P.psum("ps_mod"); ps_ss = P.psum("ps_ss")
    psg = [P.psum(f"psg{i}") for i in range(2)]; psu = [P.psum(f"psu{i}") for i in range(2)]; pso = [P.psum(f"pso{i}") for i in range(2)]
    P.memset(ones[:], 1.0, writes=["ones"])
    P.dma("sync", gp_s[:], g_preT, writes=["gp_s"], chan="ld0")
    P.dma("sync", gq_s[:], g_postT, writes=["gq_s"], chan="ld0")
    modv = emit_mod(P, w_mod, b_modT, cT, 24, ps_mod)
    for j in range(2):
        P.stt(Av[:, j * 8:(j + 1) * 8], modv[:, j * 24 + 8: j * 24 + 16], 1.0, gp_s[:], ALU.add, ALU.mult,
              reads=["mmodv", "gp_s"], writes=["Av"])
        P.tt(Gv[:, j * 8:(j + 1) * 8], modv[:, j * 24 + 16: j * 24 + 24], gq_s[:], ALU.mult,
             reads=["mmodv", "gq_s"], writes=["Gv"])
    load_cast(P, w_gate, wg, 8, 2816, "wg")
    load_cast(P, w_up, wu, 8, 2816, "wu")
    xv = xT.rearrange("(kc p) t -> p kc t", p=128); xov = xoT.rearrange("(kc p) t -> p kc t", p=128)
    slabs = list(range(0, NT, N)); n = N
    cnt = dict(ng=0, no=0, nt=0)
    def pre(si):
        t0 = slabs[si]; b = si % 2; j = 0 if t0 < 2048 else 1
        P.dma("sync", xs[b][:], xv[:, :, t0:t0 + n], writes=[("xs", b)], chan=f"xs{b}")
        emit_rstd(P, xs[b], 8, n, sqb, ones, ps_ss, sd, rstd, [("xs", b)])
        for kc in range(8):
            P.tt(tt_[kc % 2][:], xs[b][:, kc, :], rstd[:], ALU.mult, reads=[("xs", b), "rstd"], writes=[("tt", kc % 2)])
            P.act(xn[b][:, kc, :], tt_[kc % 2][:], AF.Identity, reads=[("tt", kc % 2), "Av", "mmodv"], writes=[("xn", b, kc)],
                  scale=Av[:, j * 8 + kc: j * 8 + kc + 1], bias=modv[:, j * 24 + kc: j * 24 + kc + 1])
    def gu(si):
        b = si % 2
        for jj in range(22):
            pb = cnt["ng"] % 2; cnt["ng"] += 1
            for kc in range(8):
                P.mm(psg[pb][:, 0:n], wg[:, kc, jj * 128:(jj + 1) * 128], xn[b][:, kc, :], kc == 0, kc == 7,
                     reads=[("wg", kc), ("xn", b, kc)], writes=[("psg", pb)])
            for kc in range(8):
                P.mm(psu[pb][:, 0:n], wu[:, kc, jj * 128:(jj + 1) * 128], xn[b][:, kc, :], kc == 0, kc == 7,
                     reads=[("wu", kc), ("xn", b, kc)], writes=[("psu", pb)])
            P.act(sg[pb][:], psg[pb][:, 0:n], AF.Silu, reads=[("psg", pb)], writes=[("sg", pb)])
            P.tt(hmid[:, jj, :], sg[pb][:], psu[pb][:, 0:n], ALU.mult, reads=[("sg", pb), ("psu", pb)], writes=[("hmid", jj)])
    def dn(si):
        for m in range(8):
            pb = cnt["no"] % 2; cnt["no"] += 1
            for jj in range(22):
                P.mm(pso[pb][:, 0:n], wd[:, jj, m * 128:(m + 1) * 128], hmid[:, jj, :], jj == 0, jj == 21,
                     reads=[("wd", jj), ("hmid", jj)], writes=[("pso", pb)])
            P.copy(o2[:, m, :], pso[pb][:, 0:n], reads=[("pso", pb)], writes=[("o2", m)], eng="scalar")
    def post(si):
        t0 = slabs[si]; b = si % 2; j = 0 if t0 < 2048 else 1
        emit_rstd(P, o2, 8, n, sqb, ones, ps_ss, sd2, rstd2, [("o2", m) for m in range(8)], tag="2")
        for m in range(8):
            P.stt(o2[:, m, :], o2[:, m, :], Gv[:, j * 8 + m: j * 8 + m + 1], rstd2[:], ALU.mult, ALU.mult,
                  reads=[("o2", m), "Gv", "rstd2"], writes=[("o2", m)])
            P.tt(o2[:, m, :], xs[b][:, m, :], o2[:, m, :], ALU.add, reads=[("xs", b), ("o2", m)], writes=[("o2", m)], eng="gpsimd")
        P.dma("gpsimd", xov[:, :, t0:t0 + n], o2[:], reads=[("o2", m) for m in range(8)], chan="st")
    pre(0)
    for si in range(len(slabs)):
        gu(si)
        if si == 0:
            load_cast(P, w_down, wd, 22, 1024, "wd")
        if si + 1 < len(slabs):
            pre(si + 1)
        dn(si)
        post(si)
    return P.finish()

def build_l3a(with_ctx, N=512):
    NT = 2304 if with_ctx else 2048
    P = Prog()
    xT = P.dram_in("xT", [1024, NT]); ysT = P.dram_in("ysT", [384, NT]); mxT = P.dram_in("mxT", [256, NT]); naT = P.dram_in("naT", [384, NT])
    w_mod = P.dram_in("w_mod", [1024, 1024]); b_modT = P.dram_in("b_modT", [128, 8]); cT = P.dram_in("cT", [128, 16])
    g_postT = P.dram_in("g_postT", [128, 8])
    w_glu = P.dram_in("w_glu", [384, 384]); w_fourier = P.dram_in("w_fourier", [256, 256]); w_out = P.dram_in("w_out", [1024, 1024])
    xoT = P.dram_out("xoT", [1024, NT])
    wglu = P.sbuf("wglu", [128, 3, 384], BF16); wf = P.sbuf("wf", [128, 2, 256], BF16); wo = P.sbuf("wo", [128, 8, 1024], BF16)
    xs = [P.sbuf(f"xs{i}", [128, 8, N]) for i in range(2)]
    ys = [P.sbuf(f"ys{i}", [128, 3, N]) for i in range(2)]
    mx = [P.sbuf(f"mx{i}", [128, 2, N]) for i in range(2)]
    na = [P.sbuf(f"na{i}", [128, 3, N]) for i in range(2)]
    sq = P.sbuf("sq", [128, 3, N]); t1 = P.sbuf("t1", [128, 3, N]); sgm = P.sbuf("sgm", [128, 3, N])
    zf = P.sbuf("zf", [128, 3, N]); zb = P.sbuf("zb", [128, 3, N], BF16); mxb = P.sbuf("mxb", [128, 2, N], BF16)
    sg2 = [P.sbuf(f"sg2{i}", [128, N]) for i in range(2)]
    cat = P.sbuf("cat", [128, 8, N], BF16)
    sqb = P.sbuf("sqb", [128, 8, N], BF16)
    o2 = P.sbuf("o2", [128, 8, N]); tmp = [P.sbuf(f"tmp{i}", [128, N]) for i in range(2)]
    xo = [P.sbuf(f"xo{i}", [128, N]) for i in range(2)]
    ones = P.sbuf("ones", [128, 128], BF16)
    sd = P.sbuf("sd", [128, N]); rstd = P.sbuf("rstd", [128, N])
    gq_s = P.sbuf("gq_s", [128, 8]); Gv = P.sbuf("Gv", [128, 16])
    ps_mod = P.psum("ps_mod"); ps_ss = P.psum("ps_ss")
    psa = [P.psum(f"psa{i}") for i in range(3)]; pso = [P.psum(f"pso{i}") for i in range(3)]
    P.memset(ones[:], 1.0, writes=["ones"])
    P.dma("sync", gq_s[:], g_postT, writes=["gq_s"], chan="ld0")
    modv = emit_mod(P, w_mod, b_modT, cT, 8, ps_mod)
    for j in range(2):
        P.tt(Gv[:, j * 8:(j + 1) * 8], modv[:, j * 8:(j + 1) * 8], gq_s[:], ALU.mult, reads=["mmodv", "gq_s"], writes=["Gv"])
    load_cast(P, w_glu, wglu, 3, 384, "wglu")
    load_cast(P, w_fourier, wf, 2, 256, "wf")
    load_cast(P, w_out, wo, 8, 1024, "wo")
    xv = xT.rearrange("(kc p) t -> p kc t", p=128); xov = xoT.rearrange("(kc p) t -> p kc t", p=128)
    ysv = ysT.rearrange("(kc p) t -> p kc t", p=128); mxv = mxT.rearrange("(kc p) t -> p kc t", p=128); nav = naT.rearrange("(kc p) t -> p kc t", p=128)
    na_ = 0; no = 0; nt = 0
    for si, t0 in enumerate(range(0, NT, N)):
        n = min(N, NT - t0); b = si % 2; j = 0 if t0 < 2048 else 1
        P.dma("sync", xs[b][:, :, 0:n], xv[:, :, t0:t0 + n], writes=[("xs", b)], chan=f"xs{b}")
        P.dma("sync", ys[b][:, :, 0:n], ysv[:, :, t0:t0 + n], writes=[("ys", b)], chan=f"ys{b}")
        P.dma("sync", mx[b][:, :, 0:n], mxv[:, :, t0:t0 + n], writes=[("mx", b)], chan=f"mx{b}")
        P.dma("sync", na[b][:, :, 0:n], nav[:, :, t0:t0 + n], writes=[("na", b)], chan=f"na{b}")
        Y = ys[b][:, :, 0:n]
        P.tt(sq[:, :, 0:n], Y, Y, ALU.mult, reads=[("ys", b)], writes=["sq"], eng="gpsimd")
        P.ts(t1[:, :, 0:n], sq[:, :, 0:n], 0.044715, 1.0, ALU.mult, ALU.add, reads=["sq"], writes=["t1"])
        P.tt(sq[:, :, 0:n], t1[:, :, 0:n], Y, ALU.mult, reads=["t1", ("ys", b)], writes=["sq"])
        P.act(sgm[:, :, 0:n], sq[:, :, 0:n], AF.Sigmoid, reads=["sq"], writes=["sgm"], scale=1.5957691216057308)
        P.tt(zf[:, :, 0:n], Y, sgm[:, :, 0:n], ALU.mult, reads=[("ys", b), "sgm"], writes=["zf"])
        P.copy(zb[:, :, 0:n], zf[:, :, 0:n], reads=["zf"], writes=["zb"], eng="gpsimd")
        for m in range(3):
            pb = na_ % 3; na_ += 1
            for kc in range(3):
                P.mm(psa[pb][:, 0:n], wglu[:, kc, m * 128:(m + 1) * 128], zb[:, kc, 0:n], kc == 0, kc == 2,
                     reads=[("wglu", kc), "zb"], writes=[("psa", pb)])
            P.act(sg2[m % 2][:, 0:n], psa[pb][:, 0:n], AF.Sigmoid, reads=[("psa", pb)], writes=[("sg2", m % 2)])
            P.tt(cat[:, m, 0:n], zf[:, m, 0:n], sg2[m % 2][:, 0:n], ALU.mult, reads=["zf", ("sg2", m % 2)], writes=[("cat", m)])
        P.copy(mxb[:, :, 0:n], mx[b][:, :, 0:n], reads=[("mx", b)], writes=["mxb"], eng="gpsimd")
        for m in range(2):
            pb = na_ % 3; na_ += 1
            for kc in range(2):
                P.mm(psa[pb][:, 0:n], wf[:, kc, m * 128:(m + 1) * 128], mxb[:, kc, 0:n], kc == 0, kc == 1,
                     reads=[("wf", kc), "mxb"], writes=[("psa", pb)])
            P.copy(cat[:, 3 + m, 0:n], psa[pb][:, 0:n], reads=[("psa", pb)], writes=[("cat", 3 + m)], eng="scalar")
        P.copy(cat[:, 5:8, 0:n], na[b][:, :, 0:n], reads=[("na", b)], writes=[("cat", 5), ("cat", 6), ("cat", 7)], eng="gpsimd")
        for m in range(8):
            pb = no % 3; no += 1
            for kc in range(8):
                P.mm(pso[pb][:, 0:n], wo[:, kc, m * 128:(m + 1) * 128], cat[:, kc, 0:n], kc == 0, kc == 7,
                     reads=[("wo", kc), ("cat", kc)], writes=[("pso", pb)])
            P.copy(o2[:, m, 0:n], pso[pb][:, 0:n], reads=[("pso", pb)], writes=[("o2", m)], eng="scalar")
        emit_rstd(P, o2, 8, n, sqb, ones, ps_ss, sd, rstd, [("o2", m) for m in range(8)])
        for m in range(8):
            P.stt(o2[:, m, 0:n], o2[:, m, 0:n], Gv[:, j * 8 + m: j * 8 + m + 1], rstd[:, 0:n], ALU.mult, ALU.mult,
                  reads=[("o2", m), "Gv", "rstd"], writes=[("o2", m)])
            P.tt(o2[:, m, 0:n], xs[b][:, m, 0:n], o2[:, m, 0:n], ALU.add, reads=[("xs", b), ("o2", m)], writes=[("o2", m)], eng="gpsimd")
        P.dma("gpsimd", xov[:, :, t0:t0 + n], o2[:, :, 0:n], reads=[("o2", m) for m in range(8)], chan="st")
    return P.finish()

EPS = 1e-6
NT = 2304
SLABS = [(0, 512), (512, 512), (1024, 512), (1536, 512), (2048, 256)]

def build_l1():
    P = Prog(); P._stage_n = 4
    xT = P.dram_in("xT", [1024, NT])
    w_in = P.dram_in("w_in", [1024, 1792])
    w_mod = P.dram_in("w_mod", [1024, 2048])
    b_modT = P.dram_in("b_modT", [128, 16])
    g_preT = P.dram_in("g_preT", [128, 8])
    cT = P.dram_in("cT", [128, 16])
    cos = P.dram_in("cos", [128, 2048]); sin = P.dram_in("sin", [128, 2048])
    perm = P.dram_in("perm", [128, 128])
    hT = P.dram_out("hT", [640, NT])
    hTb = P.dram_out("hTb", [1152, NT])

    xs = [P.sbuf(f"xs{i}", [128, 8, 512]) for i in range(2)]
    sqb = P.sbuf("sqb", [128, 8, 512], BF16)
    tt_ = [P.sbuf(f"tt{i}", [128, 512]) for i in range(2)]
    xn = [P.sbuf(f"xn{i}", [128, 8, 512], BF16) for i in range(2)]
    w_bf = P.sbuf("w_bf", [128, 8, 1792], BF16)
    ones = P.sbuf("ones", [128, 128], BF16)
    sd = P.sbuf("sd", [128, 512]); rstd = P.sbuf("rstd", [128, 512])
    ho = [P.sbuf(f"ho{i}", [128, 512]) for i in range(4)]
    hob = [P.sbuf(f"hob{i}", [128, 512]) for i in range(4)]
    r1 = [P.sbuf(f"r1{i}", [128, 512]) for i in range(2)]
    r2 = [P.sbuf(f"r2{i}", [128, 512]) for i in range(2)]
    cos_s = P.sbuf("cos_s", [128, 2048]); sin_s = P.sbuf("sin_s", [128, 2048])
    perm_s = P.sbuf("perm_s", [128, 128])
    gp_s = P.sbuf("gp_s", [128, 8]); Av = P.sbuf("Av", [128, 16])
    ps_mod = P.psum("ps_mod"); ps_ss = P.psum("ps_ss")
    psm = [P.psum(f"psm{i}") for i in range(4)]
    psr = [P.psum(f"psr{i}") for i in range(2)]

    P.dma("sync", gp_s[:], g_preT, writes=["gp_s"], chan="ld0")
    P.dma("sync", cos_s[:], cos, writes=["cos_s"], chan="ld0")
    P.dma("sync", sin_s[:], sin, writes=["sin_s"], chan="ld0")
    P.dma("sync", perm_s[:], perm, writes=["perm_s"], chan="ld0")
    P.memset(ones[:], 1.0, writes=["ones"])
    modv = emit_mod(P, w_mod, b_modT, cT, 16, ps_mod)
    for j in range(2):
        P.stt(Av[:, j * 8:(j + 1) * 8], modv[:, j * 16 + 8: j * 16 + 16], 1.0, gp_s[:], ALU.add, ALU.mult,
              reads=["mmodv", "gp_s"], writes=["Av"])
    load_cast(P, w_in, w_bf, 8, 1792, "w_bf", piece=1024)
    xv = xT.rearrange("(kc p) t -> p kc t", p=128)
    cnt = dict(nho=0, nr=0, nps=0)
    def pre(si):
        t0, n = SLABS[si]; b = si % 2; j = 0 if si < 4 else 1
        P.dma("sync", xs[b][:, :, 0:n], xv[:, :, t0:t0 + n], writes=[("xs", b)], chan=f"xs{b}")
        P.act(sqb[:, :, 0:n], xs[b][:, :, 0:n], AF.Square, reads=[("xs", b)], writes=["sqb"])
        for kc in range(8):
            P.mm(ps_ss[:, 0:n], ones[:], sqb[:, kc, 0:n], kc == 0, kc == 7, reads=["sqb", "ones"], writes=["ps_ss"])
        P.act(sd[:, 0:n], ps_ss[:, 0:n], AF.Sqrt, reads=["ps_ss"], writes=["sd"], scale=1.0 / 1024, bias=EPS)
        P.op("vector", lambda e: e.reciprocal(out=rstd[:, 0:n], in_=sd[:, 0:n]), reads=["sd"], writes=["rstd"])
        for kc in range(8):
            P.tt(tt_[kc % 2][:, 0:n], xs[b][:, kc, 0:n], rstd[:, 0:n], ALU.mult, reads=[("xs", b), "rstd"], writes=[("tt", kc % 2)])
            P.act(xn[b][:, kc, 0:n], tt_[kc % 2][:, 0:n], AF.Identity, reads=[("tt", kc % 2), "Av", "mmodv"], writes=[("xn", b, kc)],
                  scale=Av[:, j * 8 + kc: j * 8 + kc + 1], bias=modv[:, j * 16 + kc: j * 16 + kc + 1])
    pending = []
    def flush():
        while pending:
            pending.pop(0)()
    def main(si):
        t0, n = SLABS[si]; b = si % 2
        for m in range(14):
            pb = cnt["nps"] % 4; cnt["nps"] += 1
            for kc in range(8):
                P.mm(psm[pb][:, 0:n], w_bf[:, kc, m * 128:(m + 1) * 128], xn[b][:, kc, 0:n], kc == 0, kc == 7,
                     reads=[("w_bf", kc), ("xn", b, kc)], writes=[("psm", pb)])
            flush()
            hb = cnt["nho"] % 4; cnt["nho"] += 1
            if m < 5:
                P.copy(ho[hb][:, 0:n], psm[pb][:, 0:n], reads=[("psm", pb)], writes=[("ho", hb)], eng="scalar")
                P.dma("gpsimd", hT[m * 128:(m + 1) * 128, t0:t0 + n], ho[hb][:, 0:n], reads=[("ho", hb)], chan=f"sto{hb}")
            elif m <= 10 and si < 4:
                rb = cnt["nr"] % 2; cnt["nr"] += 1
                P.copy(ho[hb][:, 0:n], psm[pb][:, 0:n], reads=[("psm", pb)], writes=[("ho", hb)], eng="scalar")
                def rope(hb=hb, rb=rb, m=m, t0=t0, n=n):
                    P.mm(psr[rb][:, 0:n], perm_s[:], ho[hb][:, 0:n], True, True, reads=["perm_s", ("ho", hb)], writes=[("psr", rb)])
                    P.tt(r1[rb][:, 0:n], ho[hb][:, 0:n], cos_s[:, t0:t0 + n], ALU.mult, reads=[("ho", hb), "cos_s"], writes=[("r1", rb)])
                    P.tt(r2[rb][:, 0:n], psr[rb][:, 0:n], sin_s[:, t0:t0 + n], ALU.mult, reads=[("psr", rb), "sin_s"], writes=[("r2", rb)])
                    P.tt(hob[hb][:, 0:n], r1[rb][:, 0:n], r2[rb][:, 0:n], ALU.add, reads=[("r1", rb), ("r2", rb)], writes=[("hob", hb)], eng="gpsimd")
                    P.dma("gpsimd", hTb[(m - 5) * 128:(m - 4) * 128, t0:t0 + n], hob[hb][:, 0:n], reads=[("hob", hb)], chan=f"stb{hb}")
                pending.append(rope)
            else:
                P.copy(hob[hb][:, 0:n], psm[pb][:, 0:n], reads=[("psm", pb)], writes=[("hob", hb)], eng="scalar")
                P.dma("gpsimd", hTb[(m - 5) * 128:(m - 4) * 128, t0:t0 + n], hob[hb][:, 0:n], reads=[("hob", hb)], chan=f"stb{hb}")
    pre(0)
    for si in range(len(SLABS)):
        if si + 1 < len(SLABS):
            pre(si + 1)
        main(si)
    flush()
    return P.finish()


def fnet_consts():
    n1 = np.arange(128); a = 2 * np.pi * np.outer(n1, n1) / 128
    cs128 = np.concatenate([np.cos(a), -np.sin(a)], 1).astype(np.float32)
    n2 = np.arange(64); a = 2 * np.pi * np.outer(n2, n2) / 64
    C64, S64 = np.cos(a), np.sin(a)
    fb1 = np.concatenate([C64, -S64], 1).astype(np.float32); fb2 = np.concatenate([S64, C64], 1).astype(np.float32)
    a = 2 * np.pi * np.outer(n2, n1) / 8192
    tw = np.concatenate([np.cos(a), np.sin(a)], 1).astype(np.float32)
    fc = (np.concatenate([C64, S64], 1) / np.sqrt(8192 * 64)).astype(np.float32)
    n = np.arange(256); a = 2 * np.pi * np.outer(n, n) / 256
    cs = np.concatenate([np.cos(a), -np.sin(a)], 1)
    cs256 = np.ascontiguousarray(cs.reshape(2, 128, 512).transpose(1, 0, 2)).astype(np.float32)
    fcc = (np.concatenate([C64, S64], 1) / np.sqrt(256 * 64)).astype(np.float32)
    return dict(cs128=cs128, fb1=fb1, fb2=fb2, tw=tw, fc=fc, cs256=cs256, fcc=fcc)

def build_fnet(with_ctx):
    P = Prog()
    z = P.dram_in("z", [128, 4096]); cs128 = P.dram_in("cs128", [128, 256])
    fb1 = P.dram_in("fb1", [64, 128]); fb2 = P.dram_in("fb2", [64, 128]); tw = P.dram_in("tw", [64, 256]); fc = P.dram_in("fc", [64, 128])
    R = P.dram_out("R", [64, 8192])
    zs = P.sbuf("zs", [128, 64, 64]); cs_s = P.sbuf("cs_s", [128, 256])
    fb1_s = P.sbuf("fb1_s", [64, 128], BF16); fb2_s = P.sbuf("fb2_s", [64, 128], BF16); tw_s = P.sbuf("tw_s", [64, 256]); fc_s = P.sbuf("fc_s", [64, 128], BF16)
    Ych = [P.sbuf(f"Ych{i}", [64, 8, 256]) for i in range(2)]
    ta = P.sbuf("ta", [64, 8, 128]); tb = P.sbuf("tb", [64, 8, 128]); tc = P.sbuf("tc", [64, 8, 128]); td = P.sbuf("td", [64, 8, 128])
    Yp = P.sbuf("Yp", [64, 64, 256], BF16); X1 = P.sbuf("X1", [64, 2, 8192], BF16)
    Rs = [P.sbuf(f"Rs{i}", [64, 512]) for i in range(2)]
    psA = [P.psum(f"psA{i}") for i in range(3)]; psB = [P.psum(f"psB{i}") for i in range(3)]; psC = [P.psum(f"psC{i}") for i in range(2)]
    P.dma("sync", zs[:].rearrange("p a b -> p (a b)"), z, writes=["zs"], chan="ld0")
    for (s, d, nm) in ((cs_s, cs128, "cs_s"), (tw_s, tw, "tw_s")):
        P.dma("sync", s[:], d, writes=[nm], chan="ld1")
    for (s, d, nm) in ((fb1_s, fb1, "fb1_s"), (fb2_s, fb2, "fb2_s"), (fc_s, fc, "fc_s")):
        P.dma("gpsimd", s[:], d, writes=[nm], chan="ldc")
    twr = tw_s[:, 0:128].rearrange("p (o k) -> p o k", o=1).to_broadcast([64, 8, 128])
    tws = tw_s[:, 128:256].rearrange("p (o k) -> p o k", o=1).to_broadcast([64, 8, 128])
    na_ = 0
    for ch in range(8):
        cb = ch % 2
        for pair in range(4):
            pb = na_ % 3; na_ += 1
            for h in range(2):
                c = ch * 8 + pair * 2 + h
                P.mm(psA[pb][0:64, h * 256:(h + 1) * 256], zs[:, :, c], cs_s[:], True, True, reads=["zs", "cs_s"], writes=[("psA", pb)])
            P.copy(Ych[cb][:, pair * 2:pair * 2 + 2, :].rearrange("p a b -> p (a b)"), psA[pb][0:64, :], reads=[("psA", pb)],
                   writes=[("Ych", cb)], eng="scalar")
        Yr = Ych[cb][:, :, 0:128]; Yi = Ych[cb][:, :, 128:256]; c0 = ch * 8
        P.tt(ta[:], Yr, twr, ALU.mult, reads=[("Ych", cb), "tw_s"], writes=["ta"])
        P.tt(tb[:], Yi, tws, ALU.mult, reads=[("Ych", cb), "tw_s"], writes=["tb"], eng="gpsimd")
        P.tt(Yp[:, c0:c0 + 8, 0:128], ta[:], tb[:], ALU.add, reads=["ta", "tb"], writes=[("Yp", ch)])
        P.tt(tc[:], Yi, twr, ALU.mult, reads=[("Ych", cb), "tw_s"], writes=["tc"], eng="gpsimd")
        P.tt(td[:], Yr, tws, ALU.mult, reads=[("Ych", cb), "tw_s"], writes=["td"])
        P.tt(Yp[:, c0:c0 + 8, 128:256], tc[:], td[:], ALU.subtract, reads=["tc", "td"], writes=[("Yp", ch)], eng="gpsimd")
    allYp = [("Yp", ch) for ch in range(8)]
    X1v = X1[:].rearrange("p c (k2 k1) -> p c k2 k1", k1=128)
    for g in range(32):
        pb = g % 3
        for q in range(4):
            k1 = 4 * g + q
            P.mm(psB[pb][0:64, q * 128:(q + 1) * 128], Yp[:, :, k1], fb1_s[:], True, False, reads=allYp + ["fb1_s"], writes=[("psB", pb)])
            P.mm(psB[pb][0:64, q * 128:(q + 1) * 128], Yp[:, :, 128 + k1], fb2_s[:], False, True, reads=allYp + ["fb2_s"], writes=[("psB", pb)])
        pv = psB[pb][0:64, :].rearrange("p (q c k) -> p c k q", q=4, c=2)
        for comp in range(2):
            P.copy(X1v[:, comp, :, 4 * g:4 * g + 4], pv[:, comp, :, :], reads=[("psB", pb)], writes=[("X1", g)],
                   eng=("scalar" if comp == 0 else "vector"))
    allX1 = [("X1", g) for g in range(32)]
    for blk in range(16):
        pb = blk % 2
        P.mm(psC[pb][0:64, :], fc_s[:, 0:64], X1[:, 0, blk * 512:(blk + 1) * 512], True, False, reads=allX1 + ["fc_s"], writes=[("psC", pb)])
        P.mm(psC[pb][0:64, :], fc_s[:, 64:128], X1[:, 1, blk * 512:(blk + 1) * 512], False, True, reads=allX1 + ["fc_s"], writes=[("psC", pb)])
        P.copy(Rs[pb][:], psC[pb][0:64, :], reads=[("psC", pb)], writes=[("Rs", pb)], eng="scalar")
        P.dma("gpsimd", R[:, blk * 512:(blk + 1) * 512], Rs[pb][:], reads=[("Rs", pb)], chan=f"st{pb}")
    if with_ctx:
        zc = P.dram_in("zc", [128, 2, 64]); cs256 = P.dram_in("cs256", [128, 2, 512]); fcc = P.dram_in("fcc", [64, 128])
        Rc = P.dram_out("Rc", [64, 256])
        zc_s = P.sbuf("zc_s", [128, 2, 64]); c2_s = P.sbuf("c2_s", [128, 2, 512]); fcc_s = P.sbuf("fcc_s", [64, 128])
        Pc = P.sbuf("Pc", [64, 512]); Rc_s = P.sbuf("Rc_s", [64, 256])
        P.dma("sync", zc_s[:], zc, writes=["zc_s"], chan="ld2"); P.dma("sync", c2_s[:], cs256, writes=["c2_s"], chan="ld2")
        P.dma("sync", fcc_s[:], fcc, writes=["fcc_s"], chan="ld2")
        for t in range(2):
            P.mm(psA[0][0:64, :], zc_s[:, t, :], c2_s[:, t, :], t == 0, t == 1, reads=["zc_s", "c2_s"], writes=[("psA", 0)])
        P.copy(Pc[:], psA[0][0:64, :], reads=[("psA", 0)], writes=["Pc"])
        P.mm(psA[1][0:64, 0:256], fcc_s[:, 0:64], Pc[:, 0:256], True, False, reads=["Pc", "fcc_s"], writes=[("psA", 1)])
        P.mm(psA[1][0:64, 0:256], fcc_s[:, 64:128], Pc[:, 256:512], False, True, reads=["Pc", "fcc_s"], writes=[("psA", 1)])
        P.copy(Rc_s[:], psA[1][0:64, 0:256], reads=[("psA", 1)], writes=["Rc_s"])
        P.dma("gpsimd", Rc, Rc_s[:], reads=["Rc_s"], chan="st")
    return P.finish()

NEG = -30000.0

def build_na(with_ctx):
    P = Prog()
    qT = P.dram_in("qT", [384, 2048]); kwT = P.dram_in("kwT", [16, 384, 576]); vw = P.dram_in("vw", [16, 128, 5, 384])
    kcT = P.dram_in("kcT", [384, 256]); vc = P.dram_in("vc", [128, 2, 384])
    tbraw = P.dram_in("tbraw", [5, 128, 6, 576]); mask = P.dram_in("mask", [5, 128, 576]); ident = P.dram_in("ident", [128, 128])
    Y = P.dram_out("Y", [2048, 384])
    qb = P.sbuf("qb", [128, 3, 2048], BF16); kcb = P.sbuf("kcb", [128, 3, 256], BF16); vcb = P.sbuf("vcb", [128, 2, 384], BF16)
    TB = P.sbuf("TB", [128, 5, 6, 576]); mk = P.sbuf("mk", [128, 5, 576])
    idb = P.sbuf("idb", [128, 128], BF16)
    kb = [P.sbuf(f"kb{i}", [128, 3, 576], BF16) for i in range(2)]; vb = [P.sbuf(f"vb{i}", [128, 5, 384], BF16) for i in range(2)]
    S = [P.sbuf(f"S{i}", [128, 832]) for i in range(4)]; Pb = [P.sbuf(f"Pb{i}", [128, 832], BF16) for i in range(4)]
    PT = [P.sbuf(f"PT{i}", [128, 896], BF16) for i in range(4)]
    Osb = [P.sbuf(f"Osb{i}", [128, 384]) for i in range(2)]
    mx = [P.sbuf(f"mx{i}", [128, 1]) for i in range(8)]; ssum = [P.sbuf(f"ssum{i}", [128, 1]) for i in range(8)]
    rinv = [P.sbuf(f"rinv{i}", [128, 1]) for i in range(8)]
    psA = [P.psum(f"psA{i}") for i in range(2)]; psB = [P.psum(f"psB{i}") for i in range(2)]
    psT = [P.psum(f"psT{i}", [128, 1024], BF16) for i in range(2)]; psO = [P.psum(f"psO{i}") for i in range(2)]
    si = [0]
    def load_cast(dst, src_ap, n, dres):
        P.dma("gpsimd", dst, src_ap, writes=[dres], chan="ldc", max_dma_last_dim=4096)
    qv = qT.rearrange("(c p) n -> p c n", p=128)
    for c in range(3):
        load_cast(qb[:, c, :], qv[:, c, :], 2048, "qb")
    kcv = kcT.rearrange("(c p) n -> p c n", p=128)
    for c in range(3):
        load_cast(kcb[:, c, :], kcv[:, c, :], 256, "kcb")
    load_cast(vcb[:].rearrange("p a b -> p (a b)"), vc.rearrange("p a b -> p (a b)"), 768, "vcb")
    load_cast(idb[:], ident, 128, "idb")
    for ty in range(5):
        P.dma("sync", TB[:, ty, :, :], tbraw[ty], writes=[("TB", ty)], chan="ld0")
    P.dma("sync", mk[:], mask.rearrange("t p n -> p t n"), writes=["mk"], chan="ld0")
    for ty in range(5):
        mb = mk[:, ty, :].rearrange("p (o n) -> p o n", o=1).to_broadcast([128, 6, 576])
        P.tt(TB[:, ty, :, :], TB[:, ty, :, :], mb, ALU.add, reads=[("TB", ty), "mk"], writes=[("TB", ty)])
    tiles = [("main", t) for t in range(16)]
    if with_ctx:
        qcT = P.dram_in("qcT", [384, 256]); Yc = P.dram_out("Yc", [256, 384])
        qcb = P.sbuf("qcb", [128, 3, 256], BF16)
        qcv = qcT.rearrange("(c p) n -> p c n", p=128)
        for c in range(3):
            load_cast(qcb[:, c, :], qcv[:, c, :], 256, "qcb")
        tiles += [("ctx", 0), ("ctx", 1)]
    units = [(ti, kind, t, h) for ti, (kind, t) in enumerate(tiles) for h in range(6)]
    loaded = set()
    def tile_load(ti, kind, t):
        if ti in loaded or kind != "main":
            return
        loaded.add(ti)
        wb = ti % 2
        P.dma("gpsimd", kb[wb][:], kwT[t].rearrange("(c p) n -> p c n", p=128), writes=[("kb", wb)], chan=f"kb{wb}", max_dma_last_dim=4096)
        P.dma("gpsimd", vb[wb][:], vw[t], writes=[("vb", wb)], chan=f"vb{wb}", max_dma_last_dim=4096)
    def info(u):
        ti, kind, t, h = units[u]
        return ti, kind, t, h, u % 2, u % 4, ti % 2, h // 2, (h % 2) * 64, u % 8
    def stA(u):
        ti, kind, t, h, b, b3, wb, c, p0, b8 = info(u)
        tile_load(ti, kind, t)
        if kind == "main":
            qs = qb[p0:p0 + 64, c, t * 128:(t + 1) * 128]; qres = "qb"
        else:
            qs = qcb[p0:p0 + 64, c, t * 128:(t + 1) * 128]; qres = "qcb"
        P.mm(psB[b][:, 64:320], qs, kcb[p0:p0 + 64, c, :], True, True, reads=[qres, "kcb"], writes=[("psB", b)])
        if kind == "main":
            P.mm(psA[b][:, 0:512], qs, kb[wb][p0:p0 + 64, c, 0:512], True, True, reads=[qres, ("kb", wb)], writes=[("psA", b)])
            P.mm(psB[b][:, 0:64], qs, kb[wb][p0:p0 + 64, c, 512:576], True, True, reads=[qres, ("kb", wb)], writes=[("psB", b)])
        P.act(S[b3][:, 0:256], psB[b][:, 64:320], AF.Copy, reads=[("psB", b)], writes=[("Sc", b3)], scale=0.125)
    def stB(u):
        ti, kind, t, h, b, b3, wb, c, p0, b8 = info(u)
        W = 832 if kind == "main" else 256
        if kind == "main":
            ty = {0: 0, 1: 1, 14: 3, 15: 4}.get(t, 2)
            P.stt(S[b3][:, 256:768], psA[b][:, 0:512], 0.125, TB[:, ty, h, 0:512], ALU.mult, ALU.add,
                  reads=[("psA", b), ("TB", ty)], writes=[("Sw", b3)])
            P.stt(S[b3][:, 768:832], psB[b][:, 0:64], 0.125, TB[:, ty, h, 512:576], ALU.mult, ALU.add,
                  reads=[("psB", b), ("TB", ty)], writes=[("Sw2", b3)])
        P.op("vector", lambda e: e.tensor_reduce(out=mx[b8][:], in_=S[b3][:, 0:W], axis=AX.X, op=ALU.max, negate=True),
             reads=[("Sc", b3), ("Sw", b3), ("Sw2", b3)], writes=[("mx", b8)])
        P.act(Pb[b3][:, 0:W], S[b3][:, 0:W], AF.Exp, reads=[("Sc", b3), ("Sw", b3), ("Sw2", b3), ("mx", b8)], writes=[("Pb", b3), ("ssum", b8)],
              bias=mx[b8][:], scale=1.0, accum_out=ssum[b8][:])
    def stC(u):
        ti, kind, t, h, b, b3, wb, c, p0, b8 = info(u)
        nblk = 7 if kind == "main" else 2
        for kbk in range(nblk):
            kw = 64 if kbk == 6 else 128
            P.op("tensor", lambda e, kbk=kbk, kw=kw: e.transpose(psT[b][0:kw, kbk * 128:(kbk + 1) * 128], Pb[b3][:, kbk * 128:kbk * 128 + kw], idb[:]),
                 reads=[("Pb", b3), "idb"], writes=[("psT", b)], nosync_same=True)
        P.copy(PT[b3][:, 0:nblk * 128], psT[b][:, 0:nblk * 128], reads=[("psT", b)], writes=[("PT", b3)], eng="scalar")
    def stD(u):
        ti, kind, t, h, b, b3, wb, c, p0, b8 = info(u)
        ob = ti % 2
        nblk = 7 if kind == "main" else 2
        for kbk in range(nblk):
            kw = 64 if kbk == 6 else 128
            if kbk < 2:
                rhs = vcb[:, kbk, h * 64:(h + 1) * 64]; rres = "vcb"
            else:
                rhs = vb[wb][0:kw, kbk - 2, h * 64:(h + 1) * 64]; rres = ("vb", wb)
            P.mm(psO[ob][:, h * 64:(h + 1) * 64], PT[b3][0:kw, kbk * 128:(kbk + 1) * 128], rhs, kbk == 0, kbk == nblk - 1,
                 reads=[("PT", b3), rres], writes=[("psO", ob)])
        P.op("vector", lambda e: e.reciprocal(out=rinv[b8][:], in_=ssum[b8][:]), reads=[("ssum", b8)], writes=[("rinv", b8)])
        P.ts(Osb[ob][:, h * 64:(h + 1) * 64], psO[ob][:, h * 64:(h + 1) * 64], rinv[b8][:], None, ALU.mult,
             reads=[("psO", ob), ("rinv", b8)], writes=[("Osb", ob)])
        if h == 5:
            dst = Y[t * 128:(t + 1) * 128, :] if kind == "main" else Yc[t * 128:(t + 1) * 128, :]
            P.dma("gpsimd", dst, Osb[ob][:], reads=[("Osb", ob)], chan=f"st{ob}")
    NU = len(units)
    LC, LD = 3, 5
    for step in range(NU + LD):
        if step < NU: stA(step)
        if 0 <= step - 1 < NU: stB(step - 1)
        if 0 <= step - LC < NU: stC(step - LC)
        if 0 <= step - LD < NU: stD(step - LD)
    return P.finish()

def na_tile_geometry(r0):
    R = 128
    rs0 = int(np.clip(r0 - 4, 0, R - 8)); rs1 = int(np.clip(r0 + 1 - 4, 0, R - 8))
    return rs0, rs1

def na_tables(rpb, q):
    cols = np.arange(64); cs = np.clip(cols - 8, 0, 48)
    kc = np.arange(64)
    inwin = (kc[None, :] >= cs[:, None]) & (kc[None, :] < cs[:, None] + 16)
    dc = np.clip(kc[None, :] - cols[:, None] + 15, 0, 30)
    types = [32 * q, 32 * q + 2, 32 * q + 16, 32 * q + 28, 32 * q + 30]
    tbraw = np.zeros((5, 128, 6, 9, 64), np.float32); mask = np.zeros((5, 128, 9, 64), np.float32)
    for ti, r0 in enumerate(types):
        rs0, rs1 = na_tile_geometry(r0)
        for half, (r, rs) in enumerate(((r0, rs0), (r0 + 1, rs1))):
            for slot in range(9):
                krow = rs0 + slot
                valid = (krow >= rs) and (krow < rs + 8)
                dr = int(np.clip(krow - r + 7, 0, 14))
                g = rpb[:, dr][:, dc]
                tbraw[ti, half * 64:(half + 1) * 64, :, slot, :] = g.transpose(1, 0, 2)
                m = np.where(inwin & valid, 0.0, NEG).astype(np.float32)
                mask[ti, half * 64:(half + 1) * 64, slot, :] = m
    return tbraw.reshape(5, 128, 6, 576), mask.reshape(5, 128, 576)

def na_windows(k_b, v_b, q):
    kp = np.concatenate([k_b, np.zeros((64 * 16, 384), k_b.dtype)], 0); vp = np.concatenate([v_b, np.zeros((64 * 16, 384), v_b.dtype)], 0)
    kwT = np.zeros((16, 384, 576), k_b.dtype); vw = np.zeros((16, 128, 5, 384), v_b.dtype)
    for t in range(16):
        r0 = 32 * q + 2 * t
        rs0, _ = na_tile_geometry(r0)
        kwT[t] = kp[rs0 * 64:(rs0 + 9) * 64].T
        for j in range(4):
            vw[t, :, j, :] = vp[(rs0 + 2 * j) * 64:(rs0 + 2 * j + 2) * 64]
        vw[t, 0:64, 4, :] = vp[(rs0 + 8) * 64:(rs0 + 9) * 64]
    return kwT, vw

NCH = 1056
PI = math.pi

class A:
    def __init__(self, P): self.P = P
    @staticmethod
    def nm(*aps): return [a.tensor.name for a in aps if hasattr(a, "tensor")]
    def tt(self, o, a, b, op, eng="vector"): self.P.tt(o, a, b, op, reads=self.nm(a, b), writes=self.nm(o), eng=eng)
    def ts(self, o, a, s1, op0, s2=None, op1=None, eng="vector"):
        self.P.ts(o, a, s1, s2, op0, op1, reads=self.nm(a, s1, s2), writes=self.nm(o), eng=eng)
    def stt(self, o, a, s, b, op0, op1): self.P.stt(o, a, s, b, op0, op1, reads=self.nm(a, s, b), writes=self.nm(o))
    def act(self, o, a, f, **kw): self.P.act(o, a, f, reads=self.nm(a, *[v for v in kw.values()]), writes=self.nm(o), **kw)
    def copy(self, o, a, eng="vector"): self.P.copy(o, a, reads=self.nm(a), writes=self.nm(o), eng=eng)
    def memset(self, o, v, eng="vector"): self.P.memset(o, v, writes=self.nm(o), eng=eng)
    def mm(self, o, l, r, st, sp): self.P.mm(o, l, r, st, sp, reads=self.nm(l, r), writes=self.nm(o))
    def dma_in(self, o, src, chan): self.P.dma("sync", o, src, writes=self.nm(o), chan=chan)
    def dma_out(self, dst, a, chan="st"): self.P.dma("gpsimd", dst, a, reads=self.nm(a), chan=chan)
    def scan(self, o, d0, d1, init):
        self.P.op("vector", lambda e: e.tensor_tensor_scan(out=o, data0=d0, data1=d1, initial=init, op0=ALU.mult, op1=ALU.add),
                  reads=self.nm(d0, d1, init), writes=self.nm(o))
    def recip(self, o, a): self.P.op("vector", lambda e: e.reciprocal(out=o, in_=a), reads=self.nm(a), writes=self.nm(o))
    def transpose(self, o, a, ident):
        self.P.op("tensor", lambda e: e.transpose(o, a, ident), reads=self.nm(a, ident), writes=self.nm(o), nosync_same=True)
    def cmul_s(self, o_re, o_im, a_re, a_im, s_re, s_im, s_imn):
        self.ts(o_re, a_re, s_re, ALU.mult)
        self.stt(o_re, a_im, s_imn, o_re, ALU.mult, ALU.add)
        self.ts(o_im, a_re, s_im, ALU.mult)
        self.stt(o_im, a_im, s_re, o_im, ALU.mult, ALU.add)

def build_ssm():
    P = Prog(); a = A(P)
    d_in = {}
    for nm_, shp in (("are", [128, 6]), ("aim", [128, 6]), ("ldt", [128, 6]), ("Bre", [128, 96]), ("Bim", [128, 96]),
                     ("Cre", [128, 96]), ("Cim", [128, 96]), ("maskF", [128, 128]), ("maskB", [128, 128]), ("sgn", [128, 1]),
                     ("ident", [128, 128])):
        d_in[nm_] = P.dram_in(nm_, shp)
    Ddiag = P.dram_in("Ddiag", [6, 128, 128]); U = P.dram_in("U", [6, 128, NCH]); Yg = P.dram_out("Yg", [6, 128, NCH])
    s = {}
    for nm_, ap in d_in.items():
        shp = list(ap.shape)
        s[nm_] = P.sbuf("s_" + nm_, shp)
        a.dma_in(s[nm_][:], ap, "ld0")
    def T(name, shape): return P.sbuf(name, shape)
    dt = T("dt", [128, 6]); x = T("x", [128, 6]); th = T("th", [128, 6]); er = T("er", [128, 6]); m = T("m", [128, 6])
    y2 = T("y2", [128, 6]); sn = T("sn", [128, 6]); cs = T("cs", [128, 6]); lbr = T("lbr", [128, 6]); lbi = T("lbi", [128, 6])
    n2 = T("n2", [128, 6]); t1 = T("t1", [128, 6]); t2 = T("t2", [128, 6]); am1 = T("am1", [128, 6])
    qr = T("qr", [128, 6]); qi = T("qi", [128, 6]); qin = T("qin", [128, 6])
    a.act(dt[:], s["ldt"][:], AF.Exp)
    a.tt(x[:], s["are"][:], dt[:], ALU.mult); a.tt(th[:], s["aim"][:], dt[:], ALU.mult)
    a.act(er[:], x[:], AF.Exp)
    for _ in range(4):
        a.ts(m[:], th[:], PI, ALU.is_gt)
        a.stt(th[:], m[:], -2 * PI, th[:], ALU.mult, ALU.add)
    a.ts(y2[:], th[:], PI / 2, ALU.add)
    a.ts(m[:], y2[:], PI, ALU.is_gt)
    a.stt(y2[:], m[:], -2 * PI, y2[:], ALU.mult, ALU.add)
    a.act(sn[:], th[:], AF.Sin); a.act(cs[:], y2[:], AF.Sin)
    a.tt(lbr[:], er[:], cs[:], ALU.mult); a.tt(lbi[:], er[:], sn[:], ALU.mult)
    a.tt(n2[:], s["are"][:], s["are"][:], ALU.mult); a.tt(t1[:], s["aim"][:], s["aim"][:], ALU.mult); a.tt(n2[:], n2[:], t1[:], ALU.add)
    a.recip(n2[:], n2[:])
    a.ts(am1[:], lbr[:], -1.0, ALU.add)
    a.tt(t1[:], am1[:], s["are"][:], ALU.mult); a.tt(t2[:], lbi[:], s["aim"][:], ALU.mult); a.tt(t1[:], t1[:], t2[:], ALU.add)
    a.tt(qr[:], t1[:], n2[:], ALU.mult)
    a.tt(t1[:], lbi[:], s["are"][:], ALU.mult); a.tt(t2[:], am1[:], s["aim"][:], ALU.mult); a.tt(t1[:], t1[:], t2[:], ALU.subtract)
    a.tt(qi[:], t1[:], n2[:], ALU.mult)
    a.ts(qin[:], qi[:], -1.0, ALU.mult)
    Lr = T("Lr", [128, 6, 9]); Li = T("Li", [128, 6, 9]); Vr = T("Vr", [128, 6, 8]); Vi = T("Vi", [128, 6, 8])
    Rr = T("Rr", [128, 6, 9]); Ri = T("Ri", [128, 6, 9])
    e2 = T("e2", [128, 6]); ivr = T("ivr", [128, 6]); ivi = T("ivi", [128, 6])
    a.memset(Lr[:, :, 0], 1.0); a.memset(Li[:, :, 0], 0.0); a.memset(Vr[:, :, 0], 1.0); a.memset(Vi[:, :, 0], 0.0)
    a.act(e2[:], x[:], AF.Exp, scale=-2.0)
    a.tt(ivr[:], lbr[:], e2[:], ALU.mult); a.tt(ivi[:], lbi[:], e2[:], ALU.mult); a.ts(ivi[:], ivi[:], -1.0, ALU.mult)
    def cmul_t(o_r, o_i, p_r, p_i, q_r, q_i):
        a.tt(t1[:], p_r, q_r, ALU.mult); a.tt(t2[:], p_i, q_i, ALU.mult); a.tt(o_r, t1[:], t2[:], ALU.subtract)
        a.tt(t1[:], p_r, q_i, ALU.mult); a.tt(t2[:], p_i, q_r, ALU.mult); a.tt(o_i, t1[:], t2[:], ALU.add)
    for k in range(8):
        cmul_t(Lr[:, :, k + 1], Li[:, :, k + 1], Lr[:, :, k], Li[:, :, k], lbr[:], lbi[:])
    for k in range(7):
        cmul_t(Vr[:, :, k + 1], Vi[:, :, k + 1], Vr[:, :, k], Vi[:, :, k], ivr[:], ivi[:])
    for k in range(9):
        a.copy(Rr[:, :, k], Lr[:, :, 8 - k], eng="gpsimd"); a.copy(Ri[:, :, k], Li[:, :, 8 - k], eng="gpsimd")
    tabs = {}
    for nm_, (lo_r, lo_i, hi_r, hi_i) in dict(
            XL=(Vr[0:64, :, 0:8], Vi[0:64, :, 0:8], Lr[64:128, :, 0:8], Li[64:128, :, 0:8]),
            YL=(Lr[0:64, :, 0:8], Li[0:64, :, 0:8], Vr[64:128, :, 0:8], Vi[64:128, :, 0:8]),
            SL=(Rr[0:64, :, 1:9], Ri[0:64, :, 1:9], Lr[64:128, :, 0:8], Li[64:128, :, 0:8]),
            OL=(Lr[0:64, :, 1:9], Li[0:64, :, 1:9], Rr[64:128, :, 0:8], Ri[64:128, :, 0:8])).items():
        tr = T(nm_ + "r", [128, 6, 8]); ti = T(nm_ + "i", [128, 6, 8]); tn = T(nm_ + "n", [128, 6, 8])
        a.copy(tr[0:64], lo_r); a.copy(ti[0:64], lo_i); a.copy(tr[64:128], hi_r); a.copy(ti[64:128], hi_i)
        a.ts(tn[:], ti[:], -1.0, ALU.mult)
        tabs[nm_] = (tr, ti, tn)
    rho8 = T("rho8", [128, 6]); c8 = T("c8", [128, 6]); s8 = T("s8", [128, 6]); e8 = T("e8", [128, 6])
    a.act(rho8[:], x[:], AF.Exp, scale=8.0); a.act(e8[:], x[:], AF.Exp, scale=-8.0)
    a.tt(c8[:], Lr[:, :, 8], e8[:], ALU.mult); a.tt(s8[:], Li[:, :, 8], e8[:], ALU.mult)
    a.ts(s8[:], s8[:], s["sgn"][:, 0:1], ALU.mult)
    onesT = T("onesT", [128, NCH]); a.memset(onesT[:], 1.0, eng="gpsimd")
    def bc_j(ap):
        return ap.rearrange("p g (o j) -> p g o j", o=1).to_broadcast([128, 6, 8, 16])
    def bc_k(ap):
        return ap.rearrange("p g (k o) -> p g k o", o=1).to_broadcast([128, 6, 8, 16])
    Bre_v = s["Bre"][:].rearrange("p (g j) -> p g j", j=16); Bim_v = s["Bim"][:].rearrange("p (g j) -> p g j", j=16)
    Cre_v = s["Cre"][:].rearrange("p (g j) -> p g j", j=16); Cim_v = s["Cim"][:].rearrange("p (g j) -> p g j", j=16)
    Bbr_a = T("Bbr_a", [128, 6, 16]); Bbi_a = T("Bbi_a", [128, 6, 16])
    u1 = T("u1", [128, 6, 16]); u2 = T("u2", [128, 6, 16])
    qr_b = qr[:].rearrange("p (g o) -> p g o", o=1).to_broadcast([128, 6, 16]); qi_b = qi[:].rearrange("p (g o) -> p g o", o=1).to_broadcast([128, 6, 16])
    a.tt(u1[:], Bre_v, qr_b, ALU.mult); a.tt(u2[:], Bim_v, qi_b, ALU.mult, eng="gpsimd"); a.tt(Bbr_a[:], u1[:], u2[:], ALU.subtract)
    a.tt(u1[:], Bre_v, qi_b, ALU.mult); a.tt(u2[:], Bim_v, qr_b, ALU.mult, eng="gpsimd"); a.tt(Bbi_a[:], u1[:], u2[:], ALU.add)
    v1 = T("v1", [128, 6, 8, 16]); v2 = T("v2", [128, 6, 8, 16])
    def ctab(name, Ar, Ai, tb, neg_im):
        o_r = T(name + "r_a", [128, 6, 8, 16]); o_i = T(name + "i_a", [128, 6, 8, 16])
        Sr, Si = tb[0][:], tb[1][:]
        a.tt(v1[:], bc_j(Ar), bc_k(Sr), ALU.mult); a.tt(v2[:], bc_j(Ai), bc_k(Si), ALU.mult, eng="gpsimd")
        a.tt(o_r[:], v1[:], v2[:], ALU.subtract)
        a.tt(v1[:], bc_j(Ar), bc_k(Si), ALU.mult); a.tt(v2[:], bc_j(Ai), bc_k(Sr), ALU.mult, eng="gpsimd")
        a.tt(o_i[:], v1[:], v2[:], ALU.add)
        if neg_im:
            a.ts(o_i[:], o_i[:], -1.0, ALU.mult)
        return o_r, o_i
    Xr_a, Xi_a = ctab("X", Bbr_a[:], Bbi_a[:], tabs["XL"], False)
    Wtr_a, Wti_a = ctab("Wt", Bbr_a[:], Bbi_a[:], tabs["SL"], False)
    Yr_a, Yin_a = ctab("Y", Cre_v, Cim_v, tabs["YL"], True)
    Wor_a, Woin_a = ctab("Wo", Cre_v, Cim_v, tabs["OL"], True)
    Tr_all = T("Tr_all", [128, 6, NCH + 1]); Ti_all = T("Ti_all", [128, 6, NCH + 1])
    mr = [T(f"mr{k}", [128, 6]) for k in range(11)]; mi = [T(f"mi{k}", [128, 6]) for k in range(11)]
    a.copy(mr[0][:], c8[:]); a.copy(mi[0][:], s8[:])
    for k in range(1, 11):
        a.tt(t1[:], mr[k - 1][:], mr[k - 1][:], ALU.mult); a.tt(t2[:], mi[k - 1][:], mi[k - 1][:], ALU.mult)
        a.tt(mr[k][:], t1[:], t2[:], ALU.subtract)
        a.tt(t1[:], mr[k - 1][:], mi[k - 1][:], ALU.mult); a.ts(mi[k][:], t1[:], 2.0, ALU.mult)
    a.memset(Tr_all[:, :, 0:1], 1.0); a.memset(Ti_all[:, :, 0:1], 0.0)
    z1 = T("z1", [128, 6, 512]); z2 = T("z2", [128, 6, 512])
    for k in range(11):
        n = 1 << k
        cnt_ = min(n, NCH + 1 - n)
        mrb = mr[k][:].rearrange("p (g o) -> p g o", o=1).to_broadcast([128, 6, cnt_])
        mib = mi[k][:].rearrange("p (g o) -> p g o", o=1).to_broadcast([128, 6, cnt_])
        a.tt(z1[:, :, 0:cnt_], Tr_all[:, :, 0:cnt_], mrb, ALU.mult); a.tt(z2[:, :, 0:cnt_], Ti_all[:, :, 0:cnt_], mib, ALU.mult)
        a.tt(Tr_all[:, :, n:n + cnt_], z1[:, :, 0:cnt_], z2[:, :, 0:cnt_], ALU.subtract)
        a.tt(z1[:, :, 0:cnt_], Tr_all[:, :, 0:cnt_], mib, ALU.mult); a.tt(z2[:, :, 0:cnt_], Ti_all[:, :, 0:cnt_], mrb, ALU.mult)
        a.tt(Ti_all[:, :, n:n + cnt_], z1[:, :, 0:cnt_], z2[:, :, 0:cnt_], ALU.add)
    Wsr = T("Wsr", [128, 128]); Wsi = T("Wsi", [128, 128]); Msb = T("Msb", [128, 128]); Mtmp = T("Mtmp", [128, 128]); Dd = T("Dd", [128, 128])
    Us = [T(f"Us{i}", [128, NCH]) for i in range(2)]
    Sre = T("Sre", [128, NCH]); Sim = T("Sim", [128, NCH]); Spr = T("Spr", [128, NCH]); Spi = T("Spi", [128, NCH])
    Gre = T("Gre", [128, NCH]); Gim = T("Gim", [128, NCH]); Hor = T("Hor", [128, NCH]); Hoi = T("Hoi", [128, NCH])
    Hir = T("Hir", [128, NCH]); Hii = T("Hii", [128, NCH])
    rhoT = T("rhoT", [128, NCH])
    w1 = T("w1", [128, NCH]); w2 = T("w2", [128, NCH]); w3 = T("w3", [128, NCH]); w4 = T("w4", [128, NCH])
    ini = T("ini", [128, 4]); Ysb = T("Ysb", [128, NCH])
    ps = [P.psum(f"ps{i}") for i in range(8)]
    BLK = [(0, 512), (512, 512), (1024, NCH - 1024)]
    for gi in range(6):
        ub = gi % 2
        a.dma_in(Us[ub][:], U[gi], f"u{ub}")
        a.dma_in(Dd[:], Ddiag[gi], "dd")
        Xr, Xi, Yr, Yin = Xr_a[:, gi], Xi_a[:, gi], Yr_a[:, gi], Yin_a[:, gi]
        Wtr, Wti, Wor, Woin = Wtr_a[:, gi], Wti_a[:, gi], Wor_a[:, gi], Woin_a[:, gi]
        f2 = lambda t_: t_.rearrange("p a b -> p (a b)")
        for half, pb in ((0, 6), (1, 7)):
            rows = slice(half * 64, half * 64 + 64)
            a.mm(ps[pb][:, 0:128], f2(Xr)[rows], f2(Yr)[rows], True, False)
            a.mm(ps[pb][:, 0:128], f2(Xi)[rows], f2(Yin)[rows], False, True)
        a.tt(Msb[:], ps[6][:, 0:128], s["maskF"][:], ALU.mult)
        a.tt(Mtmp[:], ps[7][:, 0:128], s["maskB"][:], ALU.mult)
        a.tt(Msb[:], Msb[:], Mtmp[:], ALU.add, eng="gpsimd"); a.tt(Msb[:], Msb[:], Dd[:], ALU.add, eng="gpsimd")
        a.transpose(ps[6][:, 128:256], f2(Wtr), s["ident"][:]); a.transpose(ps[7][:, 128:256], f2(Wti), s["ident"][:])
        a.copy(Wsr[:], ps[6][:, 128:256], eng="scalar"); a.copy(Wsi[:], ps[7][:, 128:256], eng="scalar")
        for bi, (c0, cn) in enumerate(BLK):
            a.mm(ps[bi][:, 0:cn], Wsr[:], Us[ub][:, c0:c0 + cn], True, True)
            a.mm(ps[3 + bi][:, 0:cn], Wsi[:], Us[ub][:, c0:c0 + cn], True, True)
            a.copy(Sre[:, c0:c0 + cn], ps[bi][:, 0:cn], eng="scalar"); a.copy(Sim[:, c0:c0 + cn], ps[3 + bi][:, 0:cn], eng="scalar")
        Tr = Tr_all[:, gi, :]; Ti = Ti_all[:, gi, :]
        a.act(rhoT[:], onesT[:], AF.Copy, scale=rho8[:, gi:gi + 1])
        a.tt(w1[:], Sre[:], Tr[:, 0:NCH], ALU.mult); a.tt(w3[:], Sim[:], Ti[:, 0:NCH], ALU.mult)
        a.tt(Spr[:], w1[:], w3[:], ALU.subtract)
        a.tt(w2[:], Sre[:], Ti[:, 0:NCH], ALU.mult, eng="gpsimd"); a.tt(w4[:], Sim[:], Tr[:, 0:NCH], ALU.mult, eng="gpsimd")
        a.tt(Spi[:], w2[:], w4[:], ALU.add, eng="gpsimd")
        for (Gx, Sx) in ((Gre, Spr), (Gim, Spi)):
            a.scan(Gx[0:64, :], rhoT[0:64, :], Sx[0:64, :], 0.0)
            a.scan(Gx[64:128, 0:32][:, ::-1], rhoT[64:128, 0:32], Sx[64:128, 0:32][:, ::-1], 0.0)
        lo = slice(64, 128)
        a.tt(ini[lo, 0:1], Gre[lo, 0:1], Tr[lo, NCH:NCH + 1], ALU.mult); a.tt(ini[lo, 1:2], Gim[lo, 0:1], Ti[lo, NCH:NCH + 1], ALU.mult)
        a.tt(ini[lo, 2:3], ini[lo, 0:1], ini[lo, 1:2], ALU.subtract)
        a.tt(ini[lo, 0:1], Gre[lo, 0:1], Ti[lo, NCH:NCH + 1], ALU.mult); a.tt(ini[lo, 1:2], Gim[lo, 0:1], Tr[lo, NCH:NCH + 1], ALU.mult)
        a.tt(ini[lo, 3:4], ini[lo, 0:1], ini[lo, 1:2], ALU.add)
        a.scan(Gre[lo, 32:NCH][:, ::-1], rhoT[lo, 32:NCH], Spr[lo, 32:NCH][:, ::-1], ini[lo, 2:3])
        a.scan(Gim[lo, 32:NCH][:, ::-1], rhoT[lo, 32:NCH], Spi[lo, 32:NCH][:, ::-1], ini[lo, 3:4])
        a.tt(w1[:], Gre[:], Tr[:, 0:NCH], ALU.mult); a.tt(w3[:], Gim[:], Ti[:, 0:NCH], ALU.mult)
        a.tt(Hor[:], w1[:], w3[:], ALU.add)
        a.tt(w2[:], Gim[:], Tr[:, 0:NCH], ALU.mult, eng="gpsimd"); a.tt(w4[:], Gre[:], Ti[:, 0:NCH], ALU.mult, eng="gpsimd")
        a.tt(Hoi[:], w2[:], w4[:], ALU.subtract, eng="gpsimd")
        for (Hi_, Ho_, Gx) in ((Hir, Hor, Gre), (Hii, Hoi, Gim)):
            a.copy(Hi_[0:64, 1:NCH], Ho_[0:64, 0:NCH - 1], eng="scalar"); a.memset(Hi_[0:64, 0:1], 0.0)
            a.copy(Hi_[lo, 0:NCH - 1], Ho_[lo, 1:NCH], eng="scalar"); a.memset(Hi_[lo, 31:32], 0.0)
            a.copy(Hi_[lo, NCH - 1:NCH], Gx[lo, 0:1])
        for bi, (c0, cn) in enumerate(BLK):
            a.mm(ps[bi][:, 0:cn], Msb[:], Us[ub][:, c0:c0 + cn], True, False)
            a.mm(ps[bi][:, 0:cn], f2(Wor), Hir[:, c0:c0 + cn], False, False)
            a.mm(ps[bi][:, 0:cn], f2(Woin), Hii[:, c0:c0 + cn], False, True)
            a.copy(Ysb[:, c0:c0 + cn], ps[bi][:, 0:cn], eng="scalar")
        a.dma_out(Yg[gi], Ysb[:])
    return P.finish()

def ssm_inputs(inp, l, j4, u_b, uc_b):
    gs = np.arange(6 * j4, 6 * j4 + 6)
    def rows(arr):
        return np.ascontiguousarray(arr[:, gs, :].transpose(0, 2, 1).reshape(128, 6))
    are = rows(inp["ssm_a_re"][l]); aim = rows(inp["ssm_a_im"][l])
    ldt = np.ascontiguousarray(np.repeat(inp["ssm_log_dt"][l][:, gs][:, None, :], 64, axis=1).reshape(128, 6))
    def rowsB(arr):
        return np.ascontiguousarray(arr[:, gs].transpose(0, 2, 1, 3).reshape(128, 96))
    def rowsC(arr):
        return np.ascontiguousarray(arr[:, gs].transpose(0, 3, 1, 2).reshape(128, 96))
    s_ = np.arange(8)
    mF = (s_[None, :] >= s_[:, None]).astype(np.float32)
    maskF = np.kron(mF, np.ones((16, 16), np.float32)); maskB = np.kron(mF.T, np.ones((16, 16), np.float32))
    sgn = np.concatenate([-np.ones((64, 1), np.float32), np.ones((64, 1), np.float32)], 0)
    dsk = inp["ssm_d"][l]
    Dd = np.zeros((6, 128, 128), np.float32)
    for gi, g in enumerate(gs):
        dd = np.zeros((8, 16, 8, 16), np.float32)
        for t in range(8):
            dd[t, np.arange(16), t, np.arange(16)] = dsk[16 * g:16 * g + 16]
        Dd[gi] = dd.reshape(128, 128)
    seq = np.concatenate([uc_b, u_b], 0)
    U = np.zeros((6, 128, NCH), np.float32)
    for gi, g in enumerate(gs):
        U[gi] = seq[:, 16 * g:16 * g + 16].reshape(NCH, 128).T
    return dict(are=are, aim=aim, ldt=ldt, Bre=rowsB(inp["ssm_b_re"][l]), Bim=rowsB(inp["ssm_b_im"][l]),
                Cre=rowsC(inp["ssm_c_re"][l]), Cim=rowsC(inp["ssm_c_im"][l]), maskF=maskF, maskB=maskB, sgn=sgn,
                ident=np.eye(128, dtype=np.float32), Ddiag=Dd, U=U)

def ssm_unpack(Yg):
    return np.ascontiguousarray(Yg.transpose(2, 1, 0).reshape(NCH, 8, 16, 6).transpose(0, 1, 3, 2).reshape(NCH * 8, 96))


_PROGS = {}
def _prog(name, fn):
    if name not in _PROGS:
        _PROGS[name] = fn()
    return _PROGS[name]

def _run(nc, maps):
    res = run_bass_kernel_spmd(nc, maps, core_ids=list(range(8)))
    return res.results

def kernel(x, c, ctx, c_ctx, w_mod, b_mod, g_pre_mix, g_post_mix, w_in, ssm_a_re, ssm_a_im, ssm_log_dt, ssm_b_re, ssm_b_im,
           ssm_c_re, ssm_c_im, ssm_d, w_glu, w_fourier, na_rpb, w_out, g_pre_ffn, g_post_ffn, w_ffn_gate, w_ffn_up, w_ffn_down):
    f32 = lambda a: np.ascontiguousarray(np.asarray(a, dtype=np.float32))
    inp = dict(ssm_a_re=f32(ssm_a_re), ssm_a_im=f32(ssm_a_im), ssm_log_dt=f32(ssm_log_dt), ssm_b_re=f32(ssm_b_re), ssm_b_im=f32(ssm_b_im),
               ssm_c_re=f32(ssm_c_re), ssm_c_im=f32(ssm_c_im), ssm_d=f32(ssm_d))
    x = f32(x); c = f32(c); ctx = f32(ctx); c_ctx = f32(c_ctx); w_mod = f32(w_mod); b_mod = f32(b_mod)
    w_in = f32(w_in); w_glu = f32(w_glu); w_fourier = f32(w_fourier); na_rpb = f32(na_rpb); w_out = f32(w_out)
    g_pre_mix = f32(g_pre_mix); g_post_mix = f32(g_post_mix); g_pre_ffn = f32(g_pre_ffn); g_post_ffn = f32(g_post_ffn)
    w_ffn_gate = f32(w_ffn_gate); w_ffn_up = f32(w_ffn_up); w_ffn_down = f32(w_ffn_down)
    DEPTH = 2
    cores = [(k // 4, k % 4) for k in range(8)]
    cTs = [np.ascontiguousarray(np.concatenate([colT(c[b], 8), colT(c_ctx, 8)], axis=1)) for b in range(2)]
    xT = [np.ascontiguousarray(np.concatenate([x[b, q * 2048:(q + 1) * 2048].T, ctx[b].T], axis=1)) for (b, q) in cores]
    KF = fnet_consts(); permm = perm_matrix(); ident = np.eye(128, dtype=np.float32)
    ropes = [rope_tables(q * 2048, 2048) for q in range(4)]
    for l in range(DEPTH):
        maps = []
        for k, (b, q) in enumerate(cores):
            maps.append(dict(xT=xT[k], w_in=w_in[l], w_mod=np.ascontiguousarray(w_mod[l][:, 0:2048]), b_modT=colT(b_mod[l][0:2048], 16),
                             g_preT=colT(g_pre_mix[l], 8), cT=cTs[b], cos=ropes[q][0], sin=ropes[q][1], perm=permm))
        res = _run(_prog("l1", build_l1), maps)
        hfull = [np.concatenate([res[k]["hT"], res[k]["hTb"]], 0) for k in range(8)]
        h_lat = [np.concatenate([hfull[4 * b + q][:, 0:2048].T for q in range(4)], 0) for b in range(2)]
        h_ctx = [np.ascontiguousarray(hfull[4 * b][:, 2048:2304].T) for b in range(2)]
        del hfull
        del res
        maps = [ssm_inputs(inp, l, j4, h_lat[b][:, 0:384], h_ctx[b][:, 0:384]) for (b, j4) in cores]
        res = _run(_prog("ssm", build_ssm), maps)
        ys = [[ssm_unpack(res[4 * b + j4]["Yg"]) for j4 in range(4)] for b in range(2)]
        ysT = [np.ascontiguousarray(np.concatenate(ys[b], 1).T) for b in range(2)]
        del res, ys
        maps = []
        for (b, g) in cores:
            m = dict(z=np.ascontiguousarray(h_lat[b][:, 384 + 64 * g:448 + 64 * g].reshape(128, 4096)),
                     zc=np.ascontiguousarray(h_ctx[b][:, 384 + 64 * g:448 + 64 * g].reshape(2, 128, 64).transpose(1, 0, 2)))
            m.update(KF); maps.append(m)
        res = _run(_prog("fnet", lambda: build_fnet(True)), maps)
        mxT = [np.concatenate([res[4 * b + g]["R"] for g in range(4)], 0) for b in range(2)]
        mxcT = [np.concatenate([res[4 * b + g]["Rc"] for g in range(4)], 0) for b in range(2)]
        del res
        maps = []
        for (b, q) in cores:
            kwT, vw = na_windows(h_lat[b][:, 1024:1408], h_lat[b][:, 1408:1792], q)
            tbraw, mask = na_tables(na_rpb[l], q)
            maps.append(dict(qT=np.ascontiguousarray(h_lat[b][q * 2048:(q + 1) * 2048, 640:1024].T), kwT=kwT, vw=vw,
                             kcT=np.ascontiguousarray(h_ctx[b][:, 1024:1408].T),
                             vc=np.ascontiguousarray(h_ctx[b][:, 1408:1792].reshape(2, 128, 384).transpose(1, 0, 2)),
                             tbraw=tbraw, mask=mask, ident=ident, qcT=np.ascontiguousarray(h_ctx[b][:, 640:1024].T)))
        res = _run(_prog("na", lambda: build_na(True)), maps)
        naT = [np.ascontiguousarray(np.concatenate([res[4 * b + q]["Y"] for q in range(4)], 0).T) for b in range(2)]
        nacT = [np.ascontiguousarray(res[4 * b]["Yc"].T) for b in range(2)]
        del res, h_lat
        maps = []
        for k, (b, q) in enumerate(cores):
            sl = slice(q * 2048, (q + 1) * 2048)
            maps.append(dict(xT=xT[k], ysT=np.ascontiguousarray(np.concatenate([ysT[b][:, 256 + q * 2048:256 + (q + 1) * 2048], ysT[b][:, 0:256]], 1)),
                             mxT=np.ascontiguousarray(np.concatenate([mxT[b][:, sl], mxcT[b]], 1)),
                             naT=np.ascontiguousarray(np.concatenate([naT[b][:, sl], nacT[b]], 1)),
                             w_mod=np.ascontiguousarray(w_mod[l][:, 2048:3072]), b_modT=colT(b_mod[l][2048:3072], 8), cT=cTs[b],
                             g_postT=colT(g_post_mix[l], 8), w_glu=w_glu[l], w_fourier=w_fourier[l], w_out=w_out[l]))
        res = _run(_prog("l3a", lambda: build_l3a(True)), maps)
        xT = [res[k]["xoT"] for k in range(8)]
        del res
        maps = []
        for k, (b, q) in enumerate(cores):
            maps.append(dict(xT=xT[k], w_mod=np.ascontiguousarray(w_mod[l][:, 3072:6144]), b_modT=colT(b_mod[l][3072:6144], 24), cT=cTs[b],
                             g_preT=colT(g_pre_ffn[l], 8), g_postT=colT(g_post_ffn[l], 8),
                             w_gate=w_ffn_gate[l], w_up=w_ffn_up[l], w_down=w_ffn_down[l]))
        res = _run(_prog("l3b", lambda: build_l3b(True)), maps)
        xT = [np.ascontiguousarray(res[k]["xoT"]) for k in range(8)]
        del res
    out = np.empty((2, 8192, 1024), np.float32)
    for k, (b, q) in enumerate(cores):
        out[b, q * 2048:(q + 1) * 2048] = xT[k][:, 0:2048].T
    return out
```

```python
import math
import numpy as np
from contextlib import ExitStack
import concourse.bass as bass
import concourse.mybir as mybir
from concourse.bass_utils import run_bass_kernel_spmd


F32 = mybir.dt.float32
BF16 = mybir.dt.bfloat16
ALU = mybir.AluOpType
AF = mybir.ActivationFunctionType
AX = mybir.AxisListType

COMPUTE = ("tensor", "vector", "scalar", "gpsimd")


class Prog:
    def __init__(self):
        self.nc = bass.Bass("TRN2", target_bir_lowering=False)
        self.ops = []
        self.stack = ExitStack()
        self.ndram = 0

    def dram_in(self, name, shape, dtype=F32):
        return self.nc.dram_tensor(name, list(shape), dtype, kind="ExternalInput").ap()

    def dram_out(self, name, shape, dtype=F32):
        return self.nc.dram_tensor(name, list(shape), dtype, kind="ExternalOutput").ap()

    def sbuf(self, name, shape, dtype=F32):
        return self.stack.enter_context(self.nc.sbuf_tensor(name, list(shape), dtype))

    def psum(self, name, shape=(128, 512), dtype=F32):
        return self.stack.enter_context(self.nc.psum_tensor(name, list(shape), dtype))

    def op(self, eng, fn, reads=(), writes=(), chan=None, nosync_same=False, inc=True):
        self.ops.append(dict(eng=eng, fn=fn, reads=tuple(reads), writes=tuple(writes),
                             chan=chan, nosync_same=nosync_same, inc=inc))

    def dma(self, eng, out, in_, reads=(), writes=(), chan="ld", **kw):
        self.op(eng, lambda e: e.dma_start(out=out, in_=in_, **kw), reads, writes, chan=chan)

    def mm(self, out, lhsT, rhs, start, stop, reads=(), writes=()):
        self.op("tensor", lambda e: e.matmul(out, lhsT, rhs, start=start, stop=stop),
                reads, writes, nosync_same=True, inc=True)

    def act(self, out, in_, func, reads=(), writes=(), **kw):
        self.op("scalar", lambda e: e.activation(out=out, in_=in_, func=func, **kw), reads, writes)

    def tt(self, out, in0, in1, op, reads=(), writes=(), eng="vector"):
        self.op(eng, lambda e: e.tensor_tensor(out=out, in0=in0, in1=in1, op=op), reads, writes)

    def ts(self, out, in0, s1, s2, op0, op1=None, reads=(), writes=(), eng="vector"):
        if op1 is None:
            self.op(eng, lambda e: e.tensor_scalar(out=out, in0=in0, scalar1=s1, scalar2=None, op0=op0),
                    reads, writes)
        else:
            self.op(eng, lambda e: e.tensor_scalar(out=out, in0=in0, scalar1=s1, scalar2=s2, op0=op0, op1=op1),
                    reads, writes)

    def stt(self, out, in0, scalar, in1, op0, op1, reads=(), writes=()):
        self.op("vector", lambda e: e.scalar_tensor_tensor(out=out, in0=in0, scalar=scalar, in1=in1,
                                                            op0=op0, op1=op1), reads, writes)

    def copy(self, out, in_, reads=(), writes=(), eng="vector"):
        if eng == "scalar":
            self.op(eng, lambda e: e.copy(out=out, in_=in_), reads, writes)
        else:
            self.op(eng, lambda e: e.tensor_copy(out=out, in_=in_), reads, writes)

    def memset(self, ap, val, writes=(), eng="vector"):
        self.op(eng, lambda e: e.memset(ap, val), (), writes)

    def finish(self):
        nc = self.nc
        ops = self.ops
        engines = []
        for o in ops:
            if o["eng"] not in engines:
                engines.append(o["eng"])
        chans = []
        for o in ops:
            if o["chan"] is not None and o["chan"] not in chans:
                chans.append(o["chan"])
        sems = {}
        for e in engines:
            sems[("e", e)] = self.stack.enter_context(nc.semaphore("s_" + e))
        for c in chans:
            sems[("c", c)] = self.stack.enter_context(nc.semaphore("c_" + c))
        def plan_pass():
            viol = set()
            eng_count = {e: 0 for e in engines}
            chan_count = {c: 0 for c in chans}
            last_writer = {}
            readers = {}
            known = {e: {} for e in engines}
            plan = {e: [] for e in engines}
            done = []
            for i, o in enumerate(ops):
                e = o["eng"]
                deps = set()
                for r in o["reads"]:
                    if r in last_writer:
                        deps.add(last_writer[r])
                for w in o["writes"]:
                    if w in last_writer:
                        deps.add(last_writer[w])
                    for rd in readers.get(w, ()):
                        deps.add(rd)
                need = {}
                for d in deps:
                    od = ops[d]
                    if od["chan"] is not None:
                        key = ("c", od["chan"])
                        val = 16 * chan_count[od["chan"]]
                    else:
                        if od["eng"] == e and (o["nosync_same"] and od["nosync_same"]):
                            continue
                        key = ("e", od["eng"])
                        val = done[d][1]
                        if val > eng_count[od["eng"]]:
                            viol.add(d)
                    if val > need.get(key, 0):
                        need[key] = val
                waits = []
                for key, val in need.items():
                    if known[e].get(key, 0) >= val:
                        continue
                    known[e][key] = val
                    waits.append((key, val))
                if o["chan"] is not None:
                    chan_count[o["chan"]] += 1
                    done.append((("c", o["chan"]), 16 * chan_count[o["chan"]]))
                    inc = (("c", o["chan"]), 16)
                elif not o["inc"]:
                    done.append((("e", e), eng_count[e] + 1))
                    inc = None
                else:
                    eng_count[e] += 1
                    done.append((("e", e), eng_count[e]))
                    inc = (("e", e), 1)
                plan[e].append((waits, o["fn"], inc))
                for r in o["reads"]:
                    readers.setdefault(r, []).append(i)
                for w in o["writes"]:
                    last_writer[w] = i
                    readers[w] = []
            return viol, plan, chan_count
        while True:
            viol, plan, chan_count = plan_pass()
            if not viol:
                break
            for d in viol:
                ops[d]["inc"] = True
        final_waits = {e: [] for e in engines}
        chan_eng = {}
        for o in ops:
            if o["chan"] is not None:
                chan_eng[o["chan"]] = o["eng"]
        for c, e in chan_eng.items():
            final_waits[e].append((("c", c), 16 * chan_count[c]))

        semv = {k: 0 for k in sems}
        ptr = {e: 0 for e in engines}
        progressed = True
        while progressed:
            progressed = False
            for e in engines:
                while ptr[e] < len(plan[e]):
                    waits, _fn, inc = plan[e][ptr[e]]
                    if any(semv[k] < v for k, v in waits):
                        break
                    if inc is not None:
                        semv[inc[0]] += inc[1]
                    ptr[e] += 1
                    progressed = True
        stuck = {e: (ptr[e], len(plan[e])) for e in engines if ptr[e] < len(plan[e])}
        if stuck:
            det = {e: [(k, v, semv[k]) for k, v in plan[e][ptr[e]][0] if semv[k] < v] for e in stuck}
            raise RuntimeError(f"sync plan deadlocks: {stuck} waiting on {det}")

        with nc.Block() as block:
            def make(e):
                def body(eng):
                    for waits, fn, inc in plan[e]:
                        for key, val in waits:
                            eng.wait_ge(sems[key], val)
                        ins = fn(eng)
                        if inc is not None:
                            ins.then_inc(sems[inc[0]], inc[1])
                    for key, val in final_waits[e]:
                        eng.wait_ge(sems[key], val)
                return body
            for e in engines:
                getattr(block, e)(make(e))
        self.stack.close()
        return nc

GRID_W = 64
def rope_tables(tok0, n):
    t = np.arange(tok0, tok0 + n); row = (t // GRID_W).astype(np.float32); col = (t % GRID_W).astype(np.float32)
    quarter = 16
    freqs = (10000.0 ** (-np.arange(quarter, dtype=np.float32) / quarter)).astype(np.float32)
    cos = np.zeros((64, n), np.float32); sin = np.zeros((64, n), np.float32)
    for d in range(64):
        pos = row if d < 32 else col
        dd = d % 32
        f = freqs[dd % 16]
        ang = (pos * f).astype(np.float32)
        cos[d] = np.cos(ang); s = np.sin(ang)
        sin[d] = -s if dd < 16 else s
    return np.concatenate([cos, cos], 0), np.concatenate([sin, sin], 0)
def perm_matrix():
    Pm = np.zeros((128, 128), np.float32)
    for m in range(128):
        dd = m % 32
        k = m + 16 if dd < 16 else m - 16
        Pm[k, m] = 1.0
    return Pm
def colT(v, n):
    return np.ascontiguousarray(v.reshape(n, 128).T)

EPS = 1e-6

def get_stage(P):
    if not hasattr(P, "_stage"):
        P._stage_n = getattr(P, "_stage_n", 3)
        P._stage = [P.sbuf(f"stage{i}", [128, 1024]) for i in range(P._stage_n)]
        P._stage_i = 0
    return P._stage

def emit_mod(P, w_mod, b_modT, cT, nct, ps_mod, tag="m"):
    ncols = nct * 128
    c_s = P.sbuf(tag + "c_s", [128, 16])
    bm_s = P.sbuf(tag + "bm_s", [128, nct]); modv = P.sbuf(tag + "modv", [128, 2 * nct]); modc = P.sbuf(tag + "modc", [128, 2 * nct])
    modrow = [P.sbuf(f"{tag}modrow{i}", [2, 512]) for i in range(2)]
    scr = P.nc.dram_tensor(tag + "_modscr", [2, ncols], F32, kind="Internal").ap()
    stb = [P.sbuf(f"{tag}stb{i}", [128, 2, 512], BF16) for i in range(3)]
    sc_b = P.sbuf(tag + "sc_b", [128, 16], BF16)
    P.dma("sync", c_s[:], cT, writes=[tag + "c_s"], chan="ld0")
    P.dma("sync", bm_s[:], b_modT, writes=[tag + "bm_s"], chan="ld0")
    P.act(sc_b[:], c_s[:], AF.Silu, reads=[tag + "c_s"], writes=[tag + "sc_s"])
    wmv = w_mod.rearrange("(kc p) n -> p kc n", p=128)
    nst = 0
    for pc in range(ncols // 512):
        for i in range(4):
            b = nst % 3; nst += 1
            P.dma("gpsimd", stb[b][:], wmv[:, 2 * i:2 * i + 2, pc * 512:(pc + 1) * 512],
                  writes=[(tag + "stb", b)], chan=f"{tag}stb{b}", max_dma_last_dim=2048)
            for k2 in range(2):
                kc = 2 * i + k2
                P.mm(ps_mod[0:2, 0:512], sc_b[:, kc:16:8], stb[b][:, k2, :], kc == 0, kc == 7,
                     reads=[(tag + "stb", b), tag + "sc_s"], writes=["ps_mod"])
        P.copy(modrow[pc % 2][:], ps_mod[0:2, 0:512], reads=["ps_mod"], writes=[(tag + "modrow", pc % 2)], eng="scalar")
        P.dma("sync", scr[:, pc * 512:(pc + 1) * 512], modrow[pc % 2][:], reads=[(tag + "modrow", pc % 2)], writes=[tag + "scr"], chan="modw")
    P.dma("sync", modc[:].rearrange("p (j t) -> p j t", j=2), scr.rearrange("j (t p) -> p j t", p=128),
          reads=[tag + "scr"], writes=[tag + "modc"], chan="modr", allow_slow_non_contiguous=True)
    for j in range(2):
        P.tt(modv[:, j * nct:(j + 1) * nct], modc[:, j * nct:(j + 1) * nct], bm_s[:], ALU.add,
             reads=[tag + "modc", tag + "bm_s"], writes=[tag + "modv"])
    return modv

def load_cast(P, w_dram, w_bf, nk, ncols, tag, piece=1024):
    wv = w_dram.rearrange("(kc p) n -> p kc n", p=128)
    for kc in range(nk):
        P.dma("gpsimd", w_bf[:, kc, :], wv[:, kc, :], writes=[(tag, kc)], chan="wld_" + tag, max_dma_last_dim=4096)

def emit_rstd(P, src, nk, n, sqb, ones, ps_ss, sd, rstd, src_res, tag=""):
    P.act(sqb[:, 0:nk, 0:n], src[:, 0:nk, 0:n], AF.Square, reads=src_res, writes=["sqb"])
    for kc in range(nk):
        P.mm(ps_ss[:, 0:n], ones[:], sqb[:, kc, 0:n], kc == 0, kc == nk - 1, reads=["sqb", "ones"], writes=["ps_ss"])
    P.act(sd[:, 0:n], ps_ss[:, 0:n], AF.Sqrt, reads=["ps_ss"], writes=["sd" + tag], scale=1.0 / (128 * nk), bias=EPS)
    P.op("vector", lambda e: e.reciprocal(out=rstd[:, 0:n], in_=sd[:, 0:n]), reads=["sd" + tag], writes=["rstd" + tag])

def build_l3b(with_ctx, N=256):
    NT = 2304 if with_ctx else 2048
    P = Prog(); P._stage_n = 2
    xT = P.dram_in("xT", [1024, NT])
    w_mod = P.dram_in("w_mod", [1024, 3072]); b_modT = P.dram_in("b_modT", [128, 24]); cT = P.dram_in("cT", [128, 16])
    g_preT = P.dram_in("g_preT", [128, 8]); g_postT = P.dram_in("g_postT", [128, 8])
    w_gate = P.dram_in("w_gate", [1024, 2816]); w_up = P.dram_in("w_up", [1024, 2816]); w_down = P.dram_in("w_down", [2816, 1024])
    xoT = P.dram_out("xoT", [1024, NT])
    wg = P.sbuf("wg", [128, 8, 2816], BF16); wu = P.sbuf("wu", [128, 8, 2816], BF16); wd = P.sbuf("wd", [128, 22, 1024], BF16)
    xs = [P.sbuf(f"xs{i}", [128, 8, N]) for i in range(2)]
    sqb = P.sbuf("sqb", [128, 8, N], BF16)
    tt_ = [P.sbuf(f"tt{i}", [128, N]) for i in range(2)]; xn = [P.sbuf(f"xn{i}", [128, 8, N], BF16) for i in range(2)]
    sd2 = P.sbuf("sd2", [128, N]); rstd2 = P.sbuf("rstd2", [128, N])
    hmid = P.sbuf("hmid", [128, 22, N], BF16)
    sg = [P.sbuf(f"sg{i}", [128, N]) for i in range(2)]
    o2 = P.sbuf("o2", [128, 8, N]); tmp = [P.sbuf(f"tmp{i}", [128, N]) for i in range(2)]
    xo = [P.sbuf(f"xo{i}", [128, N]) for i in range(2)]
    ones = P.sbuf("ones", [128, 128], BF16)
    sd = P.sbuf("sd", [128, N]); rstd = P.sbuf("rstd", [128, N])
    gp_s = P.sbuf("gp_s", [128, 8]); gq_s = P.sbuf("gq_s", [128, 8])
    Av = P.sbuf("Av", [128, 16]); Gv = P.sbuf("Gv", [128, 16])
    ps_mod = P.psum("ps_mod"); ps_ss = P.psum("ps_ss")
    psg = [P.psum(f"psg{i}") for i in range(2)]; psu = [P.psum(f"psu{i}") for i in range(2)]; pso = [P.psum(f"pso{i}") for i in range(2)]
    P.memset(ones[:], 1.0, writes=["ones"])
    P.dma("sync", gp_s[:], g_preT, writes=["gp_s"], chan="ld0")
    P.dma("sync", gq_s[:], g_postT, writes=["gq_s"], chan="ld0")
    modv = emit_mod(P, w_mod, b_modT, cT, 24, ps_mod)
    for j in range(2):
        P.stt(Av[:, j * 8:(j + 1) * 8], modv[:, j * 24 + 8: j * 24 + 16], 1.0, gp_s[:], ALU.add, ALU.mult,
              reads=["mmodv", "gp_s"], writes=["Av"])
        P.tt(Gv[:, j * 8:(j + 1) * 8], modv[:, j * 24 + 16: j * 24 + 24], gq_s[:], ALU.mult,
             reads=["mmodv", "gq_s"], writes=["Gv"])
    load_cast(P, w_gate, wg, 8, 2816, "wg")
    load_cast(P, w_up, wu, 8, 2816, "wu")
    xv = xT.rearrange("(kc p) t -> p kc t", p=128); xov = xoT.rearrange("(kc p) t -> p kc t", p=128)
    slabs = list(range(0, NT, N)); n = N
    cnt = dict(ng=0, no=0, nt=0)
    def pre(si):
        t0 = slabs[si]; b = si % 2; j = 0 if t0 < 2048 else 1
        P.dma("sync", xs[b][:], xv[:, :, t0:t0 + n], writes=[("xs", b)], chan=f"xs{b}")
        emit_rstd(P, xs[b], 8, n, sqb, ones, ps_ss, sd, rstd, [("xs", b)])
        for kc in range(8):
            P.tt(tt_[kc % 2][:], xs[b][:, kc, :], rstd[:], ALU.mult, reads=[("xs", b), "rstd"], writes=[("tt", kc % 2)])
            P.act(xn[b][:, kc, :], tt_[kc % 2][:], AF.Identity, reads=[("tt", kc % 2), "Av", "mmodv"], writes=[("xn", b, kc)],
                  scale=Av[:, j * 8 + kc: j * 8 + kc + 1], bias=modv[:, j * 24 + kc: j * 24 + kc + 1])
    def gu(si):
        b = si % 2
        for jj in range(22):
            pb = cnt["ng"] % 2; cnt["ng"] += 1
            for kc in range(8):
                P.mm(psg[pb][:, 0:n], wg[:, kc, jj * 128:(jj + 1) * 128], xn[b][:, kc, :], kc == 0, kc == 7,
                     reads=[("wg", kc), ("xn", b, kc)], writes=[("psg", pb)])
            for kc in range(8):
                P.mm(psu[pb][:, 0:n], wu[:, kc, jj * 128:(jj + 1) * 128], xn[b][:, kc, :], kc == 0, kc == 7,
                     reads=[("wu", kc), ("xn", b, kc)], writes=[("psu", pb)])
            P.act(sg[pb][:], psg[pb][:, 0:n], AF.Silu, reads=[("psg", pb)], writes=[("sg", pb)])
            P.tt(hmid[:, jj, :], sg[pb][:], psu[pb][:, 0:n], ALU.mult, reads=[("sg", pb), ("psu", pb)], writes=[("hmid", jj)])
    def dn(si):
        for m in range(8):
            pb = cnt["no"] % 2; cnt["no"] += 1
            for jj in range(22):
                P.mm(pso[pb][:, 0:n], wd[:, jj, m * 128:(m + 1) * 128], hmid[:, jj, :], jj == 0, jj == 21,
                     reads=[("wd", jj), ("hmid", jj)], writes=[("pso", pb)])
            P.copy(o2[:, m, :], pso[pb][:, 0:n], reads=[("pso", pb)], writes=[("o2", m)], eng="scalar")
    def post(si):
        t0 = slabs[si]; b = si % 2; j = 0 if t0 < 2048 else 1
        emit_rstd(P, o2, 8, n, sqb, ones, ps_ss, sd2, rstd2, [("o2", m) for m in range(8)], tag="2")
        for m in range(8):
            P.stt(o2[:, m, :], o2[:, m, :], Gv[:, j * 8 + m: j * 8 + m + 1], rstd2[:], ALU.mult, ALU.mult,
                  reads=[("o2", m), "Gv", "rstd2"], writes=[("o2", m)])
            P.tt(o2[:, m, :], xs[b][:, m, :], o2[:, m, :], ALU.add, reads=[("xs", b), ("o2", m)], writes=[("o2", m)], eng="gpsimd")
        P.dma("gpsimd", xov[:, :, t0:t0 + n], o2[:], reads=[("o2", m) for m in range(8)], chan="st")
    pre(0)
    for si in range(len(slabs)):
        gu(si)
        if si == 0:
            load_cast(P, w_down, wd, 22, 1024, "wd")
        if si + 1 < len(slabs):
            pre(si + 1)
        dn(si)
        post(si)
    return P.finish()

def build_l3a(with_ctx, N=512):
    NT = 2304 if with_ctx else 2048
    P = Prog()
    xT = P.dram_in("xT", [1024, NT]); ysT = P.dram_in("ysT", [384, NT]); mxT = P.dram_in("mxT", [256, NT]); naT = P.dram_in("naT", [384, NT])
    w_mod = P.dram_in("w_mod", [1024, 1024]); b_modT = P.dram_in("b_modT", [128, 8]); cT = P.dram_in("cT", [128, 16])
    g_postT = P.dram_in("g_postT", [128, 8])
    w_glu = P.dram_in("w_glu", [384, 384]); w_fourier = P.dram_in("w_fourier", [256, 256]); w_out = P.dram_in("w_out", [1024, 1024])
    xoT = P.dram_out("xoT", [1024, NT])
    wglu = P.sbuf("wglu", [128, 3, 384], BF16); wf = P.sbuf("wf", [128, 2, 256], BF16); wo = P.sbuf("wo", [128, 8, 1024], BF16)
    xs = [P.sbuf(f"xs{i}", [128, 8, N]) for i in range(2)]
    ys = [P.sbuf(f"ys{i}", [128, 3, N]) for i in range(2)]
    mx = [P.sbuf(f"mx{i}", [128, 2, N]) for i in range(2)]
    na = [P.sbuf(f"na{i}", [128, 3, N]) for i in range(2)]
    sq = P.sbuf("sq", [128, 3, N]); t1 = P.sbuf("t1", [128, 3, N]); sgm = P.sbuf("sgm", [128, 3, N])
    zf = P.sbuf("zf", [128, 3, N]); zb = P.sbuf("zb", [128, 3, N], BF16); mxb = P.sbuf("mxb", [128, 2, N], BF16)
    sg2 = [P.sbuf(f"sg2{i}", [128, N]) for i in range(2)]
    cat = P.sbuf("cat", [128, 8, N], BF16)
    sqb = P.sbuf("sqb", [128, 8, N], BF16)
    o2 = P.sbuf("o2", [128, 8, N]); tmp = [P.sbuf(f"tmp{i}", [128, N]) for i in range(2)]
    xo = [P.sbuf(f"xo{i}", [128, N]) for i in range(2)]
    ones = P.sbuf("ones", [128, 128], BF16)
    sd = P.sbuf("sd", [128, N]); rstd = P.sbuf("rstd", [128, N])
    gq_s = P.sbuf("gq_s", [128, 8]); Gv = P.sbuf("Gv", [128, 16])
    ps_mod = P.psum("ps_mod"); ps_ss = P.psum("ps_ss")
    psa = [P.psum(f"psa{i}") for i in range(3)]; pso = [P.psum(f"pso{i}") for i in range(3)]
    P.memset(ones[:], 1.0, writes=["ones"])
    P.dma("sync", gq_s[:], g_postT, writes=["gq_s"], chan="ld0")
    modv = emit_mod(P, w_mod, b_modT, cT, 8, ps_mod)
    for j in range(2):
        P.tt(Gv[:, j * 8:(j + 1) * 8], modv[:, j * 8:(j + 1) * 8], gq_s[:], ALU.mult, reads=["mmodv", "gq_s"], writes=["Gv"])
    load_cast(P, w_glu, wglu, 3, 384, "wglu")
    load_cast(P, w_fourier, wf, 2, 256, "wf")
    load_cast(P, w_out, wo, 8, 1024, "wo")
    xv = xT.rearrange("(kc p) t -> p kc t", p=128); xov = xoT.rearrange("(kc p) t -> p kc t", p=128)
    ysv = ysT.rearrange("(kc p) t -> p kc t", p=128); mxv = mxT.rearrange("(kc p) t -> p kc t", p=128); nav = naT.rearrange("(kc p) t -> p kc t", p=128)
    na_ = 0; no = 0; nt = 0
    for si, t0 in enumerate(range(0, NT, N)):
        n = min(N, NT - t0); b = si % 2; j = 0 if t0 < 2048 else 1
        P.dma("sync", xs[b][:, :, 0:n], xv[:, :, t0:t0 + n], writes=[("xs", b)], chan=f"xs{b}")
        P.dma("sync", ys[b][:, :, 0:n], ysv[:, :, t0:t0 + n], writes=[("ys", b)], chan=f"ys{b}")
        P.dma("sync", mx[b][:, :, 0:n], mxv[:, :, t0:t0 + n], writes=[("mx", b)], chan=f"mx{b}")
        P.dma("sync", na[b][:, :, 0:n], nav[:, :, t0:t0 + n], writes=[("na", b)], chan=f"na{b}")
        Y = ys[b][:, :, 0:n]
        P.tt(sq[:, :, 0:n], Y, Y, ALU.mult, reads=[("ys", b)], writes=["sq"])
        P.ts(t1[:, :, 0:n], sq[:, :, 0:n], 0.044715, 1.0, ALU.mult, ALU.add, reads=["sq"], writes=["t1"])
        P.tt(sq[:, :, 0:n], t1[:, :, 0:n], Y, ALU.mult, reads=["t1", ("ys", b)], writes=["sq"])
        P.act(sgm[:, :, 0:n], sq[:, :, 0:n], AF.Sigmoid, reads=["sq"], writes=["sgm"], scale=1.5957691216057308)
        P.tt(zf[:, :, 0:n], Y, sgm[:, :, 0:n], ALU.mult, reads=[("ys", b), "sgm"], writes=["zf"])
        P.copy(zb[:, :, 0:n], zf[:, :, 0:n], reads=["zf"], writes=["zb"], eng="scalar")
        for m in range(3):
            pb = na_ % 3; na_ += 1
            for kc in range(3):
                P.mm(psa[pb][:, 0:n], wglu[:, kc, m * 128:(m + 1) * 128], zb[:, kc, 0:n], kc == 0, kc == 2,
                     reads=[("wglu", kc), "zb"], writes=[("psa", pb)])
            P.act(sg2[m % 2][:, 0:n], psa[pb][:, 0:n], AF.Sigmoid, reads=[("psa", pb)], writes=[("sg2", m % 2)])
            P.tt(cat[:, m, 0:n], zf[:, m, 0:n], sg2[m % 2][:, 0:n], ALU.mult, reads=["zf", ("sg2", m % 2)], writes=[("cat", m)])
        P.copy(mxb[:, :, 0:n], mx[b][:, :, 0:n], reads=[("mx", b)], writes=["mxb"], eng="scalar")
        for m in range(2):
            pb = na_ % 3; na_ += 1
            for kc in range(2):
                P.mm(psa[pb][:, 0:n], wf[:, kc, m * 128:(m + 1) * 128], mxb[:, kc, 0:n], kc == 0, kc == 1,
                     reads=[("wf", kc), "mxb"], writes=[("psa", pb)])
            P.copy(cat[:, 3 + m, 0:n], psa[pb][:, 0:n], reads=[("psa", pb)], writes=[("cat", 3 + m)], eng="scalar")
        P.copy(cat[:, 5:8, 0:n], na[b][:, :, 0:n], reads=[("na", b)], writes=[("cat", 5), ("cat", 6), ("cat", 7)], eng="scalar")
        for m in range(8):
            pb = no % 3; no += 1
            for kc in range(8):
                P.mm(pso[pb][:, 0:n], wo[:, kc, m * 128:(m + 1) * 128], cat[:, kc, 0:n], kc == 0, kc == 7,
                     reads=[("wo", kc), ("cat", kc)], writes=[("pso", pb)])
            P.copy(o2[:, m, 0:n], pso[pb][:, 0:n], reads=[("pso", pb)], writes=[("o2", m)], eng="scalar")
        emit_rstd(P, o2, 8, n, sqb, ones, ps_ss, sd, rstd, [("o2", m) for m in range(8)])
        for m in range(8):
            P.stt(o2[:, m, 0:n], o2[:, m, 0:n], Gv[:, j * 8 + m: j * 8 + m + 1], rstd[:, 0:n], ALU.mult, ALU.mult,
                  reads=[("o2", m), "Gv", "rstd"], writes=[("o2", m)])
            P.tt(o2[:, m, 0:n], xs[b][:, m, 0:n], o2[:, m, 0:n], ALU.add, reads=[("xs", b), ("o2", m)], writes=[("o2", m)])
        P.dma("gpsimd", xov[:, :, t0:t0 + n], o2[:, :, 0:n], reads=[("o2", m) for m in range(8)], chan="st")
    return P.finish()

EPS = 1e-6
NT = 2304
SLABS = [(0, 512), (512, 512), (1024, 512), (1536, 512), (2048, 256)]

def build_l1():
    P = Prog(); P._stage_n = 4
    xT = P.dram_in("xT", [1024, NT])
    w_in = P.dram_in("w_in", [1024, 1792])
    w_mod = P.dram_in("w_mod", [1024, 2048])
    b_modT = P.dram_in("b_modT", [128, 16])
    g_preT = P.dram_in("g_preT", [128, 8])
    cT = P.dram_in("cT", [128, 16])
    cos = P.dram_in("cos", [128, 2048]); sin = P.dram_in("sin", [128, 2048])
    perm = P.dram_in("perm", [128, 128])
    hT = P.dram_out("hT", [640, NT])
    hTb = P.dram_out("hTb", [1152, NT])

    xs = [P.sbuf(f"xs{i}", [128, 8, 512]) for i in range(2)]
    sqb = P.sbuf("sqb", [128, 8, 512], BF16)
    tt_ = [P.sbuf(f"tt{i}", [128, 512]) for i in range(2)]
    xn = [P.sbuf(f"xn{i}", [128, 8, 512], BF16) for i in range(2)]
    w_bf = P.sbuf("w_bf", [128, 8, 1792], BF16)
    ones = P.sbuf("ones", [128, 128], BF16)
    sd = P.sbuf("sd", [128, 512]); rstd = P.sbuf("rstd", [128, 512])
    ho = [P.sbuf(f"ho{i}", [128, 512]) for i in range(4)]
    hob = [P.sbuf(f"hob{i}", [128, 512]) for i in range(4)]
    r1 = [P.sbuf(f"r1{i}", [128, 512]) for i in range(2)]
    r2 = [P.sbuf(f"r2{i}", [128, 512]) for i in range(2)]
    cos_s = P.sbuf("cos_s", [128, 2048]); sin_s = P.sbuf("sin_s", [128, 2048])
    perm_s = P.sbuf("perm_s", [128, 128])
    gp_s = P.sbuf("gp_s", [128, 8]); Av = P.sbuf("Av", [128, 16])
    ps_mod = P.psum("ps_mod"); ps_ss = P.psum("ps_ss")
    psm = [P.psum(f"psm{i}") for i in range(4)]
    psr = [P.psum(f"psr{i}") for i in range(2)]

    P.dma("sync", gp_s[:], g_preT, writes=["gp_s"], chan="ld0")
    P.dma("sync", cos_s[:], cos, writes=["cos_s"], chan="ld0")
    P.dma("sync", sin_s[:], sin, writes=["sin_s"], chan="ld0")
    P.dma("sync", perm_s[:], perm, writes=["perm_s"], chan="ld0")
    P.memset(ones[:], 1.0, writes=["ones"])
    modv = emit_mod(P, w_mod, b_modT, cT, 16, ps_mod)
    for j in range(2):
        P.stt(Av[:, j * 8:(j + 1) * 8], modv[:, j * 16 + 8: j * 16 + 16], 1.0, gp_s[:], ALU.add, ALU.mult,
              reads=["mmodv", "gp_s"], writes=["Av"])
    load_cast(P, w_in, w_bf, 8, 1792, "w_bf", piece=1024)
    xv = xT.rearrange("(kc p) t -> p kc t", p=128)
    cnt = dict(nho=0, nr=0, nps=0)
    def pre(si):
        t0, n = SLABS[si]; b = si % 2; j = 0 if si < 4 else 1
        P.dma("sync", xs[b][:, :, 0:n], xv[:, :, t0:t0 + n], writes=[("xs", b)], chan=f"xs{b}")
        P.act(sqb[:, :, 0:n], xs[b][:, :, 0:n], AF.Square, reads=[("xs", b)], writes=["sqb"])
        for kc in range(8):
            P.mm(ps_ss[:, 0:n], ones[:], sqb[:, kc, 0:n], kc == 0, kc == 7, reads=["sqb", "ones"], writes=["ps_ss"])
        P.act(sd[:, 0:n], ps_ss[:, 0:n], AF.Sqrt, reads=["ps_ss"], writes=["sd"], scale=1.0 / 1024, bias=EPS)
        P.op("vector", lambda e: e.reciprocal(out=rstd[:, 0:n], in_=sd[:, 0:n]), reads=["sd"], writes=["rstd"])
        for kc in range(8):
            P.tt(tt_[kc % 2][:, 0:n], xs[b][:, kc, 0:n], rstd[:, 0:n], ALU.mult, reads=[("xs", b), "rstd"], writes=[("tt", kc % 2)])
            P.act(xn[b][:, kc, 0:n], tt_[kc % 2][:, 0:n], AF.Identity, reads=[("tt", kc % 2), "Av", "mmodv"], writes=[("xn", b, kc)],
                  scale=Av[:, j * 8 + kc: j * 8 + kc + 1], bias=modv[:, j * 16 + kc: j * 16 + kc + 1])
    pending = []
    def flush():
        while pending:
            pending.pop(0)()
    def main(si):
        t0, n = SLABS[si]; b = si % 2
        for m in range(14):
            pb = cnt["nps"] % 4; cnt["nps"] += 1
            for kc in range(8):
                P.mm(psm[pb][:, 0:n], w_bf[:, kc, m * 128:(m + 1) * 128], xn[b][:, kc, 0:n], kc == 0, kc == 7,
                     reads=[("w_bf", kc), ("xn", b, kc)], writes=[("psm", pb)])
            flush()
            hb = cnt["nho"] % 4; cnt["nho"] += 1
            if m < 5:
                P.copy(ho[hb][:, 0:n], psm[pb][:, 0:n], reads=[("psm", pb)], writes=[("ho", hb)], eng="scalar")
                P.dma("gpsimd", hT[m * 128:(m + 1) * 128, t0:t0 + n], ho[hb][:, 0:n], reads=[("ho", hb)], chan=f"sto{hb}")
            elif m <= 10 and si < 4:
                rb = cnt["nr"] % 2; cnt["nr"] += 1
                P.copy(ho[hb][:, 0:n], psm[pb][:, 0:n], reads=[("psm", pb)], writes=[("ho", hb)], eng="scalar")
                def rope(hb=hb, rb=rb, m=m, t0=t0, n=n):
                    P.mm(psr[rb][:, 0:n], perm_s[:], ho[hb][:, 0:n], True, True, reads=["perm_s", ("ho", hb)], writes=[("psr", rb)])
                    P.tt(r1[rb][:, 0:n], ho[hb][:, 0:n], cos_s[:, t0:t0 + n], ALU.mult, reads=[("ho", hb), "cos_s"], writes=[("r1", rb)])
                    P.tt(r2[rb][:, 0:n], psr[rb][:, 0:n], sin_s[:, t0:t0 + n], ALU.mult, reads=[("psr", rb), "sin_s"], writes=[("r2", rb)])
                    P.tt(hob[hb][:, 0:n], r1[rb][:, 0:n], r2[rb][:, 0:n], ALU.add, reads=[("r1", rb), ("r2", rb)], writes=[("hob", hb)], eng="gpsimd")
                    P.dma("gpsimd", hTb[(m - 5) * 128:(m - 4) * 128, t0:t0 + n], hob[hb][:, 0:n], reads=[("hob", hb)], chan=f"stb{hb}")
                pending.append(rope)
            else:
                P.copy(hob[hb][:, 0:n], psm[pb][:, 0:n], reads=[("psm", pb)], writes=[("hob", hb)], eng="scalar")
                P.dma("gpsimd", hTb[(m - 5) * 128:(m - 4) * 128, t0:t0 + n], hob[hb][:, 0:n], reads=[("hob", hb)], chan=f"stb{hb}")
    pre(0)
    for si in range(len(SLABS)):
        if si + 1 < len(SLABS):
            pre(si + 1)
        main(si)
    flush()
    return P.finish()


def fnet_consts():
    n1 = np.arange(128); a = 2 * np.pi * np.outer(n1, n1) / 128
    cs128 = np.concatenate([np.cos(a), -np.sin(a)], 1).astype(np.float32)
    n2 = np.arange(64); a = 2 * np.pi * np.outer(n2, n2) / 64
    C64, S64 = np.cos(a), np.sin(a)
    fb1 = np.concatenate([C64, -S64], 1).astype(np.float32); fb2 = np.concatenate([S64, C64], 1).astype(np.float32)
    a = 2 * np.pi * np.outer(n2, n1) / 8192
    tw = np.concatenate([np.cos(a), np.sin(a)], 1).astype(np.float32)
    fc = (np.concatenate([C64, S64], 1) / np.sqrt(8192 * 64)).astype(np.float32)
    n = np.arange(256); a = 2 * np.pi * np.outer(n, n) / 256
    cs = np.concatenate([np.cos(a), -np.sin(a)], 1)
    cs256 = np.ascontiguousarray(cs.reshape(2, 128, 512).transpose(1, 0, 2)).astype(np.float32)
    fcc = (np.concatenate([C64, S64], 1) / np.sqrt(256 * 64)).astype(np.float32)
    return dict(cs128=cs128, fb1=fb1, fb2=fb2, tw=tw, fc=fc, cs256=cs256, fcc=fcc)

def build_fnet(with_ctx):
    P = Prog()
    z = P.dram_in("z", [128, 4096]); cs128 = P.dram_in("cs128", [128, 256])
    fb1 = P.dram_in("fb1", [64, 128]); fb2 = P.dram_in("fb2", [64, 128]); tw = P.dram_in("tw", [64, 256]); fc = P.dram_in("fc", [64, 128])
    R = P.dram_out("R", [64, 8192])
    zs = P.sbuf("zs", [128, 64, 64]); cs_s = P.sbuf("cs_s", [128, 256])
    fb1_s = P.sbuf("fb1_s", [64, 128], BF16); fb2_s = P.sbuf("fb2_s", [64, 128], BF16); tw_s = P.sbuf("tw_s", [64, 256]); fc_s = P.sbuf("fc_s", [64, 128], BF16)
    Ych = [P.sbuf(f"Ych{i}", [64, 8, 256]) for i in range(2)]
    ta = P.sbuf("ta", [64, 8, 128]); tb = P.sbuf("tb", [64, 8, 128]); tc = P.sbuf("tc", [64, 8, 128]); td = P.sbuf("td", [64, 8, 128])
    Yp = P.sbuf("Yp", [64, 64, 256], BF16); X1 = P.sbuf("X1", [64, 2, 8192], BF16)
    Rs = [P.sbuf(f"Rs{i}", [64, 512]) for i in range(2)]
    psA = [P.psum(f"psA{i}") for i in range(3)]; psB = [P.psum(f"psB{i}") for i in range(3)]; psC = [P.psum(f"psC{i}") for i in range(2)]
    P.dma("sync", zs[:].rearrange("p a b -> p (a b)"), z, writes=["zs"], chan="ld0")
    for (s, d, nm) in ((cs_s, cs128, "cs_s"), (tw_s, tw, "tw_s")):
        P.dma("sync", s[:], d, writes=[nm], chan="ld1")
    for (s, d, nm) in ((fb1_s, fb1, "fb1_s"), (fb2_s, fb2, "fb2_s"), (fc_s, fc, "fc_s")):
        P.dma("gpsimd", s[:], d, writes=[nm], chan="ldc")
    twr = tw_s[:, 0:128].rearrange("p (o k) -> p o k", o=1).to_broadcast([64, 8, 128])
    tws = tw_s[:, 128:256].rearrange("p (o k) -> p o k", o=1).to_broadcast([64, 8, 128])
    na_ = 0
    for ch in range(8):
        cb = ch % 2
        for pair in range(4):
            pb = na_ % 3; na_ += 1
            for h in range(2):
                c = ch * 8 + pair * 2 + h
                P.mm(psA[pb][0:64, h * 256:(h + 1) * 256], zs[:, :, c], cs_s[:], True, True, reads=["zs", "cs_s"], writes=[("psA", pb)])
            P.copy(Ych[cb][:, pair * 2:pair * 2 + 2, :].rearrange("p a b -> p (a b)"), psA[pb][0:64, :], reads=[("psA", pb)],
                   writes=[("Ych", cb)], eng="scalar")
        Yr = Ych[cb][:, :, 0:128]; Yi = Ych[cb][:, :, 128:256]; c0 = ch * 8
        P.tt(ta[:], Yr, twr, ALU.mult, reads=[("Ych", cb), "tw_s"], writes=["ta"])
        P.tt(tb[:], Yi, tws, ALU.mult, reads=[("Ych", cb), "tw_s"], writes=["tb"])
        P.tt(Yp[:, c0:c0 + 8, 0:128], ta[:], tb[:], ALU.add, reads=["ta", "tb"], writes=[("Yp", ch)])
        P.tt(tc[:], Yi, twr, ALU.mult, reads=[("Ych", cb), "tw_s"], writes=["tc"])
        P.tt(td[:], Yr, tws, ALU.mult, reads=[("Ych", cb), "tw_s"], writes=["td"])
        P.tt(Yp[:, c0:c0 + 8, 128:256], tc[:], td[:], ALU.subtract, reads=["tc", "td"], writes=[("Yp", ch)])
    allYp = [("Yp", ch) for ch in range(8)]
    X1v = X1[:].rearrange("p c (k2 k1) -> p c k2 k1", k1=128)
    for g in range(32):
        pb = g % 3
        for q in range(4):
            k1 = 4 * g + q
            P.mm(psB[pb][0:64, q * 128:(q + 1) * 128], Yp[:, :, k1], fb1_s[:], True, False, reads=allYp + ["fb1_s"], writes=[("psB", pb)])
            P.mm(psB[pb][0:64, q * 128:(q + 1) * 128], Yp[:, :, 128 + k1], fb2_s[:], False, True, reads=allYp + ["fb2_s"], writes=[("psB", pb)])
        pv = psB[pb][0:64, :].rearrange("p (q c k) -> p c k q", q=4, c=2)
        for comp in range(2):
            P.copy(X1v[:, comp, :, 4 * g:4 * g + 4], pv[:, comp, :, :], reads=[("psB", pb)], writes=[("X1", g)],
                   eng=("scalar" if comp == 0 else "vector"))
    allX1 = [("X1", g) for g in range(32)]
    for blk in range(16):
        pb = blk % 2
        P.mm(psC[pb][0:64, :], fc_s[:, 0:64], X1[:, 0, blk * 512:(blk + 1) * 512], True, False, reads=allX1 + ["fc_s"], writes=[("psC", pb)])
        P.mm(psC[pb][0:64, :], fc_s[:, 64:128], X1[:, 1, blk * 512:(blk + 1) * 512], False, True, reads=allX1 + ["fc_s"], writes=[("psC", pb)])
        P.copy(Rs[pb][:], psC[pb][0:64, :], reads=[("psC", pb)], writes=[("Rs", pb)], eng="scalar")
        P.dma("gpsimd", R[:, blk * 512:(blk + 1) * 512], Rs[pb][:], reads=[("Rs", pb)], chan=f"st{pb}")
    if with_ctx:
        zc = P.dram_in("zc", [128, 2, 64]); cs256 = P.dram_in("cs256", [128, 2, 512]); fcc = P.dram_in("fcc", [64, 128])
        Rc = P.dram_out("Rc", [64, 256])
        zc_s = P.sbuf("zc_s", [128, 2, 64]); c2_s = P.sbuf("c2_s", [128, 2, 512]); fcc_s = P.sbuf("fcc_s", [64, 128])
        Pc = P.sbuf("Pc", [64, 512]); Rc_s = P.sbuf("Rc_s", [64, 256])
        P.dma("sync", zc_s[:], zc, writes=["zc_s"], chan="ld2"); P.dma("sync", c2_s[:], cs256, writes=["c2_s"], chan="ld2")
        P.dma("sync", fcc_s[:], fcc, writes=["fcc_s"], chan="ld2")
        for t in range(2):
            P.mm(psA[0][0:64, :], zc_s[:, t, :], c2_s[:, t, :], t == 0, t == 1, reads=["zc_s", "c2_s"], writes=[("psA", 0)])
        P.copy(Pc[:], psA[0][0:64, :], reads=[("psA", 0)], writes=["Pc"])
        P.mm(psA[1][0:64, 0:256], fcc_s[:, 0:64], Pc[:, 0:256], True, False, reads=["Pc", "fcc_s"], writes=[("psA", 1)])
        P.mm(psA[1][0:64, 0:256], fcc_s[:, 64:128], Pc[:, 256:512], False, True, reads=["Pc", "fcc_s"], writes=[("psA", 1)])
        P.copy(Rc_s[:], psA[1][0:64, 0:256], reads=[("psA", 1)], writes=["Rc_s"])
        P.dma("gpsimd", Rc, Rc_s[:], reads=["Rc_s"], chan="st")
    return P.finish()

NEG = -30000.0

def build_na(with_ctx):
    P = Prog()
    qT = P.dram_in("qT", [384, 2048]); kwT = P.dram_in("kwT", [16, 384, 576]); vw = P.dram_in("vw", [16, 128, 5, 384])
    kcT = P.dram_in("kcT", [384, 256]); vc = P.dram_in("vc", [128, 2, 384])
    tbraw = P.dram_in("tbraw", [5, 128, 6, 576]); mask = P.dram_in("mask", [5, 128, 576]); ident = P.dram_in("ident", [128, 128])
    Y = P.dram_out("Y", [2048, 384])
    qb = P.sbuf("qb", [128, 3, 2048], BF16); kcb = P.sbuf("kcb", [128, 3, 256], BF16); vcb = P.sbuf("vcb", [128, 2, 384], BF16)
    TB = P.sbuf("TB", [128, 5, 6, 576]); mk = P.sbuf("mk", [128, 5, 576])
    idb = P.sbuf("idb", [128, 128], BF16)
    kb = [P.sbuf(f"kb{i}", [128, 3, 576], BF16) for i in range(2)]; vb = [P.sbuf(f"vb{i}", [128, 5, 384], BF16) for i in range(2)]
    S = [P.sbuf(f"S{i}", [128, 832]) for i in range(4)]; Pb = [P.sbuf(f"Pb{i}", [128, 832], BF16) for i in range(4)]
    PT = [P.sbuf(f"PT{i}", [128, 896], BF16) for i in range(4)]
    Osb = [P.sbuf(f"Osb{i}", [128, 384]) for i in range(2)]
    mx = [P.sbuf(f"mx{i}", [128, 1]) for i in range(8)]; ssum = [P.sbuf(f"ssum{i}", [128, 1]) for i in range(8)]
    rinv = [P.sbuf(f"rinv{i}", [128, 1]) for i in range(8)]
    psA = [P.psum(f"psA{i}") for i in range(2)]; psB = [P.psum(f"psB{i}") for i in range(2)]
    psT = [P.psum(f"psT{i}", [128, 1024], BF16) for i in range(2)]; psO = [P.psum(f"psO{i}") for i in range(2)]
    si = [0]
    def load_cast(dst, src_ap, n, dres):
        P.dma("gpsimd", dst, src_ap, writes=[dres], chan="ldc", max_dma_last_dim=4096)
    qv = qT.rearrange("(c p) n -> p c n", p=128)
    for c in range(3):
        load_cast(qb[:, c, :], qv[:, c, :], 2048, "qb")
    kcv = kcT.rearrange("(c p) n -> p c n", p=128)
    for c in range(3):
        load_cast(kcb[:, c, :], kcv[:, c, :], 256, "kcb")
    load_cast(vcb[:].rearrange("p a b -> p (a b)"), vc.rearrange("p a b -> p (a b)"), 768, "vcb")
    load_cast(idb[:], ident, 128, "idb")
    for ty in range(5):
        P.dma("sync", TB[:, ty, :, :], tbraw[ty], writes=[("TB", ty)], chan="ld0")
    P.dma("sync", mk[:], mask.rearrange("t p n -> p t n"), writes=["mk"], chan="ld0")
    for ty in range(5):
        mb = mk[:, ty, :].rearrange("p (o n) -> p o n", o=1).to_broadcast([128, 6, 576])
        P.tt(TB[:, ty, :, :], TB[:, ty, :, :], mb, ALU.add, reads=[("TB", ty), "mk"], writes=[("TB", ty)])
    tiles = [("main", t) for t in range(16)]
    if with_ctx:
        qcT = P.dram_in("qcT", [384, 256]); Yc = P.dram_out("Yc", [256, 384])
        qcb = P.sbuf("qcb", [128, 3, 256], BF16)
        qcv = qcT.rearrange("(c p) n -> p c n", p=128)
        for c in range(3):
            load_cast(qcb[:, c, :], qcv[:, c, :], 256, "qcb")
        tiles += [("ctx", 0), ("ctx", 1)]
    units = [(ti, kind, t, h) for ti, (kind, t) in enumerate(tiles) for h in range(6)]
    loaded = set()
    def tile_load(ti, kind, t):
        if ti in loaded or kind != "main":
            return
        loaded.add(ti)
        wb = ti % 2
        P.dma("gpsimd", kb[wb][:], kwT[t].rearrange("(c p) n -> p c n", p=128), writes=[("kb", wb)], chan=f"kb{wb}", max_dma_last_dim=4096)
        P.dma("gpsimd", vb[wb][:], vw[t], writes=[("vb", wb)], chan=f"vb{wb}", max_dma_last_dim=4096)
    def info(u):
        ti, kind, t, h = units[u]
        return ti, kind, t, h, u % 2, u % 4, ti % 2, h // 2, (h % 2) * 64, u % 8
    def stA(u):
        ti, kind, t, h, b, b3, wb, c, p0, b8 = info(u)
        tile_load(ti, kind, t)
        if kind == "main":
            qs = qb[p0:p0 + 64, c, t * 128:(t + 1) * 128]; qres = "qb"
        else:
            qs = qcb[p0:p0 + 64, c, t * 128:(t + 1) * 128]; qres = "qcb"
        P.mm(psB[b][:, 64:320], qs, kcb[p0:p0 + 64, c, :], True, True, reads=[qres, "kcb"], writes=[("psB", b)])
        if kind == "main":
            P.mm(psA[b][:, 0:512], qs, kb[wb][p0:p0 + 64, c, 0:512], True, True, reads=[qres, ("kb", wb)], writes=[("psA", b)])
            P.mm(psB[b][:, 0:64], qs, kb[wb][p0:p0 + 64, c, 512:576], True, True, reads=[qres, ("kb", wb)], writes=[("psB", b)])
        P.act(S[b3][:, 0:256], psB[b][:, 64:320], AF.Copy, reads=[("psB", b)], writes=[("Sc", b3)], scale=0.125)
    def stB(u):
        ti, kind, t, h, b, b3, wb, c, p0, b8 = info(u)
        W = 832 if kind == "main" else 256
        if kind == "main":
            ty = {0: 0, 1: 1, 14: 3, 15: 4}.get(t, 2)
            P.stt(S[b3][:, 256:768], psA[b][:, 0:512], 0.125, TB[:, ty, h, 0:512], ALU.mult, ALU.add,
                  reads=[("psA", b), ("TB", ty)], writes=[("Sw", b3)])
            P.stt(S[b3][:, 768:832], psB[b][:, 0:64], 0.125, TB[:, ty, h, 512:576], ALU.mult, ALU.add,
                  reads=[("psB", b), ("TB", ty)], writes=[("Sw2", b3)])
        P.op("vector", lambda e: e.tensor_reduce(out=mx[b8][:], in_=S[b3][:, 0:W], axis=AX.X, op=ALU.max, negate=True),
             reads=[("Sc", b3), ("Sw", b3), ("Sw2", b3)], writes=[("mx", b8)])
        P.act(Pb[b3][:, 0:W], S[b3][:, 0:W], AF.Exp, reads=[("Sc", b3), ("Sw", b3), ("Sw2", b3), ("mx", b8)], writes=[("Pb", b3), ("ssum", b8)],
              bias=mx[b8][:], scale=1.0, accum_out=ssum[b8][:])
    def stC(u):
        ti, kind, t, h, b, b3, wb, c, p0, b8 = info(u)
        nblk = 7 if kind == "main" else 2
        for kbk in range(nblk):
            kw = 64 if kbk == 6 else 128
            P.op("tensor", lambda e, kbk=kbk, kw=kw: e.transpose(psT[b][0:kw, kbk * 128:(kbk + 1) * 128], Pb[b3][:, kbk * 128:kbk * 128 + kw], idb[:]),
                 reads=[("Pb", b3), "idb"], writes=[("psT", b)], nosync_same=True)
        P.copy(PT[b3][:, 0:nblk * 128], psT[b][:, 0:nblk * 128], reads=[("psT", b)], writes=[("PT", b3)], eng="scalar")
    def stD(u):
        ti, kind, t, h, b, b3, wb, c, p0, b8 = info(u)
        ob = ti % 2
        nblk = 7 if kind == "main" else 2
        for kbk in range(nblk):
            kw = 64 if kbk == 6 else 128
            if kbk < 2:
                rhs = vcb[:, kbk, h * 64:(h + 1) * 64]; rres = "vcb"
            else:
                rhs = vb[wb][0:kw, kbk - 2, h * 64:(h + 1) * 64]; rres = ("vb", wb)
            P.mm(psO[ob][:, h * 64:(h + 1) * 64], PT[b3][0:kw, kbk * 128:(kbk + 1) * 128], rhs, kbk == 0, kbk == nblk - 1,
                 reads=[("PT", b3), rres], writes=[("psO", ob)])
        P.op("vector", lambda e: e.reciprocal(out=rinv[b8][:], in_=ssum[b8][:]), reads=[("ssum", b8)], writes=[("rinv", b8)])
        P.ts(Osb[ob][:, h * 64:(h + 1) * 64], psO[ob][:, h * 64:(h + 1) * 64], rinv[b8][:], None, ALU.mult,
             reads=[("psO", ob), ("rinv", b8)], writes=[("Osb", ob)])
        if h == 5:
            dst = Y[t * 128:(t + 1) * 128, :] if kind == "main" else Yc[t * 128:(t + 1) * 128, :]
            P.dma("gpsimd", dst, Osb[ob][:], reads=[("Osb", ob)], chan=f"st{ob}")
    NU = len(units)
    LC, LD = 3, 5
    for step in range(NU + LD):
        if step < NU: stA(step)
        if 0 <= step - 1 < NU: stB(step - 1)
        if 0 <= step - LC < NU: stC(step - LC)
        if 0 <= step - LD < NU: stD(step - LD)
    return P.finish()

def na_tile_geometry(r0):
    R = 128
    rs0 = int(np.clip(r0 - 4, 0, R - 8)); rs1 = int(np.clip(r0 + 1 - 4, 0, R - 8))
    return rs0, rs1

def na_tables(rpb, q):
    cols = np.arange(64); cs = np.clip(cols - 8, 0, 48)
    kc = np.arange(64)
    inwin = (kc[None, :] >= cs[:, None]) & (kc[None, :] < cs[:, None] + 16)
    dc = np.clip(kc[None, :] - cols[:, None] + 15, 0, 30)
    types = [32 * q, 32 * q + 2, 32 * q + 16, 32 * q + 28, 32 * q + 30]
    tbraw = np.zeros((5, 128, 6, 9, 64), np.float32); mask = np.zeros((5, 128, 9, 64), np.float32)
    for ti, r0 in enumerate(types):
        rs0, rs1 = na_tile_geometry(r0)
        for half, (r, rs) in enumerate(((r0, rs0), (r0 + 1, rs1))):
            for slot in range(9):
                krow = rs0 + slot
                valid = (krow >= rs) and (krow < rs + 8)
                dr = int(np.clip(krow - r + 7, 0, 14))
                g = rpb[:, dr][:, dc]
                tbraw[ti, half * 64:(half + 1) * 64, :, slot, :] = g.transpose(1, 0, 2)
                m = np.where(inwin & valid, 0.0, NEG).astype(np.float32)
                mask[ti, half * 64:(half + 1) * 64, slot, :] = m
    return tbraw.reshape(5, 128, 6, 576), mask.reshape(5, 128, 576)

def na_windows(k_b, v_b, q):
    kp = np.concatenate([k_b, np.zeros((64 * 16, 384), k_b.dtype)], 0); vp = np.concatenate([v_b, np.zeros((64 * 16, 384), v_b.dtype)], 0)
    kwT = np.zeros((16, 384, 576), k_b.dtype); vw = np.zeros((16, 128, 5, 384), v_b.dtype)
    for t in range(16):
        r0 = 32 * q + 2 * t
        rs0, _ = na_tile_geometry(r0)
        kwT[t] = kp[rs0 * 64:(rs0 + 9) * 64].T
        for j in range(4):
            vw[t, :, j, :] = vp[(rs0 + 2 * j) * 64:(rs0 + 2 * j + 2) * 64]
        vw[t, 0:64, 4, :] = vp[(rs0 + 8) * 64:(rs0 + 9) * 64]
    return kwT, vw

NCH = 1056
PI = math.pi

class A:
    def __init__(self, P): self.P = P
    @staticmethod
    def nm(*aps): return [a.tensor.name for a in aps if hasattr(a, "tensor")]
    def tt(self, o, a, b, op, eng="vector"): self.P.tt(o, a, b, op, reads=self.nm(a, b), writes=self.nm(o), eng=eng)
    def ts(self, o, a, s1, op0, s2=None, op1=None, eng="vector"):
        self.P.ts(o, a, s1, s2, op0, op1, reads=self.nm(a, s1, s2), writes=self.nm(o), eng=eng)
    def stt(self, o, a, s, b, op0, op1): self.P.stt(o, a, s, b, op0, op1, reads=self.nm(a, s, b), writes=self.nm(o))
    def act(self, o, a, f, **kw): self.P.act(o, a, f, reads=self.nm(a, *[v for v in kw.values()]), writes=self.nm(o), **kw)
    def copy(self, o, a, eng="vector"): self.P.copy(o, a, reads=self.nm(a), writes=self.nm(o), eng=eng)
    def memset(self, o, v, eng="vector"): self.P.memset(o, v, writes=self.nm(o), eng=eng)
    def mm(self, o, l, r, st, sp): self.P.mm(o, l, r, st, sp, reads=self.nm(l, r), writes=self.nm(o))
    def dma_in(self, o, src, chan): self.P.dma("sync", o, src, writes=self.nm(o), chan=chan)
    def dma_out(self, dst, a, chan="st"): self.P.dma("gpsimd", dst, a, reads=self.nm(a), chan=chan)
    def scan(self, o, d0, d1, init):
        self.P.op("vector", lambda e: e.tensor_tensor_scan(out=o, data0=d0, data1=d1, initial=init, op0=ALU.mult, op1=ALU.add),
                  reads=self.nm(d0, d1, init), writes=self.nm(o))
    def recip(self, o, a): self.P.op("vector", lambda e: e.reciprocal(out=o, in_=a), reads=self.nm(a), writes=self.nm(o))
    def transpose(self, o, a, ident):
        self.P.op("tensor", lambda e: e.transpose(o, a, ident), reads=self.nm(a, ident), writes=self.nm(o), nosync_same=True)
    def cmul_s(self, o_re, o_im, a_re, a_im, s_re, s_im, s_imn):
        self.ts(o_re, a_re, s_re, ALU.mult)
        self.stt(o_re, a_im, s_imn, o_re, ALU.mult, ALU.add)
        self.ts(o_im, a_re, s_im, ALU.mult)
        self.stt(o_im, a_im, s_re, o_im, ALU.mult, ALU.add)

def build_ssm():
    P = Prog(); a = A(P)
    d_in = {}
    for nm_, shp in (("are", [128, 6]), ("aim", [128, 6]), ("ldt", [128, 6]), ("Bre", [128, 96]), ("Bim", [128, 96]),
                     ("Cre", [128, 96]), ("Cim", [128, 96]), ("maskF", [128, 128]), ("maskB", [128, 128]), ("sgn", [128, 1]),
                     ("ident", [128, 128])):
        d_in[nm_] = P.dram_in(nm_, shp)
    Ddiag = P.dram_in("Ddiag", [6, 128, 128]); U = P.dram_in("U", [6, 128, NCH]); Yg = P.dram_out("Yg", [6, 128, NCH])
    s = {}
    for nm_, ap in d_in.items():
        shp = list(ap.shape)
        s[nm_] = P.sbuf("s_" + nm_, shp)
        a.dma_in(s[nm_][:], ap, "ld0")
    def T(name, shape): return P.sbuf(name, shape)
    dt = T("dt", [128, 6]); x = T("x", [128, 6]); th = T("th", [128, 6]); er = T("er", [128, 6]); m = T("m", [128, 6])
    y2 = T("y2", [128, 6]); sn = T("sn", [128, 6]); cs = T("cs", [128, 6]); lbr = T("lbr", [128, 6]); lbi = T("lbi", [128, 6])
    n2 = T("n2", [128, 6]); t1 = T("t1", [128, 6]); t2 = T("t2", [128, 6]); am1 = T("am1", [128, 6])
    qr = T("qr", [128, 6]); qi = T("qi", [128, 6]); qin = T("qin", [128, 6])
    a.act(dt[:], s["ldt"][:], AF.Exp)
    a.tt(x[:], s["are"][:], dt[:], ALU.mult); a.tt(th[:], s["aim"][:], dt[:], ALU.mult)
    a.act(er[:], x[:], AF.Exp)
    for _ in range(4):
        a.ts(m[:], th[:], PI, ALU.is_gt)
        a.stt(th[:], m[:], -2 * PI, th[:], ALU.mult, ALU.add)
    a.ts(y2[:], th[:], PI / 2, ALU.add)
    a.ts(m[:], y2[:], PI, ALU.is_gt)
    a.stt(y2[:], m[:], -2 * PI, y2[:], ALU.mult, ALU.add)
    a.act(sn[:], th[:], AF.Sin); a.act(cs[:], y2[:], AF.Sin)
    a.tt(lbr[:], er[:], cs[:], ALU.mult); a.tt(lbi[:], er[:], sn[:], ALU.mult)
    a.tt(n2[:], s["are"][:], s["are"][:], ALU.mult); a.tt(t1[:], s["aim"][:], s["aim"][:], ALU.mult); a.tt(n2[:], n2[:], t1[:], ALU.add)
    a.recip(n2[:], n2[:])
    a.ts(am1[:], lbr[:], -1.0, ALU.add)
    a.tt(t1[:], am1[:], s["are"][:], ALU.mult); a.tt(t2[:], lbi[:], s["aim"][:], ALU.mult); a.tt(t1[:], t1[:], t2[:], ALU.add)
    a.tt(qr[:], t1[:], n2[:], ALU.mult)
    a.tt(t1[:], lbi[:], s["are"][:], ALU.mult); a.tt(t2[:], am1[:], s["aim"][:], ALU.mult); a.tt(t1[:], t1[:], t2[:], ALU.subtract)
    a.tt(qi[:], t1[:], n2[:], ALU.mult)
    a.ts(qin[:], qi[:], -1.0, ALU.mult)
    Lr = T("Lr", [128, 6, 9]); Li = T("Li", [128, 6, 9]); Vr = T("Vr", [128, 6, 8]); Vi = T("Vi", [128, 6, 8])
    Rr = T("Rr", [128, 6, 9]); Ri = T("Ri", [128, 6, 9])
    e2 = T("e2", [128, 6]); ivr = T("ivr", [128, 6]); ivi = T("ivi", [128, 6])
    a.memset(Lr[:, :, 0], 1.0); a.memset(Li[:, :, 0], 0.0); a.memset(Vr[:, :, 0], 1.0); a.memset(Vi[:, :, 0], 0.0)
    a.act(e2[:], x[:], AF.Exp, scale=-2.0)
    a.tt(ivr[:], lbr[:], e2[:], ALU.mult); a.tt(ivi[:], lbi[:], e2[:], ALU.mult); a.ts(ivi[:], ivi[:], -1.0, ALU.mult)
    def cmul_t(o_r, o_i, p_r, p_i, q_r, q_i):
        a.tt(t1[:], p_r, q_r, ALU.mult); a.tt(t2[:], p_i, q_i, ALU.mult); a.tt(o_r, t1[:], t2[:], ALU.subtract)
        a.tt(t1[:], p_r, q_i, ALU.mult); a.tt(t2[:], p_i, q_r, ALU.mult); a.tt(o_i, t1[:], t2[:], ALU.add)
    for k in range(8):
        cmul_t(Lr[:, :, k + 1], Li[:, :, k + 1], Lr[:, :, k], Li[:, :, k], lbr[:], lbi[:])
    for k in range(7):
        cmul_t(Vr[:, :, k + 1], Vi[:, :, k + 1], Vr[:, :, k], Vi[:, :, k], ivr[:], ivi[:])
    for k in range(9):
        a.copy(Rr[:, :, k], Lr[:, :, 8 - k], eng="gpsimd"); a.copy(Ri[:, :, k], Li[:, :, 8 - k], eng="gpsimd")
    tabs = {}
    for nm_, (lo_r, lo_i, hi_r, hi_i) in dict(
            XL=(Vr[0:64, :, 0:8], Vi[0:64, :, 0:8], Lr[64:128, :, 0:8], Li[64:128, :, 0:8]),
            YL=(Lr[0:64, :, 0:8], Li[0:64, :, 0:8], Vr[64:128, :, 0:8], Vi[64:128, :, 0:8]),
            SL=(Rr[0:64, :, 1:9], Ri[0:64, :, 1:9], Lr[64:128, :, 0:8], Li[64:128, :, 0:8]),
            OL=(Lr[0:64, :, 1:9], Li[0:64, :, 1:9], Rr[64:128, :, 0:8], Ri[64:128, :, 0:8])).items():
        tr = T(nm_ + "r", [128, 6, 8]); ti = T(nm_ + "i", [128, 6, 8]); tn = T(nm_ + "n", [128, 6, 8])
        a.copy(tr[0:64], lo_r); a.copy(ti[0:64], lo_i); a.copy(tr[64:128], hi_r); a.copy(ti[64:128], hi_i)
        a.ts(tn[:], ti[:], -1.0, ALU.mult)
        tabs[nm_] = (tr, ti, tn)
    rho8 = T("rho8", [128, 6]); c8 = T("c8", [128, 6]); s8 = T("s8", [128, 6]); e8 = T("e8", [128, 6])
    a.act(rho8[:], x[:], AF.Exp, scale=8.0); a.act(e8[:], x[:], AF.Exp, scale=-8.0)
    a.tt(c8[:], Lr[:, :, 8], e8[:], ALU.mult); a.tt(s8[:], Li[:, :, 8], e8[:], ALU.mult)
    a.ts(s8[:], s8[:], s["sgn"][:, 0:1], ALU.mult)
    onesT = T("onesT", [128, NCH]); a.memset(onesT[:], 1.0, eng="gpsimd")
    def bc_j(ap):
        return ap.rearrange("p g (o j) -> p g o j", o=1).to_broadcast([128, 6, 8, 16])
    def bc_k(ap):
        return ap.rearrange("p g (k o) -> p g k o", o=1).to_broadcast([128, 6, 8, 16])
    Bre_v = s["Bre"][:].rearrange("p (g j) -> p g j", j=16); Bim_v = s["Bim"][:].rearrange("p (g j) -> p g j", j=16)
    Cre_v = s["Cre"][:].rearrange("p (g j) -> p g j", j=16); Cim_v = s["Cim"][:].rearrange("p (g j) -> p g j", j=16)
    Bbr_a = T("Bbr_a", [128, 6, 16]); Bbi_a = T("Bbi_a", [128, 6, 16])
    u1 = T("u1", [128, 6, 16]); u2 = T("u2", [128, 6, 16])
    qr_b = qr[:].rearrange("p (g o) -> p g o", o=1).to_broadcast([128, 6, 16]); qi_b = qi[:].rearrange("p (g o) -> p g o", o=1).to_broadcast([128, 6, 16])
    a.tt(u1[:], Bre_v, qr_b, ALU.mult); a.tt(u2[:], Bim_v, qi_b, ALU.mult, eng="gpsimd"); a.tt(Bbr_a[:], u1[:], u2[:], ALU.subtract)
    a.tt(u1[:], Bre_v, qi_b, ALU.mult); a.tt(u2[:], Bim_v, qr_b, ALU.mult, eng="gpsimd"); a.tt(Bbi_a[:], u1[:], u2[:], ALU.add)
    v1 = T("v1", [128, 6, 8, 16]); v2 = T("v2", [128, 6, 8, 16])
    def ctab(name, Ar, Ai, tb, neg_im):
        o_r = T(name + "r_a", [128, 6, 8, 16]); o_i = T(name + "i_a", [128, 6, 8, 16])
        Sr, Si = tb[0][:], tb[1][:]
        a.tt(v1[:], bc_j(Ar), bc_k(Sr), ALU.mult); a.tt(v2[:], bc_j(Ai), bc_k(Si), ALU.mult, eng="gpsimd")
        a.tt(o_r[:], v1[:], v2[:], ALU.subtract)
        a.tt(v1[:], bc_j(Ar), bc_k(Si), ALU.mult); a.tt(v2[:], bc_j(Ai), bc_k(Sr), ALU.mult, eng="gpsimd")
        a.tt(o_i[:], v1[:], v2[:], ALU.add)
        if neg_im:
            a.ts(o_i[:], o_i[:], -1.0, ALU.mult)
        return o_r, o_i
    Xr_a, Xi_a = ctab("X", Bbr_a[:], Bbi_a[:], tabs["XL"], False)
    Wtr_a, Wti_a = ctab("Wt", Bbr_a[:], Bbi_a[:], tabs["SL"], False)
    Yr_a, Yin_a = ctab("Y", Cre_v, Cim_v, tabs["YL"], True)
    Wor_a, Woin_a = ctab("Wo", Cre_v, Cim_v, tabs["OL"], True)
    Tr_all = T("Tr_all", [128, 6, NCH + 1]); Ti_all = T("Ti_all", [128, 6, NCH + 1])
    mr = [T(f"mr{k}", [128, 6]) for k in range(11)]; mi = [T(f"mi{k}", [128, 6]) for k in range(11)]
    a.copy(mr[0][:], c8[:]); a.copy(mi[0][:], s8[:])
    for k in range(1, 11):
        a.tt(t1[:], mr[k - 1][:], mr[k - 1][:], ALU.mult); a.tt(t2[:], mi[k - 1][:], mi[k - 1][:], ALU.mult)
        a.tt(mr[k][:], t1[:], t2[:], ALU.subtract)
        a.tt(t1[:], mr[k - 1][:], mi[k - 1][:], ALU.mult); a.ts(mi[k][:], t1[:], 2.0, ALU.mult)
    a.memset(Tr_all[:, :, 0:1], 1.0); a.memset(Ti_all[:, :, 0:1], 0.0)
    z1 = T("z1", [128, 6, 512]); z2 = T("z2", [128, 6, 512])
    for k in range(11):
        n = 1 << k
        cnt_ = min(n, NCH + 1 - n)
        mrb = mr[k][:].rearrange("p (g o) -> p g o", o=1).to_broadcast([128, 6, cnt_])
        mib = mi[k][:].rearrange("p (g o) -> p g o", o=1).to_broadcast([128, 6, cnt_])
        a.tt(z1[:, :, 0:cnt_], Tr_all[:, :, 0:cnt_], mrb, ALU.mult); a.tt(z2[:, :, 0:cnt_], Ti_all[:, :, 0:cnt_], mib, ALU.mult)
        a.tt(Tr_all[:, :, n:n + cnt_], z1[:, :, 0:cnt_], z2[:, :, 0:cnt_], ALU.subtract)
        a.tt(z1[:, :, 0:cnt_], Tr_all[:, :, 0:cnt_], mib, ALU.mult); a.tt(z2[:, :, 0:cnt_], Ti_all[:, :, 0:cnt_], mrb, ALU.mult)
        a.tt(Ti_all[:, :, n:n + cnt_], z1[:, :, 0:cnt_], z2[:, :, 0:cnt_], ALU.add)
    Wsr = T("Wsr", [128, 128]); Wsi = T("Wsi", [128, 128]); Msb = T("Msb", [128, 128]); Mtmp = T("Mtmp", [128, 128]); Dd = T("Dd", [128, 128])
    Us = [T(f"Us{i}", [128, NCH]) for i in range(2)]
    Sre = T("Sre", [128, NCH]); Sim = T("Sim", [128, NCH]); Spr = T("Spr", [128, NCH]); Spi = T("Spi", [128, NCH])
    Gre = T("Gre", [128, NCH]); Gim = T("Gim", [128, NCH]); Hor = T("Hor", [128, NCH]); Hoi = T("Hoi", [128, NCH])
    Hir = T("Hir", [128, NCH]); Hii = T("Hii", [128, NCH])
    rhoT = T("rhoT", [128, NCH])
    w1 = T("w1", [128, NCH]); w2 = T("w2", [128, NCH]); w3 = T("w3", [128, NCH]); w4 = T("w4", [128, NCH])
    ini = T("ini", [128, 4]); Ysb = T("Ysb", [128, NCH])
    ps = [P.psum(f"ps{i}") for i in range(8)]
    BLK = [(0, 512), (512, 512), (1024, NCH - 1024)]
    for gi in range(6):
        ub = gi % 2
        a.dma_in(Us[ub][:], U[gi], f"u{ub}")
        a.dma_in(Dd[:], Ddiag[gi], "dd")
        Xr, Xi, Yr, Yin = Xr_a[:, gi], Xi_a[:, gi], Yr_a[:, gi], Yin_a[:, gi]
        Wtr, Wti, Wor, Woin = Wtr_a[:, gi], Wti_a[:, gi], Wor_a[:, gi], Woin_a[:, gi]
        f2 = lambda t_: t_.rearrange("p a b -> p (a b)")
        for half, pb in ((0, 6), (1, 7)):
            rows = slice(half * 64, half * 64 + 64)
            a.mm(ps[pb][:, 0:128], f2(Xr)[rows], f2(Yr)[rows], True, False)
            a.mm(ps[pb][:, 0:128], f2(Xi)[rows], f2(Yin)[rows], False, True)
        a.tt(Msb[:], ps[6][:, 0:128], s["maskF"][:], ALU.mult)
        a.tt(Mtmp[:], ps[7][:, 0:128], s["maskB"][:], ALU.mult)
        a.tt(Msb[:], Msb[:], Mtmp[:], ALU.add, eng="gpsimd"); a.tt(Msb[:], Msb[:], Dd[:], ALU.add, eng="gpsimd")
        a.transpose(ps[6][:, 128:256], f2(Wtr), s["ident"][:]); a.transpose(ps[7][:, 128:256], f2(Wti), s["ident"][:])
        a.copy(Wsr[:], ps[6][:, 128:256], eng="scalar"); a.copy(Wsi[:], ps[7][:, 128:256], eng="scalar")
        for bi, (c0, cn) in enumerate(BLK):
            a.mm(ps[bi][:, 0:cn], Wsr[:], Us[ub][:, c0:c0 + cn], True, True)
            a.mm(ps[3 + bi][:, 0:cn], Wsi[:], Us[ub][:, c0:c0 + cn], True, True)
            a.copy(Sre[:, c0:c0 + cn], ps[bi][:, 0:cn], eng="scalar"); a.copy(Sim[:, c0:c0 + cn], ps[3 + bi][:, 0:cn], eng="scalar")
        Tr = Tr_all[:, gi, :]; Ti = Ti_all[:, gi, :]
        a.act(rhoT[:], onesT[:], AF.Copy, scale=rho8[:, gi:gi + 1])
        a.tt(w1[:], Sre[:], Tr[:, 0:NCH], ALU.mult); a.tt(w3[:], Sim[:], Ti[:, 0:NCH], ALU.mult)
        a.tt(Spr[:], w1[:], w3[:], ALU.subtract)
        a.tt(w2[:], Sre[:], Ti[:, 0:NCH], ALU.mult, eng="gpsimd"); a.tt(w4[:], Sim[:], Tr[:, 0:NCH], ALU.mult, eng="gpsimd")
        a.tt(Spi[:], w2[:], w4[:], ALU.add, eng="gpsimd")
        for (Gx, Sx) in ((Gre, Spr), (Gim, Spi)):
            a.scan(Gx[0:64, :], rhoT[0:64, :], Sx[0:64, :], 0.0)
            a.scan(Gx[64:128, 0:32][:, ::-1], rhoT[64:128, 0:32], Sx[64:128, 0:32][:, ::-1], 0.0)
        lo = slice(64, 128)
        a.tt(ini[lo, 0:1], Gre[lo, 0:1], Tr[lo, NCH:NCH + 1], ALU.mult); a.tt(ini[lo, 1:2], Gim[lo, 0:1], Ti[lo, NCH:NCH + 1], ALU.mult)
        a.tt(ini[lo, 2:3], ini[lo, 0:1], ini[lo, 1:2], ALU.subtract)
        a.tt(ini[lo, 0:1], Gre[lo, 0:1], Ti[lo, NCH:NCH + 1], ALU.mult); a.tt(ini[lo, 1:2], Gim[lo, 0:1], Tr[lo, NCH:NCH + 1], ALU.mult)
        a.tt(ini[lo, 3:4], ini[lo, 0:1], ini[lo, 1:2], ALU.add)
        a.scan(Gre[lo, 32:NCH][:, ::-1], rhoT[lo, 32:NCH], Spr[lo, 32:NCH][:, ::-1], ini[lo, 2:3])
        a.scan(Gim[lo, 32:NCH][:, ::-1], rhoT[lo, 32:NCH], Spi[lo, 32:NCH][:, ::-1], ini[lo, 3:4])
        a.tt(w1[:], Gre[:], Tr[:, 0:NCH], ALU.mult); a.tt(w3[:], Gim[:], Ti[:, 0:NCH], ALU.mult)
        a.tt(Hor[:], w1[:], w3[:], ALU.add)
        a.tt(w2[:], Gim[:], Tr[:, 0:NCH], ALU.mult, eng="gpsimd"); a.tt(w4[:], Gre[:], Ti[:, 0:NCH], ALU.mult, eng="gpsimd")
        a.tt(Hoi[:], w2[:], w4[:], ALU.subtract, eng="gpsimd")
        for (Hi_, Ho_, Gx) in ((Hir, Hor, Gre), (Hii, Hoi, Gim)):
            a.copy(Hi_[0:64, 1:NCH], Ho_[0:64, 0:NCH - 1], eng="scalar"); a.memset(Hi_[0:64, 0:1], 0.0)
            a.copy(Hi_[lo, 0:NCH - 1], Ho_[lo, 1:NCH], eng="scalar"); a.memset(Hi_[lo, 31:32], 0.0)
            a.copy(Hi_[lo, NCH - 1:NCH], Gx[lo, 0:1])
        for bi, (c0, cn) in enumerate(BLK):
            a.mm(ps[bi][:, 0:cn], Msb[:], Us[ub][:, c0:c0 + cn], True, False)
            a.mm(ps[bi][:, 0:cn], f2(Wor), Hir[:, c0:c0 + cn], False, False)
            a.mm(ps[bi][:, 0:cn], f2(Woin), Hii[:, c0:c0 + cn], False, True)
            a.copy(Ysb[:, c0:c0 + cn], ps[bi][:, 0:cn], eng="scalar")
        a.dma_out(Yg[gi], Ysb[:])
    return P.finish()

def ssm_inputs(inp, l, j4, u_b, uc_b):
    gs = np.arange(6 * j4, 6 * j4 + 6)
    def rows(arr):
        return np.ascontiguousarray(arr[:, gs, :].transpose(0, 2, 1).reshape(128, 6))
    are = rows(inp["ssm_a_re"][l]); aim = rows(inp["ssm_a_im"][l])
    ldt = np.ascontiguousarray(np.repeat(inp["ssm_log_dt"][l][:, gs][:, None, :], 64, axis=1).reshape(128, 6))
    def rowsB(arr):
        return np.ascontiguousarray(arr[:, gs].transpose(0, 2, 1, 3).reshape(128, 96))
    def rowsC(arr):
        return np.ascontiguousarray(arr[:, gs].transpose(0, 3, 1, 2).reshape(128, 96))
    s_ = np.arange(8)
    mF = (s_[None, :] >= s_[:, None]).astype(np.float32)
    maskF = np.kron(mF, np.ones((16, 16), np.float32)); maskB = np.kron(mF.T, np.ones((16, 16), np.float32))
    sgn = np.concatenate([-np.ones((64, 1), np.float32), np.ones((64, 1), np.float32)], 0)
    dsk = inp["ssm_d"][l]
    Dd = np.zeros((6, 128, 128), np.float32)
    for gi, g in enumerate(gs):
        dd = np.zeros((8, 16, 8, 16), np.float32)
        for t in range(8):
            dd[t, np.arange(16), t, np.arange(16)] = dsk[16 * g:16 * g + 16]
        Dd[gi] = dd.reshape(128, 128)
    seq = np.concatenate([uc_b, u_b], 0)
    U = np.zeros((6, 128, NCH), np.float32)
    for gi, g in enumerate(gs):
        U[gi] = seq[:, 16 * g:16 * g + 16].reshape(NCH, 128).T
    return dict(are=are, aim=aim, ldt=ldt, Bre=rowsB(inp["ssm_b_re"][l]), Bim=rowsB(inp["ssm_b_im"][l]),
                Cre=rowsC(inp["ssm_c_re"][l]), Cim=rowsC(inp["ssm_c_im"][l]), maskF=maskF, maskB=maskB, sgn=sgn,
                ident=np.eye(128, dtype=np.float32), Ddiag=Dd, U=U)

def ssm_unpack(Yg):
    return np.ascontiguousarray(Yg.transpose(2, 1, 0).reshape(NCH, 8, 16, 6).transpose(0, 1, 3, 2).reshape(NCH * 8, 96))


_PROGS = {}
def _prog(name, fn):
    if name not in _PROGS:
        _PROGS[name] = fn()
    return _PROGS[name]

def _run(nc, maps):
    res = run_bass_kernel_spmd(nc, maps, core_ids=list(range(8)))
    return res.results

def kernel(x, c, ctx, c_ctx, w_mod, b_mod, g_pre_mix, g_post_mix, w_in, ssm_a_re, ssm_a_im, ssm_log_dt, ssm_b_re, ssm_b_im,
           ssm_c_re, ssm_c_im, ssm_d, w_glu, w_fourier, na_rpb, w_out, g_pre_ffn, g_post_ffn, w_ffn_gate, w_ffn_up, w_ffn_down):
    f32 = lambda a: np.ascontiguousarray(np.asarray(a, dtype=np.float32))
    inp = dict(ssm_a_re=f32(ssm_a_re), ssm_a_im=f32(ssm_a_im), ssm_log_dt=f32(ssm_log_dt), ssm_b_re=f32(ssm_b_re), ssm_b_im=f32(ssm_b_im),
               ssm_c_re=f32(ssm_c_re), ssm_c_im=f32(ssm_c_im), ssm_d=f32(ssm_d))
    x = f32(x); c = f32(c); ctx = f32(ctx); c_ctx = f32(c_ctx); w_mod = f32(w_mod); b_mod = f32(b_mod)
    w_in = f32(w_in); w_glu = f32(w_glu); w_fourier = f32(w_fourier); na_rpb = f32(na_rpb); w_out = f32(w_out)
    g_pre_mix = f32(g_pre_mix); g_post_mix = f32(g_post_mix); g_pre_ffn = f32(g_pre_ffn); g_post_ffn = f32(g_post_ffn)
    w_ffn_gate = f32(w_ffn_gate); w_ffn_up = f32(w_ffn_up); w_ffn_down = f32(w_ffn_down)
    DEPTH = 2
    cores = [(k // 4, k % 4) for k in range(8)]
    cTs = [np.ascontiguousarray(np.concatenate([colT(c[b], 8), colT(c_ctx, 8)], axis=1)) for b in range(2)]
    xT = [np.ascontiguousarray(np.concatenate([x[b, q * 2048:(q + 1) * 2048].T, ctx[b].T], axis=1)) for (b, q) in cores]
    KF = fnet_consts(); permm = perm_matrix(); ident = np.eye(128, dtype=np.float32)
    ropes = [rope_tables(q * 2048, 2048) for q in range(4)]
    for l in range(DEPTH):
        maps = []
        for k, (b, q) in enumerate(cores):
            maps.append(dict(xT=xT[k], w_in=w_in[l], w_mod=np.ascontiguousarray(w_mod[l][:, 0:2048]), b_modT=colT(b_mod[l][0:2048], 16),
                             g_preT=colT(g_pre_mix[l], 8), cT=cTs[b], cos=ropes[q][0], sin=ropes[q][1], perm=permm))
        res = _run(_prog("l1", build_l1), maps)
        hfull = [np.concatenate([res[k]["hT"], res[k]["hTb"]], 0) for k in range(8)]
        h_lat = [np.concatenate([hfull[4 * b + q][:, 0:2048].T for q in range(4)], 0) for b in range(2)]
        h_ctx = [np.ascontiguousarray(hfull[4 * b][:, 2048:2304].T) for b in range(2)]
        del hfull
        del res
        maps = [ssm_inputs(inp, l, j4, h_lat[b][:, 0:384], h_ctx[b][:, 0:384]) for (b, j4) in cores]
        res = _run(_prog("ssm", build_ssm), maps)
        ys = [[ssm_unpack(res[4 * b + j4]["Yg"]) for j4 in range(4)] for b in range(2)]
        ysT = [np.ascontiguousarray(np.concatenate(ys[b], 1).T) for b in range(2)]
        del res, ys
        maps = []
        for (b, g) in cores:
            m = dict(z=np.ascontiguousarray(h_lat[b][:, 384 + 64 * g:448 + 64 * g].reshape(128, 4096)),
                     zc=np.ascontiguousarray(h_ctx[b][:, 384 + 64 * g:448 + 64 * g].reshape(2, 128, 64).transpose(1, 0, 2)))
            m.update(KF); maps.append(m)
        res = _run(_prog("fnet", lambda: build_fnet(True)), maps)
        mxT = [np.concatenate([res[4 * b + g]["R"] for g in range(4)], 0) for b in range(2)]
        mxcT = [np.concatenate([res[4 * b + g]["Rc"] for g in range(4)], 0) for b in range(2)]
        del res
        maps = []
        for (b, q) in cores:
            kwT, vw = na_windows(h_lat[b][:, 1024:1408], h_lat[b][:, 1408:1792], q)
            tbraw, mask = na_tables(na_rpb[l], q)
            maps.append(dict(qT=np.ascontiguousarray(h_lat[b][q * 2048:(q + 1) * 2048, 640:1024].T), kwT=kwT, vw=vw,
                             kcT=np.ascontiguousarray(h_ctx[b][:, 1024:1408].T),
                             vc=np.ascontiguousarray(h_ctx[b][:, 1408:1792].reshape(2, 128, 384).transpose(1, 0, 2)),
                             tbraw=tbraw, mask=mask, ident=ident, qcT=np.ascontiguousarray(h_ctx[b][:, 640:1024].T)))
        res = _run(_prog("na", lambda: build_na(True)), maps)
        naT = [np.ascontiguousarray(np.concatenate([res[4 * b + q]["Y"] for q in range(4)], 0).T) for b in range(2)]
        nacT = [np.ascontiguousarray(res[4 * b]["Yc"].T) for b in range(2)]
        del res, h_lat
        maps = []
        for k, (b, q) in enumerate(cores):
            sl = slice(q * 2048, (q + 1) * 2048)
            maps.append(dict(xT=xT[k], ysT=np.ascontiguousarray(np.concatenate([ysT[b][:, 256 + q * 2048:256 + (q + 1) * 2048], ysT[b][:, 0:256]], 1)),
                             mxT=np.ascontiguousarray(np.concatenate([mxT[b][:, sl], mxcT[b]], 1)),
                             naT=np.ascontiguousarray(np.concatenate([naT[b][:, sl], nacT[b]], 1)),
                             w_mod=np.ascontiguousarray(w_mod[l][:, 2048:3072]), b_modT=colT(b_mod[l][2048:3072], 8), cT=cTs[b],
                             g_postT=colT(g_post_mix[l], 8), w_glu=w_glu[l], w_fourier=w_fourier[l], w_out=w_out[l]))
        res = _run(_prog("l3a", lambda: build_l3a(True)), maps)
        xT = [res[k]["xoT"] for k in range(8)]
        del res
        maps = []
        for k, (b, q) in enumerate(cores):
            maps.append(dict(xT=xT[k], w_mod=np.ascontiguousarray(w_mod[l][:, 3072:6144]), b_modT=colT(b_mod[l][3072:6144], 24), cT=cTs[b],
                             g_preT=colT(g_pre_ffn[l], 8), g_postT=colT(g_post_ffn[l], 8),
                             w_gate=w_ffn_gate[l], w_up=w_ffn_up[l], w_down=w_ffn_down[l]))
        res = _run(_prog("l3b", lambda: build_l3b(True)), maps)
        xT = [np.ascontiguousarray(res[k]["xoT"]) for k in range(8)]
        del res
    out = np.empty((2, 8192, 1024), np.float32)
    for k, (b, q) in enumerate(cores):
        out[b, q * 2048:(q + 1) * 2048] = xT[k][:, 0:2048].T
    return out
```

```python
import math
import numpy as np
from contextlib import ExitStack
import concourse.bass as bass
import concourse.mybir as mybir
from concourse.bass_utils import run_bass_kernel_spmd


F32 = mybir.dt.float32
BF16 = mybir.dt.bfloat16
ALU = mybir.AluOpType
AF = mybir.ActivationFunctionType
AX = mybir.AxisListType

COMPUTE = ("tensor", "vector", "scalar", "gpsimd")


class Prog:
    def __init__(self):
        self.nc = bass.Bass("TRN2", target_bir_lowering=False)
        self.ops = []
        self.stack = ExitStack()
        self.ndram = 0

    def dram_in(self, name, shape, dtype=F32):
        return self.nc.dram_tensor(name, list(shape), dtype, kind="ExternalInput").ap()

    def dram_out(self, name, shape, dtype=F32):
        return self.nc.dram_tensor(name, list(shape), dtype, kind="ExternalOutput").ap()

    def sbuf(self, name, shape, dtype=F32):
        return self.stack.enter_context(self.nc.sbuf_tensor(name, list(shape), dtype))

    def psum(self, name, shape=(128, 512), dtype=F32):
        return self.stack.enter_context(self.nc.psum_tensor(name, list(shape), dtype))

    def op(self, eng, fn, reads=(), writes=(), chan=None, nosync_same=False, inc=True):
        self.ops.append(dict(eng=eng, fn=fn, reads=tuple(reads), writes=tuple(writes),
                             chan=chan, nosync_same=nosync_same, inc=inc))

    def dma(self, eng, out, in_, reads=(), writes=(), chan="ld", **kw):
        self.op(eng, lambda e: e.dma_start(out=out, in_=in_, **kw), reads, writes, chan=chan)

    def mm(self, out, lhsT, rhs, start, stop, reads=(), writes=()):
        self.op("tensor", lambda e: e.matmul(out, lhsT, rhs, start=start, stop=stop),
                reads, writes, nosync_same=True, inc=True)

    def act(self, out, in_, func, reads=(), writes=(), **kw):
        self.op("scalar", lambda e: e.activation(out=out, in_=in_, func=func, **kw), reads, writes)

    def tt(self, out, in0, in1, op, reads=(), writes=(), eng="vector"):
        self.op(eng, lambda e: e.tensor_tensor(out=out, in0=in0, in1=in1, op=op), reads, writes)

    def ts(self, out, in0, s1, s2, op0, op1=None, reads=(), writes=(), eng="vector"):
        if op1 is None:
            self.op(eng, lambda e: e.tensor_scalar(out=out, in0=in0, scalar1=s1, scalar2=None, op0=op0),
                    reads, writes)
        else:
            self.op(eng, lambda e: e.tensor_scalar(out=out, in0=in0, scalar1=s1, scalar2=s2, op0=op0, op1=op1),
                    reads, writes)

    def stt(self, out, in0, scalar, in1, op0, op1, reads=(), writes=()):
        self.op("vector", lambda e: e.scalar_tensor_tensor(out=out, in0=in0, scalar=scalar, in1=in1,
                                                            op0=op0, op1=op1), reads, writes)

    def copy(self, out, in_, reads=(), writes=(), eng="vector"):
        if eng == "scalar":
            self.op(eng, lambda e: e.copy(out=out, in_=in_), reads, writes)
        else:
            self.op(eng, lambda e: e.tensor_copy(out=out, in_=in_), reads, writes)

    def memset(self, ap, val, writes=(), eng="vector"):
        self.op(eng, lambda e: e.memset(ap, val), (), writes)

    def finish(self):
        nc = self.nc
        ops = self.ops
        engines = []
        for o in ops:
            if o["eng"] not in engines:
                engines.append(o["eng"])
        chans = []
        for o in ops:
            if o["chan"] is not None and o["chan"] not in chans:
                chans.append(o["chan"])
        sems = {}
        for e in engines:
            sems[("e", e)] = self.stack.enter_context(nc.semaphore("s_" + e))
        for c in chans:
            sems[("c", c)] = self.stack.enter_context(nc.semaphore("c_" + c))
        def plan_pass():
            viol = set()
            eng_count = {e: 0 for e in engines}
            chan_count = {c: 0 for c in chans}
            last_writer = {}
            readers = {}
            known = {e: {} for e in engines}
            plan = {e: [] for e in engines}
            done = []
            for i, o in enumerate(ops):
                e = o["eng"]
                deps = set()
                for r in o["reads"]:
                    if r in last_writer:
                        deps.add(last_writer[r])
                for w in o["writes"]:
                    if w in last_writer:
                        deps.add(last_writer[w])
                    for rd in readers.get(w, ()):
                        deps.add(rd)
                need = {}
                for d in deps:
                    od = ops[d]
                    if od["chan"] is not None:
                        key = ("c", od["chan"])
                        val = 16 * chan_count[od["chan"]]
                    else:
                        if od["eng"] == e and (o["nosync_same"] and od["nosync_same"]):
                            continue
                        key = ("e", od["eng"])
                        val = done[d][1]
                        if val > eng_count[od["eng"]]:
                            viol.add(d)
                    if val > need.get(key, 0):
                        need[key] = val
                waits = []
                for key, val in need.items():
                    if known[e].get(key, 0) >= val:
                        continue
                    known[e][key] = val
                    waits.append((key, val))
                if o["chan"] is not None:
                    chan_count[o["chan"]] += 1
                    done.append((("c", o["chan"]), 16 * chan_count[o["chan"]]))
                    inc = (("c", o["chan"]), 16)
                elif not o["inc"]:
                    done.append((("e", e), eng_count[e] + 1))
                    inc = None
                else:
                    eng_count[e] += 1
                    done.append((("e", e), eng_count[e]))
                    inc = (("e", e), 1)
                plan[e].append((waits, o["fn"], inc))
                for r in o["reads"]:
                    readers.setdefault(r, []).append(i)
                for w in o["writes"]:
                    last_writer[w] = i
                    readers[w] = []
            return viol, plan, chan_count
        while True:
            viol, plan, chan_count = plan_pass()
            if not viol:
                break
            for d in viol:
                ops[d]["inc"] = True
        final_waits = {e: [] for e in engines}
        chan_eng = {}
        for o in ops:
            if o["chan"] is not None:
                chan_eng[o["chan"]] = o["eng"]
        for c, e in chan_eng.items():
            final_waits[e].append((("c", c), 16 * chan_count[c]))

        semv = {k: 0 for k in sems}
        ptr = {e: 0 for e in engines}
        progressed = True
        while progressed:
            progressed = False
            for e in engines:
                while ptr[e] < len(plan[e]):
                    waits, _fn, inc = plan[e][ptr[e]]
                    if any(semv[k] < v for k, v in waits):
                        break
                    if inc is not None:
                        semv[inc[0]] += inc[1]
                    ptr[e] += 1
                    progressed = True
        stuck = {e: (ptr[e], len(plan[e])) for e in engines if ptr[e] < len(plan[e])}
        if stuck:
            det = {e: [(k, v, semv[k]) for k, v in plan[e][ptr[e]][0] if semv[k] < v] for e in stuck}
            raise RuntimeError(f"sync plan deadlocks: {stuck} waiting on {det}")

        with nc.Block() as block:
            def make(e):
                def body(eng):
                    for waits, fn, inc in plan[e]:
                        for key, val in waits:
                            eng.wait_ge(sems[key], val)
                        ins = fn(eng)
                        if inc is not None:
                            ins.then_inc(sems[inc[0]], inc[1])
                    for key, val in final_waits[e]:
                        eng.wait_ge(sems[key], val)
                return body
            for e in engines:
                getattr(block, e)(make(e))
        self.stack.close()
        return nc

GRID_W = 64
def rope_tables(tok0, n):
    t = np.arange(tok0, tok0 + n); row = (t // GRID_W).astype(np.float32); col = (t % GRID_W).astype(np.float32)
    quarter = 16
    freqs = (10000.0 ** (-np.arange(quarter, dtype=np.float32) / quarter)).astype(np.float32)
    cos = np.zeros((64, n), np.float32); sin = np.zeros((64, n), np.float32)
    for d in range(64):
        pos = row if d < 32 else col
        dd = d % 32
        f = freqs[dd % 16]
        ang = (pos * f).astype(np.float32)
        cos[d] = np.cos(ang); s = np.sin(ang)
        sin[d] = -s if dd < 16 else s
    return np.concatenate([cos, cos], 0), np.concatenate([sin, sin], 0)
def perm_matrix():
    Pm = np.zeros((128, 128), np.float32)
    for m in range(128):
        dd = m % 32
        k = m + 16 if dd < 16 else m - 16
        Pm[k, m] = 1.0
    return Pm
def colT(v, n):
    return np.ascontiguousarray(v.reshape(n, 128).T)

EPS = 1e-6

def get_stage(P):
    if not hasattr(P, "_stage"):
        P._stage_n = getattr(P, "_stage_n", 3)
        P._stage = [P.sbuf(f"stage{i}", [128, 1024]) for i in range(P._stage_n)]
        P._stage_i = 0
    return P._stage

def emit_mod(P, w_mod, b_modT, cT, nct, ps_mod, tag="m"):
    ncols = nct * 128
    c_s = P.sbuf(tag + "c_s", [128, 16])
    bm_s = P.sbuf(tag + "bm_s", [128, nct]); modv = P.sbuf(tag + "modv", [128, 2 * nct]); modc = P.sbuf(tag + "modc", [128, 2 * nct])
    modrow = [P.sbuf(f"{tag}modrow{i}", [2, 512]) for i in range(2)]
    scr = P.nc.dram_tensor(tag + "_modscr", [2, ncols], F32, kind="Internal").ap()
    stb = [P.sbuf(f"{tag}stb{i}", [128, 2, 512], BF16) for i in range(3)]
    sc_b = P.sbuf(tag + "sc_b", [128, 16], BF16)
    P.dma("sync", c_s[:], cT, writes=[tag + "c_s"], chan="ld0")
    P.dma("sync", bm_s[:], b_modT, writes=[tag + "bm_s"], chan="ld0")
    P.act(sc_b[:], c_s[:], AF.Silu, reads=[tag + "c_s"], writes=[tag + "sc_s"])
    wmv = w_mod.rearrange("(kc p) n -> p kc n", p=128)
    nst = 0
    for pc in range(ncols // 512):
        for i in range(4):
            b = nst % 3; nst += 1
            P.dma("gpsimd", stb[b][:], wmv[:, 2 * i:2 * i + 2, pc * 512:(pc + 1) * 512],
                  writes=[(tag + "stb", b)], chan=f"{tag}stb{b}", max_dma_last_dim=2048)
            for k2 in range(2):
                kc = 2 * i + k2
                P.mm(ps_mod[0:2, 0:512], sc_b[:, kc:16:8], stb[b][:, k2, :], kc == 0, kc == 7,
                     reads=[(tag + "stb", b), tag + "sc_s"], writes=["ps_mod"])
        P.copy(modrow[pc % 2][:], ps_mod[0:2, 0:512], reads=["ps_mod"], writes=[(tag + "modrow", pc % 2)], eng="scalar")
        P.dma("sync", scr[:, pc * 512:(pc + 1) * 512], modrow[pc % 2][:], reads=[(tag + "modrow", pc % 2)], writes=[tag + "scr"], chan="modw")
    P.dma("sync", modc[:].rearrange("p (j t) -> p j t", j=2), scr.rearrange("j (t p) -> p j t", p=128),
          reads=[tag + "scr"], writes=[tag + "modc"], chan="modr", allow_slow_non_contiguous=True)
    for j in range(2):
        P.tt(modv[:, j * nct:(j + 1) * nct], modc[:, j * nct:(j + 1) * nct], bm_s[:], ALU.add,
             reads=[tag + "modc", tag + "bm_s"], writes=[tag + "modv"])
    return modv

def load_cast(P, w_dram, w_bf, nk, ncols, tag, piece=1024):
    wv = w_dram.rearrange("(kc p) n -> p kc n", p=128)
    for kc in range(nk):
        P.dma("gpsimd", w_bf[:, kc, :], wv[:, kc, :], writes=[(tag, kc)], chan="wld_" + tag, max_dma_last_dim=4096)

def emit_rstd(P, src, nk, n, sqb, ones, ps_ss, sd, rstd, src_res, tag=""):
    P.act(sqb[:, 0:nk, 0:n], src[:, 0:nk, 0:n], AF.Square, reads=src_res, writes=["sqb"])
    for kc in range(nk):
        P.mm(ps_ss[:, 0:n], ones[:], sqb[:, kc, 0:n], kc == 0, kc == nk - 1, reads=["sqb", "ones"], writes=["ps_ss"])
    P.act(sd[:, 0:n], ps_ss[:, 0:n], AF.Sqrt, reads=["ps_ss"], writes=["sd" + tag], scale=1.0 / (128 * nk), bias=EPS)
    P.op("vector", lambda e: e.reciprocal(out=rstd[:, 0:n], in_=sd[:, 0:n]), reads=["sd" + tag], writes=["rstd" + tag])

def build_l3b(with_ctx, N=256):
    NT = 2304 if with_ctx else 2048
    P = Prog(); P._stage_n = 2
    xT = P.dram_in("xT", [1024, NT])
    w_mod = P.dram_in("w_mod", [1024, 3072]); b_modT = P.dram_in("b_modT", [128, 24]); cT = P.dram_in("cT", [128, 16])
    g_preT = P.dram_in("g_preT", [128, 8]); g_postT = P.dram_in("g_postT", [128, 8])
    w_gate = P.dram_in("w_gate", [1024, 2816]); w_up = P.dram_in("w_up", [1024, 2816]); w_down = P.dram_in("w_down", [2816, 1024])
    xoT = P.dram_out("xoT", [1024, NT])
    wg = P.sbuf("wg", [128, 8, 2816], BF16); wu = P.sbuf("wu", [128, 8, 2816], BF16); wd = P.sbuf("wd", [128, 22, 1024], BF16)
    xs = [P.sbuf(f"xs{i}", [128, 8, N]) for i in range(2)]
    sqb = P.sbuf("sqb", [128, 8, N], BF16)
    tt_ = [P.sbuf(f"tt{i}", [128, N]) for i in range(2)]; xn = [P.sbuf(f"xn{i}", [128, 8, N], BF16) for i in range(2)]
    sd2 = P.sbuf("sd2", [128, N]); rstd2 = P.sbuf("rstd2", [128, N])
    hmid = P.sbuf("hmid", [128, 22, N], BF16)
    sg = [P.sbuf(f"sg{i}", [128, N]) for i in range(2)]
    o2 = P.sbuf("o2", [128, 8, N]); tmp = [P.sbuf(f"tmp{i}", [128, N]) for i in range(2)]
    xo = [P.sbuf(f"xo{i}", [128, N]) for i in range(2)]
    ones = P.sbuf("ones", [128, 128], BF16)
    sd = P.sbuf("sd", [128, N]); rstd = P.sbuf("rstd", [128, N])
    gp_s = P.sbuf("gp_s", [128, 8]); gq_s = P.sbuf("gq_s", [128, 8])
    Av = P.sbuf("Av", [128, 16]); Gv = P.sbuf("Gv", [128, 16])
    ps_mod = P.psum("ps_mod"); ps_ss = P.psum("ps_ss")
    psg = [P.psum(f"psg{i}") for i in range(2)]; psu = [P.psum(f"psu{i}") for i in range(2)]; pso = [P.psum(f"pso{i}") for i in range(2)]
    P.memset(ones[:], 1.0, writes=["ones"])
    P.dma("sync", gp_s[:], g_preT, writes=["gp_s"], chan="ld0")
    P.dma("sync", gq_s[:], g_postT, writes=["gq_s"], chan="ld0")
    modv = emit_mod(P, w_mod, b_modT, cT, 24, ps_mod)
    for j in range(2):
        P.stt(Av[:, j * 8:(j + 1) * 8], modv[:, j * 24 + 8: j * 24 + 16], 1.0, gp_s[:], ALU.add, ALU.mult,
              reads=["mmodv", "gp_s"], writes=["Av"])
        P.tt(Gv[:, j * 8:(j + 1) * 8], modv[:, j * 24 + 16: j * 24 + 24], gq_s[:], ALU.mult,
             reads=["mmodv", "gq_s"], writes=["Gv"])
    wvg = w_gate.rearrange("(kc p) n -> p kc n", p=128); wvu = w_up.rearrange("(kc p) n -> p kc n", p=128)
    for i, c0 in enumerate(range(0, 2816, 512)):
        cn = min(512, 2816 - c0)
        P.dma("gpsimd", wg[:, :, c0:c0 + cn], wvg[:, :, c0:c0 + cn], writes=[("wg", "c", i)], chan=f"wld_wg_{i}", max_dma_last_dim=2048)
        P.dma("gpsimd", wu[:, :, c0:c0 + cn], wvu[:, :, c0:c0 + cn], writes=[("wu", "c", i)], chan=f"wld_wu_{i}", max_dma_last_dim=2048)
    xv = xT.rearrange("(kc p) t -> p kc t", p=128); xov = xoT.rearrange("(kc p) t -> p kc t", p=128)
    slabs = list(range(0, NT, N)); n = N
    cnt = dict(ng=0, no=0, nt=0)
    def pre(si):
        t0 = slabs[si]; b = si % 2; j = 0 if t0 < 2048 else 1
        P.dma("sync", xs[b][:], xv[:, :, t0:t0 + n], writes=[("xs", b)], chan=f"xs{b}")
        emit_rstd(P, xs[b], 8, n, sqb, ones, ps_ss, sd, rstd, [("xs", b)])
        for kc in range(8):
            P.tt(tt_[kc % 2][:], xs[b][:, kc, :], rstd[:], ALU.mult, reads=[("xs", b), "rstd"], writes=[("tt", kc % 2)])
            P.act(xn[b][:, kc, :], tt_[kc % 2][:], AF.Identity, reads=[("tt", kc % 2), "Av", "mmodv"], writes=[("xn", b, kc)],
                  scale=Av[:, j * 8 + kc: j * 8 + kc + 1], bias=modv[:, j * 24 + kc: j * 24 + kc + 1])
    def gu(si):
        b = si % 2
        for jj in range(22):
            pb = cnt["ng"] % 2; cnt["ng"] += 1
            for kc in range(8):
                P.mm(psg[pb][:, 0:n], wg[:, kc, jj * 128:(jj + 1) * 128], xn[b][:, kc, :], kc == 0, kc == 7,
                     reads=[("wg", "c", jj // 4), ("xn", b, kc)], writes=[("psg", pb)])
            for kc in range(8):
                P.mm(psu[pb][:, 0:n], wu[:, kc, jj * 128:(jj + 1) * 128], xn[b][:, kc, :], kc == 0, kc == 7,
                     reads=[("wu", "c", jj // 4), ("xn", b, kc)], writes=[("psu", pb)])
            P.act(sg[pb][:], psg[pb][:, 0:n], AF.Silu, reads=[("psg", pb)], writes=[("sg", pb)])
            P.tt(hmid[:, jj, :], sg[pb][:], psu[pb][:, 0:n], ALU.mult, reads=[("sg", pb), ("psu", pb)], writes=[("hmid", jj)])
    def dn(si):
        for m in range(8):
            pb = cnt["no"] % 2; cnt["no"] += 1
            for jj in range(22):
                P.mm(pso[pb][:, 0:n], wd[:, jj, m * 128:(m + 1) * 128], hmid[:, jj, :], jj == 0, jj == 21,
                     reads=[("wd", jj), ("hmid", jj)], writes=[("pso", pb)])
            P.copy(o2[:, m, :], pso[pb][:, 0:n], reads=[("pso", pb)], writes=[("o2", m)], eng="scalar")
    def post(si):
        t0 = slabs[si]; b = si % 2; j = 0 if t0 < 2048 else 1
        emit_rstd(P, o2, 8, n, sqb, ones, ps_ss, sd2, rstd2, [("o2", m) for m in range(8)], tag="2")
        for m in range(8):
            P.stt(o2[:, m, :], o2[:, m, :], Gv[:, j * 8 + m: j * 8 + m + 1], rstd2[:], ALU.mult, ALU.mult,
                  reads=[("o2", m), "Gv", "rstd2"], writes=[("o2", m)])
            P.tt(o2[:, m, :], xs[b][:, m, :], o2[:, m, :], ALU.add, reads=[("xs", b), ("o2", m)], writes=[("o2", m)], eng="gpsimd")
        P.dma("gpsimd", xov[:, :, t0:t0 + n], o2[:], reads=[("o2", m) for m in range(8)], chan="st")
    pre(0)
    for si in range(len(slabs)):
        gu(si)
        if si == 0:
            load_cast(P, w_down, wd, 22, 1024, "wd")
        if si + 1 < len(slabs):
            pre(si + 1)
        dn(si)
        post(si)
    return P.finish()

def build_l3a(with_ctx, N=512):
    NT = 2304 if with_ctx else 2048
    P = Prog()
    xT = P.dram_in("xT", [1024, NT]); ysT = P.dram_in("ysT", [384, NT]); mxT = P.dram_in("mxT", [256, NT]); naT = P.dram_in("naT", [384, NT])
    w_mod = P.dram_in("w_mod", [1024, 1024]); b_modT = P.dram_in("b_modT", [128, 8]); cT = P.dram_in("cT", [128, 16])
    g_postT = P.dram_in("g_postT", [128, 8])
    w_glu = P.dram_in("w_glu", [384, 384]); w_fourier = P.dram_in("w_fourier", [256, 256]); w_out = P.dram_in("w_out", [1024, 1024])
    xoT = P.dram_out("xoT", [1024, NT])
    wglu = P.sbuf("wglu", [128, 3, 384], BF16); wf = P.sbuf("wf", [128, 2, 256], BF16); wo = P.sbuf("wo", [128, 8, 1024], BF16)
    xs = [P.sbuf(f"xs{i}", [128, 8, N]) for i in range(2)]
    ys = [P.sbuf(f"ys{i}", [128, 3, N]) for i in range(2)]
    mx = [P.sbuf(f"mx{i}", [128, 2, N]) for i in range(2)]
    na = [P.sbuf(f"na{i}", [128, 3, N]) for i in range(2)]
    sq = P.sbuf("sq", [128, 3, N]); t1 = P.sbuf("t1", [128, 3, N]); sgm = P.sbuf("sgm", [128, 3, N])
    zf = P.sbuf("zf", [128, 3, N]); zb = P.sbuf("zb", [128, 3, N], BF16); mxb = P.sbuf("mxb", [128, 2, N], BF16)
    sg2 = [P.sbuf(f"sg2{i}", [128, N]) for i in range(2)]
    cat = P.sbuf("cat", [128, 8, N], BF16)
    sqb = P.sbuf("sqb", [128, 8, N], BF16)
    o2 = P.sbuf("o2", [128, 8, N]); tmp = [P.sbuf(f"tmp{i}", [128, N]) for i in range(2)]
    xo = [P.sbuf(f"xo{i}", [128, N]) for i in range(2)]
    ones = P.sbuf("ones", [128, 128], BF16)
    sd = P.sbuf("sd", [128, N]); rstd = P.sbuf("rstd", [128, N])
    gq_s = P.sbuf("gq_s", [128, 8]); Gv = P.sbuf("Gv", [128, 16])
    ps_mod = P.psum("ps_mod"); ps_ss = P.psum("ps_ss")
    psa = [P.psum(f"psa{i}") for i in range(3)]; pso = [P.psum(f"pso{i}") for i in range(3)]
    P.memset(ones[:], 1.0, writes=["ones"])
    P.dma("sync", gq_s[:], g_postT, writes=["gq_s"], chan="ld0")
    modv = emit_mod(P, w_mod, b_modT, cT, 8, ps_mod)
    for j in range(2):
        P.tt(Gv[:, j * 8:(j + 1) * 8], modv[:, j * 8:(j + 1) * 8], gq_s[:], ALU.mult, reads=["mmodv", "gq_s"], writes=["Gv"])
    load_cast(P, w_glu, wglu, 3, 384, "wglu")
    load_cast(P, w_fourier, wf, 2, 256, "wf")
    load_cast(P, w_out, wo, 8, 1024, "wo")
    xv = xT.rearrange("(kc p) t -> p kc t", p=128); xov = xoT.rearrange("(kc p) t -> p kc t", p=128)
    ysv = ysT.rearrange("(kc p) t -> p kc t", p=128); mxv = mxT.rearrange("(kc p) t -> p kc t", p=128); nav = naT.rearrange("(kc p) t -> p kc t", p=128)
    na_ = 0; no = 0; nt = 0
    for si, t0 in enumerate(range(0, NT, N)):
        n = min(N, NT - t0); b = si % 2; j = 0 if t0 < 2048 else 1
        P.dma("sync", xs[b][:, :, 0:n], xv[:, :, t0:t0 + n], writes=[("xs", b)], chan=f"xs{b}")
        P.dma("sync", ys[b][:, :, 0:n], ysv[:, :, t0:t0 + n], writes=[("ys", b)], chan=f"ys{b}")
        P.dma("sync", mx[b][:, :, 0:n], mxv[:, :, t0:t0 + n], writes=[("mx", b)], chan=f"mx{b}")
        P.dma("sync", na[b][:, :, 0:n], nav[:, :, t0:t0 + n], writes=[("na", b)], chan=f"na{b}")
        Y = ys[b][:, :, 0:n]
        P.tt(sq[:, :, 0:n], Y, Y, ALU.mult, reads=[("ys", b)], writes=["sq"])
        P.ts(t1[:, :, 0:n], sq[:, :, 0:n], 0.044715, 1.0, ALU.mult, ALU.add, reads=["sq"], writes=["t1"])
        P.tt(sq[:, :, 0:n], t1[:, :, 0:n], Y, ALU.mult, reads=["t1", ("ys", b)], writes=["sq"])
        P.act(sgm[:, :, 0:n], sq[:, :, 0:n], AF.Sigmoid, reads=["sq"], writes=["sgm"], scale=1.5957691216057308)
        P.tt(zf[:, :, 0:n], Y, sgm[:, :, 0:n], ALU.mult, reads=[("ys", b), "sgm"], writes=["zf"])
        P.copy(zb[:, :, 0:n], zf[:, :, 0:n], reads=["zf"], writes=["zb"], eng="scalar")
        for m in range(3):
            pb = na_ % 3; na_ += 1
            for kc in range(3):
                P.mm(psa[pb][:, 0:n], wglu[:, kc, m * 128:(m + 1) * 128], zb[:, kc, 0:n], kc == 0, kc == 2,
                     reads=[("wglu", kc), "zb"], writes=[("psa", pb)])
            P.act(sg2[m % 2][:, 0:n], psa[pb][:, 0:n], AF.Sigmoid, reads=[("psa", pb)], writes=[("sg2", m % 2)])
            P.tt(cat[:, m, 0:n], zf[:, m, 0:n], sg2[m % 2][:, 0:n], ALU.mult, reads=["zf", ("sg2", m % 2)], writes=[("cat", m)])
        P.copy(mxb[:, :, 0:n], mx[b][:, :, 0:n], reads=[("mx", b)], writes=["mxb"], eng="scalar")
        for m in range(2):
            pb = na_ % 3; na_ += 1
            for kc in range(2):
                P.mm(psa[pb][:, 0:n], wf[:, kc, m * 128:(m + 1) * 128], mxb[:, kc, 0:n], kc == 0, kc == 1,
                     reads=[("wf", kc), "mxb"], writes=[("psa", pb)])
            P.copy(cat[:, 3 + m, 0:n], psa[pb][:, 0:n], reads=[("psa", pb)], writes=[("cat", 3 + m)], eng="scalar")
        P.copy(cat[:, 5:8, 0:n], na[b][:, :, 0:n], reads=[("na", b)], writes=[("cat", 5), ("cat", 6), ("cat", 7)], eng="scalar")
        for m in range(8):
            pb = no % 3; no += 1
            for kc in range(8):
                P.mm(pso[pb][:, 0:n], wo[:, kc, m * 128:(m + 1) * 128], cat[:, kc, 0:n], kc == 0, kc == 7,
                     reads=[("wo", kc), ("cat", kc)], writes=[("pso", pb)])
            P.copy(o2[:, m, 0:n], pso[pb][:, 0:n], reads=[("pso", pb)], writes=[("o2", m)], eng="scalar")
        emit_rstd(P, o2, 8, n, sqb, ones, ps_ss, sd, rstd, [("o2", m) for m in range(8)])
        for m in range(8):
            P.stt(o2[:, m, 0:n], o2[:, m, 0:n], Gv[:, j * 8 + m: j * 8 + m + 1], rstd[:, 0:n], ALU.mult, ALU.mult,
                  reads=[("o2", m), "Gv", "rstd"], writes=[("o2", m)])
            P.tt(o2[:, m, 0:n], xs[b][:, m, 0:n], o2[:, m, 0:n], ALU.add, reads=[("xs", b), ("o2", m)], writes=[("o2", m)])
        P.dma("gpsimd", xov[:, :, t0:t0 + n], o2[:, :, 0:n], reads=[("o2", m) for m in range(8)], chan="st")
    return P.finish()

EPS = 1e-6
NT = 2304
SLABS = [(0, 512), (512, 512), (1024, 512), (1536, 512), (2048, 256)]

def build_l1():
    P = Prog(); P._stage_n = 4
    xT = P.dram_in("xT", [1024, NT])
    w_in = P.dram_in("w_in", [1024, 1792])
    w_mod = P.dram_in("w_mod", [1024, 2048])
    b_modT = P.dram_in("b_modT", [128, 16])
    g_preT = P.dram_in("g_preT", [128, 8])
    cT = P.dram_in("cT", [128, 16])
    cos = P.dram_in("cos", [128, 2048]); sin = P.dram_in("sin", [128, 2048])
    perm = P.dram_in("perm", [128, 128])
    hT = P.dram_out("hT", [640, NT])
    hTb = P.dram_out("hTb", [1152, NT])

    xs = [P.sbuf(f"xs{i}", [128, 8, 512]) for i in range(2)]
    sqb = P.sbuf("sqb", [128, 8, 512], BF16)
    tt_ = [P.sbuf(f"tt{i}", [128, 512]) for i in range(2)]
    xn = [P.sbuf(f"xn{i}", [128, 8, 512], BF16) for i in range(2)]
    w_bf = P.sbuf("w_bf", [128, 8, 1792], BF16)
    ones = P.sbuf("ones", [128, 128], BF16)
    sd = P.sbuf("sd", [128, 512]); rstd = P.sbuf("rstd", [128, 512])
    ho = [P.sbuf(f"ho{i}", [128, 512]) for i in range(4)]
    hob = [P.sbuf(f"hob{i}", [128, 512]) for i in range(4)]
    r1 = [P.sbuf(f"r1{i}", [128, 512]) for i in range(2)]
    r2 = [P.sbuf(f"r2{i}", [128, 512]) for i in range(2)]
    cos_s = P.sbuf("cos_s", [128, 2048]); sin_s = P.sbuf("sin_s", [128, 2048])
    perm_s = P.sbuf("perm_s", [128, 128])
    gp_s = P.sbuf("gp_s", [128, 8]); Av = P.sbuf("Av", [128, 16])
    ps_mod = P.psum("ps_mod"); ps_ss = P.psum("ps_ss")
    psm = [P.psum(f"psm{i}") for i in range(4)]
    psr = [P.psum(f"psr{i}") for i in range(2)]

    P.dma("sync", gp_s[:], g_preT, writes=["gp_s"], chan="ld0")
    P.dma("sync", cos_s[:], cos, writes=["cos_s"], chan="ld0")
    P.dma("sync", sin_s[:], sin, writes=["sin_s"], chan="ld0")
    P.dma("sync", perm_s[:], perm, writes=["perm_s"], chan="ld0")
    P.memset(ones[:], 1.0, writes=["ones"])
    modv = emit_mod(P, w_mod, b_modT, cT, 16, ps_mod)
    for j in range(2):
        P.stt(Av[:, j * 8:(j + 1) * 8], modv[:, j * 16 + 8: j * 16 + 16], 1.0, gp_s[:], ALU.add, ALU.mult,
              reads=["mmodv", "gp_s"], writes=["Av"])
    load_cast(P, w_in, w_bf, 8, 1792, "w_bf", piece=1024)
    xv = xT.rearrange("(kc p) t -> p kc t", p=128)
    cnt = dict(nho=0, nr=0, nps=0)
    def pre(si):
        t0, n = SLABS[si]; b = si % 2; j = 0 if si < 4 else 1
        P.dma("sync", xs[b][:, :, 0:n], xv[:, :, t0:t0 + n], writes=[("xs", b)], chan=f"xs{b}")
        P.act(sqb[:, :, 0:n], xs[b][:, :, 0:n], AF.Square, reads=[("xs", b)], writes=["sqb"])
        for kc in range(8):
            P.mm(ps_ss[:, 0:n], ones[:], sqb[:, kc, 0:n], kc == 0, kc == 7, reads=["sqb", "ones"], writes=["ps_ss"])
        P.act(sd[:, 0:n], ps_ss[:, 0:n], AF.Sqrt, reads=["ps_ss"], writes=["sd"], scale=1.0 / 1024, bias=EPS)
        P.op("vector", lambda e: e.reciprocal(out=rstd[:, 0:n], in_=sd[:, 0:n]), reads=["sd"], writes=["rstd"])
        for kc in range(8):
            P.tt(tt_[kc % 2][:, 0:n], xs[b][:, kc, 0:n], rstd[:, 0:n], ALU.mult, reads=[("xs", b), "rstd"], writes=[("tt", kc % 2)])
            P.act(xn[b][:, kc, 0:n], tt_[kc % 2][:, 0:n], AF.Identity, reads=[("tt", kc % 2), "Av", "mmodv"], writes=[("xn", b, kc)],
                  scale=Av[:, j * 8 + kc: j * 8 + kc + 1], bias=modv[:, j * 16 + kc: j * 16 + kc + 1])
    pending = []
    def flush():
        while pending:
            pending.pop(0)()
    def main(si):
        t0, n = SLABS[si]; b = si % 2
        for m in range(14):
            pb = cnt["nps"] % 4; cnt["nps"] += 1
            for kc in range(8):
                P.mm(psm[pb][:, 0:n], w_bf[:, kc, m * 128:(m + 1) * 128], xn[b][:, kc, 0:n], kc == 0, kc == 7,
                     reads=[("w_bf", kc), ("xn", b, kc)], writes=[("psm", pb)])
            flush()
            hb = cnt["nho"] % 4; cnt["nho"] += 1
            if m < 5:
                P.copy(ho[hb][:, 0:n], psm[pb][:, 0:n], reads=[("psm", pb)], writes=[("ho", hb)], eng="scalar")
                P.dma("gpsimd", hT[m * 128:(m + 1) * 128, t0:t0 + n], ho[hb][:, 0:n], reads=[("ho", hb)], chan=f"sto{hb}")
            elif m <= 10 and si < 4:
                rb = cnt["nr"] % 2; cnt["nr"] += 1
                P.copy(ho[hb][:, 0:n], psm[pb][:, 0:n], reads=[("psm", pb)], writes=[("ho", hb)], eng="scalar")
                def rope(hb=hb, rb=rb, m=m, t0=t0, n=n):
                    P.mm(psr[rb][:, 0:n], perm_s[:], ho[hb][:, 0:n], True, True, reads=["perm_s", ("ho", hb)], writes=[("psr", rb)])
                    P.tt(r1[rb][:, 0:n], ho[hb][:, 0:n], cos_s[:, t0:t0 + n], ALU.mult, reads=[("ho", hb), "cos_s"], writes=[("r1", rb)])
                    P.tt(r2[rb][:, 0:n], psr[rb][:, 0:n], sin_s[:, t0:t0 + n], ALU.mult, reads=[("psr", rb), "sin_s"], writes=[("r2", rb)])
                    P.tt(hob[hb][:, 0:n], r1[rb][:, 0:n], r2[rb][:, 0:n], ALU.add, reads=[("r1", rb), ("r2", rb)], writes=[("hob", hb)], eng="gpsimd")
                    P.dma("gpsimd", hTb[(m - 5) * 128:(m - 4) * 128, t0:t0 + n], hob[hb][:, 0:n], reads=[("hob", hb)], chan=f"stb{hb}")
                pending.append(rope)
            else:
                P.copy(hob[hb][:, 0:n], psm[pb][:, 0:n], reads=[("psm", pb)], writes=[("hob", hb)], eng="scalar")
                P.dma("gpsimd", hTb[(m - 5) * 128:(m - 4) * 128, t0:t0 + n], hob[hb][:, 0:n], reads=[("hob", hb)], chan=f"stb{hb}")
    pre(0)
    for si in range(len(SLABS)):
        if si + 1 < len(SLABS):
            pre(si + 1)
        main(si)
    flush()
    return P.finish()


def fnet_consts():
    n1 = np.arange(128); a = 2 * np.pi * np.outer(n1, n1) / 128
    cs128 = np.concatenate([np.cos(a), -np.sin(a)], 1).astype(np.float32)
    n2 = np.arange(64); a = 2 * np.pi * np.outer(n2, n2) / 64
    C64, S64 = np.cos(a), np.sin(a)
    fb1 = np.concatenate([C64, -S64], 1).astype(np.float32); fb2 = np.concatenate([S64, C64], 1).astype(np.float32)
    a = 2 * np.pi * np.outer(n2, n1) / 8192
    tw = np.concatenate([np.cos(a), np.sin(a)], 1).astype(np.float32)
    fc = (np.concatenate([C64, S64], 1) / np.sqrt(8192 * 64)).astype(np.float32)
    n = np.arange(256); a = 2 * np.pi * np.outer(n, n) / 256
    cs = np.concatenate([np.cos(a), -np.sin(a)], 1)
    cs256 = np.ascontiguousarray(cs.reshape(2, 128, 512).transpose(1, 0, 2)).astype(np.float32)
    fcc = (np.concatenate([C64, S64], 1) / np.sqrt(256 * 64)).astype(np.float32)
    return dict(cs128=cs128, fb1=fb1, fb2=fb2, tw=tw, fc=fc, cs256=cs256, fcc=fcc)

def build_fnet(with_ctx):
    P = Prog()
    z = P.dram_in("z", [128, 4096]); cs128 = P.dram_in("cs128", [128, 256])
    fb1 = P.dram_in("fb1", [64, 128]); fb2 = P.dram_in("fb2", [64, 128]); tw = P.dram_in("tw", [64, 256]); fc = P.dram_in("fc", [64, 128])
    R = P.dram_out("R", [64, 8192])
    zs = P.sbuf("zs", [128, 64, 64]); cs_s = P.sbuf("cs_s", [128, 256])
    fb1_s = P.sbuf("fb1_s", [64, 128], BF16); fb2_s = P.sbuf("fb2_s", [64, 128], BF16); tw_s = P.sbuf("tw_s", [64, 256]); fc_s = P.sbuf("fc_s", [64, 128], BF16)
    Ych = [P.sbuf(f"Ych{i}", [64, 8, 256]) for i in range(2)]
    ta = P.sbuf("ta", [64, 8, 128]); tb = P.sbuf("tb", [64, 8, 128]); tc = P.sbuf("tc", [64, 8, 128]); td = P.sbuf("td", [64, 8, 128])
    Yp = P.sbuf("Yp", [64, 64, 256], BF16); X1 = P.sbuf("X1", [64, 2, 8192], BF16)
    Rs = [P.sbuf(f"Rs{i}", [64, 512]) for i in range(2)]
    psA = [P.psum(f"psA{i}") for i in range(3)]; psB = [P.psum(f"psB{i}") for i in range(3)]; psC = [P.psum(f"psC{i}") for i in range(2)]
    P.dma("sync", zs[:].rearrange("p a b -> p (a b)"), z, writes=["zs"], chan="ld0")
    for (s, d, nm) in ((cs_s, cs128, "cs_s"), (tw_s, tw, "tw_s")):
        P.dma("sync", s[:], d, writes=[nm], chan="ld1")
    for (s, d, nm) in ((fb1_s, fb1, "fb1_s"), (fb2_s, fb2, "fb2_s"), (fc_s, fc, "fc_s")):
        P.dma("gpsimd", s[:], d, writes=[nm], chan="ldc")
    twr = tw_s[:, 0:128].rearrange("p (o k) -> p o k", o=1).to_broadcast([64, 8, 128])
    tws = tw_s[:, 128:256].rearrange("p (o k) -> p o k", o=1).to_broadcast([64, 8, 128])
    na_ = 0
    for ch in range(8):
        cb = ch % 2
        for pair in range(4):
            pb = na_ % 3; na_ += 1
            for h in range(2):
                c = ch * 8 + pair * 2 + h
                P.mm(psA[pb][0:64, h * 256:(h + 1) * 256], zs[:, :, c], cs_s[:], True, True, reads=["zs", "cs_s"], writes=[("psA", pb)])
            P.copy(Ych[cb][:, pair * 2:pair * 2 + 2, :].rearrange("p a b -> p (a b)"), psA[pb][0:64, :], reads=[("psA", pb)],
                   writes=[("Ych", cb)], eng="scalar")
        Yr = Ych[cb][:, :, 0:128]; Yi = Ych[cb][:, :, 128:256]; c0 = ch * 8
        P.tt(ta[:], Yr, twr, ALU.mult, reads=[("Ych", cb), "tw_s"], writes=["ta"])
        P.tt(tb[:], Yi, tws, ALU.mult, reads=[("Ych", cb), "tw_s"], writes=["tb"])
        P.tt(Yp[:, c0:c0 + 8, 0:128], ta[:], tb[:], ALU.add, reads=["ta", "tb"], writes=[("Yp", ch)])
        P.tt(tc[:], Yi, twr, ALU.mult, reads=[("Ych", cb), "tw_s"], writes=["tc"])
        P.tt(td[:], Yr, tws, ALU.mult, reads=[("Ych", cb), "tw_s"], writes=["td"])
        P.tt(Yp[:, c0:c0 + 8, 128:256], tc[:], td[:], ALU.subtract, reads=["tc", "td"], writes=[("Yp", ch)])
    allYp = [("Yp", ch) for ch in range(8)]
    X1v = X1[:].rearrange("p c (k2 k1) -> p c k2 k1", k1=128)
    for g in range(32):
        pb = g % 3
        for q in range(4):
            k1 = 4 * g + q
            P.mm(psB[pb][0:64, q * 128:(q + 1) * 128], Yp[:, :, k1], fb1_s[:], True, False, reads=allYp + ["fb1_s"], writes=[("psB", pb)])
            P.mm(psB[pb][0:64, q * 128:(q + 1) * 128], Yp[:, :, 128 + k1], fb2_s[:], False, True, reads=allYp + ["fb2_s"], writes=[("psB", pb)])
        pv = psB[pb][0:64, :].rearrange("p (q c k) -> p c k q", q=4, c=2)
        for comp in range(2):
            P.copy(X1v[:, comp, :, 4 * g:4 * g + 4], pv[:, comp, :, :], reads=[("psB", pb)], writes=[("X1", g)],
                   eng=("scalar" if comp == 0 else "vector"))
    allX1 = [("X1", g) for g in range(32)]
    for blk in range(16):
        pb = blk % 2
        P.mm(psC[pb][0:64, :], fc_s[:, 0:64], X1[:, 0, blk * 512:(blk + 1) * 512], True, False, reads=allX1 + ["fc_s"], writes=[("psC", pb)])
        P.mm(psC[pb][0:64, :], fc_s[:, 64:128], X1[:, 1, blk * 512:(blk + 1) * 512], False, True, reads=allX1 + ["fc_s"], writes=[("psC", pb)])
        P.copy(Rs[pb][:], psC[pb][0:64, :], reads=[("psC", pb)], writes=[("Rs", pb)], eng="scalar")
        P.dma("gpsimd", R[:, blk * 512:(blk + 1) * 512], Rs[pb][:], reads=[("Rs", pb)], chan=f"st{pb}")
    if with_ctx:
        zc = P.dram_in("zc", [128, 2, 64]); cs256 = P.dram_in("cs256", [128, 2, 512]); fcc = P.dram_in("fcc", [64, 128])
        Rc = P.dram_out("Rc", [64, 256])
        zc_s = P.sbuf("zc_s", [128, 2, 64]); c2_s = P.sbuf("c2_s", [128, 2, 512]); fcc_s = P.sbuf("fcc_s", [64, 128])
        Pc = P.sbuf("Pc", [64, 512]); Rc_s = P.sbuf("Rc_s", [64, 256])
        P.dma("sync", zc_s[:], zc, writes=["zc_s"], chan="ld2"); P.dma("sync", c2_s[:], cs256, writes=["c2_s"], chan="ld2")
        P.dma("sync", fcc_s[:], fcc, writes=["fcc_s"], chan="ld2")
        for t in range(2):
            P.mm(psA[0][0:64, :], zc_s[:, t, :], c2_s[:, t, :], t == 0, t == 1, reads=["zc_s", "c2_s"], writes=[("psA", 0)])
        P.copy(Pc[:], psA[0][0:64, :], reads=[("psA", 0)], writes=["Pc"])
        P.mm(psA[1][0:64, 0:256], fcc_s[:, 0:64], Pc[:, 0:256], True, False, reads=["Pc", "fcc_s"], writes=[("psA", 1)])
        P.mm(psA[1][0:64, 0:256], fcc_s[:, 64:128], Pc[:, 256:512], False, True, reads=["Pc", "fcc_s"], writes=[("psA", 1)])
        P.copy(Rc_s[:], psA[1][0:64, 0:256], reads=[("psA", 1)], writes=["Rc_s"])
        P.dma("gpsimd", Rc, Rc_s[:], reads=["Rc_s"], chan="st")
    return P.finish()

NEG = -30000.0

def build_na(with_ctx):
    P = Prog()
    qT = P.dram_in("qT", [384, 2048]); kwT = P.dram_in("kwT", [16, 384, 576]); vw = P.dram_in("vw", [16, 128, 5, 384])
    kcT = P.dram_in("kcT", [384, 256]); vc = P.dram_in("vc", [128, 2, 384])
    tbraw = P.dram_in("tbraw", [5, 128, 6, 576]); mask = P.dram_in("mask", [5, 128, 576]); ident = P.dram_in("ident", [128, 128])
    Y = P.dram_out("Y", [2048, 384])
    qb = P.sbuf("qb", [128, 3, 2048], BF16); kcb = P.sbuf("kcb", [128, 3, 256], BF16); vcb = P.sbuf("vcb", [128, 2, 384], BF16)
    TB = P.sbuf("TB", [128, 5, 6, 576]); mk = P.sbuf("mk", [128, 5, 576])
    idb = P.sbuf("idb", [128, 128], BF16)
    kb = [P.sbuf(f"kb{i}", [128, 3, 576], BF16) for i in range(2)]; vb = [P.sbuf(f"vb{i}", [128, 5, 384], BF16) for i in range(2)]
    S = [P.sbuf(f"S{i}", [128, 832]) for i in range(4)]; Pb = [P.sbuf(f"Pb{i}", [128, 832], BF16) for i in range(4)]
    PT = [P.sbuf(f"PT{i}", [128, 896], BF16) for i in range(4)]
    Osb = [P.sbuf(f"Osb{i}", [128, 384]) for i in range(2)]
    mx = [P.sbuf(f"mx{i}", [128, 1]) for i in range(8)]; ssum = [P.sbuf(f"ssum{i}", [128, 1]) for i in range(8)]
    rinv = [P.sbuf(f"rinv{i}", [128, 1]) for i in range(8)]
    psA = [P.psum(f"psA{i}") for i in range(2)]; psB = [P.psum(f"psB{i}") for i in range(2)]
    psT = [P.psum(f"psT{i}", [128, 1024], BF16) for i in range(2)]; psO = [P.psum(f"psO{i}") for i in range(2)]
    si = [0]
    def load_cast(dst, src_ap, n, dres):
        P.dma("gpsimd", dst, src_ap, writes=[dres], chan="ldc", max_dma_last_dim=4096)
    qv = qT.rearrange("(c p) n -> p c n", p=128)
    for c in range(3):
        load_cast(qb[:, c, :], qv[:, c, :], 2048, "qb")
    kcv = kcT.rearrange("(c p) n -> p c n", p=128)
    for c in range(3):
        load_cast(kcb[:, c, :], kcv[:, c, :], 256, "kcb")
    load_cast(vcb[:].rearrange("p a b -> p (a b)"), vc.rearrange("p a b -> p (a b)"), 768, "vcb")
    load_cast(idb[:], ident, 128, "idb")
    for ty in range(5):
        P.dma("sync", TB[:, ty, :, :], tbraw[ty], writes=[("TB", ty)], chan="ld0")
    P.dma("sync", mk[:], mask.rearrange("t p n -> p t n"), writes=["mk"], chan="ld0")
    for ty in range(5):
        mb = mk[:, ty, :].rearrange("p (o n) -> p o n", o=1).to_broadcast([128, 6, 576])
        P.tt(TB[:, ty, :, :], TB[:, ty, :, :], mb, ALU.add, reads=[("TB", ty), "mk"], writes=[("TB", ty)])
    tiles = [("main", t) for t in range(16)]
    if with_ctx:
        qcT = P.dram_in("qcT", [384, 256]); Yc = P.dram_out("Yc", [256, 384])
        qcb = P.sbuf("qcb", [128, 3, 256], BF16)
        qcv = qcT.rearrange("(c p) n -> p c n", p=128)
        for c in range(3):
            load_cast(qcb[:, c, :], qcv[:, c, :], 256, "qcb")
        tiles += [("ctx", 0), ("ctx", 1)]
    units = [(ti, kind, t, h) for ti, (kind, t) in enumerate(tiles) for h in range(6)]
    loaded = set()
    def tile_load(ti, kind, t):
        if ti in loaded or kind != "main":
            return
        loaded.add(ti)
        wb = ti % 2
        P.dma("gpsimd", kb[wb][:], kwT[t].rearrange("(c p) n -> p c n", p=128), writes=[("kb", wb)], chan=f"kb{wb}", max_dma_last_dim=4096)
        P.dma("gpsimd", vb[wb][:], vw[t], writes=[("vb", wb)], chan=f"vb{wb}", max_dma_last_dim=4096)
    def info(u):
        ti, kind, t, h = units[u]
        return ti, kind, t, h, u % 2, u % 4, ti % 2, h // 2, (h % 2) * 64, u % 8
    def stA(u):
        ti, kind, t, h, b, b3, wb, c, p0, b8 = info(u)
        tile_load(ti, kind, t)
        if kind == "main":
            qs = qb[p0:p0 + 64, c, t * 128:(t + 1) * 128]; qres = "qb"
        else:
            qs = qcb[p0:p0 + 64, c, t * 128:(t + 1) * 128]; qres = "qcb"
        P.mm(psB[b][:, 64:320], qs, kcb[p0:p0 + 64, c, :], True, True, reads=[qres, "kcb"], writes=[("psB", b)])
        if kind == "main":
            P.mm(psA[b][:, 0:512], qs, kb[wb][p0:p0 + 64, c, 0:512], True, True, reads=[qres, ("kb", wb)], writes=[("psA", b)])
            P.mm(psB[b][:, 0:64], qs, kb[wb][p0:p0 + 64, c, 512:576], True, True, reads=[qres, ("kb", wb)], writes=[("psB", b)])
        P.act(S[b3][:, 0:256], psB[b][:, 64:320], AF.Copy, reads=[("psB", b)], writes=[("Sc", b3)], scale=0.125)
    def stB(u):
        ti, kind, t, h, b, b3, wb, c, p0, b8 = info(u)
        W = 832 if kind == "main" else 256
        if kind == "main":
            ty = {0: 0, 1: 1, 14: 3, 15: 4}.get(t, 2)
            P.stt(S[b3][:, 256:768], psA[b][:, 0:512], 0.125, TB[:, ty, h, 0:512], ALU.mult, ALU.add,
                  reads=[("psA", b), ("TB", ty)], writes=[("Sw", b3)])
            P.stt(S[b3][:, 768:832], psB[b][:, 0:64], 0.125, TB[:, ty, h, 512:576], ALU.mult, ALU.add,
                  reads=[("psB", b), ("TB", ty)], writes=[("Sw2", b3)])
        P.op("vector", lambda e: e.tensor_reduce(out=mx[b8][:], in_=S[b3][:, 0:W], axis=AX.X, op=ALU.max, negate=True),
             reads=[("Sc", b3), ("Sw", b3), ("Sw2", b3)], writes=[("mx", b8)])
        P.act(Pb[b3][:, 0:W], S[b3][:, 0:W], AF.Exp, reads=[("Sc", b3), ("Sw", b3), ("Sw2", b3), ("mx", b8)], writes=[("Pb", b3), ("ssum", b8)],
              bias=mx[b8][:], scale=1.0, accum_out=ssum[b8][:])
    def stC(u):
        ti, kind, t, h, b, b3, wb, c, p0, b8 = info(u)
        nblk = 7 if kind == "main" else 2
        for kbk in range(nblk):
            kw = 64 if kbk == 6 else 128
            P.op("tensor", lambda e, kbk=kbk, kw=kw: e.transpose(psT[b][0:kw, kbk * 128:(kbk + 1) * 128], Pb[b3][:, kbk * 128:kbk * 128 + kw], idb[:]),
                 reads=[("Pb", b3), "idb"], writes=[("psT", b)], nosync_same=True)
        P.copy(PT[b3][:, 0:nblk * 128], psT[b][:, 0:nblk * 128], reads=[("psT", b)], writes=[("PT", b3)], eng="scalar")
    def stD(u):
        ti, kind, t, h, b, b3, wb, c, p0, b8 = info(u)
        ob = ti % 2
        nblk = 7 if kind == "main" else 2
        for kbk in range(nblk):
            kw = 64 if kbk == 6 else 128
            if kbk < 2:
                rhs = vcb[:, kbk, h * 64:(h + 1) * 64]; rres = "vcb"
            else:
                rhs = vb[wb][0:kw, kbk - 2, h * 64:(h + 1) * 64]; rres = ("vb", wb)
            P.mm(psO[ob][:, h * 64:(h + 1) * 64], PT[b3][0:kw, kbk * 128:(kbk + 1) * 128], rhs, kbk == 0, kbk == nblk - 1,
                 reads=[("PT", b3), rres], writes=[("psO", ob)])
        P.op("vector", lambda e: e.reciprocal(out=rinv[b8][:], in_=ssum[b8][:]), reads=[("ssum", b8)], writes=[("rinv", b8)])
        P.ts(Osb[ob][:, h * 64:(h + 1) * 64], psO[ob][:, h * 64:(h + 1) * 64], rinv[b8][:], None, ALU.mult,
             reads=[("psO", ob), ("rinv", b8)], writes=[("Osb", ob)])
        if h == 5:
            dst = Y[t * 128:(t + 1) * 128, :] if kind == "main" else Yc[t * 128:(t + 1) * 128, :]
            P.dma("gpsimd", dst, Osb[ob][:], reads=[("Osb", ob)], chan=f"st{ob}")
    NU = len(units)
    LC, LD = 3, 5
    for step in range(NU + LD):
        if step < NU: stA(step)
        if 0 <= step - 1 < NU: stB(step - 1)
        if 0 <= step - LC < NU: stC(step - LC)
        if 0 <= step - LD < NU: stD(step - LD)
    return P.finish()

def na_tile_geometry(r0):
    R = 128
    rs0 = int(np.clip(r0 - 4, 0, R - 8)); rs1 = int(np.clip(r0 + 1 - 4, 0, R - 8))
    return rs0, rs1

def na_tables(rpb, q):
    cols = np.arange(64); cs = np.clip(cols - 8, 0, 48)
    kc = np.arange(64)
    inwin = (kc[None, :] >= cs[:, None]) & (kc[None, :] < cs[:, None] + 16)
    dc = np.clip(kc[None, :] - cols[:, None] + 15, 0, 30)
    types = [32 * q, 32 * q + 2, 32 * q + 16, 32 * q + 28, 32 * q + 30]
    tbraw = np.zeros((5, 128, 6, 9, 64), np.float32); mask = np.zeros((5, 128, 9, 64), np.float32)
    for ti, r0 in enumerate(types):
        rs0, rs1 = na_tile_geometry(r0)
        for half, (r, rs) in enumerate(((r0, rs0), (r0 + 1, rs1))):
            for slot in range(9):
                krow = rs0 + slot
                valid = (krow >= rs) and (krow < rs + 8)
                dr = int(np.clip(krow - r + 7, 0, 14))
                g = rpb[:, dr][:, dc]
                tbraw[ti, half * 64:(half + 1) * 64, :, slot, :] = g.transpose(1, 0, 2)
                m = np.where(inwin & valid, 0.0, NEG).astype(np.float32)
                mask[ti, half * 64:(half + 1) * 64, slot, :] = m
    return tbraw.reshape(5, 128, 6, 576), mask.reshape(5, 128, 576)

def na_windows(k_b, v_b, q):
    kp = np.concatenate([k_b, np.zeros((64 * 16, 384), k_b.dtype)], 0); vp = np.concatenate([v_b, np.zeros((64 * 16, 384), v_b.dtype)], 0)
    kwT = np.zeros((16, 384, 576), k_b.dtype); vw = np.zeros((16, 128, 5, 384), v_b.dtype)
    for t in range(16):
        r0 = 32 * q + 2 * t
        rs0, _ = na_tile_geometry(r0)
        kwT[t] = kp[rs0 * 64:(rs0 + 9) * 64].T
        for j in range(4):
            vw[t, :, j, :] = vp[(rs0 + 2 * j) * 64:(rs0 + 2 * j + 2) * 64]
        vw[t, 0:64, 4, :] = vp[(rs0 + 8) * 64:(rs0 + 9) * 64]
    return kwT, vw

NCH = 1056
PI = math.pi

class A:
    def __init__(self, P): self.P = P
    @staticmethod
    def nm(*aps): return [a.tensor.name for a in aps if hasattr(a, "tensor")]
    def tt(self, o, a, b, op, eng="vector"): self.P.tt(o, a, b, op, reads=self.nm(a, b), writes=self.nm(o), eng=eng)
    def ts(self, o, a, s1, op0, s2=None, op1=None, eng="vector"):
        self.P.ts(o, a, s1, s2, op0, op1, reads=self.nm(a, s1, s2), writes=self.nm(o), eng=eng)
    def stt(self, o, a, s, b, op0, op1): self.P.stt(o, a, s, b, op0, op1, reads=self.nm(a, s, b), writes=self.nm(o))
    def act(self, o, a, f, **kw): self.P.act(o, a, f, reads=self.nm(a, *[v for v in kw.values()]), writes=self.nm(o), **kw)
    def copy(self, o, a, eng="vector"): self.P.copy(o, a, reads=self.nm(a), writes=self.nm(o), eng=eng)
    def memset(self, o, v, eng="vector"): self.P.memset(o, v, writes=self.nm(o), eng=eng)
    def mm(self, o, l, r, st, sp): self.P.mm(o, l, r, st, sp, reads=self.nm(l, r), writes=self.nm(o))
    def dma_in(self, o, src, chan): self.P.dma("sync", o, src, writes=self.nm(o), chan=chan)
    def dma_out(self, dst, a, chan="st"): self.P.dma("gpsimd", dst, a, reads=self.nm(a), chan=chan)
    def scan(self, o, d0, d1, init):
        self.P.op("vector", lambda e: e.tensor_tensor_scan(out=o, data0=d0, data1=d1, initial=init, op0=ALU.mult, op1=ALU.add),
                  reads=self.nm(d0, d1, init), writes=self.nm(o))
    def recip(self, o, a): self.P.op("vector", lambda e: e.reciprocal(out=o, in_=a), reads=self.nm(a), writes=self.nm(o))
    def transpose(self, o, a, ident):
        self.P.op("tensor", lambda e: e.transpose(o, a, ident), reads=self.nm(a, ident), writes=self.nm(o), nosync_same=True)
    def cmul_s(self, o_re, o_im, a_re, a_im, s_re, s_im, s_imn):
        self.ts(o_re, a_re, s_re, ALU.mult)
        self.stt(o_re, a_im, s_imn, o_re, ALU.mult, ALU.add)
        self.ts(o_im, a_re, s_im, ALU.mult)
        self.stt(o_im, a_im, s_re, o_im, ALU.mult, ALU.add)

def build_ssm():
    P = Prog(); a = A(P)
    d_in = {}
    for nm_, shp in (("are", [128, 6]), ("aim", [128, 6]), ("ldt", [128, 6]), ("Bre", [128, 96]), ("Bim", [128, 96]),
                     ("Cre", [128, 96]), ("Cim", [128, 96]), ("maskF", [128, 128]), ("maskB", [128, 128]), ("sgn", [128, 1]),
                     ("ident", [128, 128])):
        d_in[nm_] = P.dram_in(nm_, shp)
    Ddiag = P.dram_in("Ddiag", [6, 128, 128]); U = P.dram_in("U", [6, 128, NCH]); Yg = P.dram_out("Yg", [6, 128, NCH])
    s = {}
    for nm_, ap in d_in.items():
        shp = list(ap.shape)
        s[nm_] = P.sbuf("s_" + nm_, shp)
        a.dma_in(s[nm_][:], ap, "ld0")
    def T(name, shape): return P.sbuf(name, shape)
    dt = T("dt", [128, 6]); x = T("x", [128, 6]); th = T("th", [128, 6]); er = T("er", [128, 6]); m = T("m", [128, 6])
    y2 = T("y2", [128, 6]); sn = T("sn", [128, 6]); cs = T("cs", [128, 6]); lbr = T("lbr", [128, 6]); lbi = T("lbi", [128, 6])
    n2 = T("n2", [128, 6]); t1 = T("t1", [128, 6]); t2 = T("t2", [128, 6]); am1 = T("am1", [128, 6])
    qr = T("qr", [128, 6]); qi = T("qi", [128, 6]); qin = T("qin", [128, 6])
    a.act(dt[:], s["ldt"][:], AF.Exp)
    a.tt(x[:], s["are"][:], dt[:], ALU.mult); a.tt(th[:], s["aim"][:], dt[:], ALU.mult)
    a.act(er[:], x[:], AF.Exp)
    for _ in range(4):
        a.ts(m[:], th[:], PI, ALU.is_gt)
        a.stt(th[:], m[:], -2 * PI, th[:], ALU.mult, ALU.add)
    a.ts(y2[:], th[:], PI / 2, ALU.add)
    a.ts(m[:], y2[:], PI, ALU.is_gt)
    a.stt(y2[:], m[:], -2 * PI, y2[:], ALU.mult, ALU.add)
    a.act(sn[:], th[:], AF.Sin); a.act(cs[:], y2[:], AF.Sin)
    a.tt(lbr[:], er[:], cs[:], ALU.mult); a.tt(lbi[:], er[:], sn[:], ALU.mult)
    a.tt(n2[:], s["are"][:], s["are"][:], ALU.mult); a.tt(t1[:], s["aim"][:], s["aim"][:], ALU.mult); a.tt(n2[:], n2[:], t1[:], ALU.add)
    a.recip(n2[:], n2[:])
    a.ts(am1[:], lbr[:], -1.0, ALU.add)
    a.tt(t1[:], am1[:], s["are"][:], ALU.mult); a.tt(t2[:], lbi[:], s["aim"][:], ALU.mult); a.tt(t1[:], t1[:], t2[:], ALU.add)
    a.tt(qr[:], t1[:], n2[:], ALU.mult)
    a.tt(t1[:], lbi[:], s["are"][:], ALU.mult); a.tt(t2[:], am1[:], s["aim"][:], ALU.mult); a.tt(t1[:], t1[:], t2[:], ALU.subtract)
    a.tt(qi[:], t1[:], n2[:], ALU.mult)
    a.ts(qin[:], qi[:], -1.0, ALU.mult)
    Lr = T("Lr", [128, 6, 9]); Li = T("Li", [128, 6, 9]); Vr = T("Vr", [128, 6, 8]); Vi = T("Vi", [128, 6, 8])
    Rr = T("Rr", [128, 6, 9]); Ri = T("Ri", [128, 6, 9])
    e2 = T("e2", [128, 6]); ivr = T("ivr", [128, 6]); ivi = T("ivi", [128, 6])
    a.memset(Lr[:, :, 0], 1.0); a.memset(Li[:, :, 0], 0.0); a.memset(Vr[:, :, 0], 1.0); a.memset(Vi[:, :, 0], 0.0)
    a.act(e2[:], x[:], AF.Exp, scale=-2.0)
    a.tt(ivr[:], lbr[:], e2[:], ALU.mult); a.tt(ivi[:], lbi[:], e2[:], ALU.mult); a.ts(ivi[:], ivi[:], -1.0, ALU.mult)
    def cmul_t(o_r, o_i, p_r, p_i, q_r, q_i):
        a.tt(t1[:], p_r, q_r, ALU.mult); a.tt(t2[:], p_i, q_i, ALU.mult); a.tt(o_r, t1[:], t2[:], ALU.subtract)
        a.tt(t1[:], p_r, q_i, ALU.mult); a.tt(t2[:], p_i, q_r, ALU.mult); a.tt(o_i, t1[:], t2[:], ALU.add)
    for k in range(8):
        cmul_t(Lr[:, :, k + 1], Li[:, :, k + 1], Lr[:, :, k], Li[:, :, k], lbr[:], lbi[:])
    for k in range(7):
        cmul_t(Vr[:, :, k + 1], Vi[:, :, k + 1], Vr[:, :, k], Vi[:, :, k], ivr[:], ivi[:])
    for k in range(9):
        a.copy(Rr[:, :, k], Lr[:, :, 8 - k], eng="gpsimd"); a.copy(Ri[:, :, k], Li[:, :, 8 - k], eng="gpsimd")
    tabs = {}
    for nm_, (lo_r, lo_i, hi_r, hi_i) in dict(
            XL=(Vr[0:64, :, 0:8], Vi[0:64, :, 0:8], Lr[64:128, :, 0:8], Li[64:128, :, 0:8]),
            YL=(Lr[0:64, :, 0:8], Li[0:64, :, 0:8], Vr[64:128, :, 0:8], Vi[64:128, :, 0:8]),
            SL=(Rr[0:64, :, 1:9], Ri[0:64, :, 1:9], Lr[64:128, :, 0:8], Li[64:128, :, 0:8]),
            OL=(Lr[0:64, :, 1:9], Li[0:64, :, 1:9], Rr[64:128, :, 0:8], Ri[64:128, :, 0:8])).items():
        tr = T(nm_ + "r", [128, 6, 8]); ti = T(nm_ + "i", [128, 6, 8]); tn = T(nm_ + "n", [128, 6, 8])
        a.copy(tr[0:64], lo_r); a.copy(ti[0:64], lo_i); a.copy(tr[64:128], hi_r); a.copy(ti[64:128], hi_i)
        a.ts(tn[:], ti[:], -1.0, ALU.mult)
        tabs[nm_] = (tr, ti, tn)
    rho8 = T("rho8", [128, 6]); c8 = T("c8", [128, 6]); s8 = T("s8", [128, 6]); e8 = T("e8", [128, 6])
    a.act(rho8[:], x[:], AF.Exp, scale=8.0); a.act(e8[:], x[:], AF.Exp, scale=-8.0)
    a.tt(c8[:], Lr[:, :, 8], e8[:], ALU.mult); a.tt(s8[:], Li[:, :, 8], e8[:], ALU.mult)
    a.ts(s8[:], s8[:], s["sgn"][:, 0:1], ALU.mult)
    onesT = T("onesT", [128, NCH]); a.memset(onesT[:], 1.0, eng="gpsimd")
    def bc_j(ap):
        return ap.rearrange("p g (o j) -> p g o j", o=1).to_broadcast([128, 6, 8, 16])
    def bc_k(ap):
        return ap.rearrange("p g (k o) -> p g k o", o=1).to_broadcast([128, 6, 8, 16])
    Bre_v = s["Bre"][:].rearrange("p (g j) -> p g j", j=16); Bim_v = s["Bim"][:].rearrange("p (g j) -> p g j", j=16)
    Cre_v = s["Cre"][:].rearrange("p (g j) -> p g j", j=16); Cim_v = s["Cim"][:].rearrange("p (g j) -> p g j", j=16)
    Bbr_a = T("Bbr_a", [128, 6, 16]); Bbi_a = T("Bbi_a", [128, 6, 16])
    u1 = T("u1", [128, 6, 16]); u2 = T("u2", [128, 6, 16])
    qr_b = qr[:].rearrange("p (g o) -> p g o", o=1).to_broadcast([128, 6, 16]); qi_b = qi[:].rearrange("p (g o) -> p g o", o=1).to_broadcast([128, 6, 16])
    a.tt(u1[:], Bre_v, qr_b, ALU.mult); a.tt(u2[:], Bim_v, qi_b, ALU.mult, eng="gpsimd"); a.tt(Bbr_a[:], u1[:], u2[:], ALU.subtract)
    a.tt(u1[:], Bre_v, qi_b, ALU.mult); a.tt(u2[:], Bim_v, qr_b, ALU.mult, eng="gpsimd"); a.tt(Bbi_a[:], u1[:], u2[:], ALU.add)
    v1 = T("v1", [128, 6, 8, 16]); v2 = T("v2", [128, 6, 8, 16])
    def ctab(name, Ar, Ai, tb, neg_im):
        o_r = T(name + "r_a", [128, 6, 8, 16]); o_i = T(name + "i_a", [128, 6, 8, 16])
        Sr, Si = tb[0][:], tb[1][:]
        a.tt(v1[:], bc_j(Ar), bc_k(Sr), ALU.mult); a.tt(v2[:], bc_j(Ai), bc_k(Si), ALU.mult, eng="gpsimd")
        a.tt(o_r[:], v1[:], v2[:], ALU.subtract)
        a.tt(v1[:], bc_j(Ar), bc_k(Si), ALU.mult); a.tt(v2[:], bc_j(Ai), bc_k(Sr), ALU.mult, eng="gpsimd")
        a.tt(o_i[:], v1[:], v2[:], ALU.add)
        if neg_im:
            a.ts(o_i[:], o_i[:], -1.0, ALU.mult)
        return o_r, o_i
    Xr_a, Xi_a = ctab("X", Bbr_a[:], Bbi_a[:], tabs["XL"], False)
    Wtr_a, Wti_a = ctab("Wt", Bbr_a[:], Bbi_a[:], tabs["SL"], False)
    Yr_a, Yin_a = ctab("Y", Cre_v, Cim_v, tabs["YL"], True)
    Wor_a, Woin_a = ctab("Wo", Cre_v, Cim_v, tabs["OL"], True)
    Tr_all = T("Tr_all", [128, 6, NCH + 1]); Ti_all = T("Ti_all", [128, 6, NCH + 1])
    mr = [T(f"mr{k}", [128, 6]) for k in range(11)]; mi = [T(f"mi{k}", [128, 6]) for k in range(11)]
    a.copy(mr[0][:], c8[:]); a.copy(mi[0][:], s8[:])
    for k in range(1, 11):
        a.tt(t1[:], mr[k - 1][:], mr[k - 1][:], ALU.mult); a.tt(t2[:], mi[k - 1][:], mi[k - 1][:], ALU.mult)
        a.tt(mr[k][:], t1[:], t2[:], ALU.subtract)
        a.tt(t1[:], mr[k - 1][:], mi[k - 1][:], ALU.mult); a.ts(mi[k][:], t1[:], 2.0, ALU.mult)
    a.memset(Tr_all[:, :, 0:1], 1.0); a.memset(Ti_all[:, :, 0:1], 0.0)
    z1 = T("z1", [128, 6, 512]); z2 = T("z2", [128, 6, 512])
    for k in range(11):
        n = 1 << k
        cnt_ = min(n, NCH + 1 - n)
        mrb = mr[k][:].rearrange("p (g o) -> p g o", o=1).to_broadcast([128, 6, cnt_])
        mib = mi[k][:].rearrange("p (g o) -> p g o", o=1).to_broadcast([128, 6, cnt_])
        a.tt(z1[:, :, 0:cnt_], Tr_all[:, :, 0:cnt_], mrb, ALU.mult); a.tt(z2[:, :, 0:cnt_], Ti_all[:, :, 0:cnt_], mib, ALU.mult)
        a.tt(Tr_all[:, :, n:n + cnt_], z1[:, :, 0:cnt_], z2[:, :, 0:cnt_], ALU.subtract)
        a.tt(z1[:, :, 0:cnt_], Tr_all[:, :, 0:cnt_], mib, ALU.mult); a.tt(z2[:, :, 0:cnt_], Ti_all[:, :, 0:cnt_], mrb, ALU.mult)
        a.tt(Ti_all[:, :, n:n + cnt_], z1[:, :, 0:cnt_], z2[:, :, 0:cnt_], ALU.add)
    Wsr = T("Wsr", [128, 128]); Wsi = T("Wsi", [128, 128]); Msb = T("Msb", [128, 128]); Mtmp = T("Mtmp", [128, 128]); Dd = T("Dd", [128, 128])
    Us = [T(f"Us{i}", [128, NCH]) for i in range(2)]
    Sre = T("Sre", [128, NCH]); Sim = T("Sim", [128, NCH]); Spr = T("Spr", [128, NCH]); Spi = T("Spi", [128, NCH])
    Gre = T("Gre", [128, NCH]); Gim = T("Gim", [128, NCH]); Hor = T("Hor", [128, NCH]); Hoi = T("Hoi", [128, NCH])
    Hir = T("Hir", [128, NCH]); Hii = T("Hii", [128, NCH])
    rhoT = T("rhoT", [128, NCH])
    w1 = T("w1", [128, NCH]); w2 = T("w2", [128, NCH]); w3 = T("w3", [128, NCH]); w4 = T("w4", [128, NCH])
    ini = T("ini", [128, 4]); Ysb = T("Ysb", [128, NCH])
    ps = [P.psum(f"ps{i}") for i in range(8)]
    BLK = [(0, 512), (512, 512), (1024, NCH - 1024)]
    for gi in range(6):
        ub = gi % 2
        a.dma_in(Us[ub][:], U[gi], f"u{ub}")
        a.dma_in(Dd[:], Ddiag[gi], "dd")
        Xr, Xi, Yr, Yin = Xr_a[:, gi], Xi_a[:, gi], Yr_a[:, gi], Yin_a[:, gi]
        Wtr, Wti, Wor, Woin = Wtr_a[:, gi], Wti_a[:, gi], Wor_a[:, gi], Woin_a[:, gi]
        f2 = lambda t_: t_.rearrange("p a b -> p (a b)")
        for half, pb in ((0, 6), (1, 7)):
            rows = slice(half * 64, half * 64 + 64)
            a.mm(ps[pb][:, 0:128], f2(Xr)[rows], f2(Yr)[rows], True, False)
            a.mm(ps[pb][:, 0:128], f2(Xi)[rows], f2(Yin)[rows], False, True)
        a.tt(Msb[:], ps[6][:, 0:128], s["maskF"][:], ALU.mult)
        a.tt(Mtmp[:], ps[7][:, 0:128], s["maskB"][:], ALU.mult)
        a.tt(Msb[:], Msb[:], Mtmp[:], ALU.add, eng="gpsimd"); a.tt(Msb[:], Msb[:], Dd[:], ALU.add, eng="gpsimd")
        a.transpose(ps[6][:, 128:256], f2(Wtr), s["ident"][:]); a.transpose(ps[7][:, 128:256], f2(Wti), s["ident"][:])
        a.copy(Wsr[:], ps[6][:, 128:256], eng="scalar"); a.copy(Wsi[:], ps[7][:, 128:256], eng="scalar")
        for bi, (c0, cn) in enumerate(BLK):
            a.mm(ps[bi][:, 0:cn], Wsr[:], Us[ub][:, c0:c0 + cn], True, True)
            a.mm(ps[3 + bi][:, 0:cn], Wsi[:], Us[ub][:, c0:c0 + cn], True, True)
            a.copy(Sre[:, c0:c0 + cn], ps[bi][:, 0:cn], eng="scalar"); a.copy(Sim[:, c0:c0 + cn], ps[3 + bi][:, 0:cn], eng="scalar")
        Tr = Tr_all[:, gi, :]; Ti = Ti_all[:, gi, :]
        a.act(rhoT[:], onesT[:], AF.Copy, scale=rho8[:, gi:gi + 1])
        a.tt(w1[:], Sre[:], Tr[:, 0:NCH], ALU.mult); a.tt(w3[:], Sim[:], Ti[:, 0:NCH], ALU.mult)
        a.tt(Spr[:], w1[:], w3[:], ALU.subtract)
        a.tt(w2[:], Sre[:], Ti[:, 0:NCH], ALU.mult, eng="gpsimd"); a.tt(w4[:], Sim[:], Tr[:, 0:NCH], ALU.mult, eng="gpsimd")
        a.tt(Spi[:], w2[:], w4[:], ALU.add, eng="gpsimd")
        for (Gx, Sx) in ((Gre, Spr), (Gim, Spi)):
            a.scan(Gx[0:64, :], rhoT[0:64, :], Sx[0:64, :], 0.0)
            a.scan(Gx[64:128, 0:32][:, ::-1], rhoT[64:128, 0:32], Sx[64:128, 0:32][:, ::-1], 0.0)
        lo = slice(64, 128)
        a.tt(ini[lo, 0:1], Gre[lo, 0:1], Tr[lo, NCH:NCH + 1], ALU.mult); a.tt(ini[lo, 1:2], Gim[lo, 0:1], Ti[lo, NCH:NCH + 1], ALU.mult)
        a.tt(ini[lo, 2:3], ini[lo, 0:1], ini[lo, 1:2], ALU.subtract)
        a.tt(ini[lo, 0:1], Gre[lo, 0:1], Ti[lo, NCH:NCH + 1], ALU.mult); a.tt(ini[lo, 1:2], Gim[lo, 0:1], Tr[lo, NCH:NCH + 1], ALU.mult)
        a.tt(ini[lo, 3:4], ini[lo, 0:1], ini[lo, 1:2], ALU.add)
        a.scan(Gre[lo, 32:NCH][:, ::-1], rhoT[lo, 32:NCH], Spr[lo, 32:NCH][:, ::-1], ini[lo, 2:3])
        a.scan(Gim[lo, 32:NCH][:, ::-1], rhoT[lo, 32:NCH], Spi[lo, 32:NCH][:, ::-1], ini[lo, 3:4])
        a.tt(w1[:], Gre[:], Tr[:, 0:NCH], ALU.mult); a.tt(w3[:], Gim[:], Ti[:, 0:NCH], ALU.mult)
        a.tt(Hor[:], w1[:], w3[:], ALU.add)
        a.tt(w2[:], Gim[:], Tr[:, 0:NCH], ALU.mult, eng="gpsimd"); a.tt(w4[:], Gre[:], Ti[:, 0:NCH], ALU.mult, eng="gpsimd")
        a.tt(Hoi[:], w2[:], w4[:], ALU.subtract, eng="gpsimd")
        for (Hi_, Ho_, Gx) in ((Hir, Hor, Gre), (Hii, Hoi, Gim)):
            a.copy(Hi_[0:64, 1:NCH], Ho_[0:64, 0:NCH - 1], eng="scalar"); a.memset(Hi_[0:64, 0:1], 0.0)
            a.copy(Hi_[lo, 0:NCH - 1], Ho_[lo, 1:NCH], eng="scalar"); a.memset(Hi_[lo, 31:32], 0.0)
            a.copy(Hi_[lo, NCH - 1:NCH], Gx[lo, 0:1])
        for bi, (c0, cn) in enumerate(BLK):
            a.mm(ps[bi][:, 0:cn], Msb[:], Us[ub][:, c0:c0 + cn], True, False)
            a.mm(ps[bi][:, 0:cn], f2(Wor), Hir[:, c0:c0 + cn], False, False)
            a.mm(ps[bi][:, 0:cn], f2(Woin), Hii[:, c0:c0 + cn], False, True)
            a.copy(Ysb[:, c0:c0 + cn], ps[bi][:, 0:cn], eng="scalar")
        a.dma_out(Yg[gi], Ysb[:])
    return P.finish()

def ssm_inputs(inp, l, j4, u_b, uc_b):
    gs = np.arange(6 * j4, 6 * j4 + 6)
    def rows(arr):
        return np.ascontiguousarray(arr[:, gs, :].transpose(0, 2, 1).reshape(128, 6))
    are = rows(inp["ssm_a_re"][l]); aim = rows(inp["ssm_a_im"][l])
    ldt = np.ascontiguousarray(np.repeat(inp["ssm_log_dt"][l][:, gs][:, None, :], 64, axis=1).reshape(128, 6))
    def rowsB(arr):
        return np.ascontiguousarray(arr[:, gs].transpose(0, 2, 1, 3).reshape(128, 96))
    def rowsC(arr):
        return np.ascontiguousarray(arr[:, gs].transpose(0, 3, 1, 2).reshape(128, 96))
    s_ = np.arange(8)
    mF = (s_[None, :] >= s_[:, None]).astype(np.float32)
    maskF = np.kron(mF, np.ones((16, 16), np.float32)); maskB = np.kron(mF.T, np.ones((16, 16), np.float32))
    sgn = np.concatenate([-np.ones((64, 1), np.float32), np.ones((64, 1), np.float32)], 0)
    dsk = inp["ssm_d"][l]
    Dd = np.zeros((6, 128, 128), np.float32)
    for gi, g in enumerate(gs):
        dd = np.zeros((8, 16, 8, 16), np.float32)
        for t in range(8):
            dd[t, np.arange(16), t, np.arange(16)] = dsk[16 * g:16 * g + 16]
        Dd[gi] = dd.reshape(128, 128)
    seq = np.concatenate([uc_b, u_b], 0)
    U = np.zeros((6, 128, NCH), np.float32)
    for gi, g in enumerate(gs):
        U[gi] = seq[:, 16 * g:16 * g + 16].reshape(NCH, 128).T
    return dict(are=are, aim=aim, ldt=ldt, Bre=rowsB(inp["ssm_b_re"][l]), Bim=rowsB(inp["ssm_b_im"][l]),
                Cre=rowsC(inp["ssm_c_re"][l]), Cim=rowsC(inp["ssm_c_im"][l]), maskF=maskF, maskB=maskB, sgn=sgn,
                ident=np.eye(128, dtype=np.float32), Ddiag=Dd, U=U)

def ssm_unpack(Yg):
    return np.ascontiguousarray(Yg.transpose(2, 1, 0).reshape(NCH, 8, 16, 6).transpose(0, 1, 3, 2).reshape(NCH * 8, 96))


_PROGS = {}
def _prog(name, fn):
    if name not in _PROGS:
        _PROGS[name] = fn()
    return _PROGS[name]

def _run(nc, maps):
    res = run_bass_kernel_spmd(nc, maps, core_ids=list(range(8)))
    return res.results

def kernel(x, c, ctx, c_ctx, w_mod, b_mod, g_pre_mix, g_post_mix, w_in, ssm_a_re, ssm_a_im, ssm_log_dt, ssm_b_re, ssm_b_im,
           ssm_c_re, ssm_c_im, ssm_d, w_glu, w_fourier, na_rpb, w_out, g_pre_ffn, g_post_ffn, w_ffn_gate, w_ffn_up, w_ffn_down):
    f32 = lambda a: np.ascontiguousarray(np.asarray(a, dtype=np.float32))
    inp = dict(ssm_a_re=f32(ssm_a_re), ssm_a_im=f32(ssm_a_im), ssm_log_dt=f32(ssm_log_dt), ssm_b_re=f32(ssm_b_re), ssm_b_im=f32(ssm_b_im),
               ssm_c_re=f32(ssm_c_re), ssm_c_im=f32(ssm_c_im), ssm_d=f32(ssm_d))
    x = f32(x); c = f32(c); ctx = f32(ctx); c_ctx = f32(c_ctx); w_mod = f32(w_mod); b_mod = f32(b_mod)
    w_in = f32(w_in); w_glu = f32(w_glu); w_fourier = f32(w_fourier); na_rpb = f32(na_rpb); w_out = f32(w_out)
    g_pre_mix = f32(g_pre_mix); g_post_mix = f32(g_post_mix); g_pre_ffn = f32(g_pre_ffn); g_post_ffn = f32(g_post_ffn)
    w_ffn_gate = f32(w_ffn_gate); w_ffn_up = f32(w_ffn_up); w_ffn_down = f32(w_ffn_down)
    DEPTH = 2
    cores = [(k // 4, k % 4) for k in range(8)]
    cTs = [np.ascontiguousarray(np.concatenate([colT(c[b], 8), colT(c_ctx, 8)], axis=1)) for b in range(2)]
    xT = [np.ascontiguousarray(np.concatenate([x[b, q * 2048:(q + 1) * 2048].T, ctx[b].T], axis=1)) for (b, q) in cores]
    KF = fnet_consts(); permm = perm_matrix(); ident = np.eye(128, dtype=np.float32)
    ropes = [rope_tables(q * 2048, 2048) for q in range(4)]
    for l in range(DEPTH):
        maps = []
        for k, (b, q) in enumerate(cores):
            maps.append(dict(xT=xT[k], w_in=w_in[l], w_mod=np.ascontiguousarray(w_mod[l][:, 0:2048]), b_modT=colT(b_mod[l][0:2048], 16),
                             g_preT=colT(g_pre_mix[l], 8), cT=cTs[b], cos=ropes[q][0], sin=ropes[q][1], perm=permm))
        res = _run(_prog("l1", build_l1), maps)
        hfull = [np.concatenate([res[k]["hT"], res[k]["hTb"]], 0) for k in range(8)]
        h_lat = [np.concatenate([hfull[4 * b + q][:, 0:2048].T for q in range(4)], 0) for b in range(2)]
        h_ctx = [np.ascontiguousarray(hfull[4 * b][:, 2048:2304].T) for b in range(2)]
        del hfull
        del res
        maps = [ssm_inputs(inp, l, j4, h_lat[b][:, 0:384], h_ctx[b][:, 0:384]) for (b, j4) in cores]
        res = _run(_prog("ssm", build_ssm), maps)
        ys = [[ssm_unpack(res[4 * b + j4]["Yg"]) for j4 in range(4)] for b in range(2)]
        ysT = [np.ascontiguousarray(np.concatenate(ys[b], 1).T) for b in range(2)]
        del res, ys
        maps = []
        for (b, g) in cores:
            m = dict(z=np.ascontiguousarray(h_lat[b][:, 384 + 64 * g:448 + 64 * g].reshape(128, 4096)),
                     zc=np.ascontiguousarray(h_ctx[b][:, 384 + 64 * g:448 + 64 * g].reshape(2, 128, 64).transpose(1, 0, 2)))
            m.update(KF); maps.append(m)
        res = _run(_prog("fnet", lambda: build_fnet(True)), maps)
        mxT = [np.concatenate([res[4 * b + g]["R"] for g in range(4)], 0) for b in range(2)]
        mxcT = [np.concatenate([res[4 * b + g]["Rc"] for g in range(4)], 0) for b in range(2)]
        del res
        maps = []
        for (b, q) in cores:
            kwT, vw = na_windows(h_lat[b][:, 1024:1408], h_lat[b][:, 1408:1792], q)
            tbraw, mask = na_tables(na_rpb[l], q)
            maps.append(dict(qT=np.ascontiguousarray(h_lat[b][q * 2048:(q + 1) * 2048, 640:1024].T), kwT=kwT, vw=vw,
                             kcT=np.ascontiguousarray(h_ctx[b][:, 1024:1408].T),
                             vc=np.ascontiguousarray(h_ctx[b][:, 1408:1792].reshape(2, 128, 384).transpose(1, 0, 2)),
                             tbraw=tbraw, mask=mask, ident=ident, qcT=np.ascontiguousarray(h_ctx[b][:, 640:1024].T)))
        res = _run(_prog("na", lambda: build_na(True)), maps)
        naT = [np.ascontiguousarray(np.concatenate([res[4 * b + q]["Y"] for q in range(4)], 0).T) for b in range(2)]
        nacT = [np.ascontiguousarray(res[4 * b]["Yc"].T) for b in range(2)]
        del res, h_lat
        maps = []
        for k, (b, q) in enumerate(cores):
            sl = slice(q * 2048, (q + 1) * 2048)
            maps.append(dict(xT=xT[k], ysT=np.ascontiguousarray(np.concatenate([ysT[b][:, 256 + q * 2048:256 + (q + 1) * 2048], ysT[b][:, 0:256]], 1)),
                             mxT=np.ascontiguousarray(np.concatenate([mxT[b][:, sl], mxcT[b]], 1)),
                             naT=np.ascontiguousarray(np.concatenate([naT[b][:, sl], nacT[b]], 1)),
                             w_mod=np.ascontiguousarray(w_mod[l][:, 2048:3072]), b_modT=colT(b_mod[l][2048:3072], 8), cT=cTs[b],
                             g_postT=colT(g_post_mix[l], 8), w_glu=w_glu[l], w_fourier=w_fourier[l], w_out=w_out[l]))
        res = _run(_prog("l3a", lambda: build_l3a(True)), maps)
        xT = [res[k]["xoT"] for k in range(8)]
        del res
        maps = []
        for k, (b, q) in enumerate(cores):
            maps.append(dict(xT=xT[k], w_mod=np.ascontiguousarray(w_mod[l][:, 3072:6144]), b_modT=colT(b_mod[l][3072:6144], 24), cT=cTs[b],
                             g_preT=colT(g_pre_ffn[l], 8), g_postT=colT(g_post_ffn[l], 8),
                             w_gate=w_ffn_gate[l], w_up=w_ffn_up[l], w_down=w_ffn_down[l]))
        res = _run(_prog("l3b", lambda: build_l3b(True)), maps)
        xT = [np.ascontiguousarray(res[k]["xoT"]) for k in range(8)]
        del res
    out = np.empty((2, 8192, 1024), np.float32)
    for k, (b, q) in enumerate(cores):
        out[b, q * 2048:(q + 1) * 2048] = xT[k][:, 0:2048].T
    return out
```
